# Optimizing a Trainium2 kernel written in Bass

```python
import math
import jax, jax.numpy as jnp
from jax import lax
import numpy as np

D_MODEL = 4096
BATCH = 4
SEQ = 2048
DEPTH = 2
DEC_BATCH = 16
DEC_SEQ = 16
PAST_LEN = 2048

CHUNK = 64
Q_BLOCK = 128
FOX_HEADS = 16
FOX_HEAD_DIM = 128
FOX_WIDTH = FOX_HEADS * FOX_HEAD_DIM
MLA_HEADS = 16
NOPE_DIM = 128
ROPE_DIM = 64
V_DIM = 128
Q_LORA = 1024
KV_LORA = 512
MLA_WIDTH = MLA_HEADS * V_DIM
MIX_WIDTH = FOX_WIDTH + MLA_WIDTH
ROPE_THETA = 10000.0
FORGET_BIAS_INIT = 3.0
EPS = 1e-6
IN_SIZES = (FOX_WIDTH, FOX_WIDTH, FOX_WIDTH, FOX_HEADS, FOX_WIDTH,
            Q_LORA, KV_LORA, ROPE_DIM, MLA_WIDTH)
N_IN = sum(IN_SIZES)
IN_OFFSETS = tuple(int(o) for o in np.cumsum(IN_SIZES)[:-1])
FOX_SCALE = 1.0 / math.sqrt(FOX_HEAD_DIM)
MLA_SCALE = 1.0 / math.sqrt(NOPE_DIM + ROPE_DIM)

kernel_name = "fox_mla_parallel_heads_stream_step"


def rmsnorm(x, g):
    xf = x.astype(jnp.float32)
    xf = xf * lax.rsqrt(jnp.mean(xf * xf, axis=-1, keepdims=True) + EPS)
    return (xf * g.astype(jnp.float32)).astype(x.dtype)


def rope(x, pos):
    half = x.shape[-1] // 2
    inv_freq = 1.0 / (ROPE_THETA ** (jnp.arange(half, dtype=jnp.float32) / half))
    ang = pos.astype(jnp.float32)[:, None] * inv_freq[None, :]
    shape = (1, pos.shape[0]) + (1,) * (x.ndim - 3) + (half,)
    cos = jnp.cos(ang).reshape(shape)
    sin = jnp.sin(ang).reshape(shape)
    xf = x.astype(jnp.float32)
    x1, x2 = xf[..., :half], xf[..., half:]
    return jnp.concatenate([x1 * cos - x2 * sin, x2 * cos + x1 * sin], axis=-1).astype(x.dtype)


def swept_attention(score_fn, q_inputs, q_pos, k_pos, v, per_frame):
    sq = q_pos.shape[0]
    blk = min(Q_BLOCK, sq)
    nb = sq // blk

    def split(a):
        return jnp.moveaxis(a.reshape((a.shape[0], nb, blk) + a.shape[2:]), 1, 0)

    xs = tuple(split(a) for a in q_inputs) + (q_pos.reshape(nb, blk),)

    def one(args):
        qi, pi = args[:-1], args[-1]
        s = score_fn(*qi)
        if per_frame:
            allowed = k_pos[None, :] <= pi[:, None]
        else:
            allowed = (k_pos[None, :] // CHUNK) <= (pi[:, None] // CHUNK)
        s = jnp.where(allowed[None, None], s, -jnp.inf)
        p = jax.nn.softmax(s, axis=-1).astype(v.dtype)
        return jnp.einsum('bhqk,bkhd->bqhd', p, v)

    out = lax.map(one, xs)
    out = jnp.moveaxis(out, 0, 1)
    return out.reshape((out.shape[0], sq) + out.shape[3:])


def mixer_layer(x, p, past):
    (g_norm, w_in, b_f, g_q_fox, g_k_fox, g_cq, w_qb, g_qn, g_qp,
     g_ckv, g_kp, w_kvb, g_kn, w_out) = p
    B, S, _ = x.shape
    past_len = 0 if past is None else past[0].shape[1]
    q_pos = past_len + jnp.arange(S, dtype=jnp.int32)
    k_pos = jnp.arange(past_len + S, dtype=jnp.int32)

    h = rmsnorm(x, g_norm)
    u = h @ w_in
    q_a, k_a, v_a, f_a, z_a, cq, ckv, kpe, z_b = jnp.split(u, IN_OFFSETS, axis=-1)

    q_f = rmsnorm(q_a.reshape(B, S, FOX_HEADS, FOX_HEAD_DIM), g_q_fox)
    k_f = rmsnorm(k_a.reshape(B, S, FOX_HEADS, FOX_HEAD_DIM), g_k_fox)
    v_f = v_a.reshape(B, S, FOX_HEADS, FOX_HEAD_DIM)
    logf = jax.nn.log_sigmoid(f_a.astype(jnp.float32) + b_f.astype(jnp.float32))
    if past is None:
        k_all, v_all, logf_all = k_f, v_f, logf
    else:
        k_all = jnp.concatenate([past[0].astype(k_f.dtype), k_f], axis=1)
        v_all = jnp.concatenate([past[1].astype(v_f.dtype), v_f], axis=1)
        logf_all = jnp.concatenate([past[2].astype(jnp.float32), logf], axis=1)
    cum = jnp.cumsum(logf_all, axis=1)
    f_q = cum[:, past_len:]
    f_k_t = jnp.swapaxes(cum, 1, 2)

    def fox_score(qi, fqi):
        s = jnp.einsum('bqhd,bkhd->bhqk', qi, k_all).astype(jnp.float32) * FOX_SCALE
        return s + jnp.swapaxes(fqi, 1, 2)[..., None] - f_k_t[:, :, None, :]

    o_a = swept_attention(fox_score, (q_f, f_q), q_pos, k_pos, v_all, per_frame=True)
    o_a = o_a.reshape(B, S, FOX_WIDTH) * jax.nn.silu(z_a)

    cq = rmsnorm(cq, g_cq)
    qb = (cq @ w_qb).reshape(B, S, MLA_HEADS, NOPE_DIM + ROPE_DIM)
    q_nope = rmsnorm(qb[..., :NOPE_DIM], g_qn)
    q_pe = rope(rmsnorm(qb[..., NOPE_DIM:], g_qp), q_pos)
    ckv = rmsnorm(ckv, g_ckv)
    kpe = rope(rmsnorm(kpe, g_kp), q_pos)
    if past is None:
        ckv_all, kpe_all = ckv, kpe
    else:
        ckv_all = jnp.concatenate([past[3].astype(ckv.dtype), ckv], axis=1)
        kpe_all = jnp.concatenate([past[4].astype(kpe.dtype), kpe], axis=1)
    sk = ckv_all.shape[1]
    kv = (ckv_all @ w_kvb).reshape(B, sk, MLA_HEADS, NOPE_DIM + V_DIM)
    k_nope = rmsnorm(kv[..., :NOPE_DIM], g_kn)
    v_b = kv[..., NOPE_DIM:]

    def mla_score(qn, qp):
        s = (jnp.einsum('bqhd,bkhd->bhqk', qn, k_nope)
             + jnp.einsum('bqhr,bkr->bhqk', qp, kpe_all))
        return s.astype(jnp.float32) * MLA_SCALE

    o_b = swept_attention(mla_score, (q_nope, q_pe), q_pos, k_pos, v_b, per_frame=False)
    o_b = o_b.reshape(B, S, MLA_WIDTH) * jax.nn.silu(z_b)

    y = x + jnp.concatenate([o_a, o_b], axis=-1) @ w_out
    return y, (k_f, v_f, logf, ckv, kpe)


def setup_inputs(seed: int = 0) -> dict:
    key = jax.random.key(seed)
    ks = jax.random.split(key, 24)
    f32 = jnp.float32

    def nrm(k, shape, scale=1.0):
        return jax.random.normal(k, shape, f32) * scale

    def gain(k, shape):
        return 1.0 + 0.05 * jax.random.normal(k, shape, f32)

    L = DEPTH
    return {
        "x_prompt": nrm(ks[0], (BATCH, SEQ, D_MODEL)),
        "x_sample": nrm(ks[1], (DEC_BATCH, DEC_SEQ, D_MODEL)),
        "cache_fox_k": nrm(ks[2], (L, DEC_BATCH, PAST_LEN, FOX_HEADS, FOX_HEAD_DIM)),
        "cache_fox_v": nrm(ks[3], (L, DEC_BATCH, PAST_LEN, FOX_HEADS, FOX_HEAD_DIM)),
        "cache_fox_logf": jax.nn.log_sigmoid(FORGET_BIAS_INIT + nrm(ks[4], (L, DEC_BATCH, PAST_LEN, FOX_HEADS))),
        "cache_mla_ckv": nrm(ks[5], (L, DEC_BATCH, PAST_LEN, KV_LORA)),
        "cache_mla_kpe": nrm(ks[6], (L, DEC_BATCH, PAST_LEN, ROPE_DIM)),
        "g_norm": gain(ks[7], (L, D_MODEL)),
        "w_in": nrm(ks[8], (L, D_MODEL, N_IN), D_MODEL ** -0.5),
        "b_f": FORGET_BIAS_INIT + 0.1 * jax.random.normal(ks[9], (L, FOX_HEADS), f32),
        "g_q_fox": gain(ks[10], (L, FOX_HEAD_DIM)),
        "g_k_fox": gain(ks[11], (L, FOX_HEAD_DIM)),
        "g_cq": gain(ks[12], (L, Q_LORA)),
        "w_qb": nrm(ks[13], (L, Q_LORA, MLA_HEADS * (NOPE_DIM + ROPE_DIM)), Q_LORA ** -0.5),
        "g_qn": gain(ks[14], (L, NOPE_DIM)),
        "g_qp": gain(ks[15], (L, ROPE_DIM)),
        "g_ckv": gain(ks[16], (L, KV_LORA)),
        "g_kp": gain(ks[17], (L, ROPE_DIM)),
        "w_kvb": nrm(ks[18], (L, KV_LORA, MLA_HEADS * (NOPE_DIM + V_DIM)), KV_LORA ** -0.5),
        "g_kn": gain(ks[19], (L, NOPE_DIM)),
        "w_out": nrm(ks[20], (L, MIX_WIDTH, D_MODEL), MIX_WIDTH ** -0.5),
    }


def reference(x_prompt, x_sample, cache_fox_k, cache_fox_v, cache_fox_logf, cache_mla_ckv,
              cache_mla_kpe, g_norm, w_in, b_f, g_q_fox, g_k_fox, g_cq, w_qb, g_qn, g_qp,
              g_ckv, g_kp, w_kvb, g_kn, w_out):
    y_p, y_s = x_prompt, x_sample
    rows_p, rows_s = [], []
    for l in range(DEPTH):
        p = (g_norm[l], w_in[l], b_f[l], g_q_fox[l], g_k_fox[l], g_cq[l], w_qb[l], g_qn[l],
             g_qp[l], g_ckv[l], g_kp[l], w_kvb[l], g_kn[l], w_out[l])
        y_p, r_p = mixer_layer(y_p, p, None)
        past = (cache_fox_k[l], cache_fox_v[l], cache_fox_logf[l], cache_mla_ckv[l], cache_mla_kpe[l])
        y_s, r_s = mixer_layer(y_s, p, past)
        rows_p.append(r_p)
        rows_s.append(r_s)

    def stk(rows, i):
        return jnp.stack([r[i] for r in rows], axis=0)

    return (y_p, y_s,
            stk(rows_p, 0), stk(rows_p, 1), stk(rows_p, 2), stk(rows_p, 3), stk(rows_p, 4),
            stk(rows_s, 0), stk(rows_s, 1), stk(rows_s, 2), stk(rows_s, 3), stk(rows_s, 4))
```

```python
import numpy as np
import ml_dtypes
from contextlib import ExitStack
import concourse.bass as bass
import concourse.mybir as mybir
from concourse.bass_utils import run_bass_kernel_spmd

F32 = mybir.dt.float32
BF16 = mybir.dt.bfloat16
ALU = mybir.AluOpType
AF = mybir.ActivationFunctionType
AX = mybir.AxisListType

D = 4096
NL = 2
H = 16
TP = 1024
TS = 32
T = TP + TS
NTB = 9
PAST = 2048
EPS = 1e-6
N_IN = 11856
OFF_Q, OFF_K, OFF_V, OFF_F, OFF_ZA, OFF_CQ, OFF_CKV, OFF_KPE, OFF_ZB = (
    0, 2048, 4096, 6144, 6160, 8208, 9232, 9744, 9808)
FOX_SCALE = 1.0 / np.sqrt(128.0)
MLA_SCALE = 1.0 / np.sqrt(192.0)


def gblock(r, i):
    return 2 * i + (i % 2) if r == 0 else 2 * i + 1 - (i % 2)


def tb_rows(tb):
    return 128 if tb < 8 else TS


def _freeze(fn):
    import types
    if fn is None or fn.__closure__ is None:
        return fn
    cells = []
    for c in fn.__closure__:
        try:
            cells.append(types.CellType(c.cell_contents))
        except ValueError:
            cells.append(c)
    return types.FunctionType(fn.__code__, fn.__globals__, fn.__name__, fn.__defaults__, tuple(cells))


class Tr:
    ENG = ('pe', 'act', 'dve', 'pool', 'sp')
    NDS = 20

    def __init__(s):
        s.ops = {e: [] for e in s.ENG}
        s.cnt = {e: 0 for e in s.ENG}
        s.dcnt = {e: 0 for e in s.ENG}
        s.lw = {}
        s.rd = {}
        s.waited = {e: {} for e in s.ENG}
        s.semval = {}
        s.ccn = 0
        s.maxops = None

    def _need(s, eng, deps):
        w = []
        for (k, v) in deps:
            if k == ('c', 'pe') and eng == 'pe':
                continue
            if s.waited[eng].get(k, 0) < v:
                s.waited[eng][k] = v
                w.append((k, v))
        return w

    def add(s, eng, fn, reads=(), writes=(), kind='c', signal=True):
        s.nadd = getattr(s, 'nadd', 0) + 1
        if s.maxops is not None and s.nadd > s.maxops:
            return None
        deps = set()
        for b in reads:
            if b in s.lw:
                deps.add(s.lw[b])
            if (isinstance(b, tuple) and b[0] in ('psA', 'psT', 'psS')) or b in ('psO', 'psD'):
                for t in s.rd.get(b, ()):
                    if t[0] != ('c', eng):
                        deps.add(t)
        for b in writes:
            if b in s.lw:
                deps.add(s.lw[b])
            for t in s.rd.get(b, ()):
                deps.add(t)
        if kind == 'dma':
            k = s.dcnt[eng]
            s.dcnt[eng] += 1
            slot = k % s.NDS
            m = k // s.NDS + 1
            key = ('d', eng, slot)
            if m > 1:
                deps.add((key, 16 * (m - 1)))
            tok = (key, 16 * m)
        elif kind == 'cc':
            key = ('cc',)
            s.ccn += 1
            tok = (key, s.ccn)
        else:
            key = ('c', eng)
            if signal:
                s.cnt[eng] += 1
                tok = (key, s.cnt[eng])
            else:
                tok = (key, s.cnt[eng] + 1)
        s.semval[key] = max(s.semval.get(key, 0), tok[1])
        waits = s._need(eng, sorted(deps, key=str))
        for b in writes:
            s.lw[b] = tok
            s.rd[b] = []
        for b in reads:
            s.rd.setdefault(b, []).append(tok)
        s.ops[eng].append((waits, _freeze(fn), kind, signal, key))
        return tok

    def _allvals(s):
        d = dict(s.semval)
        for e in s.ENG:
            if ('c', e) in d:
                d[('c', e)] = s.cnt[e]
        return sorted([(k, v) for k, v in d.items() if v > 0], key=str)

    def barrier(s):
        if s.maxops is not None and getattr(s, 'nadd', 0) > s.maxops:
            return
        deps = s._allvals()
        for e in s.ENG:
            w = s._need(e, deps)
            if w:
                s.ops[e].append((w, None, 'c', False, None))

    def final_wait(s, eng='sp'):
        deps = s._allvals()
        w = s._need(eng, deps)
        if w:
            s.ops[eng].append((w, None, 'c', False, None))

    def sem_keys(s):
        return list(s.semval.keys())

    def emit(s, block, sems):
        def run(name):
            def f(E):
                for waits, fn, kind, signal, key in s.ops[name]:
                    for (k, v) in waits:
                        E.wait_ge(sems[k], v)
                    if fn is None:
                        continue
                    ins = fn(E)
                    if kind == 'dma':
                        ins.then_inc(sems[key], 16)
                    elif kind == 'cc':
                        ins.then_inc(sems[key])
                    elif signal:
                        ins.then_inc(sems[key], 1)
            return f
        block.tensor(run('pe'))
        block.scalar(run('act'))
        block.vector(run('dve'))
        block.gpsimd(run('pool'))
        block.sync(run('sp'))


def build_program(stop_after=99, n_cores=8, maxops=None, lite=()):
    nc = bass.Bass("TRN2", target_bir_lowering=False)
    T_ = Tr()
    T_.maxops = maxops

    def din(name, shape, dt=F32):
        if name in lite:
            shape = [8, shape[1]]
        return nc.dram_tensor(name, list(shape), dt, kind="ExternalInput").ap()

    def dout(name, shape, dt=F32):
        return nc.dram_tensor(name, list(shape), dt, kind="ExternalOutput").ap()

    def dint(name, shape, dt=F32):
        return nc.dram_tensor(name, list(shape), dt).ap()

    xp = din("xp", [TP, D]); xs = din("xs", [TS, D])
    cfk = din("cfk", [NL * 2 * PAST, 2048]); cfv = din("cfv", [NL * 2 * PAST, 2048])
    cfl = din("cfl", [NL * 2 * PAST, 16]); cck = din("cck", [NL * 2 * PAST, 512])
    ckp = din("ckp", [NL * 2 * PAST, 64])
    g_norm = din("g_norm", [NL, D]); w_in = din("w_in", [NL * D, N_IN])
    b_f = din("b_f", [NL, 16]); g_qf = din("g_q_fox", [NL, 128]); g_kf = din("g_k_fox", [NL, 128])
    g_cq = din("g_cq", [NL, 1024]); w_qb = din("w_qb", [NL * 1024, 3072])
    g_qn = din("g_qn", [NL, 128]); g_qp = din("g_qp", [NL, 64]); g_ckv = din("g_ckv", [NL, 512])
    g_kp = din("g_kp", [NL, 64]); w_kvb = din("w_kvb", [NL * 512, 4096]); g_kn = din("g_kn", [NL, 128])
    w_out = din("w_out", [NL * D, D])
    c_identb = din("c_identb", [128, 128], BF16); c_identf = din("c_identf", [128, 128])
    c_onesb = din("c_onesb", [128, 128], BF16); c_onesf = din("c_onesf", [128, 128])
    c_trif = din("c_trif", [128, 128])
    c_cos = din("c_cos", [T, 32]); c_sin = din("c_sin", [T, 32])
    c_maskf = din("c_maskf", [128, 16 * 128], BF16); c_maskm = din("c_maskm", [128, 16 * 128], BF16)
    c_masks = din("c_masks", [32, 32], BF16)
    c_tris = din("c_tris", [32, 32]); c_inds = din("c_inds", [32, 2]); c_indw = din("c_indw", [32, 256]); c_masks2 = din("c_masks2", [32, 32], BF16)

    yp = dout("yp", [TP, D]); ys = dout("ys", [TS, D])
    nfk_p = dout("nfk_p", [NL * TP, 2048]); nfv_p = dout("nfv_p", [NL * TP, 2048])
    nfl_p = dout("nfl_p", [NL * TP, 16]); nck_p = dout("nck_p", [NL * TP, 512]); nkp_p = dout("nkp_p", [NL * TP, 64])
    nfk_s = dout("nfk_s", [NL * TS, 2048]); nfv_s = dout("nfv_s", [NL * TS, 2048])
    nfl_s = dout("nfl_s", [NL * TS, 16]); nck_s = dout("nck_s", [NL * TS, 512]); nkp_s = dout("nkp_s", [NL * TS, 64])

    ymid = dint("ymid", [T, D])
    kTx = [dint(f"kTx{l}", [2048, TP], BF16) for l in range(NL)]
    vx = [dint(f"vx{l}", [TP, 2048], BF16) for l in range(NL)]
    lfx = [dint(f"lfx{l}", [TP, 16]) for l in range(NL)]
    ckx = [dint(f"ckx{l}", [512, TP], BF16) for l in range(NL)]
    kpx = [dint(f"kpx{l}", [64, TP], BF16) for l in range(NL)]
    kTg = [dint(f"kTg{l}", [2 * 2048, TP], BF16) for l in range(NL)]
    vg = [dint(f"vg{l}", [2 * TP, 2048], BF16) for l in range(NL)]
    lfg = [dint(f"lfg{l}", [2 * TP, 16]) for l in range(NL)]
    ckg = [dint(f"ckg{l}", [2 * 512, TP], BF16) for l in range(NL)]
    kpg = [dint(f"kpg{l}", [2 * 64, TP], BF16) for l in range(NL)]
    qfx = dint("qfx", [2048, T], BF16); qnx = dint("qnx", [2048, T], BF16)
    qpx = dint("qpx", [1024, T], BF16); zax = dint("zax", [2048, T], BF16); zbx = dint("zbx", [2048, T], BF16)

    es = ExitStack()
    ph = [None]

    def sb(name, shape, dt=F32):
        return es.enter_context(nc.sbuf_tensor(name, list(shape), dt))

    def sbp(name, shape, dt=F32):
        return ph[0].enter_context(nc.sbuf_tensor(name, list(shape), dt))

    def pst(name, shape, dt=F32):
        return es.enter_context(nc.psum_tensor(name, list(shape), dt))

    A = T_.add

    identb = sb("identb", [128, 128], BF16); identf = sb("identf", [128, 128])
    onesb = sb("onesb", [128, 128], BF16); onesf = sb("onesf", [128, 128]); trif = sb("trif", [128, 128])
    cosT = sb("cosT", [128, NTB, 32]); sinT = sb("sinT", [128, NTB, 32])
    masks = sb("masks", [32, 32], BF16); tris = sb("tris", [32, 32]); inds = sb("inds", [32, 2])
    actT = sb("actT", [128, 32, T], BF16)
    wt = [sb(f"wt{j}", [128, 32, 512], BF16) for j in range(2)]
    gcol = sb("gcol", [128, 32]); g32 = sb("g32", [32, 128]); gcq_t = sb("gcq_t", [128, 1024]); gckv_t = sb("gckv_t", [128, 512])
    gqf_t = sb("gqf_t", [128, 128]); gkf_t = sb("gkf_t", [128, 128]); gqn_t = sb("gqn_t", [128, 128])
    gkn_t = sb("gkn_t", [128, 128]); gqp_t = sb("gqp_t", [128, 64]); gkp_t = sb("gkp_t", [128, 64])
    bf_t = sb("bf_t", [128, 16])
    kTn = sb("kTn", [128, H, TS], BF16)
    vn = sb("vn", [TS, 2048], BF16)
    lfn = sb("lfn", [TS, 16])
    ckTn = sb("ckTn", [128, 4, TS], BF16)
    kpTn = sb("kpTn", [64, TS], BF16)
    ss = sb("ss", [128, 16]); rs = sb("rs", [128, 16])
    rt = sb("rt", [128, 64]); rt2 = sb("rt2", [128, 64])

    psA = [pst(f"psA{j}", [128, 512]) for j in range(2)]
    psT = [pst(f"psT{j}", [128, 1024], BF16) for j in range(2)]
    psS = [pst(f"psS{j}", [128, 512]) for j in range(2)]
    psO = pst("psO", [128, 512]); psD = pst("psD", [128, 512])

    def dma(eng, out, in_, reads, writes):
        return A(eng, lambda E: E.dma_start(out=out, in_=in_), reads, writes, kind='dma')

    for (dst, src, nm) in [(identb, c_identb, 'identb'), (identf, c_identf, 'identf'), (onesb, c_onesb, 'onesb'),
                           (onesf, c_onesf, 'onesf'), (trif, c_trif, 'trif'), (masks, c_masks, 'masks'), (tris, c_tris, 'tris'),
                           (inds, c_inds, 'inds')]:
        dma('sp', dst[:], src[:, :], [], [nm])
    for tb in range(NTB):
        M = tb_rows(tb)
        dma('sp', cosT[:M, tb, :], c_cos[tb * 128: tb * 128 + M, :], [], [('cos', tb)])
        dma('sp', sinT[:M, tb, :], c_sin[tb * 128: tb * 128 + M, :], [], [('sin', tb)])

    pa_i = [0]

    def next_psA():
        j = pa_i[0] % 2
        pa_i[0] += 1
        return j

    wt_i = [0]

    def load_w(src2d, K, W, c0):
        j = wt_i[0] % 2
        wt_i[0] += 1
        KC = K // 128
        step = 8
        for k0 in range(0, KC, step):
            k1 = min(KC, k0 + step)
            src = src2d[k0 * 128:k1 * 128, c0:c0 + W].rearrange("(kc p) n -> p kc n", p=128)
            A('pool', (lambda E, o=wt[j][:, k0:k1, :W], i=src: E.dma_start(out=o, in_=i)),
              [], [('wt', j, k0)], kind='dma')
        return j

    def proj_tm(tb, actT_names, lhs_fn, KC, wj, W, wcol0=0):
        M = tb_rows(tb)
        pj = next_psA()
        for kc in range(KC):
            lh = lhs_fn(kc, tb, M)
            A('pe', (lambda E, kc=kc, lh=lh: E.matmul(psA[pj][:M, :W], lhsT=lh,
                                              rhs=wt[wj][:, kc, wcol0:wcol0 + W],
                                              start=(kc == 0), stop=(kc == KC - 1))),
              list(actT_names) + [('wt', wj, (kc // 8) * 8)], [('psA', pj)], signal=(kc == KC - 1))
        return pj, M

    def hT_lhs(kc, tb, M):
        return actT[:, kc, tb * 128: tb * 128 + M]

    def rms_groups(src3, M, G, Dg, gain_t, dst3, src_names, dst_names):
        sq3 = sq[:M, :G * Dg].rearrange("p (g d) -> p g d", g=G)
        A('act', lambda E: E.activation(out=sq3, in_=src3, func=AF.Square), src_names, ['sq'])
        A('dve', lambda E: E.tensor_reduce(out=ss[:M, :G], in_=sq3, axis=AX.X, op=ALU.add), ['sq'], ['ss'])
        A('dve', lambda E: E.tensor_scalar(out=ss[:M, :G], in0=ss[:M, :G], scalar1=1.0 / Dg, scalar2=EPS,
                                           op0=ALU.mult, op1=ALU.add), ['ss'], ['ss'])
        A('act', lambda E: E.activation(out=ss[:M, :G], in_=ss[:M, :G], func=AF.Sqrt), ['ss'], ['ss'])
        A('dve', lambda E: E.reciprocal(out=rs[:M, :G], in_=ss[:M, :G]), ['ss'], ['rs'])
        A('dve', lambda E: E.tensor_tensor(out=dst3, in0=src3,
                                           in1=rs[:M, :G].unsqueeze(2).to_broadcast([M, G, Dg]), op=ALU.mult),
          list(src_names) + ['rs'], dst_names)
        A('dve', lambda E: E.tensor_tensor(out=dst3, in0=dst3,
                                           in1=gain_t[:M, :Dg].unsqueeze(1).to_broadcast([M, G, Dg]), op=ALU.mult),
          list(dst_names) + ['gains'], dst_names)

    pt_i = [0]

    def transpose_to(src_bf, M, nblk, blkw, dst_fn, src_names, dst_names_fn, evac='act', scale_fn=None):
        pj = pt_i[0] % 2
        pt_i[0] += 1
        for b in range(nblk):
            A('pe', (lambda E, b=b: E.transpose(psT[pj][:blkw, b * 128: b * 128 + M],
                                                src_bf[:M, b * blkw:(b + 1) * blkw], identb[:M, :M])),
              list(src_names) + ['identb'], [('psT', pj)], signal=(b == nblk - 1))
        for b in range(nblk):
            d_ = dst_fn(b)
            if scale_fn is not None:
                sc_ = scale_fn(b)
                A('act', (lambda E, b=b, d_=d_, sc_=sc_: E.activation(out=d_, in_=psT[pj][:blkw, b * 128: b * 128 + M],
                                                      func=AF.Copy, scale=sc_)),
                  [('psT', pj), 'gcol'], dst_names_fn(b))
            else:
                A(evac, (lambda E, b=b, d_=d_: (E.copy if evac == 'act' else E.tensor_copy)(
                    out=d_, in_=psT[pj][:blkw, b * 128: b * 128 + M])),
                  [('psT', pj)], dst_names_fn(b))

    def rope(x3, M, G, tb, tmp_a, tmp_b, names):
        c = cosT[:M, tb, :].unsqueeze(1).to_broadcast([M, G, 32])
        s_ = sinT[:M, tb, :].unsqueeze(1).to_broadcast([M, G, 32])
        x1 = x3[:, :, 0:32]
        x2 = x3[:, :, 32:64]
        ta = tmp_a[:M, :G * 32].rearrange("p (g d) -> p g d", g=G)
        tb_ = tmp_b[:M, :G * 32].rearrange("p (g d) -> p g d", g=G)
        rn = list(names) + [('cos', tb), ('sin', tb)]
        A('dve', lambda E: E.tensor_tensor(out=ta, in0=x1, in1=s_, op=ALU.mult), rn, ['rt'])
        A('dve', lambda E: E.tensor_tensor(out=tb_, in0=x2, in1=s_, op=ALU.mult), rn, ['rt2'])
        A('dve', lambda E: E.tensor_tensor(out=x1, in0=x1, in1=c, op=ALU.mult), rn, names)
        A('dve', lambda E: E.tensor_tensor(out=x2, in0=x2, in1=c, op=ALU.mult), rn, names)
        A('dve', lambda E: E.tensor_tensor(out=x1, in0=x1, in1=tb_, op=ALU.subtract), list(names) + ['rt2'], names)
        A('dve', lambda E: E.tensor_tensor(out=x2, in0=x2, in1=ta, op=ALU.add), list(names) + ['rt'], names)

    def out_rows(l, tb, M, pbuf, sbuf_, width, c0=0):
        if tb < 8:
            return pbuf[l * TP + tb * 128: l * TP + tb * 128 + M, c0:c0 + width]
        return sbuf_[l * TS: l * TS + M, c0:c0 + width]

    for l in range(NL):
        T_.barrier()
        ph[0] = ExitStack()
        xt = [sbp(f"xt{l}", [128, D])]
        hb = sbp(f"hb{l}", [128, D], BF16)
        sq = sbp(f"sq{l}", [128, 1024])
        wf = [sbp(f"wf{l}_{j}", [128, 512]) for j in range(2)]
        wb = [sbp(f"wb{l}_{j}", [128, 512], BF16) for j in range(2)]
        tsb = [sbp(f"tsb{l}_{j}", [128, 512], BF16) for j in range(2)]
        cqT = sbp(f"cqT{l}", [128, 8, T], BF16)
        dma('sp', g32[:, :], g_norm[l, :].rearrange("(c p) -> c p", p=128), [], ['g32'])
        A('pe', lambda E: E.transpose(psS[0][:, :32], g32[:32, :], identf[:32, :32]), ['g32', 'identf'], [('psS', 0)])
        A('act', lambda E: E.copy(out=gcol[:, :], in_=psS[0][:, :32]), [('psS', 0)], ['gcol'])
        for (dst, src, w) in [(gcq_t, g_cq, 1024), (gckv_t, g_ckv, 512), (gqf_t, g_qf, 128),
                              (gkf_t, g_kf, 128), (gqn_t, g_qn, 128), (gkn_t, g_kn, 128), (gqp_t, g_qp, 64),
                              (gkp_t, g_kp, 64), (bf_t, b_f, 16)]:
            dma('sp', dst[:, :w], src[l, :].partition_broadcast(128), [], ['gains'])
        T_.barrier()

        for tb in range(NTB):
            M = tb_rows(tb)
            j = 0
            if l == 0:
                src = xp[tb * 128: tb * 128 + M, :] if tb < 8 else xs[:, :]
            else:
                src = ymid[tb * 128: tb * 128 + M, :]
            dma('sp', xt[j][:M, :], src, [('ymid', tb)], [('xt', j)])
            A('dve', lambda E, j=j, M=M: E.scalar_tensor_tensor(out=hb[:M, :], in0=xt[j][:M, :], scalar=1.0,
                                                               in1=xt[j][:M, :], op0=ALU.mult, op1=ALU.mult,
                                                               accum_out=ss[:M, 0:1]),
              [('xt', j)], ['hb', 'ss'])
            A('dve', lambda E, M=M: E.tensor_scalar(out=ss[:M, 0:1], in0=ss[:M, 0:1], scalar1=1.0 / D, scalar2=EPS,
                                                    op0=ALU.mult, op1=ALU.add), ['ss'], ['ss'])
            A('act', lambda E, M=M: E.activation(out=ss[:M, 0:1], in_=ss[:M, 0:1], func=AF.Sqrt), ['ss'], ['ss'])
            A('dve', lambda E, M=M: E.reciprocal(out=rs[:M, 0:1], in_=ss[:M, 0:1]), ['ss'], ['rs'])
            A('dve', lambda E, j=j, M=M: E.tensor_scalar(out=hb[:M, :], in0=xt[j][:M, :], scalar1=rs[:M, 0:1],
                                                        scalar2=None, op0=ALU.mult),
              [('xt', j), 'rs'], ['hb'])
            for g4 in range(4):
                transpose_to(hb[:, g4 * 1024:(g4 + 1) * 1024], M, 8, 128,
                             (lambda b, g4=g4, tb=tb, M=M: actT[:, g4 * 8 + b, tb * 128: tb * 128 + M]),
                             ['hb'], (lambda b, tb=tb: [('actT', tb)]), evac='act',
                             scale_fn=(lambda b, g4=g4: gcol[:, g4 * 8 + b: g4 * 8 + b + 1]))
        if stop_after <= 0:
            T_.barrier()
            ph[0].close()
            break

        hn = lambda tb: [('actT', tb)]
        for cb in range(4):
            wj = load_w(w_in[l * D:(l + 1) * D, :], D, 512, OFF_K + cb * 512)
            for tb in range(NTB):
                pj, M = proj_tm(tb, hn(tb), hT_lhs, 32, wj, 512)
                j = tb % 2
                src3 = psA[pj][:M, :].rearrange("p (g d) -> p g d", g=4)
                dst3 = wf[j][:M, :512].rearrange("p (g d) -> p g d", g=4)
                rms_groups(src3, M, 4, 128, gkf_t, dst3, [('psA', pj)], [('wf', j)])
                dma('sp', out_rows(l, tb, M, nfk_p, nfk_s, 512, cb * 512), wf[j][:M, :512], [('wf', j)], [])
                A('act', lambda E, j=j, M=M: E.copy(out=wb[j][:M, :512], in_=wf[j][:M, :512]), [('wf', j)], [('wb', j)])
                if tb < 8:
                    transpose_to(wb[j], M, 4, 128, (lambda b, j=j, M=M: tsb[j][:, b * 128: b * 128 + M]),
                                 [('wb', j)], (lambda b, j=j: [('tsb', j)]))
                    dma('sp', kTx[l][cb * 512:(cb + 1) * 512, tb * 128:(tb + 1) * 128].rearrange("(g d) t -> d g t", g=4),
                        tsb[j][:, :512].rearrange("d (g t) -> d g t", g=4), [('tsb', j)], [('kTx', l)])
                else:
                    transpose_to(wb[j], M, 4, 128, (lambda b, cb=cb, M=M: kTn[:, cb * 4 + b, :M]),
                                 [('wb', j)], (lambda b: ['kTn']))
        for cb in range(4):
            wj = load_w(w_in[l * D:(l + 1) * D, :], D, 512, OFF_V + cb * 512)
            for tb in range(NTB):
                pj, M = proj_tm(tb, hn(tb), hT_lhs, 32, wj, 512)
                j = tb % 2
                A('act', lambda E, j=j, M=M, pj=pj: E.copy(out=wf[j][:M, :512], in_=psA[pj][:M, :]),
                  [('psA', pj)], [('wf', j)])
                dma('sp', out_rows(l, tb, M, nfv_p, nfv_s, 512, cb * 512), wf[j][:M, :512], [('wf', j)], [])
                if tb < 8:
                    A('dve', lambda E, j=j, M=M, pj=pj: E.tensor_copy(out=wb[j][:M, :512], in_=psA[pj][:M, :]),
                      [('psA', pj)], [('wb', j)])
                    dma('sp', vx[l][tb * 128:(tb + 1) * 128, cb * 512:(cb + 1) * 512], wb[j][:M, :512],
                        [('wb', j)], [('vx', l)])
                else:
                    A('dve', lambda E, M=M, pj=pj, cb=cb: E.tensor_copy(out=vn[:M, cb * 512:(cb + 1) * 512],
                                                                        in_=psA[pj][:M, :]),
                      [('psA', pj)], ['vn'])
        wj = load_w(w_in[l * D:(l + 1) * D, :], D, 16, OFF_F)
        for tb in range(NTB):
            pj, M = proj_tm(tb, hn(tb), hT_lhs, 32, wj, 16)
            j = tb % 2
            A('dve', lambda E, j=j, M=M, pj=pj: E.tensor_tensor(out=wf[j][:M, :16], in0=psA[pj][:M, :16],
                                                               in1=bf_t[:M, :16], op=ALU.add),
              [('psA', pj), 'gains'], [('wf', j)])
            A('act', lambda E, j=j, M=M: E.activation(out=wf[j][:M, :16], in_=wf[j][:M, :16], func=AF.Exp, scale=-1.0),
              [('wf', j)], [('wf', j)])
            A('act', lambda E, j=j, M=M: E.activation(out=wf[j][:M, :16], in_=wf[j][:M, :16], func=AF.Ln, bias=1.0),
              [('wf', j)], [('wf', j)])
            if tb < 8:
                A('dve', lambda E, j=j, M=M: E.tensor_scalar(out=wf[j][:M, :16], in0=wf[j][:M, :16], scalar1=-1.0,
                                                            scalar2=None, op0=ALU.mult), [('wf', j)], [('wf', j)])
                dma('sp', out_rows(l, tb, M, nfl_p, nfl_s, 16), wf[j][:M, :16], [('wf', j)], [])
                dma('sp', lfx[l][tb * 128:(tb + 1) * 128, :], wf[j][:M, :16], [('wf', j)], [('lfx', l)])
            else:
                A('dve', lambda E, j=j, M=M: E.tensor_scalar(out=lfn[:M, :16], in0=wf[j][:M, :16], scalar1=-1.0,
                                                            scalar2=None, op0=ALU.mult), [('wf', j)], ['lfn'])
                dma('sp', out_rows(l, tb, M, nfl_p, nfl_s, 16), lfn[:M, :16], ['lfn'], [])
        wj = load_w(w_in[l * D:(l + 1) * D, :], D, 512, OFF_CKV)
        for tb in range(NTB):
            pj, M = proj_tm(tb, hn(tb), hT_lhs, 32, wj, 512)
            j = tb % 2
            src3 = psA[pj][:M, :].rearrange("p (g d) -> p g d", g=1)
            dst3 = wf[j][:M, :512].rearrange("p (g d) -> p g d", g=1)
            rms_groups(src3, M, 1, 512, gckv_t, dst3, [('psA', pj)], [('wf', j)])
            dma('sp', out_rows(l, tb, M, nck_p, nck_s, 512), wf[j][:M, :512], [('wf', j)], [])
            A('act', lambda E, j=j, M=M: E.copy(out=wb[j][:M, :512], in_=wf[j][:M, :512]), [('wf', j)], [('wb', j)])
            if tb < 8:
                transpose_to(wb[j], M, 4, 128, (lambda b, j=j, M=M: tsb[j][:, b * 128: b * 128 + M]),
                             [('wb', j)], (lambda b, j=j: [('tsb', j)]))
                dma('sp', ckx[l][:, tb * 128:(tb + 1) * 128].rearrange("(g d) t -> d g t", g=4),
                    tsb[j][:, :512].rearrange("d (g t) -> d g t", g=4), [('tsb', j)], [('ckx', l)])
            else:
                transpose_to(wb[j], M, 4, 128, (lambda b, M=M: ckTn[:, b, :M]), [('wb', j)], (lambda b: ['ckTn']))
        wj = load_w(w_in[l * D:(l + 1) * D, :], D, 64, OFF_KPE)
        for tb in range(NTB):
            pj, M = proj_tm(tb, hn(tb), hT_lhs, 32, wj, 64)
            j = tb % 2
            src3 = psA[pj][:M, :64].rearrange("p (g d) -> p g d", g=1)
            dst3 = wf[j][:M, :64].rearrange("p (g d) -> p g d", g=1)
            rms_groups(src3, M, 1, 64, gkp_t, dst3, [('psA', pj)], [('wf', j)])
            rope(dst3, M, 1, tb, rt, rt2, [('wf', j)])
            dma('sp', out_rows(l, tb, M, nkp_p, nkp_s, 64), wf[j][:M, :64], [('wf', j)], [])
            A('act', lambda E, j=j, M=M: E.copy(out=wb[j][:M, :64], in_=wf[j][:M, :64]), [('wf', j)], [('wb', j)])
            if tb < 8:
                transpose_to(wb[j], M, 1, 64, (lambda b, j=j, M=M: tsb[j][:64, :M]),
                             [('wb', j)], (lambda b, j=j: [('tsb', j)]))
                dma('sp', kpx[l][:, tb * 128:(tb + 1) * 128], tsb[j][:64, :128], [('tsb', j)], [('kpx', l)])
            else:
                transpose_to(wb[j], M, 1, 64, (lambda b, M=M: kpTn[:, :M]), [('wb', j)], (lambda b: ['kpTn']))
        if stop_after <= 1:
            T_.barrier()
            ph[0].close()
            break

        T_.barrier()
        groups = [[2 * i, 2 * i + 1] for i in range(n_cores // 2)]
        for (src, dst, nm, nchunk) in [(kTx[l], kTg[l], 'kTg', 4), (vx[l], vg[l], 'vg', 4), (lfx[l], lfg[l], 'lfg', 1),
                                       (ckx[l], ckg[l], 'ckg', 1), (kpx[l], kpg[l], 'kpg', 1)]:
            rows = src.shape[0] // nchunk
            for c in range(nchunk):
                A('pool', (lambda E, src=src, dst=dst, c=c, rows=rows: E.collective_compute(
                    "AllGather", ALU.bypass, replica_groups=groups, ins=[src[c * rows:(c + 1) * rows, :]],
                    outs=[dst[2 * c * rows:2 * (c + 1) * rows, :]])),
                  [], [(nm, l, c)], kind='cc')
        T_.barrier()

        for cb in range(4):
            wj = load_w(w_in[l * D:(l + 1) * D, :], D, 512, OFF_Q + cb * 512)
            for tb in range(NTB):
                pj, M = proj_tm(tb, hn(tb), hT_lhs, 32, wj, 512)
                j = tb % 2
                src3 = psA[pj][:M, :].rearrange("p (g d) -> p g d", g=4)
                dst3 = wf[j][:M, :512].rearrange("p (g d) -> p g d", g=4)
                rms_groups(src3, M, 4, 128, gqf_t, dst3, [('psA', pj)], [('wf', j)])
                A('act', lambda E, j=j, M=M: E.copy(out=wb[j][:M, :512], in_=wf[j][:M, :512]), [('wf', j)], [('wb', j)])
                transpose_to(wb[j], M, 4, 128, (lambda b, j=j, M=M: tsb[j][:, b * 128: b * 128 + M]),
                             [('wb', j)], (lambda b, j=j: [('tsb', j)]))
                dma('sp', qfx[cb * 512:(cb + 1) * 512, tb * 128:tb * 128 + M].rearrange("(g d) t -> d g t", g=4),
                    tsb[j][:, :512].rearrange("d (g t) -> d g t", g=4)[:, :, :M], [('tsb', j)], [('qfx', cb, tb)])
        wj0 = load_w(w_in[l * D:(l + 1) * D, :], D, 512, OFF_CQ)
        wj1 = load_w(w_in[l * D:(l + 1) * D, :], D, 512, OFF_CQ + 512)
        for tb in range(NTB):
            p0, M = proj_tm(tb, hn(tb), hT_lhs, 32, wj0, 512)
            p1, M = proj_tm(tb, hn(tb), hT_lhs, 32, wj1, 512)
            pp = [p0, p1]
            for hf in range(2):
                A('act', lambda E, hf=hf, M=M, pp=pp: E.activation(out=sq[:M, hf * 512:(hf + 1) * 512],
                                                                  in_=psA[pp[hf]][:M, :], func=AF.Square),
                  [('psA', pp[hf])], ['sq'])
            A('dve', lambda E, M=M: E.tensor_reduce(out=ss[:M, 0:1], in_=sq[:M, :1024], axis=AX.X, op=ALU.add), ['sq'], ['ss'])
            A('dve', lambda E, M=M: E.tensor_scalar(out=ss[:M, 0:1], in0=ss[:M, 0:1], scalar1=1.0 / 1024, scalar2=EPS,
                                                    op0=ALU.mult, op1=ALU.add), ['ss'], ['ss'])
            A('act', lambda E, M=M: E.activation(out=ss[:M, 0:1], in_=ss[:M, 0:1], func=AF.Sqrt), ['ss'], ['ss'])
            A('dve', lambda E, M=M: E.reciprocal(out=rs[:M, 0:1], in_=ss[:M, 0:1]), ['ss'], ['rs'])
            for hf in range(2):
                A('dve', lambda E, hf=hf, M=M, pp=pp: E.tensor_scalar(out=wf[hf][:M, :512], in0=psA[pp[hf]][:M, :],
                                                                     scalar1=rs[:M, 0:1], scalar2=None, op0=ALU.mult),
                  [('psA', pp[hf]), 'rs'], [('wf', hf)])
                A('dve', lambda E, hf=hf, M=M: E.tensor_tensor(out=wb[hf][:M, :512], in0=wf[hf][:M, :512],
                                                              in1=gcq_t[:M, hf * 512:(hf + 1) * 512], op=ALU.mult),
                  [('wf', hf), 'gains'], [('wb', hf)])
                transpose_to(wb[hf], M, 4, 128, (lambda b, hf=hf, tb=tb, M=M: cqT[:, hf * 4 + b, tb * 128: tb * 128 + M]),
                             [('wb', hf)], (lambda b, tb=tb: [('cqT', tb)]))
        cq_lhs = lambda kc, tb, M: cqT[:, kc, tb * 128: tb * 128 + M]
        for hp in range(8):
            wj = load_w(w_qb[l * 1024:(l + 1) * 1024, :], 1024, 384, hp * 384)
            for tb in range(NTB):
                pj, M = proj_tm(tb, [('cqT', tb)], cq_lhs, 8, wj, 384)
                j = tb % 2
                ps3 = psA[pj][:M, :384].rearrange("p (g d) -> p g d", g=2)
                dn3 = wf[j][:M, 0:256].rearrange("p (g d) -> p g d", g=2)
                dp3 = wf[j][:M, 256:384].rearrange("p (g d) -> p g d", g=2)
                rms_groups(ps3[:, :, 0:128], M, 2, 128, gqn_t, dn3, [('psA', pj)], [('wf', j)])
                rms_groups(ps3[:, :, 128:192], M, 2, 64, gqp_t, dp3, [('psA', pj)], [('wf', j)])
                rope(dp3, M, 2, tb, rt, rt2, [('wf', j)])
                A('act', lambda E, j=j, M=M: E.copy(out=wb[j][:M, :384], in_=wf[j][:M, :384]), [('wf', j)], [('wb', j)])
                transpose_to(wb[j], M, 2, 128, (lambda b, j=j, M=M: tsb[j][:, b * 128: b * 128 + M]),
                             [('wb', j)], (lambda b, j=j: [('tsb', j)]))
                dma('sp', qnx[hp * 256:(hp + 1) * 256, tb * 128:tb * 128 + M].rearrange("(g d) t -> d g t", g=2),
                    tsb[j][:, :256].rearrange("d (g t) -> d g t", g=2)[:, :, :M], [('tsb', j)], [('qnx', hp, tb)])
                transpose_to(wb[j][:, 256:384], M, 2, 64, (lambda b, j=j, M=M: tsb[j][:64, 256 + b * 128: 256 + b * 128 + M]),
                             [('wb', j)], (lambda b, j=j: [('tsb', j)]))
                dma('sp', qpx[hp * 128:(hp + 1) * 128, tb * 128:tb * 128 + M].rearrange("(g d) t -> d g t", g=2),
                    tsb[j][:64, 256:512].rearrange("d (g t) -> d g t", g=2)[:, :, :M], [('tsb', j)], [('qpx', hp, tb)])
        ttiles = [(0, 512), (512, 512), (1024, TS)]
        for (zoff, zdst, znm) in [(OFF_ZA, zax, 'zax'), (OFF_ZB, zbx, 'zbx')]:
            for cb in range(4):
                wj = load_w(w_in[l * D:(l + 1) * D, :], D, 512, zoff + cb * 512)
                for ct in range(4):
                    for ti, (t0, tn) in enumerate(ttiles):
                        pj = next_psA()
                        tbs = [('actT', tb) for tb in range(NTB) if t0 <= tb * 128 < t0 + tn]
                        for kc in range(32):
                            A('pe', (lambda E, kc=kc, pj=pj, wj=wj, ct=ct, t0=t0, tn=tn: E.matmul(
                                psA[pj][:, :tn], lhsT=wt[wj][:, kc, ct * 128:(ct + 1) * 128],
                                rhs=actT[:, kc, t0:t0 + tn], start=(kc == 0), stop=(kc == 31))),
                              tbs + [('wt', wj, (kc // 8) * 8)], [('psA', pj)], signal=(kc == 31))
                        j = (ct * 3 + ti) % 2
                        A('act', lambda E, pj=pj, tn=tn, j=j: E.activation(out=tsb[j][:, :tn], in_=psA[pj][:, :tn], func=AF.Silu),
                          [('psA', pj)], [('tsb', j)])
                        r0 = (cb * 4 + ct) * 128
                        dma('sp', zdst[r0:r0 + 128, t0:t0 + tn], tsb[j][:, :tn], [('tsb', j)], [(znm, cb * 4 + ct, ti)])
        T_.barrier()
        ph[0].close()
        if stop_after <= 2:
            break

        ph[0] = ExitStack()
        gidx = lambda g: (0 if gblock(0, g // 2) == g else 1) * 8 + g // 2
        maskf = sbp(f"maskf{l}", [128, 2048], BF16)
        lf_t = sbp(f"lf_t{l}", [128, 16, 32]); win = sbp(f"win{l}", [128, 16, 32]); tots = sbp(f"tots{l}", [128, 16, 32])
        carry = sbp(f"carry{l}", [128, 17, 32]); ck = sbp(f"ck{l}", [128, 16, 32]); bias = sbp(f"bias{l}", [128, 8, 16, 16])
        crefs = sbp(f"crefs{l}", [128, 32]); ckn = sbp(f"ckn{l}", [32, 16]); biasn = sbp(f"biasn{l}", [32, 32])
        kTh = [sbp(f"kTh{l}_{j}", [128, 2, TP], BF16) for j in range(2)]
        qTh = [sbp(f"qTh{l}_{j}", [128, T], BF16) for j in range(2)]
        zTh = [sbp(f"zTh{l}_{j}", [128, T], BF16) for j in range(2)]
        pT = [sbp(f"pT{l}_{j}", [128, 512], BF16) for j in range(2)]
        rden = sbp(f"rden{l}", [128, 512]); osb = sbp(f"osb{l}", [128, 512])
        qTs = sbp(f"qTs{l}", [128, 16, TS], BF16); zTs = sbp(f"zTs{l}", [128, 16, TS], BF16)
        indw = sbp(f"indw{l}", [32, 256])
        vgrp = wt[0][:, 0:16, :]; kc_t = wt[0][:, 16:32, :]; vc_t = wt[1][:, 0:16, :]
        kTc = wt[1][:, 16:32, :].rearrange("p a b -> p (a b)").rearrange("p (h t) -> p h t", h=4)
        dma('sp', maskf[:, :], c_maskf[:, :], [], ['maskf'])
        dma('sp', indw[:, :], c_indw[:, :], [], ['indw'])
        all_q = lambda nm, n0: [(nm, a, tb) for a in range(n0) for tb in range(NTB)]
        all_z = lambda nm: [(nm, a, ti) for a in range(16) for ti in range(3)]

        def cumsum_tables(ncol, carry_order):
            flat = lf_t[:, :, :ncol]
            A('pe', lambda E: E.matmul(psS[0][:, :16 * ncol].rearrange("p (j c) -> p j c", j=16), lhsT=trif[:, :], rhs=flat,
                                       start=True, stop=True), ['lf_t', 'trif'], [('psS', 0)])
            A('pe', lambda E: E.matmul(psS[1][:, :16 * ncol].rearrange("p (j c) -> p j c", j=16), lhsT=onesf[:, :], rhs=flat,
                                       start=True, stop=True), ['lf_t', 'onesf'], [('psS', 1)])
            A('act', lambda E: E.copy(out=win[:, :, :ncol], in_=psS[0][:, :16 * ncol].rearrange("p (j c) -> p j c", j=16)),
              [('psS', 0)], ['win'])
            A('act', lambda E: E.copy(out=tots[:, :, :ncol], in_=psS[1][:, :16 * ncol].rearrange("p (j c) -> p j c", j=16)),
              [('psS', 1)], ['tots'])
            A('dve', lambda E: E.memset(carry[:, 0, :], 0.0), [], ['carry'])
            for g in range(16):
                A('dve', lambda E, g=g: E.tensor_tensor(out=carry[:, g + 1, :ncol], in0=carry[:, g, :ncol],
                                                        in1=tots[:, carry_order(g), :ncol], op=ALU.add),
                  ['carry', 'tots'], ['carry'])

        dma('sp', lf_t[:, :, :16], lfg[l].rearrange("(j p) h -> p j h", p=128), [('lfg', l, 0)], ['lf_t'])
        cumsum_tables(16, gidx)
        for j in range(16):
            g = gblock(j // 8, j % 8)
            A('dve', lambda E, j=j, g=g: E.tensor_tensor(out=ck[:, j, :16], in0=win[:, j, :16], in1=carry[:, g, :16], op=ALU.add),
              ['win', 'carry'], ['ck'])
        for i in range(8):
            A('dve', lambda E, i=i: E.tensor_tensor(out=bias[:, i, :, :],
                                                    in0=carry[:, 2 * i + 2, :16].unsqueeze(1).to_broadcast([128, 16, 16]),
                                                    in1=ck[:, :, :16], op=ALU.subtract), ['carry', 'ck'], ['bias'])
        for hg in range(4):
            for r in range(2):
                for c in range(4):
                    r0 = c * 512 + r * 256
                    dma('sp', vgrp[:, r * 8 + c * 2: r * 8 + c * 2 + 2, :],
                        vg[l][r0:r0 + 256, hg * 512:(hg + 1) * 512].rearrange("(q p) n -> p q n", p=128),
                        [], [('vgrp', r, c)])
            for hh in range(4):
                h = hg * 4 + hh
                hb_ = h % 2
                dma('sp', kTh[hb_][:, :, :],
                    kTg[l][(h // 4) * 1024:(h // 4 + 1) * 1024, :].rearrange("(r f) t -> f r t", r=2)[(h % 4) * 128:(h % 4 + 1) * 128, :, :],
                    [], [('kTh', hb_)])
                dma('sp', qTh[hb_][:, :], qfx[h * 128:(h + 1) * 128, :], all_q('qfx', 4), [('qTh', hb_)])
                dma('sp', zTh[hb_][:, :], zax[h * 128:(h + 1) * 128, :], all_z('zax'), [('zTh', hb_)])
                for t in range(2):
                    blocks = [(ip, r) for ip in range(4 * t + 4) for r in range(2)]
                    for bi, (ip, r) in enumerate(blocks):
                        j = r * 8 + ip
                        c0 = max(0, ip - 4 * t) * 128
                        sj = bi % 2
                        A('pe', (lambda E, sj=sj, c0=c0, r=r, ip=ip, t=t, hb_=hb_: E.matmul(
                            psS[sj][:, c0:512], lhsT=kTh[hb_][:, r, ip * 128:(ip + 1) * 128],
                            rhs=qTh[hb_][:, t * 512 + c0:(t + 1) * 512], start=True, stop=True)),
                          [('kTh', hb_), ('qTh', hb_)], [('psS', sj)])
                        for isub in range(c0 // 128, 4):
                            i = 4 * t + isub
                            A('act', (lambda E, sj=sj, isub=isub, i=i, j=j, h=h: E.activation(
                                out=pT[sj][:, isub * 128:(isub + 1) * 128], in_=psS[sj][:, isub * 128:(isub + 1) * 128],
                                func=AF.Exp, bias=bias[:, i, j, h:h + 1], scale=float(FOX_SCALE))),
                              [('psS', sj), 'bias'], [('pT', sj, isub)])
                            if ip == i:
                                A('dve', (lambda E, sj=sj, isub=isub, i=i, r=r: E.tensor_tensor(
                                    out=pT[sj][:, isub * 128:(isub + 1) * 128], in0=pT[sj][:, isub * 128:(isub + 1) * 128],
                                    in1=maskf[:, (i * 2 + r) * 128:(i * 2 + r + 1) * 128], op=ALU.mult)),
                                  [('pT', sj, isub), 'maskf'], [('pT', sj, isub)])
                        first = (bi == 0)
                        last = (bi == len(blocks) - 1)
                        pnames = [('pT', sj, isub) for isub in range(c0 // 128, 4)]
                        A('pe', (lambda E, sj=sj, c0=c0, j=j, hh=hh, first=first, last=last: E.matmul(
                            psO[:, c0:512], lhsT=vgrp[:, j, hh * 128:(hh + 1) * 128], rhs=pT[sj][:, c0:512],
                            start=first, stop=last)), pnames + [('vgrp', r, ip // 2)], ['psO'], signal=False)
                        A('pe', (lambda E, sj=sj, c0=c0, first=first, last=last: E.matmul(
                            psD[:, c0:512], lhsT=onesb[:, :], rhs=pT[sj][:, c0:512], start=first, stop=last)),
                          pnames + ['onesb'], ['psD'])
                    A('dve', lambda E: E.reciprocal(out=rden[:, :], in_=psD[:, :]), ['psD'], ['rden'])
                    A('dve', lambda E: E.tensor_tensor(out=osb[:, :], in0=psO[:, :], in1=rden[:, :], op=ALU.mult),
                      ['psO', 'rden'], ['osb'])
                    A('dve', (lambda E, h=h, t=t, hb_=hb_: E.tensor_tensor(
                        out=actT[:, h, t * 512:(t + 1) * 512], in0=osb[:, :], in1=zTh[hb_][:, t * 512:(t + 1) * 512], op=ALU.mult)),
                      ['osb', ('zTh', hb_)], [('actT', tb) for tb in range(4 * t, 4 * t + 4)])
        T_.barrier()
        for b in range(2):
            dma('sp', lf_t[:, :, b * 16:(b + 1) * 16],
                cfl[(l * 2 + b) * PAST:(l * 2 + b + 1) * PAST, :].rearrange("(j p) h -> p j h", p=128), [], [('lf_tb', b)])
        T_.barrier()
        cumsum_tables(32, (lambda g: g))
        A('dve', lambda E: E.tensor_tensor(out=ck[:, :, :], in0=win[:, :, :], in1=carry[:, 0:16, :], op=ALU.add),
          ['win', 'carry'], ['ck'])
        A('pe', lambda E: E.matmul(psS[0][:32, :16], lhsT=tris[:32, :32], rhs=lfn[:32, :16], start=True, stop=True),
          ['tris', 'lfn'], [('psS', 0)])
        for b in range(2):
            A('pe', lambda E, b=b: E.matmul(psS[1][:, b * 16:(b + 1) * 16], lhsT=indw[:32, b * 128:(b + 1) * 128],
                                            rhs=lfn[:32, :16], start=True, stop=True), ['indw', 'lfn'], [('psS', 1)])
        A('act', lambda E: E.copy(out=ckn[:, :], in_=psS[0][:32, :16]), [('psS', 0)], ['ckn'])
        for b in range(2):
            A('dve', lambda E, b=b: E.scalar_tensor_tensor(out=ckn[:, :], in0=carry[:32, 16, b * 16:(b + 1) * 16],
                                                          scalar=inds[:32, b:b + 1], in1=ckn[:, :], op0=ALU.mult, op1=ALU.add),
              ['carry', 'ckn', 'inds'], ['ckn'])
        A('dve', lambda E: E.tensor_tensor(out=crefs[:, :], in0=carry[:, 16, :], in1=psS[1][:, :32], op=ALU.add),
          ['carry', ('psS', 1)], ['crefs'])
        biasc = win
        A('dve', lambda E: E.tensor_tensor(out=biasc[:, :, :], in0=crefs[:, :].unsqueeze(1).to_broadcast([128, 16, 32]),
                                           in1=ck[:, :, :], op=ALU.subtract), ['crefs', 'ck', 'win'], ['biasc'])
        for b in range(2):
            A('dve', lambda E, b=b: E.tensor_tensor(out=biasn[:, b * 16:(b + 1) * 16], in0=crefs[:32, b * 16:(b + 1) * 16],
                                                    in1=ckn[:, :], op=ALU.subtract), ['crefs', 'ckn'], ['biasn'])
        dma('sp', qTs[:, :, :], qfx[:, TP:T].rearrange("(h d) t -> d h t", d=128), all_q('qfx', 4), ['qTs'])
        dma('sp', zTs[:, :, :], zax[:, TP:T].rearrange("(h d) t -> d h t", d=128), all_z('zax'), ['zTs'])

        def sample_attn(b, kT_fn, v_fn, kTnew_list, vnew, bias_fn, biasn_ap, msk, scale, zrow, out_chunk):
            for j in range(16):
                parts = kT_fn(j)
                for pi, (ka, qa, kn) in enumerate(parts):
                    A('pe', (lambda E, j=j, ka=ka, qa=qa, pi=pi, n=len(parts): E.matmul(
                        psS[0][:, j * 16:(j + 1) * 16], lhsT=ka, rhs=qa, start=(pi == 0), stop=(pi == n - 1))),
                      kn, [('psS', 0)], signal=(pi == len(parts) - 1))
            for pi, (ka, qa, kn) in enumerate(kTnew_list):
                A('pe', (lambda E, ka=ka, qa=qa, pi=pi, n=len(kTnew_list): E.matmul(
                    psS[0][:32, 256:272], lhsT=ka, rhs=qa, start=(pi == 0), stop=(pi == n - 1))),
                  kn, [('psS', 0)], signal=(pi == len(kTnew_list) - 1))
            for j in range(16):
                if bias_fn is None:
                    A('act', (lambda E, j=j: E.activation(out=pT[0][:, j * 16:(j + 1) * 16], in_=psS[0][:, j * 16:(j + 1) * 16],
                                                          func=AF.Exp, scale=float(scale))), [('psS', 0)], [('pTs', j)])
                else:
                    bj_ = bias_fn(j)
                    A('act', (lambda E, j=j, bj_=bj_: E.activation(out=pT[0][:, j * 16:(j + 1) * 16], in_=psS[0][:, j * 16:(j + 1) * 16],
                                                          func=AF.Exp, bias=bj_, scale=float(scale))),
                      [('psS', 0), 'biasc'], [('pTs', j)])
            if biasn_ap is None:
                A('act', (lambda E: E.activation(out=pT[0][:32, 256:272], in_=psS[0][:32, 256:272], func=AF.Exp,
                                                 scale=float(scale))), [('psS', 0)], [('pTs', 16)])
            else:
                A('act', (lambda E: E.activation(out=pT[0][:32, 256:272], in_=psS[0][:32, 256:272], func=AF.Exp,
                                                 bias=biasn_ap, scale=float(scale))), [('psS', 0), 'biasn'], [('pTs', 16)])
            A('dve', (lambda E: E.tensor_tensor(out=pT[0][:32, 256:272], in0=pT[0][:32, 256:272], in1=msk, op=ALU.mult)),
              [('pTs', 16), 'masks'], [('pTs', 16)])
            for j in range(17):
                if j < 16:
                    va, vnm = v_fn(j)
                    pa = pT[0][:, j * 16:(j + 1) * 16]
                    oa = onesb[:, :]
                else:
                    va, vnm = vnew
                    pa = pT[0][:32, 256:272]
                    oa = onesb[:32, :]
                A('pe', (lambda E, va=va, pa=pa, j=j: E.matmul(psO[:, :16], lhsT=va, rhs=pa, start=(j == 0), stop=(j == 16))),
                  [('pTs', j)] + vnm, ['psO'], signal=False)
                A('pe', (lambda E, oa=oa, pa=pa, j=j: E.matmul(psD[:, :16], lhsT=oa, rhs=pa, start=(j == 0), stop=(j == 16))),
                  [('pTs', j), 'onesb'], ['psD'])
            A('dve', lambda E: E.reciprocal(out=rden[:, :16], in_=psD[:, :16]), ['psD'], ['rden'])
            A('dve', lambda E: E.tensor_tensor(out=osb[:, :16], in0=psO[:, :16], in1=rden[:, :16], op=ALU.mult),
              ['psO', 'rden'], ['osb'])
            A('dve', (lambda E: E.tensor_tensor(out=actT[:, out_chunk, TP + 16 * b:TP + 16 * b + 16], in0=osb[:, :16],
                                                in1=zrow, op=ALU.mult)), ['osb', 'zTs'], [('actT', 8)])

        for b in range(2):
            for hg in range(4):
                rows = slice((l * 2 + b) * PAST, (l * 2 + b + 1) * PAST)
                A('pool', (lambda E, rows=rows, hg=hg: E.dma_start(
                    out=kc_t, in_=cfk[rows, hg * 512:(hg + 1) * 512].rearrange("(j p) c -> p j c", p=128))),
                  [], ['kc_t'], kind='dma')
                A('pool', (lambda E, rows=rows, hg=hg: E.dma_start(
                    out=vc_t, in_=cfv[rows, hg * 512:(hg + 1) * 512].rearrange("(j p) c -> p j c", p=128))),
                  [], ['vc_t'], kind='dma')
                for hh in range(4):
                    for half in range(2):
                        pj = pt_i[0] % 2
                        pt_i[0] += 1
                        for bb in range(8):
                            A('pe', (lambda E, pj=pj, bb=bb, half=half, hh=hh: E.transpose(
                                psT[pj][:, bb * 128:(bb + 1) * 128], kc_t[:, half * 8 + bb, hh * 128:(hh + 1) * 128], identb[:, :])),
                              ['kc_t', 'identb'], [('psT', pj)], signal=(bb == 7))
                        A('act', (lambda E, pj=pj, half=half, hh=hh: E.copy(
                            out=kTc[:, hh, half * 1024:(half + 1) * 1024], in_=psT[pj][:, :])), [('psT', pj)], [('kTc', hh)])
                for hh in range(4):
                    h = hg * 4 + hh
                    qa = qTs[:, h, 16 * b:16 * b + 16]
                    sample_attn(
                        b,
                        (lambda j, hh=hh, qa=qa: [(kTc[:, hh, j * 128:(j + 1) * 128], qa, [('kTc', hh), 'qTs'])]),
                        (lambda j, hh=hh: (vc_t[:, j, hh * 128:(hh + 1) * 128], ['vc_t'])),
                        [(kTn[:, h, :], qa, ['kTn', 'qTs'])],
                        (vn[:32, h * 128:(h + 1) * 128], ['vn']),
                        (lambda j, b=b, h=h: biasc[:, j, b * 16 + h:b * 16 + h + 1]),
                        biasn[:32, b * 16 + h:b * 16 + h + 1], masks[:32, b * 16:(b + 1) * 16], FOX_SCALE,
                        zTs[:, h, 16 * b:16 * b + 16], h)
        T_.barrier()
        ph[0].close()
        if stop_after <= 3:
            break

        ph[0] = ExitStack()
        maskm = sbp(f"maskm{l}", [128, 2048], BF16); masks2 = sbp(f"masks2{l}", [32, 32], BF16)
        kpTg = sbp(f"kpTg{l}", [64, 2, TP], BF16)
        knT = sbp(f"knT{l}", [128, 2, 2048], BF16); vb = sbp(f"vb{l}", [128, 16, 256], BF16)
        vbn = sbp(f"vbn{l}", [32, 256], BF16); knTn = sbp(f"knTn{l}", [128, 2, 32], BF16)
        qnh = sbp(f"qnh{l}", [128, T], BF16); qph = sbp(f"qph{l}", [64, T], BF16); zTh2 = sbp(f"zTh2{l}", [128, T], BF16)
        pT = [sbp(f"pTm{l}_{j}", [128, 512], BF16) for j in range(2)]
        rden = sbp(f"rdenm{l}", [128, 512]); osb = sbp(f"osbm{l}", [128, 512])
        sq = sbp(f"sqm{l}", [128, 256]); wf2 = sbp(f"wf2{l}", [128, 256]); wb2 = sbp(f"wb2{l}", [128, 256], BF16)
        qnTs = sbp(f"qnTs{l}", [128, 16, TS], BF16); qpTs = sbp(f"qpTs{l}", [64, 16, TS], BF16)
        zTs = sbp(f"zTs2{l}", [128, 16, TS], BF16)
        wkv = wt[0][:, :, :].rearrange("p a b -> p (a b)").rearrange("p (kc n) -> p kc n", kc=4)
        ckTg = wt[1][:, 0:16, :].rearrange("p a b -> p (a b)").rearrange("p (kc r t) -> p kc r t", kc=4, r=2)
        ckc = wt[1][:, 16:32, :]
        dma('sp', maskm[:, :], c_maskm[:, :], [], ['maskm'])
        dma('sp', masks2[:, :], c_masks2[:, :], [], ['masks'])
        for kc in range(4):
            A('pool', (lambda E, kc=kc: E.dma_start(out=wkv[:, kc, :], in_=w_kvb[l * 512 + kc * 128: l * 512 + (kc + 1) * 128, :])),
              [], [('wkv', kc)], kind='dma')
            for r in range(2):
                dma('sp', ckTg[:, kc, r, :], ckg[l][r * 512 + kc * 128: r * 512 + (kc + 1) * 128, :], [('ckg', l, 0)], [('ckTg', kc, r)])
        dma('sp', kpTg[:, :, :], kpg[l].rearrange("(r d) t -> d r t", r=2), [('kpg', l, 0)], ['kpTg'])
        T_.barrier()

        def upproj(lhs_fn, lnames, M, hp, knT_dst_fn, v_dst, vnames, knames):
            pj = next_psA()
            for kc in range(4):
                lh = lhs_fn(kc)
                A('pe', (lambda E, kc=kc, pj=pj, lh=lh: E.matmul(psA[pj][:M, :512], lhsT=lh, rhs=wkv[:, kc, hp * 512:(hp + 1) * 512],
                                                         start=(kc == 0), stop=(kc == 3))),
                  list(lnames) + [('wkv', kc)], [('psA', pj)], signal=(kc == 3))
            ps3 = psA[pj][:M, :].rearrange("p (g c) -> p g c", g=2)
            dst3 = wf2[:M, :256].rearrange("p (g d) -> p g d", g=2)
            rms_groups(ps3[:, :, 0:128], M, 2, 128, gkn_t, dst3, [('psA', pj)], ['wf2'])
            A('act', lambda E: E.copy(out=wb2[:M, :256], in_=wf2[:M, :256]), ['wf2'], ['wb2'])
            A('dve', lambda E: E.tensor_copy(out=v_dst, in_=ps3[:, :, 128:256]), [('psA', pj)], vnames)
            transpose_to(wb2, M, 2, 128, knT_dst_fn, ['wb2'], (lambda b: knames))

        def mla_head_prompt(h, hh):
            dma('sp', qnh[:, :], qnx[h * 128:(h + 1) * 128, :], [('qnx', h // 2, tb) for tb in range(NTB)], ['qnh'])
            dma('sp', qph[:, :], qpx[h * 64:(h + 1) * 64, :], [('qpx', h // 2, tb) for tb in range(NTB)], ['qph'])
            dma('sp', zTh2[:, :], zbx[h * 128:(h + 1) * 128, :], [('zbx', h, ti) for ti in range(3)], ['zTh2'])
            for t in range(2):
                blocks = [(ip, r) for ip in range(4 * t + 4) for r in range(2)]
                for bi, (ip, r) in enumerate(blocks):
                    j = r * 8 + ip
                    c0 = max(0, ip - 4 * t) * 128
                    sj = bi % 2
                    A('pe', (lambda E, sj=sj, c0=c0, j=j, t=t: E.matmul(
                        psS[sj][:, c0:512], lhsT=knT[:, hh, j * 128:(j + 1) * 128],
                        rhs=qnh[:, t * 512 + c0:(t + 1) * 512], start=True, stop=False)),
                      [('knT', j), 'qnh'], [('psS', sj)], signal=False)
                    A('pe', (lambda E, sj=sj, c0=c0, r=r, ip=ip, t=t: E.matmul(
                        psS[sj][:, c0:512], lhsT=kpTg[:, r, ip * 128:(ip + 1) * 128],
                        rhs=qph[:, t * 512 + c0:(t + 1) * 512], start=False, stop=True)),
                      ['kpTg', 'qph'], [('psS', sj)])
                    for isub in range(c0 // 128, 4):
                        i = 4 * t + isub
                        A('act', (lambda E, sj=sj, isub=isub: E.activation(
                            out=pT[sj][:, isub * 128:(isub + 1) * 128], in_=psS[sj][:, isub * 128:(isub + 1) * 128],
                            func=AF.Exp, scale=float(MLA_SCALE))), [('psS', sj)], [('pT', sj, isub)])
                        if ip == i:
                            A('dve', (lambda E, sj=sj, isub=isub, i=i, r=r: E.tensor_tensor(
                                out=pT[sj][:, isub * 128:(isub + 1) * 128], in0=pT[sj][:, isub * 128:(isub + 1) * 128],
                                in1=maskm[:, (i * 2 + r) * 128:(i * 2 + r + 1) * 128], op=ALU.mult)),
                              [('pT', sj, isub), 'maskm'], [('pT', sj, isub)])
                    first = (bi == 0)
                    last = (bi == len(blocks) - 1)
                    pnames = [('pT', sj, isub) for isub in range(c0 // 128, 4)]
                    A('pe', (lambda E, sj=sj, c0=c0, j=j, first=first, last=last: E.matmul(
                        psO[:, c0:512], lhsT=vb[:, j, hh * 128:(hh + 1) * 128], rhs=pT[sj][:, c0:512],
                        start=first, stop=last)), pnames + [('vb', j)], ['psO'], signal=False)
                    A('pe', (lambda E, sj=sj, c0=c0, first=first, last=last: E.matmul(
                        psD[:, c0:512], lhsT=onesb[:, :], rhs=pT[sj][:, c0:512], start=first, stop=last)),
                      pnames + ['onesb'], ['psD'])
                A('dve', lambda E: E.reciprocal(out=rden[:, :], in_=psD[:, :]), ['psD'], ['rden'])
                A('dve', lambda E: E.tensor_tensor(out=osb[:, :], in0=psO[:, :], in1=rden[:, :], op=ALU.mult),
                  ['psO', 'rden'], ['osb'])
                A('dve', (lambda E, t=t: E.tensor_tensor(
                    out=actT[:, 16 + h, t * 512:(t + 1) * 512], in0=osb[:, :], in1=zTh2[:, t * 512:(t + 1) * 512], op=ALU.mult)),
                  ['osb', 'zTh2'], [('actT', tb) for tb in range(4 * t, 4 * t + 4)])

        for hp in range(8):
            for j in range(16):
                r, ip = divmod(j, 8)
                upproj((lambda kc, r=r, ip=ip: ckTg[:, kc, r, ip * 128:(ip + 1) * 128]),
                       [('ckTg', kc, r) for kc in range(4)], 128, hp,
                       (lambda b, j=j: knT[:, b, j * 128:(j + 1) * 128]),
                       vb[:, j, :].rearrange("p (g d) -> p g d", g=2), [('vb', j)], [('knT', j)])
            for hh in range(2):
                mla_head_prompt(hp * 2 + hh, hh)
        T_.barrier()
        ckTc = wt[1][:, 0:16, :].rearrange("p a b -> p (a b)").rearrange("p (kc t) -> p kc t", kc=4)
        kpTc = kpTg[:, :, :].rearrange("d r t -> d (r t)")
        kpc = wt[0][:, 0:2, :].rearrange("p a b -> p (a b)").rearrange("p (j c) -> p j c", j=16)
        dma('sp', qnTs[:, :, :], qnx[:, TP:T].rearrange("(h d) t -> d h t", d=128), [('qnx', a, 8) for a in range(8)], ['qTs'])
        dma('sp', qpTs[:, :, :], qpx[:, TP:T].rearrange("(h d) t -> d h t", d=64), [('qpx', a, 8) for a in range(8)], ['qTs2'])
        dma('sp', zTs[:, :, :], zbx[:, TP:T].rearrange("(h d) t -> d h t", d=128), all_z('zbx'), ['zTs'])
        kpc = sq
        kpcb = sbp(f"kpcb{l}", [128, 16, 64], BF16)
        for b in range(2):
            rows = slice((l * 2 + b) * PAST, (l * 2 + b + 1) * PAST)
            A('pool', (lambda E, rows=rows: E.dma_start(out=ckc, in_=cck[rows, :].rearrange("(j p) c -> p j c", p=128))),
              [], ['ckc'], kind='dma')
            A('pool', (lambda E, rows=rows: E.dma_start(out=kpcb[:, :, :], in_=ckp[rows, :].rearrange("(j p) c -> p j c", p=128))),
              [], ['kpcb'], kind='dma')
            for j in range(16):
                pj = pt_i[0] % 2
                pt_i[0] += 1
                for kc in range(4):
                    A('pe', (lambda E, pj=pj, j=j, kc=kc: E.transpose(psT[pj][:, kc * 128:(kc + 1) * 128],
                                                                     ckc[:, j, kc * 128:(kc + 1) * 128], identb[:, :])),
                      ['ckc', 'identb'], [('psT', pj)], signal=(kc == 3))
                A('act', (lambda E, pj=pj, j=j: E.copy(out=ckTc[:, :, j * 128:(j + 1) * 128],
                                                       in_=psT[pj][:, :512].rearrange("p (kc t) -> p kc t", kc=4))),
                  [('psT', pj)], ['ckTc'])
            for half in range(2):
                pj = pt_i[0] % 2
                pt_i[0] += 1
                for bb in range(8):
                    A('pe', (lambda E, pj=pj, bb=bb, half=half: E.transpose(psT[pj][:64, bb * 128:(bb + 1) * 128],
                                                                           kpcb[:, half * 8 + bb, :], identb[:, :])),
                      ['kpcb', 'identb'], [('psT', pj)], signal=(bb == 7))
                A('act', (lambda E, pj=pj, half=half: E.copy(out=kpTc[:, half * 1024:(half + 1) * 1024], in_=psT[pj][:64, :])),
                  [('psT', pj)], ['kpTc'])
            for hp in range(8):
                for j in range(16):
                    upproj((lambda kc, j=j: ckTc[:, kc, j * 128:(j + 1) * 128]), ['ckTc'], 128, hp,
                           (lambda bq, j=j: knT[:, bq, j * 128:(j + 1) * 128]),
                           vb[:, j, :].rearrange("p (g d) -> p g d", g=2), [('vb', j)], [('knT', j)])
                upproj((lambda kc: ckTn[:, kc, :]), ['ckTn'], 32, hp, (lambda bq: knTn[:, bq, :]),
                       vbn[:32, :].rearrange("p (g d) -> p g d", g=2), ['vbn'], ['knTn'])
                for hh in range(2):
                    h = hp * 2 + hh
                    qn_ = qnTs[:, h, 16 * b:16 * b + 16]
                    qp_ = qpTs[:, h, 16 * b:16 * b + 16]
                    sample_attn(
                        b,
                        (lambda j, hh=hh, qn_=qn_, qp_=qp_: [(knT[:, hh, j * 128:(j + 1) * 128], qn_, [('knT', j), 'qTs']),
                                                            (kpTc[:, j * 128:(j + 1) * 128], qp_, ['kpTc', 'qTs2'])]),
                        (lambda j, hh=hh: (vb[:, j, hh * 128:(hh + 1) * 128], [('vb', j)])),
                        [(knTn[:, hh, :], qn_, ['knTn', 'qTs']), (kpTn[:, :], qp_, ['kpTn', 'qTs2'])],
                        (vbn[:32, hh * 128:(hh + 1) * 128], ['vbn']),
                        None, None, masks2[:32, b * 16:(b + 1) * 16], MLA_SCALE,
                        zTs[:, h, 16 * b:16 * b + 16], 16 + h)
        T_.barrier()
        ph[0].close()
        if stop_after <= 4:
            break

        ph[0] = ExitStack()
        xr = [sbp(f"xr{l}_{j}", [128, 512]) for j in range(2)]
        yo = [sbp(f"yo{l}_{j}", [128, 512]) for j in range(2)]
        for cb in range(8):
            wj = load_w(w_out[l * D:(l + 1) * D, :], D, 512, cb * 512)
            for tb in range(NTB):
                pj, M = proj_tm(tb, [('actT', tb)], hT_lhs, 32, wj, 512)
                j = tb % 2
                cs = slice(cb * 512, (cb + 1) * 512)
                if l == 0:
                    xsrc = xp[tb * 128: tb * 128 + M, cs] if tb < 8 else xs[:, cs]
                else:
                    xsrc = ymid[tb * 128: tb * 128 + M, cs]
                dma('sp', xr[j][:M, :], xsrc, [], [('xr', j)])
                A('dve', lambda E, j=j, M=M, pj=pj: E.tensor_tensor(out=yo[j][:M, :], in0=psA[pj][:M, :], in1=xr[j][:M, :], op=ALU.add),
                  [('psA', pj), ('xr', j)], [('yo', j)])
                if l == 0:
                    dst = ymid[tb * 128: tb * 128 + M, cs]
                else:
                    dst = yp[tb * 128: tb * 128 + M, cs] if tb < 8 else ys[:, cs]
                dma('sp', dst, yo[j][:M, :], [('yo', j)], [])
        T_.barrier()
        ph[0].close()

    T_.final_wait('sp')
    sem_keys = T_.sem_keys()
    sems = {}
    for k in sem_keys:
        sems[k] = es.enter_context(nc.semaphore("s_" + "_".join(str(x) for x in k)))
    with nc.Block() as block:
        T_.emit(block, sems)
    es.close()
    return nc


def host_constants(r):
    bf = ml_dtypes.bfloat16
    c = {}
    c["c_identb"] = np.eye(128, dtype=np.float32).astype(bf)
    c["c_identf"] = np.eye(128, dtype=np.float32)
    c["c_onesb"] = np.ones((128, 128), np.float32).astype(bf)
    c["c_onesf"] = np.ones((128, 128), np.float32)
    c["c_trif"] = np.triu(np.ones((128, 128), np.float32))
    pos = np.concatenate([np.concatenate([gblock(r, i) * 128 + np.arange(128) for i in range(8)]),
                          PAST + np.arange(16), PAST + np.arange(16)]).astype(np.float32)
    half = 32
    inv_freq = (1.0 / (np.float32(10000.0) ** (np.arange(half, dtype=np.float32) / np.float32(half)))).astype(np.float32)
    ang = (pos[:, None] * inv_freq[None, :]).astype(np.float32)
    c["c_cos"] = np.cos(ang).astype(np.float32)
    c["c_sin"] = np.sin(ang).astype(np.float32)
    mf = np.zeros((128, 16, 128), np.float32)
    mm = np.zeros((128, 16, 128), np.float32)
    for i in range(8):
        qpos = gblock(r, i) * 128 + np.arange(128)
        for rk in range(2):
            kpos = gblock(rk, i) * 128 + np.arange(128)
            mf[:, i * 2 + rk, :] = (kpos[:, None] <= qpos[None, :])
            mm[:, i * 2 + rk, :] = ((kpos[:, None] // 64) <= (qpos[None, :] // 64))
    c["c_maskf"] = mf.reshape(128, 2048).astype(bf)
    c["c_maskm"] = mm.reshape(128, 2048).astype(bf)
    ms = np.zeros((32, 2, 16), np.float32)
    for b in range(2):
        for k in range(16):
            ms[b * 16 + k, b, :] = (k <= np.arange(16))
    c["c_masks"] = ms.reshape(32, 32).astype(bf)
    ts_ = np.zeros((32, 32), np.float32)
    for b in range(2):
        ts_[b * 16:(b + 1) * 16, b * 16:(b + 1) * 16] = np.triu(np.ones((16, 16), np.float32))
    c["c_tris"] = ts_
    ind = np.zeros((32, 2), np.float32)
    ind[:16, 0] = 1
    ind[16:, 1] = 1
    c["c_inds"] = ind
    c["c_indw"] = np.repeat(ind[:, :, None], 128, axis=2).reshape(32, 256).astype(np.float32)
    m2 = np.zeros((32, 2, 16), np.float32)
    m2[:16, 0, :] = 1
    m2[16:, 1, :] = 1
    c["c_masks2"] = m2.reshape(32, 32).astype(bf)
    return c


_NC_CACHE = {}


def make_in_maps(inp):
    f = lambda a: np.ascontiguousarray(np.asarray(a, dtype=np.float32))
    xpr = f(inp["x_prompt"]); xsm = f(inp["x_sample"])
    maps = []
    wshared = {
        "g_norm": f(inp["g_norm"]), "w_in": f(inp["w_in"]).reshape(NL * D, N_IN), "b_f": f(inp["b_f"]),
        "g_q_fox": f(inp["g_q_fox"]), "g_k_fox": f(inp["g_k_fox"]), "g_cq": f(inp["g_cq"]),
        "w_qb": f(inp["w_qb"]).reshape(NL * 1024, 3072), "g_qn": f(inp["g_qn"]), "g_qp": f(inp["g_qp"]),
        "g_ckv": f(inp["g_ckv"]), "g_kp": f(inp["g_kp"]), "w_kvb": f(inp["w_kvb"]).reshape(NL * 512, 4096),
        "g_kn": f(inp["g_kn"]), "w_out": f(inp["w_out"]).reshape(NL * D, D),
    }
    cfk = f(inp["cache_fox_k"]); cfv = f(inp["cache_fox_v"]); cfl = f(inp["cache_fox_logf"])
    cck = f(inp["cache_mla_ckv"]); ckp = f(inp["cache_mla_kpe"])
    for c in range(8):
        p, r = c // 2, c % 2
        m = dict(wshared)
        m["xp"] = np.concatenate([xpr[p, gblock(r, i) * 128:(gblock(r, i) + 1) * 128] for i in range(8)], axis=0)
        m["xs"] = xsm[2 * c:2 * c + 2].reshape(TS, D)
        m["cfk"] = cfk[:, 2 * c:2 * c + 2].reshape(NL * 2 * PAST, 2048)
        m["cfv"] = cfv[:, 2 * c:2 * c + 2].reshape(NL * 2 * PAST, 2048)
        m["cfl"] = cfl[:, 2 * c:2 * c + 2].reshape(NL * 2 * PAST, 16)
        m["cck"] = cck[:, 2 * c:2 * c + 2].reshape(NL * 2 * PAST, 512)
        m["ckp"] = ckp[:, 2 * c:2 * c + 2].reshape(NL * 2 * PAST, 64)
        m.update(host_constants(r))
        maps.append(m)
    return maps


def assemble(results):
    y_p = np.zeros((4, 2048, D), np.float32); y_s = np.zeros((16, 16, D), np.float32)
    fk_p = np.zeros((NL, 4, 2048, H, 128), np.float32); fv_p = np.zeros_like(fk_p)
    fl_p = np.zeros((NL, 4, 2048, H), np.float32); ck_p = np.zeros((NL, 4, 2048, 512), np.float32)
    kp_p = np.zeros((NL, 4, 2048, 64), np.float32)
    fk_s = np.zeros((NL, 16, 16, H, 128), np.float32); fv_s = np.zeros_like(fk_s)
    fl_s = np.zeros((NL, 16, 16, H), np.float32); ck_s = np.zeros((NL, 16, 16, 512), np.float32)
    kp_s = np.zeros((NL, 16, 16, 64), np.float32)
    for c in range(8):
        p, r = c // 2, c % 2
        R = results[c]
        for i in range(8):
            g = gblock(r, i)
            sl = slice(g * 128, (g + 1) * 128)
            y_p[p, sl] = R["yp"][i * 128:(i + 1) * 128]
            for l in range(NL):
                rows = slice(l * TP + i * 128, l * TP + (i + 1) * 128)
                fk_p[l, p, sl] = R["nfk_p"][rows].reshape(128, H, 128)
                fv_p[l, p, sl] = R["nfv_p"][rows].reshape(128, H, 128)
                fl_p[l, p, sl] = R["nfl_p"][rows]
                ck_p[l, p, sl] = R["nck_p"][rows]
                kp_p[l, p, sl] = R["nkp_p"][rows]
        y_s[2 * c:2 * c + 2] = R["ys"].reshape(2, 16, D)
        for l in range(NL):
            rows = slice(l * TS, (l + 1) * TS)
            fk_s[l, 2 * c:2 * c + 2] = R["nfk_s"][rows].reshape(2, 16, H, 128)
            fv_s[l, 2 * c:2 * c + 2] = R["nfv_s"][rows].reshape(2, 16, H, 128)
            fl_s[l, 2 * c:2 * c + 2] = R["nfl_s"][rows].reshape(2, 16, H)
            ck_s[l, 2 * c:2 * c + 2] = R["nck_s"][rows].reshape(2, 16, 512)
            kp_s[l, 2 * c:2 * c + 2] = R["nkp_s"][rows].reshape(2, 16, 64)
    return (y_p, y_s, fk_p, fv_p, fl_p, ck_p, kp_p, fk_s, fv_s, fl_s, ck_s, kp_s)


def kernel(**inputs):
    nc = build_program()
    maps = make_in_maps(inputs)
    res = run_bass_kernel_spmd(nc, maps, core_ids=list(range(8)))
    return assemble(res.results)
```

```python
import numpy as np
import ml_dtypes
from contextlib import ExitStack
import concourse.bass as bass
import concourse.mybir as mybir
from concourse.bass_utils import run_bass_kernel_spmd

F32 = mybir.dt.float32
BF16 = mybir.dt.bfloat16
ALU = mybir.AluOpType
AF = mybir.ActivationFunctionType
AX = mybir.AxisListType

D = 4096
NL = 2
H = 16
TP = 1024
TS = 32
T = TP + TS
NTB = 9
PAST = 2048
EPS = 1e-6
N_IN = 11856
OFF_Q, OFF_K, OFF_V, OFF_F, OFF_ZA, OFF_CQ, OFF_CKV, OFF_KPE, OFF_ZB = (
    0, 2048, 4096, 6144, 6160, 8208, 9232, 9744, 9808)
FOX_SCALE = 1.0 / np.sqrt(128.0)
MLA_SCALE = 1.0 / np.sqrt(192.0)


def gblock(r, i):
    return 2 * i + (i % 2) if r == 0 else 2 * i + 1 - (i % 2)


def tb_rows(tb):
    return 128 if tb < 8 else TS


def _freeze(fn):
    import types
    if fn is None or fn.__closure__ is None:
        return fn
    cells = []
    for c in fn.__closure__:
        try:
            cells.append(types.CellType(c.cell_contents))
        except ValueError:
            cells.append(c)
    return types.FunctionType(fn.__code__, fn.__globals__, fn.__name__, fn.__defaults__, tuple(cells))


class Tr:
    ENG = ('pe', 'act', 'dve', 'pool', 'sp')
    NDS = 20

    def __init__(s):
        s.ops = {e: [] for e in s.ENG}
        s.cnt = {e: 0 for e in s.ENG}
        s.dcnt = {e: 0 for e in s.ENG}
        s.lw = {}
        s.rd = {}
        s.waited = {e: {} for e in s.ENG}
        s.semval = {}
        s.ccn = 0
        s.maxops = None

    def _need(s, eng, deps):
        w = []
        for (k, v) in deps:
            if k == ('c', 'pe') and eng == 'pe':
                continue
            if s.waited[eng].get(k, 0) < v:
                s.waited[eng][k] = v
                w.append((k, v))
        return w

    def add(s, eng, fn, reads=(), writes=(), kind='c', signal=True):
        s.nadd = getattr(s, 'nadd', 0) + 1
        if s.maxops is not None and s.nadd > s.maxops:
            return None
        deps = set()
        for b in reads:
            if b in s.lw:
                deps.add(s.lw[b])
            if (isinstance(b, tuple) and b[0] in ('psA', 'psT', 'psS')) or b in ('psO', 'psD'):
                for t in s.rd.get(b, ()):
                    if t[0] != ('c', eng):
                        deps.add(t)
        for b in writes:
            if b in s.lw and s.lw[b][0] != ('c', eng):
                deps.add(s.lw[b])
            for t in s.rd.get(b, ()):
                if t[0] != ('c', eng):
                    deps.add(t)
        if kind == 'dma':
            k = s.dcnt[eng]
            s.dcnt[eng] += 1
            slot = k % s.NDS
            m = k // s.NDS + 1
            key = ('d', eng, slot)
            if m > 1:
                deps.add((key, 16 * (m - 1)))
            tok = (key, 16 * m)
        elif kind == 'cc':
            key = ('cc',)
            s.ccn += 1
            tok = (key, s.ccn)
        else:
            key = ('c', eng)
            if signal:
                s.cnt[eng] += 1
                tok = (key, s.cnt[eng])
            else:
                tok = (key, s.cnt[eng] + 1)
        s.semval[key] = max(s.semval.get(key, 0), tok[1])
        waits = s._need(eng, sorted(deps, key=str))
        for b in writes:
            s.lw[b] = tok
            s.rd[b] = []
        for b in reads:
            s.rd.setdefault(b, []).append(tok)
        s.ops[eng].append((waits, _freeze(fn), kind, signal, key))
        return tok

    def _allvals(s):
        d = dict(s.semval)
        for e in s.ENG:
            if ('c', e) in d:
                d[('c', e)] = s.cnt[e]
        return sorted([(k, v) for k, v in d.items() if v > 0], key=str)

    def barrier(s):
        if s.maxops is not None and getattr(s, 'nadd', 0) > s.maxops:
            return
        deps = s._allvals()
        for e in s.ENG:
            w = s._need(e, deps)
            if w:
                s.ops[e].append((w, None, 'c', False, None))

    def final_wait(s, eng='sp'):
        deps = s._allvals()
        w = s._need(eng, deps)
        if w:
            s.ops[eng].append((w, None, 'c', False, None))

    def sem_keys(s):
        return list(s.semval.keys())

    def emit(s, block, sems):
        def run(name):
            def f(E):
                for waits, fn, kind, signal, key in s.ops[name]:
                    for (k, v) in waits:
                        E.wait_ge(sems[k], v)
                    if fn is None:
                        continue
                    ins = fn(E)
                    if kind == 'dma':
                        ins.then_inc(sems[key], 16)
                    elif kind == 'cc':
                        ins.then_inc(sems[key])
                    elif signal:
                        ins.then_inc(sems[key], 1)
            return f
        block.tensor(run('pe'))
        block.scalar(run('act'))
        block.vector(run('dve'))
        block.gpsimd(run('pool'))
        block.sync(run('sp'))


def build_program(stop_after=99, n_cores=8, maxops=None, lite=()):
    nc = bass.Bass("TRN2", target_bir_lowering=False)
    T_ = Tr()
    T_.maxops = maxops

    def din(name, shape, dt=F32):
        if name in lite:
            shape = [8, shape[1]]
        return nc.dram_tensor(name, list(shape), dt, kind="ExternalInput").ap()

    def dout(name, shape, dt=F32):
        return nc.dram_tensor(name, list(shape), dt, kind="ExternalOutput").ap()

    def dint(name, shape, dt=F32):
        return nc.dram_tensor(name, list(shape), dt).ap()

    xp = din("xp", [TP, D]); xs = din("xs", [TS, D])
    cfk = din("cfk", [NL * 2 * PAST, 2048]); cfv = din("cfv", [NL * 2 * PAST, 2048])
    cfl = din("cfl", [NL * 2 * PAST, 16]); cck = din("cck", [NL * 2 * PAST, 512])
    ckp = din("ckp", [NL * 2 * PAST, 64])
    g_norm = din("g_norm", [NL, D]); w_in = din("w_in", [NL * D, N_IN])
    b_f = din("b_f", [NL, 16]); g_qf = din("g_q_fox", [NL, 128]); g_kf = din("g_k_fox", [NL, 128])
    g_cq = din("g_cq", [NL, 1024]); w_qb = din("w_qb", [NL * 1024, 3072])
    g_qn = din("g_qn", [NL, 128]); g_qp = din("g_qp", [NL, 64]); g_ckv = din("g_ckv", [NL, 512])
    g_kp = din("g_kp", [NL, 64]); w_kvb = din("w_kvb", [NL * 512, 4096]); g_kn = din("g_kn", [NL, 128])
    w_out = din("w_out", [NL * D, D])
    c_identb = din("c_identb", [128, 128], BF16); c_identf = din("c_identf", [128, 128])
    c_onesb = din("c_onesb", [128, 128], BF16); c_onesf = din("c_onesf", [128, 128])
    c_trif = din("c_trif", [128, 128])
    c_cos = din("c_cos", [T, 32]); c_sin = din("c_sin", [T, 32])
    c_maskf = din("c_maskf", [128, 16 * 128], BF16); c_maskm = din("c_maskm", [128, 16 * 128], BF16)
    c_masks = din("c_masks", [32, 32], BF16)
    c_tris = din("c_tris", [32, 32]); c_inds = din("c_inds", [32, 2]); c_indw = din("c_indw", [32, 256]); c_masks2 = din("c_masks2", [32, 32], BF16)

    yp = dout("yp", [TP, D]); ys = dout("ys", [TS, D])
    nfk_p = dout("nfk_p", [NL * TP, 2048]); nfv_p = dout("nfv_p", [NL * TP, 2048])
    nfl_p = dout("nfl_p", [NL * TP, 16]); nck_p = dout("nck_p", [NL * TP, 512]); nkp_p = dout("nkp_p", [NL * TP, 64])
    nfk_s = dout("nfk_s", [NL * TS, 2048]); nfv_s = dout("nfv_s", [NL * TS, 2048])
    nfl_s = dout("nfl_s", [NL * TS, 16]); nck_s = dout("nck_s", [NL * TS, 512]); nkp_s = dout("nkp_s", [NL * TS, 64])

    ymid = dint("ymid", [T, D])
    kTx = [dint(f"kTx{l}", [2048, TP], BF16) for l in range(NL)]
    vx = [dint(f"vx{l}", [TP, 2048], BF16) for l in range(NL)]
    lfx = [dint(f"lfx{l}", [TP, 16]) for l in range(NL)]
    ckx = [dint(f"ckx{l}", [512, TP], BF16) for l in range(NL)]
    kpx = [dint(f"kpx{l}", [64, TP], BF16) for l in range(NL)]
    kTg = [dint(f"kTg{l}", [2 * 2048, TP], BF16) for l in range(NL)]
    vg = [dint(f"vg{l}", [2 * TP, 2048], BF16) for l in range(NL)]
    lfg = [dint(f"lfg{l}", [2 * TP, 16]) for l in range(NL)]
    ckg = [dint(f"ckg{l}", [2 * 512, TP], BF16) for l in range(NL)]
    kpg = [dint(f"kpg{l}", [2 * 64, TP], BF16) for l in range(NL)]
    qfx = dint("qfx", [2048, T], BF16); qnx = dint("qnx", [2048, T], BF16)
    qpx = dint("qpx", [1024, T], BF16); zax = dint("zax", [2048, T], BF16); zbx = dint("zbx", [2048, T], BF16)

    es = ExitStack()
    ph = [None]

    def sb(name, shape, dt=F32):
        return es.enter_context(nc.sbuf_tensor(name, list(shape), dt))

    def sbp(name, shape, dt=F32):
        return ph[0].enter_context(nc.sbuf_tensor(name, list(shape), dt))

    def pst(name, shape, dt=F32):
        return es.enter_context(nc.psum_tensor(name, list(shape), dt))

    A = T_.add

    identb = sb("identb", [128, 128], BF16); identf = sb("identf", [128, 128])
    onesb = sb("onesb", [128, 128], BF16); onesf = sb("onesf", [128, 128]); trif = sb("trif", [128, 128])
    cosT = sb("cosT", [128, NTB, 32]); sinT = sb("sinT", [128, NTB, 32])
    masks = sb("masks", [32, 32], BF16); tris = sb("tris", [32, 32]); inds = sb("inds", [32, 2])
    actT = sb("actT", [128, 32, T], BF16)
    wt = [sb(f"wt{j}", [128, 32, 512], BF16) for j in range(2)]
    gcol = sb("gcol", [128, 32]); g32 = sb("g32", [32, 128]); gcq_t = sb("gcq_t", [128, 1024]); gckv_t = sb("gckv_t", [128, 512])
    gqf_t = sb("gqf_t", [128, 128]); gkf_t = sb("gkf_t", [128, 128]); gqn_t = sb("gqn_t", [128, 128])
    gkn_t = sb("gkn_t", [128, 128]); gqp_t = sb("gqp_t", [128, 64]); gkp_t = sb("gkp_t", [128, 64])
    bf_t = sb("bf_t", [128, 16])
    kTn = sb("kTn", [128, H, TS], BF16)
    vn = sb("vn", [TS, 2048], BF16)
    lfn = sb("lfn", [TS, 16])
    ckTn = sb("ckTn", [128, 4, TS], BF16)
    kpTn = sb("kpTn", [64, TS], BF16)
    ss = sb("ss", [128, 16]); rs = sb("rs", [128, 16])
    ssr = [sb(f"ssr{k}", [128, 16]) for k in range(4)]; rsr = [sb(f"rsr{k}", [128, 16]) for k in range(4)]
    rms_i = [0]
    rt = sb("rt", [128, 64]); rt2 = sb("rt2", [128, 64])

    psA = [pst(f"psA{j}", [128, 512]) for j in range(2)]
    psT = [pst(f"psT{j}", [128, 1024], BF16) for j in range(2)]
    psS = [pst(f"psS{j}", [128, 512]) for j in range(2)]
    psO = pst("psO", [128, 512]); psD = pst("psD", [128, 512])
    psA = psA + [psS[0], psS[1], psO, psD]
    npsA = [6]

    def dma(eng, out, in_, reads, writes):
        return A(eng, lambda E: E.dma_start(out=out, in_=in_), reads, writes, kind='dma')

    for (dst, src, nm) in [(identb, c_identb, 'identb'), (identf, c_identf, 'identf'), (onesb, c_onesb, 'onesb'),
                           (onesf, c_onesf, 'onesf'), (trif, c_trif, 'trif'), (masks, c_masks, 'masks'), (tris, c_tris, 'tris'),
                           (inds, c_inds, 'inds')]:
        dma('sp', dst[:], src[:, :], [], [nm])
    for tb in range(NTB):
        M = tb_rows(tb)
        dma('sp', cosT[:M, tb, :], c_cos[tb * 128: tb * 128 + M, :], [], [('cos', tb)])
        dma('sp', sinT[:M, tb, :], c_sin[tb * 128: tb * 128 + M, :], [], [('sin', tb)])

    pa_i = [0]

    def next_psA():
        j = pa_i[0] % npsA[0]
        pa_i[0] += 1
        return j

    wt_i = [0]

    def load_w(src2d, K, W, c0):
        j = wt_i[0] % 2
        wt_i[0] += 1
        KC = K // 128
        step = 8
        for k0 in range(0, KC, step):
            k1 = min(KC, k0 + step)
            src = src2d[k0 * 128:k1 * 128, c0:c0 + W].rearrange("(kc p) n -> p kc n", p=128)
            A('pool', (lambda E, o=wt[j][:, k0:k1, :W], i=src: E.dma_start(out=o, in_=i)),
              [], [('wt', j, k0)], kind='dma')
        return j

    def proj_tm(tb, actT_names, lhs_fn, KC, wj, W, wcol0=0):
        M = tb_rows(tb)
        pj = next_psA()
        for kc in range(KC):
            lh = lhs_fn(kc, tb, M)
            A('pe', (lambda E, kc=kc, lh=lh: E.matmul(psA[pj][:M, :W], lhsT=lh,
                                              rhs=wt[wj][:, kc, wcol0:wcol0 + W],
                                              start=(kc == 0), stop=(kc == KC - 1))),
              list(actT_names) + [('wt', wj, (kc // 8) * 8)], [('psA', pj)], signal=(kc == KC - 1))
        return pj, M

    def hT_lhs(kc, tb, M):
        return actT[:, kc, tb * 128: tb * 128 + M]

    def rms_groups(src3, M, G, Dg, gain_t, dst3, src_names, dst_names):
        sq3 = sq[:M, :G * Dg].rearrange("p (g d) -> p g d", g=G)
        k_ = rms_i[0] % 4
        rms_i[0] += 1
        ss_, rs_, sn, rn = ssr[k_], rsr[k_], ('ss', k_), ('rs', k_)
        A('act', lambda E: E.activation(out=sq3, in_=src3, func=AF.Square), src_names, ['sq'])
        A('dve', lambda E: E.tensor_reduce(out=ss_[:M, :G], in_=sq3, axis=AX.X, op=ALU.add), ['sq'], [sn])
        A('dve', lambda E: E.tensor_scalar(out=ss_[:M, :G], in0=ss_[:M, :G], scalar1=1.0 / Dg, scalar2=EPS,
                                           op0=ALU.mult, op1=ALU.add), [sn], [sn])
        A('act', lambda E: E.activation(out=ss_[:M, :G], in_=ss_[:M, :G], func=AF.Sqrt), [sn], [sn])
        A('dve', lambda E: E.reciprocal(out=rs_[:M, :G], in_=ss_[:M, :G]), [sn], [rn])
        A('dve', lambda E: E.tensor_tensor(out=dst3, in0=src3,
                                           in1=rs_[:M, :G].unsqueeze(2).to_broadcast([M, G, Dg]), op=ALU.mult),
          list(src_names) + [rn], dst_names)
        A('dve', lambda E: E.tensor_tensor(out=dst3, in0=dst3,
                                           in1=gain_t[:M, :Dg].unsqueeze(1).to_broadcast([M, G, Dg]), op=ALU.mult),
          list(dst_names) + ['gains'], dst_names)

    pt_i = [0]

    def transpose_to(src_bf, M, nblk, blkw, dst_fn, src_names, dst_names_fn, evac='act', scale_fn=None):
        pj = pt_i[0] % 2
        pt_i[0] += 1
        for b in range(nblk):
            A('pe', (lambda E, b=b: E.transpose(psT[pj][:blkw, b * 128: b * 128 + M],
                                                src_bf[:M, b * blkw:(b + 1) * blkw], identb[:M, :M])),
              list(src_names) + ['identb'], [('psT', pj)], signal=(b == nblk - 1))
        for b in range(nblk):
            d_ = dst_fn(b)
            if scale_fn is not None:
                sc_ = scale_fn(b)
                A('act', (lambda E, b=b, d_=d_, sc_=sc_: E.activation(out=d_, in_=psT[pj][:blkw, b * 128: b * 128 + M],
                                                      func=AF.Copy, scale=sc_)),
                  [('psT', pj), 'gcol'], dst_names_fn(b))
            else:
                A(evac, (lambda E, b=b, d_=d_: (E.copy if evac == 'act' else E.tensor_copy)(
                    out=d_, in_=psT[pj][:blkw, b * 128: b * 128 + M])),
                  [('psT', pj)], dst_names_fn(b))

    def rope(x3, M, G, tb, tmp_a, tmp_b, names):
        c = cosT[:M, tb, :].unsqueeze(1).to_broadcast([M, G, 32])
        s_ = sinT[:M, tb, :].unsqueeze(1).to_broadcast([M, G, 32])
        x1 = x3[:, :, 0:32]
        x2 = x3[:, :, 32:64]
        ta = tmp_a[:M, :G * 32].rearrange("p (g d) -> p g d", g=G)
        tb_ = tmp_b[:M, :G * 32].rearrange("p (g d) -> p g d", g=G)
        rn = list(names) + [('cos', tb), ('sin', tb)]
        A('dve', lambda E: E.tensor_tensor(out=ta, in0=x1, in1=s_, op=ALU.mult), rn, ['rt'])
        A('dve', lambda E: E.tensor_tensor(out=tb_, in0=x2, in1=s_, op=ALU.mult), rn, ['rt2'])
        A('dve', lambda E: E.tensor_tensor(out=x1, in0=x1, in1=c, op=ALU.mult), rn, names)
        A('dve', lambda E: E.tensor_tensor(out=x2, in0=x2, in1=c, op=ALU.mult), rn, names)
        A('dve', lambda E: E.tensor_tensor(out=x1, in0=x1, in1=tb_, op=ALU.subtract), list(names) + ['rt2'], names)
        A('dve', lambda E: E.tensor_tensor(out=x2, in0=x2, in1=ta, op=ALU.add), list(names) + ['rt'], names)

    def out_rows(l, tb, M, pbuf, sbuf_, width, c0=0):
        if tb < 8:
            return pbuf[l * TP + tb * 128: l * TP + tb * 128 + M, c0:c0 + width]
        return sbuf_[l * TS: l * TS + M, c0:c0 + width]

    for l in range(NL):
        T_.barrier()
        ph[0] = ExitStack()
        xt = [sbp(f"xt{l}", [128, D])]
        hb = sbp(f"hb{l}", [128, D], BF16)
        sq = sbp(f"sq{l}", [128, 1024])
        wf = [sbp(f"wf{l}_{j}", [128, 512]) for j in range(2)]
        wb = [sbp(f"wb{l}_{j}", [128, 512], BF16) for j in range(2)]
        tsb = [sbp(f"tsb{l}_{j}", [128, 512], BF16) for j in range(2)]
        cqT = sbp(f"cqT{l}", [128, 8, T], BF16)
        dma('sp', g32[:, :], g_norm[l, :].rearrange("(c p) -> c p", p=128), [], ['g32'])
        A('pe', lambda E: E.transpose(psS[0][:, :32], g32[:32, :], identf[:32, :32]), ['g32', 'identf'], [('psS', 0)])
        A('act', lambda E: E.copy(out=gcol[:, :], in_=psS[0][:, :32]), [('psS', 0)], ['gcol'])
        for (dst, src, w) in [(gcq_t, g_cq, 1024), (gckv_t, g_ckv, 512), (gqf_t, g_qf, 128),
                              (gkf_t, g_kf, 128), (gqn_t, g_qn, 128), (gkn_t, g_kn, 128), (gqp_t, g_qp, 64),
                              (gkp_t, g_kp, 64), (bf_t, b_f, 16)]:
            dma('sp', dst[:, :w], src[l, :].partition_broadcast(128), [], ['gains'])
        T_.barrier()

        for tb in range(NTB):
            M = tb_rows(tb)
            j = 0
            if l == 0:
                src = xp[tb * 128: tb * 128 + M, :] if tb < 8 else xs[:, :]
            else:
                src = ymid[tb * 128: tb * 128 + M, :]
            dma('sp', xt[j][:M, :], src, [('ymid', tb)], [('xt', j)])
            A('dve', lambda E, j=j, M=M: E.scalar_tensor_tensor(out=hb[:M, :], in0=xt[j][:M, :], scalar=1.0,
                                                               in1=xt[j][:M, :], op0=ALU.mult, op1=ALU.mult,
                                                               accum_out=ss[:M, 0:1]),
              [('xt', j)], ['hb', 'ss'])
            A('dve', lambda E, M=M: E.tensor_scalar(out=ss[:M, 0:1], in0=ss[:M, 0:1], scalar1=1.0 / D, scalar2=EPS,
                                                    op0=ALU.mult, op1=ALU.add), ['ss'], ['ss'])
            A('act', lambda E, M=M: E.activation(out=ss[:M, 0:1], in_=ss[:M, 0:1], func=AF.Sqrt), ['ss'], ['ss'])
            A('dve', lambda E, M=M: E.reciprocal(out=rs[:M, 0:1], in_=ss[:M, 0:1]), ['ss'], ['rs'])
            A('dve', lambda E, j=j, M=M: E.tensor_scalar(out=hb[:M, :], in0=xt[j][:M, :], scalar1=rs[:M, 0:1],
                                                        scalar2=None, op0=ALU.mult),
              [('xt', j), 'rs'], ['hb'])
            for g4 in range(4):
                transpose_to(hb[:, g4 * 1024:(g4 + 1) * 1024], M, 8, 128,
                             (lambda b, g4=g4, tb=tb, M=M: actT[:, g4 * 8 + b, tb * 128: tb * 128 + M]),
                             ['hb'], (lambda b, tb=tb: [('actT', tb)]), evac='act',
                             scale_fn=(lambda b, g4=g4: gcol[:, g4 * 8 + b: g4 * 8 + b + 1]))
        if stop_after <= 0:
            T_.barrier()
            ph[0].close()
            break

        hn = lambda tb: [('actT', tb)]
        for cb in range(4):
            wj = load_w(w_in[l * D:(l + 1) * D, :], D, 512, OFF_K + cb * 512)
            for tb in range(NTB):
                pj, M = proj_tm(tb, hn(tb), hT_lhs, 32, wj, 512)
                j = tb % 2
                src3 = psA[pj][:M, :].rearrange("p (g d) -> p g d", g=4)
                dst3 = wf[j][:M, :512].rearrange("p (g d) -> p g d", g=4)
                rms_groups(src3, M, 4, 128, gkf_t, dst3, [('psA', pj)], [('wf', j)])
                dma('sp', out_rows(l, tb, M, nfk_p, nfk_s, 512, cb * 512), wf[j][:M, :512], [('wf', j)], [])
                A('act', lambda E, j=j, M=M: E.copy(out=wb[j][:M, :512], in_=wf[j][:M, :512]), [('wf', j)], [('wb', j)])
                if tb < 8:
                    transpose_to(wb[j], M, 4, 128, (lambda b, j=j, M=M: tsb[j][:, b * 128: b * 128 + M]),
                                 [('wb', j)], (lambda b, j=j: [('tsb', j)]))
                    dma('sp', kTx[l][cb * 512:(cb + 1) * 512, tb * 128:(tb + 1) * 128].rearrange("(g d) t -> d g t", g=4),
                        tsb[j][:, :512].rearrange("d (g t) -> d g t", g=4), [('tsb', j)], [('kTx', l)])
                else:
                    transpose_to(wb[j], M, 4, 128, (lambda b, cb=cb, M=M: kTn[:, cb * 4 + b, :M]),
                                 [('wb', j)], (lambda b: ['kTn']))
        for cb in range(4):
            wj = load_w(w_in[l * D:(l + 1) * D, :], D, 512, OFF_V + cb * 512)
            for tb in range(NTB):
                pj, M = proj_tm(tb, hn(tb), hT_lhs, 32, wj, 512)
                j = tb % 2
                A('act', lambda E, j=j, M=M, pj=pj: E.copy(out=wf[j][:M, :512], in_=psA[pj][:M, :]),
                  [('psA', pj)], [('wf', j)])
                dma('sp', out_rows(l, tb, M, nfv_p, nfv_s, 512, cb * 512), wf[j][:M, :512], [('wf', j)], [])
                if tb < 8:
                    A('dve', lambda E, j=j, M=M, pj=pj: E.tensor_copy(out=wb[j][:M, :512], in_=psA[pj][:M, :]),
                      [('psA', pj)], [('wb', j)])
                    dma('sp', vx[l][tb * 128:(tb + 1) * 128, cb * 512:(cb + 1) * 512], wb[j][:M, :512],
                        [('wb', j)], [('vx', l)])
                else:
                    A('dve', lambda E, M=M, pj=pj, cb=cb: E.tensor_copy(out=vn[:M, cb * 512:(cb + 1) * 512],
                                                                        in_=psA[pj][:M, :]),
                      [('psA', pj)], ['vn'])
        wj = load_w(w_in[l * D:(l + 1) * D, :], D, 16, OFF_F)
        for tb in range(NTB):
            pj, M = proj_tm(tb, hn(tb), hT_lhs, 32, wj, 16)
            j = tb % 2
            A('dve', lambda E, j=j, M=M, pj=pj: E.tensor_tensor(out=wf[j][:M, :16], in0=psA[pj][:M, :16],
                                                               in1=bf_t[:M, :16], op=ALU.add),
              [('psA', pj), 'gains'], [('wf', j)])
            A('act', lambda E, j=j, M=M: E.activation(out=wf[j][:M, :16], in_=wf[j][:M, :16], func=AF.Exp, scale=-1.0),
              [('wf', j)], [('wf', j)])
            A('act', lambda E, j=j, M=M: E.activation(out=wf[j][:M, :16], in_=wf[j][:M, :16], func=AF.Ln, bias=1.0),
              [('wf', j)], [('wf', j)])
            if tb < 8:
                A('dve', lambda E, j=j, M=M: E.tensor_scalar(out=wf[j][:M, :16], in0=wf[j][:M, :16], scalar1=-1.0,
                                                            scalar2=None, op0=ALU.mult), [('wf', j)], [('wf', j)])
                dma('sp', out_rows(l, tb, M, nfl_p, nfl_s, 16), wf[j][:M, :16], [('wf', j)], [])
                dma('sp', lfx[l][tb * 128:(tb + 1) * 128, :], wf[j][:M, :16], [('wf', j)], [('lfx', l)])
            else:
                A('dve', lambda E, j=j, M=M: E.tensor_scalar(out=lfn[:M, :16], in0=wf[j][:M, :16], scalar1=-1.0,
                                                            scalar2=None, op0=ALU.mult), [('wf', j)], ['lfn'])
                dma('sp', out_rows(l, tb, M, nfl_p, nfl_s, 16), lfn[:M, :16], ['lfn'], [])
        wj = load_w(w_in[l * D:(l + 1) * D, :], D, 512, OFF_CKV)
        for tb in range(NTB):
            pj, M = proj_tm(tb, hn(tb), hT_lhs, 32, wj, 512)
            j = tb % 2
            src3 = psA[pj][:M, :].rearrange("p (g d) -> p g d", g=1)
            dst3 = wf[j][:M, :512].rearrange("p (g d) -> p g d", g=1)
            rms_groups(src3, M, 1, 512, gckv_t, dst3, [('psA', pj)], [('wf', j)])
            dma('sp', out_rows(l, tb, M, nck_p, nck_s, 512), wf[j][:M, :512], [('wf', j)], [])
            A('act', lambda E, j=j, M=M: E.copy(out=wb[j][:M, :512], in_=wf[j][:M, :512]), [('wf', j)], [('wb', j)])
            if tb < 8:
                transpose_to(wb[j], M, 4, 128, (lambda b, j=j, M=M: tsb[j][:, b * 128: b * 128 + M]),
                             [('wb', j)], (lambda b, j=j: [('tsb', j)]))
                dma('sp', ckx[l][:, tb * 128:(tb + 1) * 128].rearrange("(g d) t -> d g t", g=4),
                    tsb[j][:, :512].rearrange("d (g t) -> d g t", g=4), [('tsb', j)], [('ckx', l)])
            else:
                transpose_to(wb[j], M, 4, 128, (lambda b, M=M: ckTn[:, b, :M]), [('wb', j)], (lambda b: ['ckTn']))
        wj = load_w(w_in[l * D:(l + 1) * D, :], D, 64, OFF_KPE)
        for tb in range(NTB):
            pj, M = proj_tm(tb, hn(tb), hT_lhs, 32, wj, 64)
            j = tb % 2
            src3 = psA[pj][:M, :64].rearrange("p (g d) -> p g d", g=1)
            dst3 = wf[j][:M, :64].rearrange("p (g d) -> p g d", g=1)
            rms_groups(src3, M, 1, 64, gkp_t, dst3, [('psA', pj)], [('wf', j)])
            rope(dst3, M, 1, tb, rt, rt2, [('wf', j)])
            dma('sp', out_rows(l, tb, M, nkp_p, nkp_s, 64), wf[j][:M, :64], [('wf', j)], [])
            A('act', lambda E, j=j, M=M: E.copy(out=wb[j][:M, :64], in_=wf[j][:M, :64]), [('wf', j)], [('wb', j)])
            if tb < 8:
                transpose_to(wb[j], M, 1, 64, (lambda b, j=j, M=M: tsb[j][:64, :M]),
                             [('wb', j)], (lambda b, j=j: [('tsb', j)]))
                dma('sp', kpx[l][:, tb * 128:(tb + 1) * 128], tsb[j][:64, :128], [('tsb', j)], [('kpx', l)])
            else:
                transpose_to(wb[j], M, 1, 64, (lambda b, M=M: kpTn[:, :M]), [('wb', j)], (lambda b: ['kpTn']))
        if stop_after <= 1:
            T_.barrier()
            ph[0].close()
            break

        T_.barrier()
        groups = [[2 * i, 2 * i + 1] for i in range(n_cores // 2)]
        for (src, dst, nm, nchunk) in [(kTx[l], kTg[l], 'kTg', 4), (vx[l], vg[l], 'vg', 4), (lfx[l], lfg[l], 'lfg', 1),
                                       (ckx[l], ckg[l], 'ckg', 1), (kpx[l], kpg[l], 'kpg', 1)]:
            rows = src.shape[0] // nchunk
            for c in range(nchunk):
                A('pool', (lambda E, src=src, dst=dst, c=c, rows=rows: E.collective_compute(
                    "AllGather", ALU.bypass, replica_groups=groups, ins=[src[c * rows:(c + 1) * rows, :]],
                    outs=[dst[2 * c * rows:2 * (c + 1) * rows, :]])),
                  [], [(nm, l, c)], kind='cc')
        T_.barrier()

        for cb in range(4):
            wj = load_w(w_in[l * D:(l + 1) * D, :], D, 512, OFF_Q + cb * 512)
            for tb in range(NTB):
                pj, M = proj_tm(tb, hn(tb), hT_lhs, 32, wj, 512)
                j = tb % 2
                src3 = psA[pj][:M, :].rearrange("p (g d) -> p g d", g=4)
                dst3 = wf[j][:M, :512].rearrange("p (g d) -> p g d", g=4)
                rms_groups(src3, M, 4, 128, gqf_t, dst3, [('psA', pj)], [('wf', j)])
                A('act', lambda E, j=j, M=M: E.copy(out=wb[j][:M, :512], in_=wf[j][:M, :512]), [('wf', j)], [('wb', j)])
                transpose_to(wb[j], M, 4, 128, (lambda b, j=j, M=M: tsb[j][:, b * 128: b * 128 + M]),
                             [('wb', j)], (lambda b, j=j: [('tsb', j)]))
                dma('sp', qfx[cb * 512:(cb + 1) * 512, tb * 128:tb * 128 + M].rearrange("(g d) t -> d g t", g=4),
                    tsb[j][:, :512].rearrange("d (g t) -> d g t", g=4)[:, :, :M], [('tsb', j)], [('qfx', cb, tb)])
        wj0 = load_w(w_in[l * D:(l + 1) * D, :], D, 512, OFF_CQ)
        wj1 = load_w(w_in[l * D:(l + 1) * D, :], D, 512, OFF_CQ + 512)
        for tb in range(NTB):
            p0, M = proj_tm(tb, hn(tb), hT_lhs, 32, wj0, 512)
            p1, M = proj_tm(tb, hn(tb), hT_lhs, 32, wj1, 512)
            pp = [p0, p1]
            for hf in range(2):
                A('act', lambda E, hf=hf, M=M, pp=pp: E.activation(out=sq[:M, hf * 512:(hf + 1) * 512],
                                                                  in_=psA[pp[hf]][:M, :], func=AF.Square),
                  [('psA', pp[hf])], ['sq'])
            A('dve', lambda E, M=M: E.tensor_reduce(out=ss[:M, 0:1], in_=sq[:M, :1024], axis=AX.X, op=ALU.add), ['sq'], ['ss'])
            A('dve', lambda E, M=M: E.tensor_scalar(out=ss[:M, 0:1], in0=ss[:M, 0:1], scalar1=1.0 / 1024, scalar2=EPS,
                                                    op0=ALU.mult, op1=ALU.add), ['ss'], ['ss'])
            A('act', lambda E, M=M: E.activation(out=ss[:M, 0:1], in_=ss[:M, 0:1], func=AF.Sqrt), ['ss'], ['ss'])
            A('dve', lambda E, M=M: E.reciprocal(out=rs[:M, 0:1], in_=ss[:M, 0:1]), ['ss'], ['rs'])
            for hf in range(2):
                A('dve', lambda E, hf=hf, M=M, pp=pp: E.tensor_scalar(out=wf[hf][:M, :512], in0=psA[pp[hf]][:M, :],
                                                                     scalar1=rs[:M, 0:1], scalar2=None, op0=ALU.mult),
                  [('psA', pp[hf]), 'rs'], [('wf', hf)])
                A('dve', lambda E, hf=hf, M=M: E.tensor_tensor(out=wb[hf][:M, :512], in0=wf[hf][:M, :512],
                                                              in1=gcq_t[:M, hf * 512:(hf + 1) * 512], op=ALU.mult),
                  [('wf', hf), 'gains'], [('wb', hf)])
                transpose_to(wb[hf], M, 4, 128, (lambda b, hf=hf, tb=tb, M=M: cqT[:, hf * 4 + b, tb * 128: tb * 128 + M]),
                             [('wb', hf)], (lambda b, tb=tb: [('cqT', tb)]))
        cq_lhs = lambda kc, tb, M: cqT[:, kc, tb * 128: tb * 128 + M]
        for hp in range(8):
            wj = load_w(w_qb[l * 1024:(l + 1) * 1024, :], 1024, 384, hp * 384)
            for tb in range(NTB):
                pj, M = proj_tm(tb, [('cqT', tb)], cq_lhs, 8, wj, 384)
                j = tb % 2
                ps3 = psA[pj][:M, :384].rearrange("p (g d) -> p g d", g=2)
                dn3 = wf[j][:M, 0:256].rearrange("p (g d) -> p g d", g=2)
                dp3 = wf[j][:M, 256:384].rearrange("p (g d) -> p g d", g=2)
                rms_groups(ps3[:, :, 0:128], M, 2, 128, gqn_t, dn3, [('psA', pj)], [('wf', j)])
                rms_groups(ps3[:, :, 128:192], M, 2, 64, gqp_t, dp3, [('psA', pj)], [('wf', j)])
                rope(dp3, M, 2, tb, rt, rt2, [('wf', j)])
                A('act', lambda E, j=j, M=M: E.copy(out=wb[j][:M, :384], in_=wf[j][:M, :384]), [('wf', j)], [('wb', j)])
                transpose_to(wb[j], M, 2, 128, (lambda b, j=j, M=M: tsb[j][:, b * 128: b * 128 + M]),
                             [('wb', j)], (lambda b, j=j: [('tsb', j)]))
                dma('sp', qnx[hp * 256:(hp + 1) * 256, tb * 128:tb * 128 + M].rearrange("(g d) t -> d g t", g=2),
                    tsb[j][:, :256].rearrange("d (g t) -> d g t", g=2)[:, :, :M], [('tsb', j)], [('qnx', hp, tb)])
                transpose_to(wb[j][:, 256:384], M, 2, 64, (lambda b, j=j, M=M: tsb[j][:64, 256 + b * 128: 256 + b * 128 + M]),
                             [('wb', j)], (lambda b, j=j: [('tsb', j)]))
                dma('sp', qpx[hp * 128:(hp + 1) * 128, tb * 128:tb * 128 + M].rearrange("(g d) t -> d g t", g=2),
                    tsb[j][:64, 256:512].rearrange("d (g t) -> d g t", g=2)[:, :, :M], [('tsb', j)], [('qpx', hp, tb)])
        ttiles = [(0, 512), (512, 512), (1024, TS)]
        for (zoff, zdst, znm) in [(OFF_ZA, zax, 'zax'), (OFF_ZB, zbx, 'zbx')]:
            for cb in range(4):
                wj = load_w(w_in[l * D:(l + 1) * D, :], D, 512, zoff + cb * 512)
                for ct in range(4):
                    for ti, (t0, tn) in enumerate(ttiles):
                        pj = next_psA()
                        tbs = [('actT', tb) for tb in range(NTB) if t0 <= tb * 128 < t0 + tn]
                        for kc in range(32):
                            A('pe', (lambda E, kc=kc, pj=pj, wj=wj, ct=ct, t0=t0, tn=tn: E.matmul(
                                psA[pj][:, :tn], lhsT=wt[wj][:, kc, ct * 128:(ct + 1) * 128],
                                rhs=actT[:, kc, t0:t0 + tn], start=(kc == 0), stop=(kc == 31))),
                              tbs + [('wt', wj, (kc // 8) * 8)], [('psA', pj)], signal=(kc == 31))
                        j = (ct * 3 + ti) % 2
                        A('act', lambda E, pj=pj, tn=tn, j=j: E.activation(out=tsb[j][:, :tn], in_=psA[pj][:, :tn], func=AF.Silu),
                          [('psA', pj)], [('tsb', j)])
                        r0 = (cb * 4 + ct) * 128
                        dma('sp', zdst[r0:r0 + 128, t0:t0 + tn], tsb[j][:, :tn], [('tsb', j)], [(znm, cb * 4 + ct, ti)])
        T_.barrier()
        ph[0].close()
        if stop_after <= 2:
            break

        ph[0] = ExitStack()
        npsA[0] = 2
        gidx = lambda g: (0 if gblock(0, g // 2) == g else 1) * 8 + g // 2
        maskf = sbp(f"maskf{l}", [128, 2048], BF16)
        lf_t = sbp(f"lf_t{l}", [128, 16, 32]); win = sbp(f"win{l}", [128, 16, 32]); tots = sbp(f"tots{l}", [128, 16, 32])
        carry = sbp(f"carry{l}", [128, 17, 32]); ck = sbp(f"ck{l}", [128, 16, 32]); bias = sbp(f"bias{l}", [128, 8, 16, 16])
        crefs = sbp(f"crefs{l}", [128, 32]); ckn = sbp(f"ckn{l}", [32, 16]); biasn = sbp(f"biasn{l}", [32, 32])
        kTh = [sbp(f"kTh{l}_{j}", [128, 2, TP], BF16) for j in range(2)]
        qTh = [sbp(f"qTh{l}_{j}", [128, T], BF16) for j in range(2)]
        zTh = [sbp(f"zTh{l}_{j}", [128, T], BF16) for j in range(2)]
        pT = [sbp(f"pT{l}_{j}", [128, 512], BF16) for j in range(2)]
        rden = sbp(f"rden{l}", [128, 512]); osb = sbp(f"osb{l}", [128, 512])
        qTs = sbp(f"qTs{l}", [128, 16, TS], BF16); zTs = sbp(f"zTs{l}", [128, 16, TS], BF16)
        indw = sbp(f"indw{l}", [32, 256])
        vgrp = wt[0][:, 0:16, :]; kc_t = wt[0][:, 16:32, :]; vc_t = wt[1][:, 0:16, :]
        kTc = wt[1][:, 16:32, :].rearrange("p a b -> p (a b)").rearrange("p (h t) -> p h t", h=4)
        dma('sp', maskf[:, :], c_maskf[:, :], [], ['maskf'])
        dma('sp', indw[:, :], c_indw[:, :], [], ['indw'])
        all_q = lambda nm, n0: [(nm, a, tb) for a in range(n0) for tb in range(NTB)]
        all_z = lambda nm: [(nm, a, ti) for a in range(16) for ti in range(3)]

        def cumsum_tables(ncol, carry_order):
            flat = lf_t[:, :, :ncol]
            A('pe', lambda E: E.matmul(psS[0][:, :16 * ncol].rearrange("p (j c) -> p j c", j=16), lhsT=trif[:, :], rhs=flat,
                                       start=True, stop=True), ['lf_t', 'trif'], [('psS', 0)])
            A('pe', lambda E: E.matmul(psS[1][:, :16 * ncol].rearrange("p (j c) -> p j c", j=16), lhsT=onesf[:, :], rhs=flat,
                                       start=True, stop=True), ['lf_t', 'onesf'], [('psS', 1)])
            A('act', lambda E: E.copy(out=win[:, :, :ncol], in_=psS[0][:, :16 * ncol].rearrange("p (j c) -> p j c", j=16)),
              [('psS', 0)], ['win'])
            A('act', lambda E: E.copy(out=tots[:, :, :ncol], in_=psS[1][:, :16 * ncol].rearrange("p (j c) -> p j c", j=16)),
              [('psS', 1)], ['tots'])
            A('dve', lambda E: E.memset(carry[:, 0, :], 0.0), [], ['carry'])
            for g in range(16):
                A('dve', lambda E, g=g: E.tensor_tensor(out=carry[:, g + 1, :ncol], in0=carry[:, g, :ncol],
                                                        in1=tots[:, carry_order(g), :ncol], op=ALU.add),
                  ['carry', 'tots'], ['carry'])

        dma('sp', lf_t[:, :, :16], lfg[l].rearrange("(j p) h -> p j h", p=128), [('lfg', l, 0)], ['lf_t'])
        cumsum_tables(16, gidx)
        for j in range(16):
            g = gblock(j // 8, j % 8)
            A('dve', lambda E, j=j, g=g: E.tensor_tensor(out=ck[:, j, :16], in0=win[:, j, :16], in1=carry[:, g, :16], op=ALU.add),
              ['win', 'carry'], ['ck'])
        for i in range(8):
            A('dve', lambda E, i=i: E.tensor_tensor(out=bias[:, i, :, :],
                                                    in0=carry[:, 2 * i + 2, :16].unsqueeze(1).to_broadcast([128, 16, 16]),
                                                    in1=ck[:, :, :16], op=ALU.subtract), ['carry', 'ck'], ['bias'])
        for hg in range(4):
            for r in range(2):
                for c in range(4):
                    r0 = c * 512 + r * 256
                    dma('sp', vgrp[:, r * 8 + c * 2: r * 8 + c * 2 + 2, :],
                        vg[l][r0:r0 + 256, hg * 512:(hg + 1) * 512].rearrange("(q p) n -> p q n", p=128),
                        [], [('vgrp', r, c)])
            for hh in range(4):
                h = hg * 4 + hh
                hb_ = h % 2
                dma('sp', kTh[hb_][:, :, :],
                    kTg[l][(h // 4) * 1024:(h // 4 + 1) * 1024, :].rearrange("(r f) t -> f r t", r=2)[(h % 4) * 128:(h % 4 + 1) * 128, :, :],
                    [], [('kTh', hb_)])
                dma('sp', qTh[hb_][:, :], qfx[h * 128:(h + 1) * 128, :], all_q('qfx', 4), [('qTh', hb_)])
                dma('sp', zTh[hb_][:, :], zax[h * 128:(h + 1) * 128, :], all_z('zax'), [('zTh', hb_)])
                for t in range(2):
                    blocks = [(ip, r) for ip in range(4 * t + 4) for r in range(2)]
                    for bi, (ip, r) in enumerate(blocks):
                        j = r * 8 + ip
                        c0 = max(0, ip - 4 * t) * 128
                        sj = bi % 2
                        A('pe', (lambda E, sj=sj, c0=c0, r=r, ip=ip, t=t, hb_=hb_: E.matmul(
                            psS[sj][:, c0:512], lhsT=kTh[hb_][:, r, ip * 128:(ip + 1) * 128],
                            rhs=qTh[hb_][:, t * 512 + c0:(t + 1) * 512], start=True, stop=True)),
                          [('kTh', hb_), ('qTh', hb_)], [('psS', sj)])
                        for isub in range(c0 // 128, 4):
                            i = 4 * t + isub
                            A('act', (lambda E, sj=sj, isub=isub, i=i, j=j, h=h: E.activation(
                                out=pT[sj][:, isub * 128:(isub + 1) * 128], in_=psS[sj][:, isub * 128:(isub + 1) * 128],
                                func=AF.Exp, bias=bias[:, i, j, h:h + 1], scale=float(FOX_SCALE))),
                              [('psS', sj), 'bias'], [('pT', sj, isub)])
                            if ip == i:
                                A('dve', (lambda E, sj=sj, isub=isub, i=i, r=r: E.tensor_tensor(
                                    out=pT[sj][:, isub * 128:(isub + 1) * 128], in0=pT[sj][:, isub * 128:(isub + 1) * 128],
                                    in1=maskf[:, (i * 2 + r) * 128:(i * 2 + r + 1) * 128], op=ALU.mult)),
                                  [('pT', sj, isub), 'maskf'], [('pT', sj, isub)])
                        first = (bi == 0)
                        last = (bi == len(blocks) - 1)
                        pnames = [('pT', sj, isub) for isub in range(c0 // 128, 4)]
                        A('pe', (lambda E, sj=sj, c0=c0, j=j, hh=hh, first=first, last=last: E.matmul(
                            psO[:, c0:512], lhsT=vgrp[:, j, hh * 128:(hh + 1) * 128], rhs=pT[sj][:, c0:512],
                            start=first, stop=last)), pnames + [('vgrp', r, ip // 2)], ['psO'], signal=False)
                        A('pe', (lambda E, sj=sj, c0=c0, first=first, last=last: E.matmul(
                            psD[:, c0:512], lhsT=onesb[:, :], rhs=pT[sj][:, c0:512], start=first, stop=last)),
                          pnames + ['onesb'], ['psD'])
                    A('dve', lambda E: E.reciprocal(out=rden[:, :], in_=psD[:, :]), ['psD'], ['rden'])
                    A('dve', lambda E: E.tensor_tensor(out=osb[:, :], in0=psO[:, :], in1=rden[:, :], op=ALU.mult),
                      ['psO', 'rden'], ['osb'])
                    A('dve', (lambda E, h=h, t=t, hb_=hb_: E.tensor_tensor(
                        out=actT[:, h, t * 512:(t + 1) * 512], in0=osb[:, :], in1=zTh[hb_][:, t * 512:(t + 1) * 512], op=ALU.mult)),
                      ['osb', ('zTh', hb_)], [('actT', tb) for tb in range(4 * t, 4 * t + 4)])
        T_.barrier()
        for b in range(2):
            dma('sp', lf_t[:, :, b * 16:(b + 1) * 16],
                cfl[(l * 2 + b) * PAST:(l * 2 + b + 1) * PAST, :].rearrange("(j p) h -> p j h", p=128), [], [('lf_tb', b)])
        T_.barrier()
        cumsum_tables(32, (lambda g: g))
        A('dve', lambda E: E.tensor_tensor(out=ck[:, :, :], in0=win[:, :, :], in1=carry[:, 0:16, :], op=ALU.add),
          ['win', 'carry'], ['ck'])
        A('pe', lambda E: E.matmul(psS[0][:32, :16], lhsT=tris[:32, :32], rhs=lfn[:32, :16], start=True, stop=True),
          ['tris', 'lfn'], [('psS', 0)])
        for b in range(2):
            A('pe', lambda E, b=b: E.matmul(psS[1][:, b * 16:(b + 1) * 16], lhsT=indw[:32, b * 128:(b + 1) * 128],
                                            rhs=lfn[:32, :16], start=True, stop=True), ['indw', 'lfn'], [('psS', 1)])
        A('act', lambda E: E.copy(out=ckn[:, :], in_=psS[0][:32, :16]), [('psS', 0)], ['ckn'])
        for b in range(2):
            A('dve', lambda E, b=b: E.scalar_tensor_tensor(out=ckn[:, :], in0=carry[:32, 16, b * 16:(b + 1) * 16],
                                                          scalar=inds[:32, b:b + 1], in1=ckn[:, :], op0=ALU.mult, op1=ALU.add),
              ['carry', 'ckn', 'inds'], ['ckn'])
        A('dve', lambda E: E.tensor_tensor(out=crefs[:, :], in0=carry[:, 16, :], in1=psS[1][:, :32], op=ALU.add),
          ['carry', ('psS', 1)], ['crefs'])
        biasc = win
        A('dve', lambda E: E.tensor_tensor(out=biasc[:, :, :], in0=crefs[:, :].unsqueeze(1).to_broadcast([128, 16, 32]),
                                           in1=ck[:, :, :], op=ALU.subtract), ['crefs', 'ck', 'win'], ['biasc'])
        for b in range(2):
            A('dve', lambda E, b=b: E.tensor_tensor(out=biasn[:, b * 16:(b + 1) * 16], in0=crefs[:32, b * 16:(b + 1) * 16],
                                                    in1=ckn[:, :], op=ALU.subtract), ['crefs', 'ckn'], ['biasn'])
        dma('sp', qTs[:, :, :], qfx[:, TP:T].rearrange("(h d) t -> d h t", d=128), all_q('qfx', 4), ['qTs'])
        dma('sp', zTs[:, :, :], zax[:, TP:T].rearrange("(h d) t -> d h t", d=128), all_z('zax'), ['zTs'])

        def sample_attn(b, kT_fn, v_fn, kTnew_list, vnew, bias_fn, biasn_ap, msk, scale, zrow, out_chunk):
            for j in range(16):
                parts = kT_fn(j)
                for pi, (ka, qa, kn) in enumerate(parts):
                    A('pe', (lambda E, j=j, ka=ka, qa=qa, pi=pi, n=len(parts): E.matmul(
                        psS[0][:, j * 16:(j + 1) * 16], lhsT=ka, rhs=qa, start=(pi == 0), stop=(pi == n - 1))),
                      kn, [('psS', 0)], signal=(pi == len(parts) - 1))
            for pi, (ka, qa, kn) in enumerate(kTnew_list):
                A('pe', (lambda E, ka=ka, qa=qa, pi=pi, n=len(kTnew_list): E.matmul(
                    psS[0][:32, 256:272], lhsT=ka, rhs=qa, start=(pi == 0), stop=(pi == n - 1))),
                  kn, [('psS', 0)], signal=(pi == len(kTnew_list) - 1))
            if bias_fn is None:
                A('act', (lambda E: E.activation(out=pT[0][:, 0:256], in_=psS[0][:, 0:256], func=AF.Exp, scale=float(scale))),
                  [('psS', 0)], [('pTs', j) for j in range(16)])
            for j in range(16):
                if bias_fn is None:
                    pass
                else:
                    bj_ = bias_fn(j)
                    A('act', (lambda E, j=j, bj_=bj_: E.activation(out=pT[0][:, j * 16:(j + 1) * 16], in_=psS[0][:, j * 16:(j + 1) * 16],
                                                          func=AF.Exp, bias=bj_, scale=float(scale))),
                      [('psS', 0), 'biasc'], [('pTs', j)])
            if biasn_ap is None:
                A('act', (lambda E: E.activation(out=pT[0][:32, 256:272], in_=psS[0][:32, 256:272], func=AF.Exp,
                                                 scale=float(scale))), [('psS', 0)], [('pTs', 16)])
            else:
                A('act', (lambda E: E.activation(out=pT[0][:32, 256:272], in_=psS[0][:32, 256:272], func=AF.Exp,
                                                 bias=biasn_ap, scale=float(scale))), [('psS', 0), 'biasn'], [('pTs', 16)])
            A('dve', (lambda E: E.tensor_tensor(out=pT[0][:32, 256:272], in0=pT[0][:32, 256:272], in1=msk, op=ALU.mult)),
              [('pTs', 16), 'masks'], [('pTs', 16)])
            for j in range(17):
                if j < 16:
                    va, vnm = v_fn(j)
                    pa = pT[0][:, j * 16:(j + 1) * 16]
                    oa = onesb[:, :]
                else:
                    va, vnm = vnew
                    pa = pT[0][:32, 256:272]
                    oa = onesb[:32, :]
                A('pe', (lambda E, va=va, pa=pa, j=j: E.matmul(psO[:, :16], lhsT=va, rhs=pa, start=(j == 0), stop=(j == 16))),
                  [('pTs', j)] + vnm, ['psO'], signal=False)
                A('pe', (lambda E, oa=oa, pa=pa, j=j: E.matmul(psD[:, :16], lhsT=oa, rhs=pa, start=(j == 0), stop=(j == 16))),
                  [('pTs', j), 'onesb'], ['psD'])
            A('dve', lambda E: E.reciprocal(out=rden[:, :16], in_=psD[:, :16]), ['psD'], ['rden'])
            A('dve', lambda E: E.tensor_tensor(out=osb[:, :16], in0=psO[:, :16], in1=rden[:, :16], op=ALU.mult),
              ['psO', 'rden'], ['osb'])
            A('dve', (lambda E: E.tensor_tensor(out=actT[:, out_chunk, TP + 16 * b:TP + 16 * b + 16], in0=osb[:, :16],
                                                in1=zrow, op=ALU.mult)), ['osb', 'zTs'], [('actT', 8)])

        for b in range(2):
            for hg in range(4):
                rows = slice((l * 2 + b) * PAST, (l * 2 + b + 1) * PAST)
                A('pool', (lambda E, rows=rows, hg=hg: E.dma_start(
                    out=kc_t, in_=cfk[rows, hg * 512:(hg + 1) * 512].rearrange("(j p) c -> p j c", p=128))),
                  [], ['kc_t'], kind='dma')
                A('pool', (lambda E, rows=rows, hg=hg: E.dma_start(
                    out=vc_t, in_=cfv[rows, hg * 512:(hg + 1) * 512].rearrange("(j p) c -> p j c", p=128))),
                  [], ['vc_t'], kind='dma')
                for hh in range(4):
                    for half in range(2):
                        pj = pt_i[0] % 2
                        pt_i[0] += 1
                        for bb in range(8):
                            A('pe', (lambda E, pj=pj, bb=bb, half=half, hh=hh: E.transpose(
                                psT[pj][:, bb * 128:(bb + 1) * 128], kc_t[:, half * 8 + bb, hh * 128:(hh + 1) * 128], identb[:, :])),
                              ['kc_t', 'identb'], [('psT', pj)], signal=(bb == 7))
                        A('act', (lambda E, pj=pj, half=half, hh=hh: E.copy(
                            out=kTc[:, hh, half * 1024:(half + 1) * 1024], in_=psT[pj][:, :])), [('psT', pj)], [('kTc', hh)])
                for hh in range(4):
                    h = hg * 4 + hh
                    qa = qTs[:, h, 16 * b:16 * b + 16]
                    sample_attn(
                        b,
                        (lambda j, hh=hh, qa=qa: [(kTc[:, hh, j * 128:(j + 1) * 128], qa, [('kTc', hh), 'qTs'])]),
                        (lambda j, hh=hh: (vc_t[:, j, hh * 128:(hh + 1) * 128], ['vc_t'])),
                        [(kTn[:, h, :], qa, ['kTn', 'qTs'])],
                        (vn[:32, h * 128:(h + 1) * 128], ['vn']),
                        (lambda j, b=b, h=h: biasc[:, j, b * 16 + h:b * 16 + h + 1]),
                        biasn[:32, b * 16 + h:b * 16 + h + 1], masks[:32, b * 16:(b + 1) * 16], FOX_SCALE,
                        zTs[:, h, 16 * b:16 * b + 16], h)
        T_.barrier()
        ph[0].close()
        if stop_after <= 3:
            break

        ph[0] = ExitStack()
        maskm = sbp(f"maskm{l}", [128, 2048], BF16); masks2 = sbp(f"masks2{l}", [32, 32], BF16)
        kpTg = sbp(f"kpTg{l}", [64, 2, TP], BF16)
        knT = sbp(f"knT{l}", [128, 2, 2048], BF16); vb = sbp(f"vb{l}", [128, 16, 256], BF16)
        vbn = sbp(f"vbn{l}", [32, 256], BF16); knTn = sbp(f"knTn{l}", [128, 2, 32], BF16)
        qnh = sbp(f"qnh{l}", [128, T], BF16); qph = sbp(f"qph{l}", [64, T], BF16); zTh2 = sbp(f"zTh2{l}", [128, T], BF16)
        pT = [sbp(f"pTm{l}_{j}", [128, 512], BF16) for j in range(2)]
        rden = sbp(f"rdenm{l}", [128, 512]); osb = sbp(f"osbm{l}", [128, 512])
        sq = sbp(f"sqm{l}", [128, 256]); wf2 = sbp(f"wf2{l}", [128, 256]); wb2 = sbp(f"wb2{l}", [128, 256], BF16)
        qnTs = sbp(f"qnTs{l}", [128, 16, TS], BF16); qpTs = sbp(f"qpTs{l}", [64, 16, TS], BF16)
        zTs = sbp(f"zTs2{l}", [128, 16, TS], BF16)
        wkv = wt[0][:, :, :].rearrange("p a b -> p (a b)").rearrange("p (kc n) -> p kc n", kc=4)
        ckTg = wt[1][:, 0:16, :].rearrange("p a b -> p (a b)").rearrange("p (kc r t) -> p kc r t", kc=4, r=2)
        ckc = wt[1][:, 16:32, :]
        dma('sp', maskm[:, :], c_maskm[:, :], [], ['maskm'])
        dma('sp', masks2[:, :], c_masks2[:, :], [], ['masks'])
        for kc in range(4):
            A('pool', (lambda E, kc=kc: E.dma_start(out=wkv[:, kc, :], in_=w_kvb[l * 512 + kc * 128: l * 512 + (kc + 1) * 128, :])),
              [], [('wkv', kc)], kind='dma')
            for r in range(2):
                dma('sp', ckTg[:, kc, r, :], ckg[l][r * 512 + kc * 128: r * 512 + (kc + 1) * 128, :], [('ckg', l, 0)], [('ckTg', kc, r)])
        dma('sp', kpTg[:, :, :], kpg[l].rearrange("(r d) t -> d r t", r=2), [('kpg', l, 0)], ['kpTg'])
        T_.barrier()

        def upproj(lhs_fn, lnames, M, hp, knT_dst_fn, v_dst, vnames, knames):
            pj = next_psA()
            for kc in range(4):
                lh = lhs_fn(kc)
                A('pe', (lambda E, kc=kc, pj=pj, lh=lh: E.matmul(psA[pj][:M, :512], lhsT=lh, rhs=wkv[:, kc, hp * 512:(hp + 1) * 512],
                                                         start=(kc == 0), stop=(kc == 3))),
                  list(lnames) + [('wkv', kc)], [('psA', pj)], signal=(kc == 3))
            ps3 = psA[pj][:M, :].rearrange("p (g c) -> p g c", g=2)
            dst3 = wf2[:M, :256].rearrange("p (g d) -> p g d", g=2)
            rms_groups(ps3[:, :, 0:128], M, 2, 128, gkn_t, dst3, [('psA', pj)], ['wf2'])
            A('act', lambda E: E.copy(out=wb2[:M, :256], in_=wf2[:M, :256]), ['wf2'], ['wb2'])
            A('dve', lambda E: E.tensor_copy(out=v_dst, in_=ps3[:, :, 128:256]), [('psA', pj)], vnames)
            transpose_to(wb2, M, 2, 128, knT_dst_fn, ['wb2'], (lambda b: knames))

        def mla_head_prompt(h, hh):
            dma('sp', qnh[:, :], qnx[h * 128:(h + 1) * 128, :], [('qnx', h // 2, tb) for tb in range(NTB)], ['qnh'])
            dma('sp', qph[:, :], qpx[h * 64:(h + 1) * 64, :], [('qpx', h // 2, tb) for tb in range(NTB)], ['qph'])
            dma('sp', zTh2[:, :], zbx[h * 128:(h + 1) * 128, :], [('zbx', h, ti) for ti in range(3)], ['zTh2'])
            for t in range(2):
                blocks = [(ip, r) for ip in range(4 * t + 4) for r in range(2)]
                for bi, (ip, r) in enumerate(blocks):
                    j = r * 8 + ip
                    c0 = max(0, ip - 4 * t) * 128
                    sj = bi % 2
                    A('pe', (lambda E, sj=sj, c0=c0, j=j, t=t: E.matmul(
                        psS[sj][:, c0:512], lhsT=knT[:, hh, j * 128:(j + 1) * 128],
                        rhs=qnh[:, t * 512 + c0:(t + 1) * 512], start=True, stop=False)),
                      [('knT', j), 'qnh'], [('psS', sj)], signal=False)
                    A('pe', (lambda E, sj=sj, c0=c0, r=r, ip=ip, t=t: E.matmul(
                        psS[sj][:, c0:512], lhsT=kpTg[:, r, ip * 128:(ip + 1) * 128],
                        rhs=qph[:, t * 512 + c0:(t + 1) * 512], start=False, stop=True)),
                      ['kpTg', 'qph'], [('psS', sj)])
                    A('act', (lambda E, sj=sj, c0=c0: E.activation(
                        out=pT[sj][:, c0:512], in_=psS[sj][:, c0:512], func=AF.Exp, scale=float(MLA_SCALE))),
                      [('psS', sj)], [('pT', sj, isub) for isub in range(c0 // 128, 4)])
                    for isub in range(c0 // 128, 4):
                        i = 4 * t + isub
                        if ip == i:
                            A('dve', (lambda E, sj=sj, isub=isub, i=i, r=r: E.tensor_tensor(
                                out=pT[sj][:, isub * 128:(isub + 1) * 128], in0=pT[sj][:, isub * 128:(isub + 1) * 128],
                                in1=maskm[:, (i * 2 + r) * 128:(i * 2 + r + 1) * 128], op=ALU.mult)),
                              [('pT', sj, isub), 'maskm'], [('pT', sj, isub)])
                    first = (bi == 0)
                    last = (bi == len(blocks) - 1)
                    pnames = [('pT', sj, isub) for isub in range(c0 // 128, 4)]
                    A('pe', (lambda E, sj=sj, c0=c0, j=j, first=first, last=last: E.matmul(
                        psO[:, c0:512], lhsT=vb[:, j, hh * 128:(hh + 1) * 128], rhs=pT[sj][:, c0:512],
                        start=first, stop=last)), pnames + [('vb', j)], ['psO'], signal=False)
                    A('pe', (lambda E, sj=sj, c0=c0, first=first, last=last: E.matmul(
                        psD[:, c0:512], lhsT=onesb[:, :], rhs=pT[sj][:, c0:512], start=first, stop=last)),
                      pnames + ['onesb'], ['psD'])
                A('dve', lambda E: E.reciprocal(out=rden[:, :], in_=psD[:, :]), ['psD'], ['rden'])
                A('dve', lambda E: E.tensor_tensor(out=osb[:, :], in0=psO[:, :], in1=rden[:, :], op=ALU.mult),
                  ['psO', 'rden'], ['osb'])
                A('dve', (lambda E, t=t: E.tensor_tensor(
                    out=actT[:, 16 + h, t * 512:(t + 1) * 512], in0=osb[:, :], in1=zTh2[:, t * 512:(t + 1) * 512], op=ALU.mult)),
                  ['osb', 'zTh2'], [('actT', tb) for tb in range(4 * t, 4 * t + 4)])

        for hp in range(8):
            for j in range(16):
                r, ip = divmod(j, 8)
                upproj((lambda kc, r=r, ip=ip: ckTg[:, kc, r, ip * 128:(ip + 1) * 128]),
                       [('ckTg', kc, r) for kc in range(4)], 128, hp,
                       (lambda b, j=j: knT[:, b, j * 128:(j + 1) * 128]),
                       vb[:, j, :].rearrange("p (g d) -> p g d", g=2), [('vb', j)], [('knT', j)])
            for hh in range(2):
                mla_head_prompt(hp * 2 + hh, hh)
        T_.barrier()
        ckTc = wt[1][:, 0:16, :].rearrange("p a b -> p (a b)").rearrange("p (kc t) -> p kc t", kc=4)
        kpTc = kpTg[:, :, :].rearrange("d r t -> d (r t)")
        kpc = wt[0][:, 0:2, :].rearrange("p a b -> p (a b)").rearrange("p (j c) -> p j c", j=16)
        dma('sp', qnTs[:, :, :], qnx[:, TP:T].rearrange("(h d) t -> d h t", d=128), [('qnx', a, 8) for a in range(8)], ['qTs'])
        dma('sp', qpTs[:, :, :], qpx[:, TP:T].rearrange("(h d) t -> d h t", d=64), [('qpx', a, 8) for a in range(8)], ['qTs2'])
        dma('sp', zTs[:, :, :], zbx[:, TP:T].rearrange("(h d) t -> d h t", d=128), all_z('zbx'), ['zTs'])
        kpc = sq
        kpcb = sbp(f"kpcb{l}", [128, 16, 64], BF16)
        for b in range(2):
            rows = slice((l * 2 + b) * PAST, (l * 2 + b + 1) * PAST)
            A('pool', (lambda E, rows=rows: E.dma_start(out=ckc, in_=cck[rows, :].rearrange("(j p) c -> p j c", p=128))),
              [], ['ckc'], kind='dma')
            A('pool', (lambda E, rows=rows: E.dma_start(out=kpcb[:, :, :], in_=ckp[rows, :].rearrange("(j p) c -> p j c", p=128))),
              [], ['kpcb'], kind='dma')
            for j in range(16):
                pj = pt_i[0] % 2
                pt_i[0] += 1
                for kc in range(4):
                    A('pe', (lambda E, pj=pj, j=j, kc=kc: E.transpose(psT[pj][:, kc * 128:(kc + 1) * 128],
                                                                     ckc[:, j, kc * 128:(kc + 1) * 128], identb[:, :])),
                      ['ckc', 'identb'], [('psT', pj)], signal=(kc == 3))
                A('act', (lambda E, pj=pj, j=j: E.copy(out=ckTc[:, :, j * 128:(j + 1) * 128],
                                                       in_=psT[pj][:, :512].rearrange("p (kc t) -> p kc t", kc=4))),
                  [('psT', pj)], ['ckTc'])
            for half in range(2):
                pj = pt_i[0] % 2
                pt_i[0] += 1
                for bb in range(8):
                    A('pe', (lambda E, pj=pj, bb=bb, half=half: E.transpose(psT[pj][:64, bb * 128:(bb + 1) * 128],
                                                                           kpcb[:, half * 8 + bb, :], identb[:, :])),
                      ['kpcb', 'identb'], [('psT', pj)], signal=(bb == 7))
                A('act', (lambda E, pj=pj, half=half: E.copy(out=kpTc[:, half * 1024:(half + 1) * 1024], in_=psT[pj][:64, :])),
                  [('psT', pj)], ['kpTc'])
            for hp in range(8):
                for j in range(16):
                    upproj((lambda kc, j=j: ckTc[:, kc, j * 128:(j + 1) * 128]), ['ckTc'], 128, hp,
                           (lambda bq, j=j: knT[:, bq, j * 128:(j + 1) * 128]),
                           vb[:, j, :].rearrange("p (g d) -> p g d", g=2), [('vb', j)], [('knT', j)])
                upproj((lambda kc: ckTn[:, kc, :]), ['ckTn'], 32, hp, (lambda bq: knTn[:, bq, :]),
                       vbn[:32, :].rearrange("p (g d) -> p g d", g=2), ['vbn'], ['knTn'])
                for hh in range(2):
                    h = hp * 2 + hh
                    qn_ = qnTs[:, h, 16 * b:16 * b + 16]
                    qp_ = qpTs[:, h, 16 * b:16 * b + 16]
                    sample_attn(
                        b,
                        (lambda j, hh=hh, qn_=qn_, qp_=qp_: [(knT[:, hh, j * 128:(j + 1) * 128], qn_, [('knT', j), 'qTs']),
                                                            (kpTc[:, j * 128:(j + 1) * 128], qp_, ['kpTc', 'qTs2'])]),
                        (lambda j, hh=hh: (vb[:, j, hh * 128:(hh + 1) * 128], [('vb', j)])),
                        [(knTn[:, hh, :], qn_, ['knTn', 'qTs']), (kpTn[:, :], qp_, ['kpTn', 'qTs2'])],
                        (vbn[:32, hh * 128:(hh + 1) * 128], ['vbn']),
                        None, None, masks2[:32, b * 16:(b + 1) * 16], MLA_SCALE,
                        zTs[:, h, 16 * b:16 * b + 16], 16 + h)
        T_.barrier()
        ph[0].close()
        if stop_after <= 4:
            break

        ph[0] = ExitStack()
        npsA[0] = 6
        xr = [sbp(f"xr{l}_{j}", [128, 512]) for j in range(2)]
        yo = [sbp(f"yo{l}_{j}", [128, 512]) for j in range(2)]
        for cb in range(8):
            wj = load_w(w_out[l * D:(l + 1) * D, :], D, 512, cb * 512)
            for tb in range(NTB):
                pj, M = proj_tm(tb, [('actT', tb)], hT_lhs, 32, wj, 512)
                j = tb % 2
                cs = slice(cb * 512, (cb + 1) * 512)
                if l == 0:
                    xsrc = xp[tb * 128: tb * 128 + M, cs] if tb < 8 else xs[:, cs]
                else:
                    xsrc = ymid[tb * 128: tb * 128 + M, cs]
                dma('sp', xr[j][:M, :], xsrc, [], [('xr', j)])
                A('dve', lambda E, j=j, M=M, pj=pj: E.tensor_tensor(out=yo[j][:M, :], in0=psA[pj][:M, :], in1=xr[j][:M, :], op=ALU.add),
                  [('psA', pj), ('xr', j)], [('yo', j)])
                if l == 0:
                    dst = ymid[tb * 128: tb * 128 + M, cs]
                else:
                    dst = yp[tb * 128: tb * 128 + M, cs] if tb < 8 else ys[:, cs]
                dma('sp', dst, yo[j][:M, :], [('yo', j)], [])
        T_.barrier()
        ph[0].close()

    T_.final_wait('sp')
    sem_keys = T_.sem_keys()
    sems = {}
    for k in sem_keys:
        sems[k] = es.enter_context(nc.semaphore("s_" + "_".join(str(x) for x in k)))
    build_program.sem_map = {k: repr(v) for k, v in sems.items()}
    with nc.Block() as block:
        T_.emit(block, sems)
    es.close()
    return nc


def host_constants(r):
    bf = ml_dtypes.bfloat16
    c = {}
    c["c_identb"] = np.eye(128, dtype=np.float32).astype(bf)
    c["c_identf"] = np.eye(128, dtype=np.float32)
    c["c_onesb"] = np.ones((128, 128), np.float32).astype(bf)
    c["c_onesf"] = np.ones((128, 128), np.float32)
    c["c_trif"] = np.triu(np.ones((128, 128), np.float32))
    pos = np.concatenate([np.concatenate([gblock(r, i) * 128 + np.arange(128) for i in range(8)]),
                          PAST + np.arange(16), PAST + np.arange(16)]).astype(np.float32)
    half = 32
    inv_freq = (1.0 / (np.float32(10000.0) ** (np.arange(half, dtype=np.float32) / np.float32(half)))).astype(np.float32)
    ang = (pos[:, None] * inv_freq[None, :]).astype(np.float32)
    c["c_cos"] = np.cos(ang).astype(np.float32)
    c["c_sin"] = np.sin(ang).astype(np.float32)
    mf = np.zeros((128, 16, 128), np.float32)
    mm = np.zeros((128, 16, 128), np.float32)
    for i in range(8):
        qpos = gblock(r, i) * 128 + np.arange(128)
        for rk in range(2):
            kpos = gblock(rk, i) * 128 + np.arange(128)
            mf[:, i * 2 + rk, :] = (kpos[:, None] <= qpos[None, :])
            mm[:, i * 2 + rk, :] = ((kpos[:, None] // 64) <= (qpos[None, :] // 64))
    c["c_maskf"] = mf.reshape(128, 2048).astype(bf)
    c["c_maskm"] = mm.reshape(128, 2048).astype(bf)
    ms = np.zeros((32, 2, 16), np.float32)
    for b in range(2):
        for k in range(16):
            ms[b * 16 + k, b, :] = (k <= np.arange(16))
    c["c_masks"] = ms.reshape(32, 32).astype(bf)
    ts_ = np.zeros((32, 32), np.float32)
    for b in range(2):
        ts_[b * 16:(b + 1) * 16, b * 16:(b + 1) * 16] = np.triu(np.ones((16, 16), np.float32))
    c["c_tris"] = ts_
    ind = np.zeros((32, 2), np.float32)
    ind[:16, 0] = 1
    ind[16:, 1] = 1
    c["c_inds"] = ind
    c["c_indw"] = np.repeat(ind[:, :, None], 128, axis=2).reshape(32, 256).astype(np.float32)
    m2 = np.zeros((32, 2, 16), np.float32)
    m2[:16, 0, :] = 1
    m2[16:, 1, :] = 1
    c["c_masks2"] = m2.reshape(32, 32).astype(bf)
    return c


_NC_CACHE = {}


def make_in_maps(inp):
    f = lambda a: np.ascontiguousarray(np.asarray(a, dtype=np.float32))
    xpr = f(inp["x_prompt"]); xsm = f(inp["x_sample"])
    maps = []
    wshared = {
        "g_norm": f(inp["g_norm"]), "w_in": f(inp["w_in"]).reshape(NL * D, N_IN), "b_f": f(inp["b_f"]),
        "g_q_fox": f(inp["g_q_fox"]), "g_k_fox": f(inp["g_k_fox"]), "g_cq": f(inp["g_cq"]),
        "w_qb": f(inp["w_qb"]).reshape(NL * 1024, 3072), "g_qn": f(inp["g_qn"]), "g_qp": f(inp["g_qp"]),
        "g_ckv": f(inp["g_ckv"]), "g_kp": f(inp["g_kp"]), "w_kvb": f(inp["w_kvb"]).reshape(NL * 512, 4096),
        "g_kn": f(inp["g_kn"]), "w_out": f(inp["w_out"]).reshape(NL * D, D),
    }
    cfk = f(inp["cache_fox_k"]); cfv = f(inp["cache_fox_v"]); cfl = f(inp["cache_fox_logf"])
    cck = f(inp["cache_mla_ckv"]); ckp = f(inp["cache_mla_kpe"])
    for c in range(8):
        p, r = c // 2, c % 2
        m = dict(wshared)
        m["xp"] = np.concatenate([xpr[p, gblock(r, i) * 128:(gblock(r, i) + 1) * 128] for i in range(8)], axis=0)
        m["xs"] = xsm[2 * c:2 * c + 2].reshape(TS, D)
        m["cfk"] = cfk[:, 2 * c:2 * c + 2].reshape(NL * 2 * PAST, 2048)
        m["cfv"] = cfv[:, 2 * c:2 * c + 2].reshape(NL * 2 * PAST, 2048)
        m["cfl"] = cfl[:, 2 * c:2 * c + 2].reshape(NL * 2 * PAST, 16)
        m["cck"] = cck[:, 2 * c:2 * c + 2].reshape(NL * 2 * PAST, 512)
        m["ckp"] = ckp[:, 2 * c:2 * c + 2].reshape(NL * 2 * PAST, 64)
        m.update(host_constants(r))
        maps.append(m)
    return maps


def assemble(results):
    y_p = np.zeros((4, 2048, D), np.float32); y_s = np.zeros((16, 16, D), np.float32)
    fk_p = np.zeros((NL, 4, 2048, H, 128), np.float32); fv_p = np.zeros_like(fk_p)
    fl_p = np.zeros((NL, 4, 2048, H), np.float32); ck_p = np.zeros((NL, 4, 2048, 512), np.float32)
    kp_p = np.zeros((NL, 4, 2048, 64), np.float32)
    fk_s = np.zeros((NL, 16, 16, H, 128), np.float32); fv_s = np.zeros_like(fk_s)
    fl_s = np.zeros((NL, 16, 16, H), np.float32); ck_s = np.zeros((NL, 16, 16, 512), np.float32)
    kp_s = np.zeros((NL, 16, 16, 64), np.float32)
    for c in range(8):
        p, r = c // 2, c % 2
        R = results[c]
        for i in range(8):
            g = gblock(r, i)
            sl = slice(g * 128, (g + 1) * 128)
            y_p[p, sl] = R["yp"][i * 128:(i + 1) * 128]
            for l in range(NL):
                rows = slice(l * TP + i * 128, l * TP + (i + 1) * 128)
                fk_p[l, p, sl] = R["nfk_p"][rows].reshape(128, H, 128)
                fv_p[l, p, sl] = R["nfv_p"][rows].reshape(128, H, 128)
                fl_p[l, p, sl] = R["nfl_p"][rows]
                ck_p[l, p, sl] = R["nck_p"][rows]
                kp_p[l, p, sl] = R["nkp_p"][rows]
        y_s[2 * c:2 * c + 2] = R["ys"].reshape(2, 16, D)
        for l in range(NL):
            rows = slice(l * TS, (l + 1) * TS)
            fk_s[l, 2 * c:2 * c + 2] = R["nfk_s"][rows].reshape(2, 16, H, 128)
            fv_s[l, 2 * c:2 * c + 2] = R["nfv_s"][rows].reshape(2, 16, H, 128)
            fl_s[l, 2 * c:2 * c + 2] = R["nfl_s"][rows].reshape(2, 16, H)
            ck_s[l, 2 * c:2 * c + 2] = R["nck_s"][rows].reshape(2, 16, 512)
            kp_s[l, 2 * c:2 * c + 2] = R["nkp_s"][rows].reshape(2, 16, 64)
    return (y_p, y_s, fk_p, fv_p, fl_p, ck_p, kp_p, fk_s, fv_s, fl_s, ck_s, kp_s)


def kernel(**inputs):
    nc = build_program()
    maps = make_in_maps(inputs)
    res = run_bass_kernel_spmd(nc, maps, core_ids=list(range(8)))
    return assemble(res.results)
```

```python
import numpy as np
import ml_dtypes
from contextlib import ExitStack
import concourse.bass as bass
import concourse.mybir as mybir
from concourse.bass_utils import run_bass_kernel_spmd

F32 = mybir.dt.float32
BF16 = mybir.dt.bfloat16
ALU = mybir.AluOpType
AF = mybir.ActivationFunctionType
AX = mybir.AxisListType

D = 4096
NL = 2
H = 16
TP = 1024
TS = 32
T = TP + TS
NTB = 9
PAST = 2048
EPS = 1e-6
N_IN = 11856
OFF_Q, OFF_K, OFF_V, OFF_F, OFF_ZA, OFF_CQ, OFF_CKV, OFF_KPE, OFF_ZB = (
    0, 2048, 4096, 6144, 6160, 8208, 9232, 9744, 9808)
FOX_SCALE = 1.0 / np.sqrt(128.0)
MLA_SCALE = 1.0 / np.sqrt(192.0)


def gblock(r, i):
    return 2 * i + (i % 2) if r == 0 else 2 * i + 1 - (i % 2)


def tb_rows(tb):
    return 128 if tb < 8 else TS


def _freeze(fn):
    import types
    if fn is None or fn.__closure__ is None:
        return fn
    cells = []
    for c in fn.__closure__:
        try:
            cells.append(types.CellType(c.cell_contents))
        except ValueError:
            cells.append(c)
    return types.FunctionType(fn.__code__, fn.__globals__, fn.__name__, fn.__defaults__, tuple(cells))


class Tr:
    ENG = ('pe', 'act', 'dve', 'pool', 'sp')
    NDS = 20

    def __init__(s):
        s.ops = {e: [] for e in s.ENG}
        s.cnt = {e: 0 for e in s.ENG}
        s.dcnt = {e: 0 for e in s.ENG}
        s.lw = {}
        s.rd = {}
        s.waited = {e: {} for e in s.ENG}
        s.semval = {}
        s.ccn = 0
        s.maxops = None

    def _need(s, eng, deps):
        w = []
        for (k, v) in deps:
            if k == ('c', 'pe') and eng == 'pe':
                continue
            if s.waited[eng].get(k, 0) < v:
                s.waited[eng][k] = v
                w.append((k, v))
        return w

    def add(s, eng, fn, reads=(), writes=(), kind='c', signal=True):
        s.nadd = getattr(s, 'nadd', 0) + 1
        if s.maxops is not None and s.nadd > s.maxops:
            return None
        deps = set()
        for b in reads:
            if b in s.lw:
                deps.add(s.lw[b])
            if (isinstance(b, tuple) and b[0] in ('psA', 'psT', 'psS')) or b in ('psO', 'psD'):
                for t in s.rd.get(b, ()):
                    if t[0] != ('c', eng):
                        deps.add(t)
        for b in writes:
            if b in s.lw and s.lw[b][0] != ('c', eng):
                deps.add(s.lw[b])
            for t in s.rd.get(b, ()):
                if t[0] != ('c', eng):
                    deps.add(t)
        if kind == 'dma':
            k = s.dcnt[eng]
            s.dcnt[eng] += 1
            slot = k % s.NDS
            m = k // s.NDS + 1
            key = ('d', eng, slot)
            if m > 1:
                deps.add((key, 16 * (m - 1)))
            tok = (key, 16 * m)
        elif kind == 'cc':
            key = ('cc',)
            s.ccn += 1
            tok = (key, s.ccn)
        else:
            key = ('c', eng)
            if signal:
                s.cnt[eng] += 1
                tok = (key, s.cnt[eng])
            else:
                tok = (key, s.cnt[eng] + 1)
        s.semval[key] = max(s.semval.get(key, 0), tok[1])
        waits = s._need(eng, sorted(deps, key=str))
        for b in writes:
            s.lw[b] = tok
            s.rd[b] = []
        for b in reads:
            s.rd.setdefault(b, []).append(tok)
        s.ops[eng].append((waits, _freeze(fn), kind, signal, key))
        return tok

    def _allvals(s):
        d = dict(s.semval)
        for e in s.ENG:
            if ('c', e) in d:
                d[('c', e)] = s.cnt[e]
        return sorted([(k, v) for k, v in d.items() if v > 0], key=str)

    def barrier(s):
        if s.maxops is not None and getattr(s, 'nadd', 0) > s.maxops:
            return
        deps = s._allvals()
        for e in s.ENG:
            w = s._need(e, deps)
            if w:
                s.ops[e].append((w, None, 'c', False, None))

    def final_wait(s, eng='sp'):
        deps = s._allvals()
        w = s._need(eng, deps)
        if w:
            s.ops[eng].append((w, None, 'c', False, None))

    def sem_keys(s):
        return list(s.semval.keys())

    def emit(s, block, sems):
        def run(name):
            def f(E):
                for waits, fn, kind, signal, key in s.ops[name]:
                    for (k, v) in waits:
                        E.wait_ge(sems[k], v)
                    if fn is None:
                        continue
                    ins = fn(E)
                    if kind == 'dma':
                        ins.then_inc(sems[key], 16)
                    elif kind == 'cc':
                        ins.then_inc(sems[key])
                    elif signal:
                        ins.then_inc(sems[key], 1)
            return f
        block.tensor(run('pe'))
        block.scalar(run('act'))
        block.vector(run('dve'))
        block.gpsimd(run('pool'))
        block.sync(run('sp'))


def build_program(stop_after=99, n_cores=8, maxops=None, lite=()):
    nc = bass.Bass("TRN2", target_bir_lowering=False)
    T_ = Tr()
    T_.maxops = maxops

    def din(name, shape, dt=F32):
        if name in lite:
            shape = [8, shape[1]]
        return nc.dram_tensor(name, list(shape), dt, kind="ExternalInput").ap()

    def dout(name, shape, dt=F32):
        return nc.dram_tensor(name, list(shape), dt, kind="ExternalOutput").ap()

    def dint(name, shape, dt=F32):
        return nc.dram_tensor(name, list(shape), dt).ap()

    xp = din("xp", [TP, D]); xs = din("xs", [TS, D])
    cfk = din("cfk", [NL * 2 * PAST, 2048]); cfv = din("cfv", [NL * 2 * PAST, 2048])
    cfl = din("cfl", [NL * 2 * PAST, 16]); cck = din("cck", [NL * 2 * PAST, 512])
    ckp = din("ckp", [NL * 2 * PAST, 64])
    g_norm = din("g_norm", [NL, D]); w_in = din("w_in", [NL * D, N_IN])
    b_f = din("b_f", [NL, 16]); g_qf = din("g_q_fox", [NL, 128]); g_kf = din("g_k_fox", [NL, 128])
    g_cq = din("g_cq", [NL, 1024]); w_qb = din("w_qb", [NL * 1024, 3072])
    g_qn = din("g_qn", [NL, 128]); g_qp = din("g_qp", [NL, 64]); g_ckv = din("g_ckv", [NL, 512])
    g_kp = din("g_kp", [NL, 64]); w_kvb = din("w_kvb", [NL * 512, 4096]); g_kn = din("g_kn", [NL, 128])
    w_out = din("w_out", [NL * D, D])
    c_identb = din("c_identb", [128, 128], BF16); c_identf = din("c_identf", [128, 128])
    c_onesb = din("c_onesb", [128, 128], BF16); c_onesf = din("c_onesf", [128, 128])
    c_trif = din("c_trif", [128, 128])
    c_cos = din("c_cos", [T, 32]); c_sin = din("c_sin", [T, 32])
    c_maskf = din("c_maskf", [128, 16 * 128], BF16); c_maskm = din("c_maskm", [128, 16 * 128], BF16)
    c_masks = din("c_masks", [32, 32], BF16)
    c_tris = din("c_tris", [32, 32]); c_inds = din("c_inds", [32, 2]); c_indw = din("c_indw", [32, 256]); c_masks2 = din("c_masks2", [32, 32], BF16)

    yp = dout("yp", [TP, D]); ys = dout("ys", [TS, D])
    nfk_p = dout("nfk_p", [NL * TP, 2048]); nfv_p = dout("nfv_p", [NL * TP, 2048])
    nfl_p = dout("nfl_p", [NL * TP, 16]); nck_p = dout("nck_p", [NL * TP, 512]); nkp_p = dout("nkp_p", [NL * TP, 64])
    nfk_s = dout("nfk_s", [NL * TS, 2048]); nfv_s = dout("nfv_s", [NL * TS, 2048])
    nfl_s = dout("nfl_s", [NL * TS, 16]); nck_s = dout("nck_s", [NL * TS, 512]); nkp_s = dout("nkp_s", [NL * TS, 64])

    ymid = dint("ymid", [T, D])
    kTx = [dint(f"kTx{l}", [2048, TP], BF16) for l in range(NL)]
    vx = [dint(f"vx{l}", [TP, 2048], BF16) for l in range(NL)]
    lfx = [dint(f"lfx{l}", [TP, 16]) for l in range(NL)]
    ckx = [dint(f"ckx{l}", [512, TP], BF16) for l in range(NL)]
    kpx = [dint(f"kpx{l}", [64, TP], BF16) for l in range(NL)]
    kTg = [dint(f"kTg{l}", [2 * 2048, TP], BF16) for l in range(NL)]
    vg = [dint(f"vg{l}", [2 * TP, 2048], BF16) for l in range(NL)]
    lfg = [dint(f"lfg{l}", [2 * TP, 16]) for l in range(NL)]
    ckg = [dint(f"ckg{l}", [2 * 512, TP], BF16) for l in range(NL)]
    kpg = [dint(f"kpg{l}", [2 * 64, TP], BF16) for l in range(NL)]
    qfx = dint("qfx", [2048, T], BF16); qnx = dint("qnx", [2048, T], BF16)
    qpx = dint("qpx", [1024, T], BF16); zax = dint("zax", [2048, T], BF16); zbx = dint("zbx", [2048, T], BF16)

    es = ExitStack()
    ph = [None]

    def sb(name, shape, dt=F32):
        return es.enter_context(nc.sbuf_tensor(name, list(shape), dt))

    def sbp(name, shape, dt=F32):
        return ph[0].enter_context(nc.sbuf_tensor(name, list(shape), dt))

    def pst(name, shape, dt=F32):
        return es.enter_context(nc.psum_tensor(name, list(shape), dt))

    A = T_.add

    identb = sb("identb", [128, 128], BF16); identf = sb("identf", [128, 128])
    onesb = sb("onesb", [128, 128], BF16); onesf = sb("onesf", [128, 128]); trif = sb("trif", [128, 128])
    cosT = sb("cosT", [128, NTB, 32]); sinT = sb("sinT", [128, NTB, 32])
    masks = sb("masks", [32, 32], BF16); tris = sb("tris", [32, 32]); inds = sb("inds", [32, 2])
    actT = sb("actT", [128, 32, T], BF16)
    wt = [sb(f"wt{j}", [128, 32, 512], BF16) for j in range(2)]
    gcol = sb("gcol", [128, 32]); g32 = sb("g32", [32, 128]); gcq_t = sb("gcq_t", [128, 1024]); gckv_t = sb("gckv_t", [128, 512])
    gqf_t = sb("gqf_t", [128, 128]); gkf_t = sb("gkf_t", [128, 128]); gqn_t = sb("gqn_t", [128, 128])
    gkn_t = sb("gkn_t", [128, 128]); gqp_t = sb("gqp_t", [128, 64]); gkp_t = sb("gkp_t", [128, 64])
    bf_t = sb("bf_t", [128, 16])
    kTn = sb("kTn", [128, H, TS], BF16)
    vn = sb("vn", [TS, 2048], BF16)
    lfn = sb("lfn", [TS, 16])
    ckTn = sb("ckTn", [128, 4, TS], BF16)
    kpTn = sb("kpTn", [64, TS], BF16)
    ss = sb("ss", [128, 16]); rs = sb("rs", [128, 16])
    ssr = [sb(f"ssr{k}", [128, 16]) for k in range(4)]; rsr = [sb(f"rsr{k}", [128, 16]) for k in range(4)]
    rms_i = [0]
    rt = sb("rt", [128, 64]); rt2 = sb("rt2", [128, 64])

    psA = [pst(f"psA{j}", [128, 512]) for j in range(2)]
    psT = [pst(f"psT{j}", [128, 1024], BF16) for j in range(2)]
    psS = [pst(f"psS{j}", [128, 512]) for j in range(2)]
    psO = pst("psO", [128, 512]); psD = pst("psD", [128, 512])
    psA = psA + [psS[0], psS[1], psO, psD]
    npsA = [6]

    def dma(eng, out, in_, reads, writes):
        return A(eng, lambda E: E.dma_start(out=out, in_=in_), reads, writes, kind='dma')

    for (dst, src, nm) in [(identb, c_identb, 'identb'), (identf, c_identf, 'identf'), (onesb, c_onesb, 'onesb'),
                           (onesf, c_onesf, 'onesf'), (trif, c_trif, 'trif'), (masks, c_masks, 'masks'), (tris, c_tris, 'tris'),
                           (inds, c_inds, 'inds')]:
        dma('sp', dst[:], src[:, :], [], [nm])
    for tb in range(NTB):
        M = tb_rows(tb)
        dma('sp', cosT[:M, tb, :], c_cos[tb * 128: tb * 128 + M, :], [], [('cos', tb)])
        dma('sp', sinT[:M, tb, :], c_sin[tb * 128: tb * 128 + M, :], [], [('sin', tb)])

    pa_i = [0]

    def next_psA():
        j = pa_i[0] % npsA[0]
        pa_i[0] += 1
        return j

    wt_i = [0]

    def load_w(src2d, K, W, c0):
        j = wt_i[0] % 2
        wt_i[0] += 1
        KC = K // 128
        step = 8
        for k0 in range(0, KC, step):
            k1 = min(KC, k0 + step)
            src = src2d[k0 * 128:k1 * 128, c0:c0 + W].rearrange("(kc p) n -> p kc n", p=128)
            A('pool', (lambda E, o=wt[j][:, k0:k1, :W], i=src: E.dma_start(out=o, in_=i)),
              [], [('wt', j, k0)], kind='dma')
        return j

    def proj_tm(tb, actT_names, lhs_fn, KC, wj, W, wcol0=0):
        M = tb_rows(tb)
        pj = next_psA()
        for kc in range(KC):
            lh = lhs_fn(kc, tb, M)
            A('pe', (lambda E, kc=kc, lh=lh: E.matmul(psA[pj][:M, :W], lhsT=lh,
                                              rhs=wt[wj][:, kc, wcol0:wcol0 + W],
                                              start=(kc == 0), stop=(kc == KC - 1))),
              list(actT_names) + [('wt', wj, (kc // 8) * 8)], [('psA', pj)], signal=(kc == KC - 1))
        return pj, M

    def hT_lhs(kc, tb, M):
        return actT[:, kc, tb * 128: tb * 128 + M]

    def rms_groups(src3, M, G, Dg, gain_t, dst3, src_names, dst_names):
        sq3 = sq[:M, :G * Dg].rearrange("p (g d) -> p g d", g=G)
        k_ = rms_i[0] % 4
        rms_i[0] += 1
        ss_, rs_, sn, rn = ssr[k_], rsr[k_], ('ss', k_), ('rs', k_)
        A('act', lambda E: E.activation(out=sq3, in_=src3, func=AF.Square), src_names, ['sq'])
        A('dve', lambda E: E.tensor_reduce(out=ss_[:M, :G], in_=sq3, axis=AX.X, op=ALU.add), ['sq'], [sn])
        A('dve', lambda E: E.tensor_scalar(out=ss_[:M, :G], in0=ss_[:M, :G], scalar1=1.0 / Dg, scalar2=EPS,
                                           op0=ALU.mult, op1=ALU.add), [sn], [sn])
        A('act', lambda E: E.activation(out=ss_[:M, :G], in_=ss_[:M, :G], func=AF.Sqrt), [sn], [sn])
        A('dve', lambda E: E.reciprocal(out=rs_[:M, :G], in_=ss_[:M, :G]), [sn], [rn])
        A('dve', lambda E: E.tensor_tensor(out=dst3, in0=src3,
                                           in1=rs_[:M, :G].unsqueeze(2).to_broadcast([M, G, Dg]), op=ALU.mult),
          list(src_names) + [rn], dst_names)
        A('dve', lambda E: E.tensor_tensor(out=dst3, in0=dst3,
                                           in1=gain_t[:M, :Dg].unsqueeze(1).to_broadcast([M, G, Dg]), op=ALU.mult),
          list(dst_names) + ['gains'], dst_names)

    pt_i = [0]

    def transpose_to(src_bf, M, nblk, blkw, dst_fn, src_names, dst_names_fn, evac='act', scale_fn=None):
        pj = pt_i[0] % 2
        pt_i[0] += 1
        for b in range(nblk):
            A('pe', (lambda E, b=b: E.transpose(psT[pj][:blkw, b * 128: b * 128 + M],
                                                src_bf[:M, b * blkw:(b + 1) * blkw], identb[:M, :M])),
              list(src_names) + ['identb'], [('psT', pj)], signal=(b == nblk - 1))
        for b in range(nblk):
            d_ = dst_fn(b)
            if scale_fn is not None:
                sc_ = scale_fn(b)
                A('act', (lambda E, b=b, d_=d_, sc_=sc_: E.activation(out=d_, in_=psT[pj][:blkw, b * 128: b * 128 + M],
                                                      func=AF.Copy, scale=sc_)),
                  [('psT', pj), 'gcol'], dst_names_fn(b))
            else:
                A(evac, (lambda E, b=b, d_=d_: (E.copy if evac == 'act' else E.tensor_copy)(
                    out=d_, in_=psT[pj][:blkw, b * 128: b * 128 + M])),
                  [('psT', pj)], dst_names_fn(b))

    def rope(x3, M, G, tb, tmp_a, tmp_b, names):
        c = cosT[:M, tb, :].unsqueeze(1).to_broadcast([M, G, 32])
        s_ = sinT[:M, tb, :].unsqueeze(1).to_broadcast([M, G, 32])
        x1 = x3[:, :, 0:32]
        x2 = x3[:, :, 32:64]
        ta = tmp_a[:M, :G * 32].rearrange("p (g d) -> p g d", g=G)
        tb_ = tmp_b[:M, :G * 32].rearrange("p (g d) -> p g d", g=G)
        rn = list(names) + [('cos', tb), ('sin', tb)]
        A('dve', lambda E: E.tensor_tensor(out=ta, in0=x1, in1=s_, op=ALU.mult), rn, ['rt'])
        A('dve', lambda E: E.tensor_tensor(out=tb_, in0=x2, in1=s_, op=ALU.mult), rn, ['rt2'])
        A('dve', lambda E: E.tensor_tensor(out=x1, in0=x1, in1=c, op=ALU.mult), rn, names)
        A('dve', lambda E: E.tensor_tensor(out=x2, in0=x2, in1=c, op=ALU.mult), rn, names)
        A('dve', lambda E: E.tensor_tensor(out=x1, in0=x1, in1=tb_, op=ALU.subtract), list(names) + ['rt2'], names)
        A('dve', lambda E: E.tensor_tensor(out=x2, in0=x2, in1=ta, op=ALU.add), list(names) + ['rt'], names)

    pend = []

    def defer(f):
        pend.append(f)
        while len(pend) > 1:
            pend.pop(0)()

    def flush():
        while pend:
            pend.pop(0)()

    def out_rows(l, tb, M, pbuf, sbuf_, width, c0=0):
        if tb < 8:
            return pbuf[l * TP + tb * 128: l * TP + tb * 128 + M, c0:c0 + width]
        return sbuf_[l * TS: l * TS + M, c0:c0 + width]

    for l in range(NL):
        T_.barrier()
        ph[0] = ExitStack()
        xt = [sbp(f"xt{l}", [128, D])]
        hb = sbp(f"hb{l}", [128, D], BF16)
        sq = sbp(f"sq{l}", [128, 1024])
        wf = [sbp(f"wf{l}_{j}", [128, 512]) for j in range(2)]
        wb = [sbp(f"wb{l}_{j}", [128, 512], BF16) for j in range(2)]
        tsb = [sbp(f"tsb{l}_{j}", [128, 512], BF16) for j in range(2)]
        cqT = sbp(f"cqT{l}", [128, 8, T], BF16)
        dma('sp', g32[:, :], g_norm[l, :].rearrange("(c p) -> c p", p=128), [], ['g32'])
        A('pe', lambda E: E.transpose(psS[0][:, :32], g32[:32, :], identf[:32, :32]), ['g32', 'identf'], [('psS', 0)])
        A('act', lambda E: E.copy(out=gcol[:, :], in_=psS[0][:, :32]), [('psS', 0)], ['gcol'])
        for (dst, src, w) in [(gcq_t, g_cq, 1024), (gckv_t, g_ckv, 512), (gqf_t, g_qf, 128),
                              (gkf_t, g_kf, 128), (gqn_t, g_qn, 128), (gkn_t, g_kn, 128), (gqp_t, g_qp, 64),
                              (gkp_t, g_kp, 64), (bf_t, b_f, 16)]:
            dma('sp', dst[:, :w], src[l, :].partition_broadcast(128), [], ['gains'])
        T_.barrier()

        for tb in range(NTB):
            M = tb_rows(tb)
            j = 0
            if l == 0:
                src = xp[tb * 128: tb * 128 + M, :] if tb < 8 else xs[:, :]
            else:
                src = ymid[tb * 128: tb * 128 + M, :]
            dma('sp', xt[j][:M, :], src, [('ymid', tb)], [('xt', j)])
            A('dve', lambda E, j=j, M=M: E.scalar_tensor_tensor(out=hb[:M, :], in0=xt[j][:M, :], scalar=1.0,
                                                               in1=xt[j][:M, :], op0=ALU.mult, op1=ALU.mult,
                                                               accum_out=ss[:M, 0:1]),
              [('xt', j)], ['hb', 'ss'])
            A('dve', lambda E, M=M: E.tensor_scalar(out=ss[:M, 0:1], in0=ss[:M, 0:1], scalar1=1.0 / D, scalar2=EPS,
                                                    op0=ALU.mult, op1=ALU.add), ['ss'], ['ss'])
            A('act', lambda E, M=M: E.activation(out=ss[:M, 0:1], in_=ss[:M, 0:1], func=AF.Sqrt), ['ss'], ['ss'])
            A('dve', lambda E, M=M: E.reciprocal(out=rs[:M, 0:1], in_=ss[:M, 0:1]), ['ss'], ['rs'])
            A('dve', lambda E, j=j, M=M: E.tensor_scalar(out=hb[:M, :], in0=xt[j][:M, :], scalar1=rs[:M, 0:1],
                                                        scalar2=None, op0=ALU.mult),
              [('xt', j), 'rs'], ['hb'])
            for g4 in range(4):
                transpose_to(hb[:, g4 * 1024:(g4 + 1) * 1024], M, 8, 128,
                             (lambda b, g4=g4, tb=tb, M=M: actT[:, g4 * 8 + b, tb * 128: tb * 128 + M]),
                             ['hb'], (lambda b, tb=tb: [('actT', tb)]), evac='act',
                             scale_fn=(lambda b, g4=g4: gcol[:, g4 * 8 + b: g4 * 8 + b + 1]))
        if stop_after <= 0:
            T_.barrier()
            ph[0].close()
            break

        hn = lambda tb: [('actT', tb)]
        for cb in range(4):
            wj = load_w(w_in[l * D:(l + 1) * D, :], D, 512, OFF_K + cb * 512)
            for tb in range(NTB):
                pj, M = proj_tm(tb, hn(tb), hT_lhs, 32, wj, 512)
                j = tb % 2
                src3 = psA[pj][:M, :].rearrange("p (g d) -> p g d", g=4)
                dst3 = wf[j][:M, :512].rearrange("p (g d) -> p g d", g=4)
                rms_groups(src3, M, 4, 128, gkf_t, dst3, [('psA', pj)], [('wf', j)])
                dma('sp', out_rows(l, tb, M, nfk_p, nfk_s, 512, cb * 512), wf[j][:M, :512], [('wf', j)], [])
                A('act', lambda E, j=j, M=M: E.copy(out=wb[j][:M, :512], in_=wf[j][:M, :512]), [('wf', j)], [('wb', j)])
                def tail_k(tb=tb, j=j, M=M, cb=cb):
                    if tb < 8:
                        transpose_to(wb[j], M, 4, 128, (lambda b, j=j, M=M: tsb[j][:, b * 128: b * 128 + M]),
                                     [('wb', j)], (lambda b, j=j: [('tsb', j)]))
                        dma('sp', kTx[l][cb * 512:(cb + 1) * 512, tb * 128:(tb + 1) * 128].rearrange("(g d) t -> d g t", g=4),
                            tsb[j][:, :512].rearrange("d (g t) -> d g t", g=4), [('tsb', j)], [('kTx', l)])
                    else:
                        transpose_to(wb[j], M, 4, 128, (lambda b, cb=cb, M=M: kTn[:, cb * 4 + b, :M]),
                                     [('wb', j)], (lambda b: ['kTn']))
                defer(tail_k)
            flush()
        for cb in range(4):
            wj = load_w(w_in[l * D:(l + 1) * D, :], D, 512, OFF_V + cb * 512)
            for tb in range(NTB):
                pj, M = proj_tm(tb, hn(tb), hT_lhs, 32, wj, 512)
                j = tb % 2
                A('act', lambda E, j=j, M=M, pj=pj: E.copy(out=wf[j][:M, :512], in_=psA[pj][:M, :]),
                  [('psA', pj)], [('wf', j)])
                dma('sp', out_rows(l, tb, M, nfv_p, nfv_s, 512, cb * 512), wf[j][:M, :512], [('wf', j)], [])
                if tb < 8:
                    A('dve', lambda E, j=j, M=M, pj=pj: E.tensor_copy(out=wb[j][:M, :512], in_=psA[pj][:M, :]),
                      [('psA', pj)], [('wb', j)])
                    dma('sp', vx[l][tb * 128:(tb + 1) * 128, cb * 512:(cb + 1) * 512], wb[j][:M, :512],
                        [('wb', j)], [('vx', l)])
                else:
                    A('dve', lambda E, M=M, pj=pj, cb=cb: E.tensor_copy(out=vn[:M, cb * 512:(cb + 1) * 512],
                                                                        in_=psA[pj][:M, :]),
                      [('psA', pj)], ['vn'])
        wj = load_w(w_in[l * D:(l + 1) * D, :], D, 16, OFF_F)
        for tb in range(NTB):
            pj, M = proj_tm(tb, hn(tb), hT_lhs, 32, wj, 16)
            j = tb % 2
            A('dve', lambda E, j=j, M=M, pj=pj: E.tensor_tensor(out=wf[j][:M, :16], in0=psA[pj][:M, :16],
                                                               in1=bf_t[:M, :16], op=ALU.add),
              [('psA', pj), 'gains'], [('wf', j)])
            A('act', lambda E, j=j, M=M: E.activation(out=wf[j][:M, :16], in_=wf[j][:M, :16], func=AF.Exp, scale=-1.0),
              [('wf', j)], [('wf', j)])
            A('act', lambda E, j=j, M=M: E.activation(out=wf[j][:M, :16], in_=wf[j][:M, :16], func=AF.Ln, bias=1.0),
              [('wf', j)], [('wf', j)])
            if tb < 8:
                A('dve', lambda E, j=j, M=M: E.tensor_scalar(out=wf[j][:M, :16], in0=wf[j][:M, :16], scalar1=-1.0,
                                                            scalar2=None, op0=ALU.mult), [('wf', j)], [('wf', j)])
                dma('sp', out_rows(l, tb, M, nfl_p, nfl_s, 16), wf[j][:M, :16], [('wf', j)], [])
                dma('sp', lfx[l][tb * 128:(tb + 1) * 128, :], wf[j][:M, :16], [('wf', j)], [('lfx', l)])
            else:
                A('dve', lambda E, j=j, M=M: E.tensor_scalar(out=lfn[:M, :16], in0=wf[j][:M, :16], scalar1=-1.0,
                                                            scalar2=None, op0=ALU.mult), [('wf', j)], ['lfn'])
                dma('sp', out_rows(l, tb, M, nfl_p, nfl_s, 16), lfn[:M, :16], ['lfn'], [])
        wj = load_w(w_in[l * D:(l + 1) * D, :], D, 512, OFF_CKV)
        for tb in range(NTB):
            pj, M = proj_tm(tb, hn(tb), hT_lhs, 32, wj, 512)
            j = tb % 2
            src3 = psA[pj][:M, :].rearrange("p (g d) -> p g d", g=1)
            dst3 = wf[j][:M, :512].rearrange("p (g d) -> p g d", g=1)
            rms_groups(src3, M, 1, 512, gckv_t, dst3, [('psA', pj)], [('wf', j)])
            dma('sp', out_rows(l, tb, M, nck_p, nck_s, 512), wf[j][:M, :512], [('wf', j)], [])
            A('act', lambda E, j=j, M=M: E.copy(out=wb[j][:M, :512], in_=wf[j][:M, :512]), [('wf', j)], [('wb', j)])
            if tb < 8:
                transpose_to(wb[j], M, 4, 128, (lambda b, j=j, M=M: tsb[j][:, b * 128: b * 128 + M]),
                             [('wb', j)], (lambda b, j=j: [('tsb', j)]))
                dma('sp', ckx[l][:, tb * 128:(tb + 1) * 128].rearrange("(g d) t -> d g t", g=4),
                    tsb[j][:, :512].rearrange("d (g t) -> d g t", g=4), [('tsb', j)], [('ckx', l)])
            else:
                transpose_to(wb[j], M, 4, 128, (lambda b, M=M: ckTn[:, b, :M]), [('wb', j)], (lambda b: ['ckTn']))
        wj = load_w(w_in[l * D:(l + 1) * D, :], D, 64, OFF_KPE)
        for tb in range(NTB):
            pj, M = proj_tm(tb, hn(tb), hT_lhs, 32, wj, 64)
            j = tb % 2
            src3 = psA[pj][:M, :64].rearrange("p (g d) -> p g d", g=1)
            dst3 = wf[j][:M, :64].rearrange("p (g d) -> p g d", g=1)
            rms_groups(src3, M, 1, 64, gkp_t, dst3, [('psA', pj)], [('wf', j)])
            rope(dst3, M, 1, tb, rt, rt2, [('wf', j)])
            dma('sp', out_rows(l, tb, M, nkp_p, nkp_s, 64), wf[j][:M, :64], [('wf', j)], [])
            A('act', lambda E, j=j, M=M: E.copy(out=wb[j][:M, :64], in_=wf[j][:M, :64]), [('wf', j)], [('wb', j)])
            if tb < 8:
                transpose_to(wb[j], M, 1, 64, (lambda b, j=j, M=M: tsb[j][:64, :M]),
                             [('wb', j)], (lambda b, j=j: [('tsb', j)]))
                dma('sp', kpx[l][:, tb * 128:(tb + 1) * 128], tsb[j][:64, :128], [('tsb', j)], [('kpx', l)])
            else:
                transpose_to(wb[j], M, 1, 64, (lambda b, M=M: kpTn[:, :M]), [('wb', j)], (lambda b: ['kpTn']))
        if stop_after <= 1:
            T_.barrier()
            ph[0].close()
            break

        T_.barrier()
        groups = [[2 * i, 2 * i + 1] for i in range(n_cores // 2)]
        for (src, dst, nm, nchunk) in [(kTx[l], kTg[l], 'kTg', 4), (vx[l], vg[l], 'vg', 4), (lfx[l], lfg[l], 'lfg', 1),
                                       (ckx[l], ckg[l], 'ckg', 1), (kpx[l], kpg[l], 'kpg', 1)]:
            rows = src.shape[0] // nchunk
            for c in range(nchunk):
                A('pool', (lambda E, src=src, dst=dst, c=c, rows=rows: E.collective_compute(
                    "AllGather", ALU.bypass, replica_groups=groups, ins=[src[c * rows:(c + 1) * rows, :]],
                    outs=[dst[2 * c * rows:2 * (c + 1) * rows, :]])),
                  [], [(nm, l, c)], kind='cc')
        T_.barrier()

        for cb in range(4):
            wj = load_w(w_in[l * D:(l + 1) * D, :], D, 512, OFF_Q + cb * 512)
            for tb in range(NTB):
                pj, M = proj_tm(tb, hn(tb), hT_lhs, 32, wj, 512)
                j = tb % 2
                src3 = psA[pj][:M, :].rearrange("p (g d) -> p g d", g=4)
                dst3 = wf[j][:M, :512].rearrange("p (g d) -> p g d", g=4)
                rms_groups(src3, M, 4, 128, gqf_t, dst3, [('psA', pj)], [('wf', j)])
                A('act', lambda E, j=j, M=M: E.copy(out=wb[j][:M, :512], in_=wf[j][:M, :512]), [('wf', j)], [('wb', j)])
                def tail_q(tb=tb, j=j, M=M, cb=cb):
                    transpose_to(wb[j], M, 4, 128, (lambda b, j=j, M=M: tsb[j][:, b * 128: b * 128 + M]),
                                 [('wb', j)], (lambda b, j=j: [('tsb', j)]))
                    dma('sp', qfx[cb * 512:(cb + 1) * 512, tb * 128:tb * 128 + M].rearrange("(g d) t -> d g t", g=4),
                        tsb[j][:, :512].rearrange("d (g t) -> d g t", g=4)[:, :, :M], [('tsb', j)], [('qfx', cb, tb)])
                defer(tail_q)
            flush()
        wj0 = load_w(w_in[l * D:(l + 1) * D, :], D, 512, OFF_CQ)
        wj1 = load_w(w_in[l * D:(l + 1) * D, :], D, 512, OFF_CQ + 512)
        for tb in range(NTB):
            p0, M = proj_tm(tb, hn(tb), hT_lhs, 32, wj0, 512)
            p1, M = proj_tm(tb, hn(tb), hT_lhs, 32, wj1, 512)
            pp = [p0, p1]
            for hf in range(2):
                A('act', lambda E, hf=hf, M=M, pp=pp: E.activation(out=sq[:M, hf * 512:(hf + 1) * 512],
                                                                  in_=psA[pp[hf]][:M, :], func=AF.Square),
                  [('psA', pp[hf])], ['sq'])
            A('dve', lambda E, M=M: E.tensor_reduce(out=ss[:M, 0:1], in_=sq[:M, :1024], axis=AX.X, op=ALU.add), ['sq'], ['ss'])
            A('dve', lambda E, M=M: E.tensor_scalar(out=ss[:M, 0:1], in0=ss[:M, 0:1], scalar1=1.0 / 1024, scalar2=EPS,
                                                    op0=ALU.mult, op1=ALU.add), ['ss'], ['ss'])
            A('act', lambda E, M=M: E.activation(out=ss[:M, 0:1], in_=ss[:M, 0:1], func=AF.Sqrt), ['ss'], ['ss'])
            A('dve', lambda E, M=M: E.reciprocal(out=rs[:M, 0:1], in_=ss[:M, 0:1]), ['ss'], ['rs'])
            for hf in range(2):
                A('dve', lambda E, hf=hf, M=M, pp=pp: E.tensor_scalar(out=wf[hf][:M, :512], in0=psA[pp[hf]][:M, :],
                                                                     scalar1=rs[:M, 0:1], scalar2=None, op0=ALU.mult),
                  [('psA', pp[hf]), 'rs'], [('wf', hf)])
                A('dve', lambda E, hf=hf, M=M: E.tensor_tensor(out=wb[hf][:M, :512], in0=wf[hf][:M, :512],
                                                              in1=gcq_t[:M, hf * 512:(hf + 1) * 512], op=ALU.mult),
                  [('wf', hf), 'gains'], [('wb', hf)])
                transpose_to(wb[hf], M, 4, 128, (lambda b, hf=hf, tb=tb, M=M: cqT[:, hf * 4 + b, tb * 128: tb * 128 + M]),
                             [('wb', hf)], (lambda b, tb=tb: [('cqT', tb)]))
        cq_lhs = lambda kc, tb, M: cqT[:, kc, tb * 128: tb * 128 + M]
        for hp in range(8):
            wj = load_w(w_qb[l * 1024:(l + 1) * 1024, :], 1024, 384, hp * 384)
            for tb in range(NTB):
                pj, M = proj_tm(tb, [('cqT', tb)], cq_lhs, 8, wj, 384)
                j = tb % 2
                ps3 = psA[pj][:M, :384].rearrange("p (g d) -> p g d", g=2)
                dn3 = wf[j][:M, 0:256].rearrange("p (g d) -> p g d", g=2)
                dp3 = wf[j][:M, 256:384].rearrange("p (g d) -> p g d", g=2)
                rms_groups(ps3[:, :, 0:128], M, 2, 128, gqn_t, dn3, [('psA', pj)], [('wf', j)])
                rms_groups(ps3[:, :, 128:192], M, 2, 64, gqp_t, dp3, [('psA', pj)], [('wf', j)])
                rope(dp3, M, 2, tb, rt, rt2, [('wf', j)])
                A('act', lambda E, j=j, M=M: E.copy(out=wb[j][:M, :384], in_=wf[j][:M, :384]), [('wf', j)], [('wb', j)])
                transpose_to(wb[j], M, 2, 128, (lambda b, j=j, M=M: tsb[j][:, b * 128: b * 128 + M]),
                             [('wb', j)], (lambda b, j=j: [('tsb', j)]))
                dma('sp', qnx[hp * 256:(hp + 1) * 256, tb * 128:tb * 128 + M].rearrange("(g d) t -> d g t", g=2),
                    tsb[j][:, :256].rearrange("d (g t) -> d g t", g=2)[:, :, :M], [('tsb', j)], [('qnx', hp, tb)])
                transpose_to(wb[j][:, 256:384], M, 2, 64, (lambda b, j=j, M=M: tsb[j][:64, 256 + b * 128: 256 + b * 128 + M]),
                             [('wb', j)], (lambda b, j=j: [('tsb', j)]))
                dma('sp', qpx[hp * 128:(hp + 1) * 128, tb * 128:tb * 128 + M].rearrange("(g d) t -> d g t", g=2),
                    tsb[j][:64, 256:512].rearrange("d (g t) -> d g t", g=2)[:, :, :M], [('tsb', j)], [('qpx', hp, tb)])
        ttiles = [(0, 512), (512, 512), (1024, TS)]
        for (zoff, zdst, znm) in [(OFF_ZA, zax, 'zax'), (OFF_ZB, zbx, 'zbx')]:
            for cb in range(4):
                wj = load_w(w_in[l * D:(l + 1) * D, :], D, 512, zoff + cb * 512)
                for ct in range(4):
                    for ti, (t0, tn) in enumerate(ttiles):
                        pj = next_psA()
                        tbs = [('actT', tb) for tb in range(NTB) if t0 <= tb * 128 < t0 + tn]
                        for kc in range(32):
                            A('pe', (lambda E, kc=kc, pj=pj, wj=wj, ct=ct, t0=t0, tn=tn: E.matmul(
                                psA[pj][:, :tn], lhsT=wt[wj][:, kc, ct * 128:(ct + 1) * 128],
                                rhs=actT[:, kc, t0:t0 + tn], start=(kc == 0), stop=(kc == 31))),
                              tbs + [('wt', wj, (kc // 8) * 8)], [('psA', pj)], signal=(kc == 31))
                        j = (ct * 3 + ti) % 2
                        A('act', lambda E, pj=pj, tn=tn, j=j: E.activation(out=tsb[j][:, :tn], in_=psA[pj][:, :tn], func=AF.Silu),
                          [('psA', pj)], [('tsb', j)])
                        r0 = (cb * 4 + ct) * 128
                        dma('sp', zdst[r0:r0 + 128, t0:t0 + tn], tsb[j][:, :tn], [('tsb', j)], [(znm, cb * 4 + ct, ti)])
        T_.barrier()
        ph[0].close()
        if stop_after <= 2:
            break

        ph[0] = ExitStack()
        npsA[0] = 2
        gidx = lambda g: (0 if gblock(0, g // 2) == g else 1) * 8 + g // 2
        maskf = sbp(f"maskf{l}", [128, 2048], BF16)
        lf_t = sbp(f"lf_t{l}", [128, 16, 32]); win = sbp(f"win{l}", [128, 16, 32]); tots = sbp(f"tots{l}", [128, 16, 32])
        carry = sbp(f"carry{l}", [128, 17, 32]); ck = sbp(f"ck{l}", [128, 16, 32]); bias = sbp(f"bias{l}", [128, 8, 16, 16])
        crefs = sbp(f"crefs{l}", [128, 32]); ckn = sbp(f"ckn{l}", [32, 16]); biasn = sbp(f"biasn{l}", [32, 32])
        kTh = [sbp(f"kTh{l}_{j}", [128, 2, TP], BF16) for j in range(2)]
        qTh = [sbp(f"qTh{l}_{j}", [128, T], BF16) for j in range(2)]
        zTh = [sbp(f"zTh{l}_{j}", [128, T], BF16) for j in range(2)]
        pT = [sbp(f"pT{l}_{j}", [128, 512], BF16) for j in range(2)]
        rden = sbp(f"rden{l}", [128, 512]); osb = sbp(f"osb{l}", [128, 512])
        qTs = sbp(f"qTs{l}", [128, 16, TS], BF16); zTs = sbp(f"zTs{l}", [128, 16, TS], BF16)
        indw = sbp(f"indw{l}", [32, 256])
        vgrp = wt[0][:, 0:16, :]; kc_t = wt[0][:, 16:32, :]; vc_t = wt[1][:, 0:16, :]
        kTc = wt[1][:, 16:32, :].rearrange("p a b -> p (a b)").rearrange("p (h t) -> p h t", h=4)
        dma('sp', maskf[:, :], c_maskf[:, :], [], ['maskf'])
        dma('sp', indw[:, :], c_indw[:, :], [], ['indw'])
        all_q = lambda nm, n0: [(nm, a, tb) for a in range(n0) for tb in range(NTB)]
        all_z = lambda nm: [(nm, a, ti) for a in range(16) for ti in range(3)]

        def cumsum_tables(ncol, carry_order):
            flat = lf_t[:, :, :ncol]
            A('pe', lambda E: E.matmul(psS[0][:, :16 * ncol].rearrange("p (j c) -> p j c", j=16), lhsT=trif[:, :], rhs=flat,
                                       start=True, stop=True), ['lf_t', 'trif'], [('psS', 0)])
            A('pe', lambda E: E.matmul(psS[1][:, :16 * ncol].rearrange("p (j c) -> p j c", j=16), lhsT=onesf[:, :], rhs=flat,
                                       start=True, stop=True), ['lf_t', 'onesf'], [('psS', 1)])
            A('act', lambda E: E.copy(out=win[:, :, :ncol], in_=psS[0][:, :16 * ncol].rearrange("p (j c) -> p j c", j=16)),
              [('psS', 0)], ['win'])
            A('act', lambda E: E.copy(out=tots[:, :, :ncol], in_=psS[1][:, :16 * ncol].rearrange("p (j c) -> p j c", j=16)),
              [('psS', 1)], ['tots'])
            A('dve', lambda E: E.memset(carry[:, 0, :], 0.0), [], ['carry'])
            for g in range(16):
                A('dve', lambda E, g=g: E.tensor_tensor(out=carry[:, g + 1, :ncol], in0=carry[:, g, :ncol],
                                                        in1=tots[:, carry_order(g), :ncol], op=ALU.add),
                  ['carry', 'tots'], ['carry'])

        dma('sp', lf_t[:, :, :16], lfg[l].rearrange("(j p) h -> p j h", p=128), [('lfg', l, 0)], ['lf_t'])
        cumsum_tables(16, gidx)
        for j in range(16):
            g = gblock(j // 8, j % 8)
            A('dve', lambda E, j=j, g=g: E.tensor_tensor(out=ck[:, j, :16], in0=win[:, j, :16], in1=carry[:, g, :16], op=ALU.add),
              ['win', 'carry'], ['ck'])
        for i in range(8):
            A('dve', lambda E, i=i: E.tensor_tensor(out=bias[:, i, :, :],
                                                    in0=carry[:, 2 * i + 2, :16].unsqueeze(1).to_broadcast([128, 16, 16]),
                                                    in1=ck[:, :, :16], op=ALU.subtract), ['carry', 'ck'], ['bias'])
        for hg in range(4):
            for r in range(2):
                for c in range(4):
                    r0 = c * 512 + r * 256
                    dma('sp', vgrp[:, r * 8 + c * 2: r * 8 + c * 2 + 2, :],
                        vg[l][r0:r0 + 256, hg * 512:(hg + 1) * 512].rearrange("(q p) n -> p q n", p=128),
                        [], [('vgrp', r, c)])
            for hh in range(4):
                h = hg * 4 + hh
                hb_ = h % 2
                dma('sp', kTh[hb_][:, :, :],
                    kTg[l][(h // 4) * 1024:(h // 4 + 1) * 1024, :].rearrange("(r f) t -> f r t", r=2)[(h % 4) * 128:(h % 4 + 1) * 128, :, :],
                    [], [('kTh', hb_)])
                dma('sp', qTh[hb_][:, :], qfx[h * 128:(h + 1) * 128, :], all_q('qfx', 4), [('qTh', hb_)])
                dma('sp', zTh[hb_][:, :], zax[h * 128:(h + 1) * 128, :], all_z('zax'), [('zTh', hb_)])
                for t in range(2):
                    blocks = [(ip, r) for ip in range(4 * t + 4) for r in range(2)]
                    for bi, (ip, r) in enumerate(blocks):
                        j = r * 8 + ip
                        c0 = max(0, ip - 4 * t) * 128
                        sj = bi % 2
                        A('pe', (lambda E, sj=sj, c0=c0, r=r, ip=ip, t=t, hb_=hb_: E.matmul(
                            psS[sj][:, c0:512], lhsT=kTh[hb_][:, r, ip * 128:(ip + 1) * 128],
                            rhs=qTh[hb_][:, t * 512 + c0:(t + 1) * 512], start=True, stop=True)),
                          [('kTh', hb_), ('qTh', hb_)], [('psS', sj)])
                        for isub in range(c0 // 128, 4):
                            i = 4 * t + isub
                            A('act', (lambda E, sj=sj, isub=isub, i=i, j=j, h=h: E.activation(
                                out=pT[sj][:, isub * 128:(isub + 1) * 128], in_=psS[sj][:, isub * 128:(isub + 1) * 128],
                                func=AF.Exp, bias=bias[:, i, j, h:h + 1], scale=float(FOX_SCALE))),
                              [('psS', sj), 'bias'], [('pT', sj, isub)])
                            if ip == i:
                                A('dve', (lambda E, sj=sj, isub=isub, i=i, r=r: E.tensor_tensor(
                                    out=pT[sj][:, isub * 128:(isub + 1) * 128], in0=pT[sj][:, isub * 128:(isub + 1) * 128],
                                    in1=maskf[:, (i * 2 + r) * 128:(i * 2 + r + 1) * 128], op=ALU.mult)),
                                  [('pT', sj, isub), 'maskf'], [('pT', sj, isub)])
                        first = (bi == 0)
                        last = (bi == len(blocks) - 1)
                        pnames = [('pT', sj, isub) for isub in range(c0 // 128, 4)]
                        A('pe', (lambda E, sj=sj, c0=c0, j=j, hh=hh, first=first, last=last: E.matmul(
                            psO[:, c0:512], lhsT=vgrp[:, j, hh * 128:(hh + 1) * 128], rhs=pT[sj][:, c0:512],
                            start=first, stop=last)), pnames + [('vgrp', r, ip // 2)], ['psO'], signal=False)
                        A('pe', (lambda E, sj=sj, c0=c0, first=first, last=last: E.matmul(
                            psD[:, c0:512], lhsT=onesb[:, :], rhs=pT[sj][:, c0:512], start=first, stop=last)),
                          pnames + ['onesb'], ['psD'])
                    A('dve', lambda E: E.reciprocal(out=rden[:, :], in_=psD[:, :]), ['psD'], ['rden'])
                    A('dve', lambda E: E.tensor_tensor(out=osb[:, :], in0=psO[:, :], in1=rden[:, :], op=ALU.mult),
                      ['psO', 'rden'], ['osb'])
                    A('dve', (lambda E, h=h, t=t, hb_=hb_: E.tensor_tensor(
                        out=actT[:, h, t * 512:(t + 1) * 512], in0=osb[:, :], in1=zTh[hb_][:, t * 512:(t + 1) * 512], op=ALU.mult)),
                      ['osb', ('zTh', hb_)], [('actT', tb) for tb in range(4 * t, 4 * t + 4)])
        T_.barrier()
        for b in range(2):
            dma('sp', lf_t[:, :, b * 16:(b + 1) * 16],
                cfl[(l * 2 + b) * PAST:(l * 2 + b + 1) * PAST, :].rearrange("(j p) h -> p j h", p=128), [], [('lf_tb', b)])
        T_.barrier()
        cumsum_tables(32, (lambda g: g))
        A('dve', lambda E: E.tensor_tensor(out=ck[:, :, :], in0=win[:, :, :], in1=carry[:, 0:16, :], op=ALU.add),
          ['win', 'carry'], ['ck'])
        A('pe', lambda E: E.matmul(psS[0][:32, :16], lhsT=tris[:32, :32], rhs=lfn[:32, :16], start=True, stop=True),
          ['tris', 'lfn'], [('psS', 0)])
        for b in range(2):
            A('pe', lambda E, b=b: E.matmul(psS[1][:, b * 16:(b + 1) * 16], lhsT=indw[:32, b * 128:(b + 1) * 128],
                                            rhs=lfn[:32, :16], start=True, stop=True), ['indw', 'lfn'], [('psS', 1)])
        A('act', lambda E: E.copy(out=ckn[:, :], in_=psS[0][:32, :16]), [('psS', 0)], ['ckn'])
        for b in range(2):
            A('dve', lambda E, b=b: E.scalar_tensor_tensor(out=ckn[:, :], in0=carry[:32, 16, b * 16:(b + 1) * 16],
                                                          scalar=inds[:32, b:b + 1], in1=ckn[:, :], op0=ALU.mult, op1=ALU.add),
              ['carry', 'ckn', 'inds'], ['ckn'])
        A('dve', lambda E: E.tensor_tensor(out=crefs[:, :], in0=carry[:, 16, :], in1=psS[1][:, :32], op=ALU.add),
          ['carry', ('psS', 1)], ['crefs'])
        biasc = win
        A('dve', lambda E: E.tensor_tensor(out=biasc[:, :, :], in0=crefs[:, :].unsqueeze(1).to_broadcast([128, 16, 32]),
                                           in1=ck[:, :, :], op=ALU.subtract), ['crefs', 'ck', 'win'], ['biasc'])
        for b in range(2):
            A('dve', lambda E, b=b: E.tensor_tensor(out=biasn[:, b * 16:(b + 1) * 16], in0=crefs[:32, b * 16:(b + 1) * 16],
                                                    in1=ckn[:, :], op=ALU.subtract), ['crefs', 'ckn'], ['biasn'])
        dma('sp', qTs[:, :, :], qfx[:, TP:T].rearrange("(h d) t -> d h t", d=128), all_q('qfx', 4), ['qTs'])
        dma('sp', zTs[:, :, :], zax[:, TP:T].rearrange("(h d) t -> d h t", d=128), all_z('zax'), ['zTs'])

        def sample_attn(b, kT_fn, v_fn, kTnew_list, vnew, bias_fn, biasn_ap, msk, scale, zrow, out_chunk):
            for j in range(16):
                parts = kT_fn(j)
                for pi, (ka, qa, kn) in enumerate(parts):
                    A('pe', (lambda E, j=j, ka=ka, qa=qa, pi=pi, n=len(parts): E.matmul(
                        psS[0][:, j * 16:(j + 1) * 16], lhsT=ka, rhs=qa, start=(pi == 0), stop=(pi == n - 1))),
                      kn, [('psS', 0)], signal=(pi == len(parts) - 1))
            for pi, (ka, qa, kn) in enumerate(kTnew_list):
                A('pe', (lambda E, ka=ka, qa=qa, pi=pi, n=len(kTnew_list): E.matmul(
                    psS[0][:32, 256:272], lhsT=ka, rhs=qa, start=(pi == 0), stop=(pi == n - 1))),
                  kn, [('psS', 0)], signal=(pi == len(kTnew_list) - 1))
            if bias_fn is None:
                A('act', (lambda E: E.activation(out=pT[0][:, 0:256], in_=psS[0][:, 0:256], func=AF.Exp, scale=float(scale))),
                  [('psS', 0)], [('pTs', j) for j in range(16)])
            for j in range(16):
                if bias_fn is None:
                    pass
                else:
                    bj_ = bias_fn(j)
                    A('act', (lambda E, j=j, bj_=bj_: E.activation(out=pT[0][:, j * 16:(j + 1) * 16], in_=psS[0][:, j * 16:(j + 1) * 16],
                                                          func=AF.Exp, bias=bj_, scale=float(scale))),
                      [('psS', 0), 'biasc'], [('pTs', j)])
            if biasn_ap is None:
                A('act', (lambda E: E.activation(out=pT[0][:32, 256:272], in_=psS[0][:32, 256:272], func=AF.Exp,
                                                 scale=float(scale))), [('psS', 0)], [('pTs', 16)])
            else:
                A('act', (lambda E: E.activation(out=pT[0][:32, 256:272], in_=psS[0][:32, 256:272], func=AF.Exp,
                                                 bias=biasn_ap, scale=float(scale))), [('psS', 0), 'biasn'], [('pTs', 16)])
            A('dve', (lambda E: E.tensor_tensor(out=pT[0][:32, 256:272], in0=pT[0][:32, 256:272], in1=msk, op=ALU.mult)),
              [('pTs', 16), 'masks'], [('pTs', 16)])
            for j in range(17):
                if j < 16:
                    va, vnm = v_fn(j)
                    pa = pT[0][:, j * 16:(j + 1) * 16]
                    oa = onesb[:, :]
                else:
                    va, vnm = vnew
                    pa = pT[0][:32, 256:272]
                    oa = onesb[:32, :]
                A('pe', (lambda E, va=va, pa=pa, j=j: E.matmul(psO[:, :16], lhsT=va, rhs=pa, start=(j == 0), stop=(j == 16))),
                  [('pTs', j)] + vnm, ['psO'], signal=False)
                A('pe', (lambda E, oa=oa, pa=pa, j=j: E.matmul(psD[:, :16], lhsT=oa, rhs=pa, start=(j == 0), stop=(j == 16))),
                  [('pTs', j), 'onesb'], ['psD'])
            A('dve', lambda E: E.reciprocal(out=rden[:, :16], in_=psD[:, :16]), ['psD'], ['rden'])
            A('dve', lambda E: E.tensor_tensor(out=osb[:, :16], in0=psO[:, :16], in1=rden[:, :16], op=ALU.mult),
              ['psO', 'rden'], ['osb'])
            A('dve', (lambda E: E.tensor_tensor(out=actT[:, out_chunk, TP + 16 * b:TP + 16 * b + 16], in0=osb[:, :16],
                                                in1=zrow, op=ALU.mult)), ['osb', 'zTs'], [('actT', 8)])

        for b in range(2):
            for hg in range(4):
                rows = slice((l * 2 + b) * PAST, (l * 2 + b + 1) * PAST)
                A('pool', (lambda E, rows=rows, hg=hg: E.dma_start(
                    out=kc_t, in_=cfk[rows, hg * 512:(hg + 1) * 512].rearrange("(j p) c -> p j c", p=128))),
                  [], ['kc_t'], kind='dma')
                A('pool', (lambda E, rows=rows, hg=hg: E.dma_start(
                    out=vc_t, in_=cfv[rows, hg * 512:(hg + 1) * 512].rearrange("(j p) c -> p j c", p=128))),
                  [], ['vc_t'], kind='dma')
                for hh in range(4):
                    for half in range(2):
                        pj = pt_i[0] % 2
                        pt_i[0] += 1
                        for bb in range(8):
                            A('pe', (lambda E, pj=pj, bb=bb, half=half, hh=hh: E.transpose(
                                psT[pj][:, bb * 128:(bb + 1) * 128], kc_t[:, half * 8 + bb, hh * 128:(hh + 1) * 128], identb[:, :])),
                              ['kc_t', 'identb'], [('psT', pj)], signal=(bb == 7))
                        A('act', (lambda E, pj=pj, half=half, hh=hh: E.copy(
                            out=kTc[:, hh, half * 1024:(half + 1) * 1024], in_=psT[pj][:, :])), [('psT', pj)], [('kTc', hh)])
                for hh in range(4):
                    h = hg * 4 + hh
                    qa = qTs[:, h, 16 * b:16 * b + 16]
                    sample_attn(
                        b,
                        (lambda j, hh=hh, qa=qa: [(kTc[:, hh, j * 128:(j + 1) * 128], qa, [('kTc', hh), 'qTs'])]),
                        (lambda j, hh=hh: (vc_t[:, j, hh * 128:(hh + 1) * 128], ['vc_t'])),
                        [(kTn[:, h, :], qa, ['kTn', 'qTs'])],
                        (vn[:32, h * 128:(h + 1) * 128], ['vn']),
                        (lambda j, b=b, h=h: biasc[:, j, b * 16 + h:b * 16 + h + 1]),
                        biasn[:32, b * 16 + h:b * 16 + h + 1], masks[:32, b * 16:(b + 1) * 16], FOX_SCALE,
                        zTs[:, h, 16 * b:16 * b + 16], h)
        T_.barrier()
        ph[0].close()
        if stop_after <= 3:
            break

        ph[0] = ExitStack()
        maskm = sbp(f"maskm{l}", [128, 2048], BF16); masks2 = sbp(f"masks2{l}", [32, 32], BF16)
        kpTg = sbp(f"kpTg{l}", [64, 2, TP], BF16)
        knT = sbp(f"knT{l}", [128, 2, 2048], BF16); vb = sbp(f"vb{l}", [128, 16, 256], BF16)
        vbn = sbp(f"vbn{l}", [32, 256], BF16); knTn = sbp(f"knTn{l}", [128, 2, 32], BF16)
        qnh = sbp(f"qnh{l}", [128, T], BF16); qph = sbp(f"qph{l}", [64, T], BF16); zTh2 = sbp(f"zTh2{l}", [128, T], BF16)
        pT = [sbp(f"pTm{l}_{j}", [128, 512], BF16) for j in range(2)]
        rden = sbp(f"rdenm{l}", [128, 512]); osb = sbp(f"osbm{l}", [128, 512])
        sq = sbp(f"sqm{l}", [128, 256]); wf2 = sbp(f"wf2{l}", [128, 256]); wb2 = sbp(f"wb2{l}", [128, 256], BF16)
        qnTs = sbp(f"qnTs{l}", [128, 16, TS], BF16); qpTs = sbp(f"qpTs{l}", [64, 16, TS], BF16)
        zTs = sbp(f"zTs2{l}", [128, 16, TS], BF16)
        wkv = wt[0][:, :, :].rearrange("p a b -> p (a b)").rearrange("p (kc n) -> p kc n", kc=4)
        ckTg = wt[1][:, 0:16, :].rearrange("p a b -> p (a b)").rearrange("p (kc r t) -> p kc r t", kc=4, r=2)
        ckc = wt[1][:, 16:32, :]
        dma('sp', maskm[:, :], c_maskm[:, :], [], ['maskm'])
        dma('sp', masks2[:, :], c_masks2[:, :], [], ['masks'])
        for kc in range(4):
            A('pool', (lambda E, kc=kc: E.dma_start(out=wkv[:, kc, :], in_=w_kvb[l * 512 + kc * 128: l * 512 + (kc + 1) * 128, :])),
              [], [('wkv', kc)], kind='dma')
            for r in range(2):
                dma('sp', ckTg[:, kc, r, :], ckg[l][r * 512 + kc * 128: r * 512 + (kc + 1) * 128, :], [('ckg', l, 0)], [('ckTg', kc, r)])
        dma('sp', kpTg[:, :, :], kpg[l].rearrange("(r d) t -> d r t", r=2), [('kpg', l, 0)], ['kpTg'])
        T_.barrier()

        def upproj(lhs_fn, lnames, M, hp, knT_dst_fn, v_dst, vnames, knames):
            pj = next_psA()
            for kc in range(4):
                lh = lhs_fn(kc)
                A('pe', (lambda E, kc=kc, pj=pj, lh=lh: E.matmul(psA[pj][:M, :512], lhsT=lh, rhs=wkv[:, kc, hp * 512:(hp + 1) * 512],
                                                         start=(kc == 0), stop=(kc == 3))),
                  list(lnames) + [('wkv', kc)], [('psA', pj)], signal=(kc == 3))
            ps3 = psA[pj][:M, :].rearrange("p (g c) -> p g c", g=2)
            dst3 = wf2[:M, :256].rearrange("p (g d) -> p g d", g=2)
            rms_groups(ps3[:, :, 0:128], M, 2, 128, gkn_t, dst3, [('psA', pj)], ['wf2'])
            A('act', lambda E: E.copy(out=wb2[:M, :256], in_=wf2[:M, :256]), ['wf2'], ['wb2'])
            A('dve', lambda E: E.tensor_copy(out=v_dst, in_=ps3[:, :, 128:256]), [('psA', pj)], vnames)
            transpose_to(wb2, M, 2, 128, knT_dst_fn, ['wb2'], (lambda b: knames))

        def mla_head_prompt(h, hh):
            dma('sp', qnh[:, :], qnx[h * 128:(h + 1) * 128, :], [('qnx', h // 2, tb) for tb in range(NTB)], ['qnh'])
            dma('sp', qph[:, :], qpx[h * 64:(h + 1) * 64, :], [('qpx', h // 2, tb) for tb in range(NTB)], ['qph'])
            dma('sp', zTh2[:, :], zbx[h * 128:(h + 1) * 128, :], [('zbx', h, ti) for ti in range(3)], ['zTh2'])
            for t in range(2):
                blocks = [(ip, r) for ip in range(4 * t + 4) for r in range(2)]
                for bi, (ip, r) in enumerate(blocks):
                    j = r * 8 + ip
                    c0 = max(0, ip - 4 * t) * 128
                    sj = bi % 2
                    A('pe', (lambda E, sj=sj, c0=c0, j=j, t=t: E.matmul(
                        psS[sj][:, c0:512], lhsT=knT[:, hh, j * 128:(j + 1) * 128],
                        rhs=qnh[:, t * 512 + c0:(t + 1) * 512], start=True, stop=False)),
                      [('knT', j), 'qnh'], [('psS', sj)], signal=False)
                    A('pe', (lambda E, sj=sj, c0=c0, r=r, ip=ip, t=t: E.matmul(
                        psS[sj][:, c0:512], lhsT=kpTg[:, r, ip * 128:(ip + 1) * 128],
                        rhs=qph[:, t * 512 + c0:(t + 1) * 512], start=False, stop=True)),
                      ['kpTg', 'qph'], [('psS', sj)])
                    A('act', (lambda E, sj=sj, c0=c0: E.activation(
                        out=pT[sj][:, c0:512], in_=psS[sj][:, c0:512], func=AF.Exp, scale=float(MLA_SCALE))),
                      [('psS', sj)], [('pT', sj, isub) for isub in range(c0 // 128, 4)])
                    for isub in range(c0 // 128, 4):
                        i = 4 * t + isub
                        if ip == i:
                            A('dve', (lambda E, sj=sj, isub=isub, i=i, r=r: E.tensor_tensor(
                                out=pT[sj][:, isub * 128:(isub + 1) * 128], in0=pT[sj][:, isub * 128:(isub + 1) * 128],
                                in1=maskm[:, (i * 2 + r) * 128:(i * 2 + r + 1) * 128], op=ALU.mult)),
                              [('pT', sj, isub), 'maskm'], [('pT', sj, isub)])
                    first = (bi == 0)
                    last = (bi == len(blocks) - 1)
                    pnames = [('pT', sj, isub) for isub in range(c0 // 128, 4)]
                    A('pe', (lambda E, sj=sj, c0=c0, j=j, first=first, last=last: E.matmul(
                        psO[:, c0:512], lhsT=vb[:, j, hh * 128:(hh + 1) * 128], rhs=pT[sj][:, c0:512],
                        start=first, stop=last)), pnames + [('vb', j)], ['psO'], signal=False)
                    A('pe', (lambda E, sj=sj, c0=c0, first=first, last=last: E.matmul(
                        psD[:, c0:512], lhsT=onesb[:, :], rhs=pT[sj][:, c0:512], start=first, stop=last)),
                      pnames + ['onesb'], ['psD'])
                A('dve', lambda E: E.reciprocal(out=rden[:, :], in_=psD[:, :]), ['psD'], ['rden'])
                A('dve', lambda E: E.tensor_tensor(out=osb[:, :], in0=psO[:, :], in1=rden[:, :], op=ALU.mult),
                  ['psO', 'rden'], ['osb'])
                A('dve', (lambda E, t=t: E.tensor_tensor(
                    out=actT[:, 16 + h, t * 512:(t + 1) * 512], in0=osb[:, :], in1=zTh2[:, t * 512:(t + 1) * 512], op=ALU.mult)),
                  ['osb', 'zTh2'], [('actT', tb) for tb in range(4 * t, 4 * t + 4)])

        for hp in range(8):
            for j in range(16):
                r, ip = divmod(j, 8)
                upproj((lambda kc, r=r, ip=ip: ckTg[:, kc, r, ip * 128:(ip + 1) * 128]),
                       [('ckTg', kc, r) for kc in range(4)], 128, hp,
                       (lambda b, j=j: knT[:, b, j * 128:(j + 1) * 128]),
                       vb[:, j, :].rearrange("p (g d) -> p g d", g=2), [('vb', j)], [('knT', j)])
            for hh in range(2):
                mla_head_prompt(hp * 2 + hh, hh)
        T_.barrier()
        ckTc = wt[1][:, 0:16, :].rearrange("p a b -> p (a b)").rearrange("p (kc t) -> p kc t", kc=4)
        kpTc = kpTg[:, :, :].rearrange("d r t -> d (r t)")
        kpc = wt[0][:, 0:2, :].rearrange("p a b -> p (a b)").rearrange("p (j c) -> p j c", j=16)
        dma('sp', qnTs[:, :, :], qnx[:, TP:T].rearrange("(h d) t -> d h t", d=128), [('qnx', a, 8) for a in range(8)], ['qTs'])
        dma('sp', qpTs[:, :, :], qpx[:, TP:T].rearrange("(h d) t -> d h t", d=64), [('qpx', a, 8) for a in range(8)], ['qTs2'])
        dma('sp', zTs[:, :, :], zbx[:, TP:T].rearrange("(h d) t -> d h t", d=128), all_z('zbx'), ['zTs'])
        kpc = sq
        kpcb = sbp(f"kpcb{l}", [128, 16, 64], BF16)
        for b in range(2):
            rows = slice((l * 2 + b) * PAST, (l * 2 + b + 1) * PAST)
            A('pool', (lambda E, rows=rows: E.dma_start(out=ckc, in_=cck[rows, :].rearrange("(j p) c -> p j c", p=128))),
              [], ['ckc'], kind='dma')
            A('pool', (lambda E, rows=rows: E.dma_start(out=kpcb[:, :, :], in_=ckp[rows, :].rearrange("(j p) c -> p j c", p=128))),
              [], ['kpcb'], kind='dma')
            for j in range(16):
                pj = pt_i[0] % 2
                pt_i[0] += 1
                for kc in range(4):
                    A('pe', (lambda E, pj=pj, j=j, kc=kc: E.transpose(psT[pj][:, kc * 128:(kc + 1) * 128],
                                                                     ckc[:, j, kc * 128:(kc + 1) * 128], identb[:, :])),
                      ['ckc', 'identb'], [('psT', pj)], signal=(kc == 3))
                A('act', (lambda E, pj=pj, j=j: E.copy(out=ckTc[:, :, j * 128:(j + 1) * 128],
                                                       in_=psT[pj][:, :512].rearrange("p (kc t) -> p kc t", kc=4))),
                  [('psT', pj)], ['ckTc'])
            for half in range(2):
                pj = pt_i[0] % 2
                pt_i[0] += 1
                for bb in range(8):
                    A('pe', (lambda E, pj=pj, bb=bb, half=half: E.transpose(psT[pj][:64, bb * 128:(bb + 1) * 128],
                                                                           kpcb[:, half * 8 + bb, :], identb[:, :])),
                      ['kpcb', 'identb'], [('psT', pj)], signal=(bb == 7))
                A('act', (lambda E, pj=pj, half=half: E.copy(out=kpTc[:, half * 1024:(half + 1) * 1024], in_=psT[pj][:64, :])),
                  [('psT', pj)], ['kpTc'])
            for hp in range(8):
                for j in range(16):
                    upproj((lambda kc, j=j: ckTc[:, kc, j * 128:(j + 1) * 128]), ['ckTc'], 128, hp,
                           (lambda bq, j=j: knT[:, bq, j * 128:(j + 1) * 128]),
                           vb[:, j, :].rearrange("p (g d) -> p g d", g=2), [('vb', j)], [('knT', j)])
                upproj((lambda kc: ckTn[:, kc, :]), ['ckTn'], 32, hp, (lambda bq: knTn[:, bq, :]),
                       vbn[:32, :].rearrange("p (g d) -> p g d", g=2), ['vbn'], ['knTn'])
                for hh in range(2):
                    h = hp * 2 + hh
                    qn_ = qnTs[:, h, 16 * b:16 * b + 16]
                    qp_ = qpTs[:, h, 16 * b:16 * b + 16]
                    sample_attn(
                        b,
                        (lambda j, hh=hh, qn_=qn_, qp_=qp_: [(knT[:, hh, j * 128:(j + 1) * 128], qn_, [('knT', j), 'qTs']),
                                                            (kpTc[:, j * 128:(j + 1) * 128], qp_, ['kpTc', 'qTs2'])]),
                        (lambda j, hh=hh: (vb[:, j, hh * 128:(hh + 1) * 128], [('vb', j)])),
                        [(knTn[:, hh, :], qn_, ['knTn', 'qTs']), (kpTn[:, :], qp_, ['kpTn', 'qTs2'])],
                        (vbn[:32, hh * 128:(hh + 1) * 128], ['vbn']),
                        None, None, masks2[:32, b * 16:(b + 1) * 16], MLA_SCALE,
                        zTs[:, h, 16 * b:16 * b + 16], 16 + h)
        T_.barrier()
        ph[0].close()
        if stop_after <= 4:
            break

        ph[0] = ExitStack()
        npsA[0] = 6
        xr = [sbp(f"xr{l}_{j}", [128, 512]) for j in range(2)]
        yo = [sbp(f"yo{l}_{j}", [128, 512]) for j in range(2)]
        for cb in range(8):
            wj = load_w(w_out[l * D:(l + 1) * D, :], D, 512, cb * 512)
            for tb in range(NTB):
                pj, M = proj_tm(tb, [('actT', tb)], hT_lhs, 32, wj, 512)
                j = tb % 2
                cs = slice(cb * 512, (cb + 1) * 512)
                if l == 0:
                    xsrc = xp[tb * 128: tb * 128 + M, cs] if tb < 8 else xs[:, cs]
                else:
                    xsrc = ymid[tb * 128: tb * 128 + M, cs]
                dma('sp', xr[j][:M, :], xsrc, [], [('xr', j)])
                A('dve', lambda E, j=j, M=M, pj=pj: E.tensor_tensor(out=yo[j][:M, :], in0=psA[pj][:M, :], in1=xr[j][:M, :], op=ALU.add),
                  [('psA', pj), ('xr', j)], [('yo', j)])
                if l == 0:
                    dst = ymid[tb * 128: tb * 128 + M, cs]
                else:
                    dst = yp[tb * 128: tb * 128 + M, cs] if tb < 8 else ys[:, cs]
                dma('sp', dst, yo[j][:M, :], [('yo', j)], [])
        T_.barrier()
        ph[0].close()

    T_.final_wait('sp')
    sem_keys = T_.sem_keys()
    sems = {}
    for k in sem_keys:
        sems[k] = es.enter_context(nc.semaphore("s_" + "_".join(str(x) for x in k)))
    build_program.sem_map = {k: repr(v) for k, v in sems.items()}
    with nc.Block() as block:
        T_.emit(block, sems)
    es.close()
    return nc


def host_constants(r):
    bf = ml_dtypes.bfloat16
    c = {}
    c["c_identb"] = np.eye(128, dtype=np.float32).astype(bf)
    c["c_identf"] = np.eye(128, dtype=np.float32)
    c["c_onesb"] = np.ones((128, 128), np.float32).astype(bf)
    c["c_onesf"] = np.ones((128, 128), np.float32)
    c["c_trif"] = np.triu(np.ones((128, 128), np.float32))
    pos = np.concatenate([np.concatenate([gblock(r, i) * 128 + np.arange(128) for i in range(8)]),
                          PAST + np.arange(16), PAST + np.arange(16)]).astype(np.float32)
    half = 32
    inv_freq = (1.0 / (np.float32(10000.0) ** (np.arange(half, dtype=np.float32) / np.float32(half)))).astype(np.float32)
    ang = (pos[:, None] * inv_freq[None, :]).astype(np.float32)
    c["c_cos"] = np.cos(ang).astype(np.float32)
    c["c_sin"] = np.sin(ang).astype(np.float32)
    mf = np.zeros((128, 16, 128), np.float32)
    mm = np.zeros((128, 16, 128), np.float32)
    for i in range(8):
        qpos = gblock(r, i) * 128 + np.arange(128)
        for rk in range(2):
            kpos = gblock(rk, i) * 128 + np.arange(128)
            mf[:, i * 2 + rk, :] = (kpos[:, None] <= qpos[None, :])
            mm[:, i * 2 + rk, :] = ((kpos[:, None] // 64) <= (qpos[None, :] // 64))
    c["c_maskf"] = mf.reshape(128, 2048).astype(bf)
    c["c_maskm"] = mm.reshape(128, 2048).astype(bf)
    ms = np.zeros((32, 2, 16), np.float32)
    for b in range(2):
        for k in range(16):
            ms[b * 16 + k, b, :] = (k <= np.arange(16))
    c["c_masks"] = ms.reshape(32, 32).astype(bf)
    ts_ = np.zeros((32, 32), np.float32)
    for b in range(2):
        ts_[b * 16:(b + 1) * 16, b * 16:(b + 1) * 16] = np.triu(np.ones((16, 16), np.float32))
    c["c_tris"] = ts_
    ind = np.zeros((32, 2), np.float32)
    ind[:16, 0] = 1
    ind[16:, 1] = 1
    c["c_inds"] = ind
    c["c_indw"] = np.repeat(ind[:, :, None], 128, axis=2).reshape(32, 256).astype(np.float32)
    m2 = np.zeros((32, 2, 16), np.float32)
    m2[:16, 0, :] = 1
    m2[16:, 1, :] = 1
    c["c_masks2"] = m2.reshape(32, 32).astype(bf)
    return c


_NC_CACHE = {}


def make_in_maps(inp):
    f = lambda a: np.ascontiguousarray(np.asarray(a, dtype=np.float32))
    xpr = f(inp["x_prompt"]); xsm = f(inp["x_sample"])
    maps = []
    wshared = {
        "g_norm": f(inp["g_norm"]), "w_in": f(inp["w_in"]).reshape(NL * D, N_IN), "b_f": f(inp["b_f"]),
        "g_q_fox": f(inp["g_q_fox"]), "g_k_fox": f(inp["g_k_fox"]), "g_cq": f(inp["g_cq"]),
        "w_qb": f(inp["w_qb"]).reshape(NL * 1024, 3072), "g_qn": f(inp["g_qn"]), "g_qp": f(inp["g_qp"]),
        "g_ckv": f(inp["g_ckv"]), "g_kp": f(inp["g_kp"]), "w_kvb": f(inp["w_kvb"]).reshape(NL * 512, 4096),
        "g_kn": f(inp["g_kn"]), "w_out": f(inp["w_out"]).reshape(NL * D, D),
    }
    cfk = f(inp["cache_fox_k"]); cfv = f(inp["cache_fox_v"]); cfl = f(inp["cache_fox_logf"])
    cck = f(inp["cache_mla_ckv"]); ckp = f(inp["cache_mla_kpe"])
    for c in range(8):
        p, r = c // 2, c % 2
        m = dict(wshared)
        m["xp"] = np.concatenate([xpr[p, gblock(r, i) * 128:(gblock(r, i) + 1) * 128] for i in range(8)], axis=0)
        m["xs"] = xsm[2 * c:2 * c + 2].reshape(TS, D)
        m["cfk"] = cfk[:, 2 * c:2 * c + 2].reshape(NL * 2 * PAST, 2048)
        m["cfv"] = cfv[:, 2 * c:2 * c + 2].reshape(NL * 2 * PAST, 2048)
        m["cfl"] = cfl[:, 2 * c:2 * c + 2].reshape(NL * 2 * PAST, 16)
        m["cck"] = cck[:, 2 * c:2 * c + 2].reshape(NL * 2 * PAST, 512)
        m["ckp"] = ckp[:, 2 * c:2 * c + 2].reshape(NL * 2 * PAST, 64)
        m.update(host_constants(r))
        maps.append(m)
    return maps


def assemble(results):
    y_p = np.zeros((4, 2048, D), np.float32); y_s = np.zeros((16, 16, D), np.float32)
    fk_p = np.zeros((NL, 4, 2048, H, 128), np.float32); fv_p = np.zeros_like(fk_p)
    fl_p = np.zeros((NL, 4, 2048, H), np.float32); ck_p = np.zeros((NL, 4, 2048, 512), np.float32)
    kp_p = np.zeros((NL, 4, 2048, 64), np.float32)
    fk_s = np.zeros((NL, 16, 16, H, 128), np.float32); fv_s = np.zeros_like(fk_s)
    fl_s = np.zeros((NL, 16, 16, H), np.float32); ck_s = np.zeros((NL, 16, 16, 512), np.float32)
    kp_s = np.zeros((NL, 16, 16, 64), np.float32)
    for c in range(8):
        p, r = c // 2, c % 2
        R = results[c]
        for i in range(8):
            g = gblock(r, i)
            sl = slice(g * 128, (g + 1) * 128)
            y_p[p, sl] = R["yp"][i * 128:(i + 1) * 128]
            for l in range(NL):
                rows = slice(l * TP + i * 128, l * TP + (i + 1) * 128)
                fk_p[l, p, sl] = R["nfk_p"][rows].reshape(128, H, 128)
                fv_p[l, p, sl] = R["nfv_p"][rows].reshape(128, H, 128)
                fl_p[l, p, sl] = R["nfl_p"][rows]
                ck_p[l, p, sl] = R["nck_p"][rows]
                kp_p[l, p, sl] = R["nkp_p"][rows]
        y_s[2 * c:2 * c + 2] = R["ys"].reshape(2, 16, D)
        for l in range(NL):
            rows = slice(l * TS, (l + 1) * TS)
            fk_s[l, 2 * c:2 * c + 2] = R["nfk_s"][rows].reshape(2, 16, H, 128)
            fv_s[l, 2 * c:2 * c + 2] = R["nfv_s"][rows].reshape(2, 16, H, 128)
            fl_s[l, 2 * c:2 * c + 2] = R["nfl_s"][rows].reshape(2, 16, H)
            ck_s[l, 2 * c:2 * c + 2] = R["nck_s"][rows].reshape(2, 16, 512)
            kp_s[l, 2 * c:2 * c + 2] = R["nkp_s"][rows].reshape(2, 16, 64)
    return (y_p, y_s, fk_p, fv_p, fl_p, ck_p, kp_p, fk_s, fv_s, fl_s, ck_s, kp_s)


def kernel(**inputs):
    nc = build_program()
    maps = make_in_maps(inputs)
    res = run_bass_kernel_spmd(nc, maps, core_ids=list(range(8)))
    return assemble(res.results)
```

```python
import numpy as np
import ml_dtypes
from contextlib import ExitStack
import concourse.bass as bass
import concourse.mybir as mybir
from concourse.bass_utils import run_bass_kernel_spmd

F32 = mybir.dt.float32
BF16 = mybir.dt.bfloat16
ALU = mybir.AluOpType
AF = mybir.ActivationFunctionType
AX = mybir.AxisListType

D = 4096
NL = 2
H = 16
TP = 1024
TS = 32
T = TP + TS
NTB = 9
PAST = 2048
EPS = 1e-6
N_IN = 11856
OFF_Q, OFF_K, OFF_V, OFF_F, OFF_ZA, OFF_CQ, OFF_CKV, OFF_KPE, OFF_ZB = (
    0, 2048, 4096, 6144, 6160, 8208, 9232, 9744, 9808)
FOX_SCALE = 1.0 / np.sqrt(128.0)
MLA_SCALE = 1.0 / np.sqrt(192.0)


def gblock(r, i):
    return 2 * i + (i % 2) if r == 0 else 2 * i + 1 - (i % 2)


def tb_rows(tb):
    return 128 if tb < 8 else TS


def _freeze(fn):
    import types
    if fn is None or fn.__closure__ is None:
        return fn
    cells = []
    for c in fn.__closure__:
        try:
            cells.append(types.CellType(c.cell_contents))
        except ValueError:
            cells.append(c)
    return types.FunctionType(fn.__code__, fn.__globals__, fn.__name__, fn.__defaults__, tuple(cells))


class Tr:
    ENG = ('pe', 'act', 'dve', 'pool', 'sp')
    NDS = 20

    def __init__(s):
        s.ops = {e: [] for e in s.ENG}
        s.cnt = {e: 0 for e in s.ENG}
        s.dcnt = {e: 0 for e in s.ENG}
        s.lw = {}
        s.rd = {}
        s.waited = {e: {} for e in s.ENG}
        s.semval = {}
        s.ccn = 0
        s.maxops = None

    def _need(s, eng, deps):
        w = []
        for (k, v) in deps:
            if k == ('c', 'pe') and eng == 'pe':
                continue
            if s.waited[eng].get(k, 0) < v:
                s.waited[eng][k] = v
                w.append((k, v))
        return w

    def add(s, eng, fn, reads=(), writes=(), kind='c', signal=True):
        s.nadd = getattr(s, 'nadd', 0) + 1
        if s.maxops is not None and s.nadd > s.maxops:
            return None
        deps = set()
        for b in reads:
            if b in s.lw:
                deps.add(s.lw[b])
            if (isinstance(b, tuple) and b[0] in ('psA', 'psT', 'psS')) or b in ('psO', 'psD'):
                for t in s.rd.get(b, ()):
                    if t[0] != ('c', eng):
                        deps.add(t)
        for b in writes:
            if b in s.lw and s.lw[b][0] != ('c', eng):
                deps.add(s.lw[b])
            for t in s.rd.get(b, ()):
                if t[0] != ('c', eng):
                    deps.add(t)
        if kind == 'dma':
            k = s.dcnt[eng]
            s.dcnt[eng] += 1
            slot = k % s.NDS
            m = k // s.NDS + 1
            key = ('d', eng, slot)
            if m > 1:
                deps.add((key, 16 * (m - 1)))
            tok = (key, 16 * m)
        elif kind == 'cc':
            key = ('cc',)
            s.ccn += 1
            tok = (key, s.ccn)
        else:
            key = ('c', eng)
            if signal:
                s.cnt[eng] += 1
                tok = (key, s.cnt[eng])
            else:
                tok = (key, s.cnt[eng] + 1)
        s.semval[key] = max(s.semval.get(key, 0), tok[1])
        waits = s._need(eng, sorted(deps, key=str))
        for b in writes:
            s.lw[b] = tok
            s.rd[b] = []
        for b in reads:
            s.rd.setdefault(b, []).append(tok)
        s.ops[eng].append((waits, _freeze(fn), kind, signal, key))
        return tok

    def _allvals(s):
        d = dict(s.semval)
        for e in s.ENG:
            if ('c', e) in d:
                d[('c', e)] = s.cnt[e]
        return sorted([(k, v) for k, v in d.items() if v > 0], key=str)

    def barrier(s):
        if s.maxops is not None and getattr(s, 'nadd', 0) > s.maxops:
            return
        deps = s._allvals()
        for e in s.ENG:
            w = s._need(e, deps)
            if w:
                s.ops[e].append((w, None, 'c', False, None))

    def final_wait(s, eng='sp'):
        deps = s._allvals()
        w = s._need(eng, deps)
        if w:
            s.ops[eng].append((w, None, 'c', False, None))

    def sem_keys(s):
        return list(s.semval.keys())

    def emit(s, block, sems):
        def run(name):
            def f(E):
                for waits, fn, kind, signal, key in s.ops[name]:
                    for (k, v) in waits:
                        E.wait_ge(sems[k], v)
                    if fn is None:
                        continue
                    ins = fn(E)
                    if kind == 'dma':
                        ins.then_inc(sems[key], 16)
                    elif kind == 'cc':
                        ins.then_inc(sems[key])
                    elif signal:
                        ins.then_inc(sems[key], 1)
            return f
        block.tensor(run('pe'))
        block.scalar(run('act'))
        block.vector(run('dve'))
        block.gpsimd(run('pool'))
        block.sync(run('sp'))


def build_program(stop_after=99, n_cores=8, maxops=None, lite=()):
    nc = bass.Bass("TRN2", target_bir_lowering=False)
    T_ = Tr()
    T_.maxops = maxops

    def din(name, shape, dt=F32):
        if name in lite:
            shape = [8, shape[1]]
        return nc.dram_tensor(name, list(shape), dt, kind="ExternalInput").ap()

    def dout(name, shape, dt=F32):
        return nc.dram_tensor(name, list(shape), dt, kind="ExternalOutput").ap()

    def dint(name, shape, dt=F32):
        return nc.dram_tensor(name, list(shape), dt).ap()

    xp = din("xp", [TP, D]); xs = din("xs", [TS, D])
    cfk = din("cfk", [NL * 2 * PAST, 2048]); cfv = din("cfv", [NL * 2 * PAST, 2048])
    cfl = din("cfl", [NL * 2 * PAST, 16]); cck = din("cck", [NL * 2 * PAST, 512])
    ckp = din("ckp", [NL * 2 * PAST, 64])
    g_norm = din("g_norm", [NL, D]); w_in = din("w_in", [NL * D, N_IN])
    b_f = din("b_f", [NL, 16]); g_qf = din("g_q_fox", [NL, 128]); g_kf = din("g_k_fox", [NL, 128])
    g_cq = din("g_cq", [NL, 1024]); w_qb = din("w_qb", [NL * 1024, 3072])
    g_qn = din("g_qn", [NL, 128]); g_qp = din("g_qp", [NL, 64]); g_ckv = din("g_ckv", [NL, 512])
    g_kp = din("g_kp", [NL, 64]); w_kvb = din("w_kvb", [NL * 512, 4096]); g_kn = din("g_kn", [NL, 128])
    w_out = din("w_out", [NL * D, D])
    c_identb = din("c_identb", [128, 128], BF16); c_identf = din("c_identf", [128, 128])
    c_onesb = din("c_onesb", [128, 128], BF16); c_onesf = din("c_onesf", [128, 128])
    c_trif = din("c_trif", [128, 128])
    c_cos = din("c_cos", [T, 32]); c_sin = din("c_sin", [T, 32])
    c_maskf = din("c_maskf", [128, 16 * 128], BF16); c_maskm = din("c_maskm", [128, 16 * 128], BF16)
    c_masks = din("c_masks", [32, 32], BF16)
    c_tris = din("c_tris", [32, 32]); c_inds = din("c_inds", [32, 2]); c_indw = din("c_indw", [32, 256]); c_masks2 = din("c_masks2", [32, 32], BF16)

    yp = dout("yp", [TP, D]); ys = dout("ys", [TS, D])
    nfk_p = dout("nfk_p", [NL * TP, 2048]); nfv_p = dout("nfv_p", [NL * TP, 2048])
    nfl_p = dout("nfl_p", [NL * TP, 16]); nck_p = dout("nck_p", [NL * TP, 512]); nkp_p = dout("nkp_p", [NL * TP, 64])
    nfk_s = dout("nfk_s", [NL * TS, 2048]); nfv_s = dout("nfv_s", [NL * TS, 2048])
    nfl_s = dout("nfl_s", [NL * TS, 16]); nck_s = dout("nck_s", [NL * TS, 512]); nkp_s = dout("nkp_s", [NL * TS, 64])

    ymid = dint("ymid", [T, D])
    kTx = [dint(f"kTx{l}", [2048, TP], BF16) for l in range(NL)]
    vx = [dint(f"vx{l}", [TP, 2048], BF16) for l in range(NL)]
    lfx = [dint(f"lfx{l}", [TP, 16]) for l in range(NL)]
    ckx = [dint(f"ckx{l}", [512, TP], BF16) for l in range(NL)]
    kpx = [dint(f"kpx{l}", [64, TP], BF16) for l in range(NL)]
    kTg = [dint(f"kTg{l}", [2 * 2048, TP], BF16) for l in range(NL)]
    vg = [dint(f"vg{l}", [2 * TP, 2048], BF16) for l in range(NL)]
    lfg = [dint(f"lfg{l}", [2 * TP, 16]) for l in range(NL)]
    ckg = [dint(f"ckg{l}", [2 * 512, TP], BF16) for l in range(NL)]
    kpg = [dint(f"kpg{l}", [2 * 64, TP], BF16) for l in range(NL)]
    qfx = dint("qfx", [2048, T], BF16); qnx = dint("qnx", [2048, T], BF16)
    qpx = dint("qpx", [1024, T], BF16); zax = dint("zax", [2048, T], BF16); zbx = dint("zbx", [2048, T], BF16)

    es = ExitStack()
    ph = [None]

    def sb(name, shape, dt=F32):
        return es.enter_context(nc.sbuf_tensor(name, list(shape), dt))

    def sbp(name, shape, dt=F32):
        return ph[0].enter_context(nc.sbuf_tensor(name, list(shape), dt))

    def pst(name, shape, dt=F32):
        return es.enter_context(nc.psum_tensor(name, list(shape), dt))

    A = T_.add

    identb = sb("identb", [128, 128], BF16); identf = sb("identf", [128, 128])
    onesb = sb("onesb", [128, 128], BF16); onesf = sb("onesf", [128, 128]); trif = sb("trif", [128, 128])
    cosT = sb("cosT", [128, NTB, 32]); sinT = sb("sinT", [128, NTB, 32])
    masks = sb("masks", [32, 32], BF16); tris = sb("tris", [32, 32]); inds = sb("inds", [32, 2])
    actT = sb("actT", [128, 32, T], BF16)
    wt = [sb(f"wt{j}", [128, 32, 512], BF16) for j in range(2)]
    gcol = sb("gcol", [128, 32]); g32 = sb("g32", [32, 128]); gcq_t = sb("gcq_t", [128, 1024]); gckv_t = sb("gckv_t", [128, 512])
    gqf_t = sb("gqf_t", [128, 128]); gkf_t = sb("gkf_t", [128, 128]); gqn_t = sb("gqn_t", [128, 128])
    gkn_t = sb("gkn_t", [128, 128]); gqp_t = sb("gqp_t", [128, 64]); gkp_t = sb("gkp_t", [128, 64])
    bf_t = sb("bf_t", [128, 16])
    kTn = sb("kTn", [128, H, TS], BF16)
    vn = sb("vn", [TS, 2048], BF16)
    lfn = sb("lfn", [TS, 16])
    ckTn = sb("ckTn", [128, 4, TS], BF16)
    kpTn = sb("kpTn", [64, TS], BF16)
    ss = sb("ss", [128, 16]); rs = sb("rs", [128, 16])
    ssr = [sb(f"ssr{k}", [128, 16]) for k in range(4)]; rsr = [sb(f"rsr{k}", [128, 16]) for k in range(4)]
    rms_i = [0]
    rt = sb("rt", [128, 64]); rt2 = sb("rt2", [128, 64])

    psA = [pst(f"psA{j}", [128, 512]) for j in range(2)]
    psT = [pst(f"psT{j}", [128, 1024], BF16) for j in range(2)]
    psS = [pst(f"psS{j}", [128, 512]) for j in range(2)]
    psO = pst("psO", [128, 512]); psD = pst("psD", [128, 512])
    psA = psA + [psS[0], psS[1], psO, psD]
    npsA = [6]

    def dma(eng, out, in_, reads, writes):
        return A(eng, lambda E: E.dma_start(out=out, in_=in_), reads, writes, kind='dma')

    for (dst, src, nm) in [(identb, c_identb, 'identb'), (identf, c_identf, 'identf'), (onesb, c_onesb, 'onesb'),
                           (onesf, c_onesf, 'onesf'), (trif, c_trif, 'trif'), (masks, c_masks, 'masks'), (tris, c_tris, 'tris'),
                           (inds, c_inds, 'inds')]:
        dma('sp', dst[:], src[:, :], [], [nm])
    for tb in range(NTB):
        M = tb_rows(tb)
        dma('sp', cosT[:M, tb, :], c_cos[tb * 128: tb * 128 + M, :], [], [('cos', tb)])
        dma('sp', sinT[:M, tb, :], c_sin[tb * 128: tb * 128 + M, :], [], [('sin', tb)])

    pa_i = [0]

    def next_psA():
        j = pa_i[0] % npsA[0]
        pa_i[0] += 1
        return j

    wt_i = [0]

    def load_w(src2d, K, W, c0):
        j = wt_i[0] % 2
        wt_i[0] += 1
        KC = K // 128
        step = 8
        for k0 in range(0, KC, step):
            k1 = min(KC, k0 + step)
            src = src2d[k0 * 128:k1 * 128, c0:c0 + W].rearrange("(kc p) n -> p kc n", p=128)
            A('pool', (lambda E, o=wt[j][:, k0:k1, :W], i=src: E.dma_start(out=o, in_=i)),
              [], [('wt', j, k0)], kind='dma')
        return j

    def proj_tm(tb, actT_names, lhs_fn, KC, wj, W, wcol0=0):
        M = tb_rows(tb)
        pj = next_psA()
        for kc in range(KC):
            lh = lhs_fn(kc, tb, M)
            A('pe', (lambda E, kc=kc, lh=lh: E.matmul(psA[pj][:M, :W], lhsT=lh,
                                              rhs=wt[wj][:, kc, wcol0:wcol0 + W],
                                              start=(kc == 0), stop=(kc == KC - 1))),
              list(actT_names) + [('wt', wj, (kc // 8) * 8)], [('psA', pj)], signal=(kc == KC - 1))
        return pj, M

    def hT_lhs(kc, tb, M):
        return actT[:, kc, tb * 128: tb * 128 + M]

    def rms_groups(src3, M, G, Dg, gain_t, dst3, src_names, dst_names):
        sq3 = sq[:M, :G * Dg].rearrange("p (g d) -> p g d", g=G)
        k_ = rms_i[0] % 4
        rms_i[0] += 1
        ss_, rs_, sn, rn = ssr[k_], rsr[k_], ('ss', k_), ('rs', k_)
        A('act', lambda E: E.activation(out=sq3, in_=src3, func=AF.Square), src_names, ['sq'])
        A('dve', lambda E: E.tensor_reduce(out=ss_[:M, :G], in_=sq3, axis=AX.X, op=ALU.add), ['sq'], [sn])
        A('dve', lambda E: E.tensor_scalar(out=ss_[:M, :G], in0=ss_[:M, :G], scalar1=1.0 / Dg, scalar2=EPS,
                                           op0=ALU.mult, op1=ALU.add), [sn], [sn])
        A('act', lambda E: E.activation(out=ss_[:M, :G], in_=ss_[:M, :G], func=AF.Sqrt), [sn], [sn])
        A('dve', lambda E: E.reciprocal(out=rs_[:M, :G], in_=ss_[:M, :G]), [sn], [rn])
        A('dve', lambda E: E.tensor_tensor(out=dst3, in0=src3,
                                           in1=rs_[:M, :G].unsqueeze(2).to_broadcast([M, G, Dg]), op=ALU.mult),
          list(src_names) + [rn], dst_names)
        A('dve', lambda E: E.tensor_tensor(out=dst3, in0=dst3,
                                           in1=gain_t[:M, :Dg].unsqueeze(1).to_broadcast([M, G, Dg]), op=ALU.mult),
          list(dst_names) + ['gains'], dst_names)

    pt_i = [0]

    def transpose_to(src_bf, M, nblk, blkw, dst_fn, src_names, dst_names_fn, evac='act', scale_fn=None):
        pj = pt_i[0] % 2
        pt_i[0] += 1
        for b in range(nblk):
            A('pe', (lambda E, b=b: E.transpose(psT[pj][:blkw, b * 128: b * 128 + M],
                                                src_bf[:M, b * blkw:(b + 1) * blkw], identb[:M, :M])),
              list(src_names) + ['identb'], [('psT', pj)], signal=(b == nblk - 1))
        for b in range(nblk):
            d_ = dst_fn(b)
            if scale_fn is not None:
                sc_ = scale_fn(b)
                A('act', (lambda E, b=b, d_=d_, sc_=sc_: E.activation(out=d_, in_=psT[pj][:blkw, b * 128: b * 128 + M],
                                                      func=AF.Copy, scale=sc_)),
                  [('psT', pj), 'gcol'], dst_names_fn(b))
            else:
                A(evac, (lambda E, b=b, d_=d_: (E.copy if evac == 'act' else E.tensor_copy)(
                    out=d_, in_=psT[pj][:blkw, b * 128: b * 128 + M])),
                  [('psT', pj)], dst_names_fn(b))

    def rope(x3, M, G, tb, tmp_a, tmp_b, names):
        c = cosT[:M, tb, :].unsqueeze(1).to_broadcast([M, G, 32])
        s_ = sinT[:M, tb, :].unsqueeze(1).to_broadcast([M, G, 32])
        x1 = x3[:, :, 0:32]
        x2 = x3[:, :, 32:64]
        ta = tmp_a[:M, :G * 32].rearrange("p (g d) -> p g d", g=G)
        tb_ = tmp_b[:M, :G * 32].rearrange("p (g d) -> p g d", g=G)
        rn = list(names) + [('cos', tb), ('sin', tb)]
        A('dve', lambda E: E.tensor_tensor(out=ta, in0=x1, in1=s_, op=ALU.mult), rn, ['rt'])
        A('dve', lambda E: E.tensor_tensor(out=tb_, in0=x2, in1=s_, op=ALU.mult), rn, ['rt2'])
        A('dve', lambda E: E.tensor_tensor(out=x1, in0=x1, in1=c, op=ALU.mult), rn, names)
        A('dve', lambda E: E.tensor_tensor(out=x2, in0=x2, in1=c, op=ALU.mult), rn, names)
        A('dve', lambda E: E.tensor_tensor(out=x1, in0=x1, in1=tb_, op=ALU.subtract), list(names) + ['rt2'], names)
        A('dve', lambda E: E.tensor_tensor(out=x2, in0=x2, in1=ta, op=ALU.add), list(names) + ['rt'], names)

    pend = []

    def defer(f):
        pend.append(f)
        while len(pend) > 1:
            pend.pop(0)()

    def flush():
        while pend:
            pend.pop(0)()

    def out_rows(l, tb, M, pbuf, sbuf_, width, c0=0):
        if tb < 8:
            return pbuf[l * TP + tb * 128: l * TP + tb * 128 + M, c0:c0 + width]
        return sbuf_[l * TS: l * TS + M, c0:c0 + width]

    for l in range(NL):
        T_.barrier()
        ph[0] = ExitStack()
        xt = [sbp(f"xt{l}", [128, D])]
        hb = sbp(f"hb{l}", [128, D], BF16)
        sq = sbp(f"sq{l}", [128, 1024])
        wf = [sbp(f"wf{l}_{j}", [128, 512]) for j in range(2)]
        wb = [sbp(f"wb{l}_{j}", [128, 512], BF16) for j in range(2)]
        tsb = [sbp(f"tsb{l}_{j}", [128, 512], BF16) for j in range(2)]
        cqT = sbp(f"cqT{l}", [128, 8, T], BF16)
        dma('sp', g32[:, :], g_norm[l, :].rearrange("(c p) -> c p", p=128), [], ['g32'])
        A('pe', lambda E: E.transpose(psS[0][:, :32], g32[:32, :], identf[:32, :32]), ['g32', 'identf'], [('psS', 0)])
        A('act', lambda E: E.copy(out=gcol[:, :], in_=psS[0][:, :32]), [('psS', 0)], ['gcol'])
        for (dst, src, w) in [(gcq_t, g_cq, 1024), (gckv_t, g_ckv, 512), (gqf_t, g_qf, 128),
                              (gkf_t, g_kf, 128), (gqn_t, g_qn, 128), (gkn_t, g_kn, 128), (gqp_t, g_qp, 64),
                              (gkp_t, g_kp, 64), (bf_t, b_f, 16)]:
            dma('sp', dst[:, :w], src[l, :].partition_broadcast(128), [], ['gains'])
        T_.barrier()

        for tb in range(NTB):
            M = tb_rows(tb)
            j = 0
            if l == 0:
                src = xp[tb * 128: tb * 128 + M, :] if tb < 8 else xs[:, :]
            else:
                src = ymid[tb * 128: tb * 128 + M, :]
            dma('sp', xt[j][:M, :], src, [('ymid', tb)], [('xt', j)])
            A('dve', lambda E, j=j, M=M: E.scalar_tensor_tensor(out=hb[:M, :], in0=xt[j][:M, :], scalar=1.0,
                                                               in1=xt[j][:M, :], op0=ALU.mult, op1=ALU.mult,
                                                               accum_out=ss[:M, 0:1]),
              [('xt', j)], ['hb', 'ss'])
            A('dve', lambda E, M=M: E.tensor_scalar(out=ss[:M, 0:1], in0=ss[:M, 0:1], scalar1=1.0 / D, scalar2=EPS,
                                                    op0=ALU.mult, op1=ALU.add), ['ss'], ['ss'])
            A('act', lambda E, M=M: E.activation(out=ss[:M, 0:1], in_=ss[:M, 0:1], func=AF.Sqrt), ['ss'], ['ss'])
            A('dve', lambda E, M=M: E.reciprocal(out=rs[:M, 0:1], in_=ss[:M, 0:1]), ['ss'], ['rs'])
            A('dve', lambda E, j=j, M=M: E.tensor_scalar(out=hb[:M, :], in0=xt[j][:M, :], scalar1=rs[:M, 0:1],
                                                        scalar2=None, op0=ALU.mult),
              [('xt', j), 'rs'], ['hb'])
            for g4 in range(4):
                transpose_to(hb[:, g4 * 1024:(g4 + 1) * 1024], M, 8, 128,
                             (lambda b, g4=g4, tb=tb, M=M: actT[:, g4 * 8 + b, tb * 128: tb * 128 + M]),
                             ['hb'], (lambda b, tb=tb: [('actT', tb)]), evac='act',
                             scale_fn=(lambda b, g4=g4: gcol[:, g4 * 8 + b: g4 * 8 + b + 1]))
        if stop_after <= 0:
            T_.barrier()
            ph[0].close()
            break

        hn = lambda tb: [('actT', tb)]
        for cb in range(4):
            wj = load_w(w_in[l * D:(l + 1) * D, :], D, 512, OFF_K + cb * 512)
            for tb in range(NTB):
                pj, M = proj_tm(tb, hn(tb), hT_lhs, 32, wj, 512)
                j = tb % 2
                src3 = psA[pj][:M, :].rearrange("p (g d) -> p g d", g=4)
                dst3 = wf[j][:M, :512].rearrange("p (g d) -> p g d", g=4)
                rms_groups(src3, M, 4, 128, gkf_t, dst3, [('psA', pj)], [('wf', j)])
                dma('sp', out_rows(l, tb, M, nfk_p, nfk_s, 512, cb * 512), wf[j][:M, :512], [('wf', j)], [])
                A('act', lambda E, j=j, M=M: E.copy(out=wb[j][:M, :512], in_=wf[j][:M, :512]), [('wf', j)], [('wb', j)])
                def tail_k(tb=tb, j=j, M=M, cb=cb):
                    if tb < 8:
                        transpose_to(wb[j], M, 4, 128, (lambda b, j=j, M=M: tsb[j][:, b * 128: b * 128 + M]),
                                     [('wb', j)], (lambda b, j=j: [('tsb', j)]))
                        dma('sp', kTx[l][cb * 512:(cb + 1) * 512, tb * 128:(tb + 1) * 128].rearrange("(g d) t -> d g t", g=4),
                            tsb[j][:, :512].rearrange("d (g t) -> d g t", g=4), [('tsb', j)], [('kTx', l)])
                    else:
                        transpose_to(wb[j], M, 4, 128, (lambda b, cb=cb, M=M: kTn[:, cb * 4 + b, :M]),
                                     [('wb', j)], (lambda b: ['kTn']))
                defer(tail_k)
            flush()
        for cb in range(4):
            wj = load_w(w_in[l * D:(l + 1) * D, :], D, 512, OFF_V + cb * 512)
            for tb in range(NTB):
                pj, M = proj_tm(tb, hn(tb), hT_lhs, 32, wj, 512)
                j = tb % 2
                A('act', lambda E, j=j, M=M, pj=pj: E.copy(out=wf[j][:M, :512], in_=psA[pj][:M, :]),
                  [('psA', pj)], [('wf', j)])
                dma('sp', out_rows(l, tb, M, nfv_p, nfv_s, 512, cb * 512), wf[j][:M, :512], [('wf', j)], [])
                if tb < 8:
                    A('dve', lambda E, j=j, M=M, pj=pj: E.tensor_copy(out=wb[j][:M, :512], in_=psA[pj][:M, :]),
                      [('psA', pj)], [('wb', j)])
                    dma('sp', vx[l][tb * 128:(tb + 1) * 128, cb * 512:(cb + 1) * 512], wb[j][:M, :512],
                        [('wb', j)], [('vx', l)])
                else:
                    A('dve', lambda E, M=M, pj=pj, cb=cb: E.tensor_copy(out=vn[:M, cb * 512:(cb + 1) * 512],
                                                                        in_=psA[pj][:M, :]),
                      [('psA', pj)], ['vn'])
        wj = load_w(w_in[l * D:(l + 1) * D, :], D, 16, OFF_F)
        for tb in range(NTB):
            pj, M = proj_tm(tb, hn(tb), hT_lhs, 32, wj, 16)
            j = tb % 2
            A('dve', lambda E, j=j, M=M, pj=pj: E.tensor_tensor(out=wf[j][:M, :16], in0=psA[pj][:M, :16],
                                                               in1=bf_t[:M, :16], op=ALU.add),
              [('psA', pj), 'gains'], [('wf', j)])
            A('act', lambda E, j=j, M=M: E.activation(out=wf[j][:M, :16], in_=wf[j][:M, :16], func=AF.Exp, scale=-1.0),
              [('wf', j)], [('wf', j)])
            A('act', lambda E, j=j, M=M: E.activation(out=wf[j][:M, :16], in_=wf[j][:M, :16], func=AF.Ln, bias=1.0),
              [('wf', j)], [('wf', j)])
            if tb < 8:
                A('dve', lambda E, j=j, M=M: E.tensor_scalar(out=wf[j][:M, :16], in0=wf[j][:M, :16], scalar1=-1.0,
                                                            scalar2=None, op0=ALU.mult), [('wf', j)], [('wf', j)])
                dma('sp', out_rows(l, tb, M, nfl_p, nfl_s, 16), wf[j][:M, :16], [('wf', j)], [])
                dma('sp', lfx[l][tb * 128:(tb + 1) * 128, :], wf[j][:M, :16], [('wf', j)], [('lfx', l)])
            else:
                A('dve', lambda E, j=j, M=M: E.tensor_scalar(out=lfn[:M, :16], in0=wf[j][:M, :16], scalar1=-1.0,
                                                            scalar2=None, op0=ALU.mult), [('wf', j)], ['lfn'])
                dma('sp', out_rows(l, tb, M, nfl_p, nfl_s, 16), lfn[:M, :16], ['lfn'], [])
        wj = load_w(w_in[l * D:(l + 1) * D, :], D, 512, OFF_CKV)
        for tb in range(NTB):
            pj, M = proj_tm(tb, hn(tb), hT_lhs, 32, wj, 512)
            j = tb % 2
            src3 = psA[pj][:M, :].rearrange("p (g d) -> p g d", g=1)
            dst3 = wf[j][:M, :512].rearrange("p (g d) -> p g d", g=1)
            rms_groups(src3, M, 1, 512, gckv_t, dst3, [('psA', pj)], [('wf', j)])
            dma('sp', out_rows(l, tb, M, nck_p, nck_s, 512), wf[j][:M, :512], [('wf', j)], [])
            A('act', lambda E, j=j, M=M: E.copy(out=wb[j][:M, :512], in_=wf[j][:M, :512]), [('wf', j)], [('wb', j)])
            if tb < 8:
                transpose_to(wb[j], M, 4, 128, (lambda b, j=j, M=M: tsb[j][:, b * 128: b * 128 + M]),
                             [('wb', j)], (lambda b, j=j: [('tsb', j)]))
                dma('sp', ckx[l][:, tb * 128:(tb + 1) * 128].rearrange("(g d) t -> d g t", g=4),
                    tsb[j][:, :512].rearrange("d (g t) -> d g t", g=4), [('tsb', j)], [('ckx', l)])
            else:
                transpose_to(wb[j], M, 4, 128, (lambda b, M=M: ckTn[:, b, :M]), [('wb', j)], (lambda b: ['ckTn']))
        wj = load_w(w_in[l * D:(l + 1) * D, :], D, 64, OFF_KPE)
        for tb in range(NTB):
            pj, M = proj_tm(tb, hn(tb), hT_lhs, 32, wj, 64)
            j = tb % 2
            src3 = psA[pj][:M, :64].rearrange("p (g d) -> p g d", g=1)
            dst3 = wf[j][:M, :64].rearrange("p (g d) -> p g d", g=1)
            rms_groups(src3, M, 1, 64, gkp_t, dst3, [('psA', pj)], [('wf', j)])
            rope(dst3, M, 1, tb, rt, rt2, [('wf', j)])
            dma('sp', out_rows(l, tb, M, nkp_p, nkp_s, 64), wf[j][:M, :64], [('wf', j)], [])
            A('act', lambda E, j=j, M=M: E.copy(out=wb[j][:M, :64], in_=wf[j][:M, :64]), [('wf', j)], [('wb', j)])
            if tb < 8:
                transpose_to(wb[j], M, 1, 64, (lambda b, j=j, M=M: tsb[j][:64, :M]),
                             [('wb', j)], (lambda b, j=j: [('tsb', j)]))
                dma('sp', kpx[l][:, tb * 128:(tb + 1) * 128], tsb[j][:64, :128], [('tsb', j)], [('kpx', l)])
            else:
                transpose_to(wb[j], M, 1, 64, (lambda b, M=M: kpTn[:, :M]), [('wb', j)], (lambda b: ['kpTn']))
        if stop_after <= 1:
            T_.barrier()
            ph[0].close()
            break

        T_.barrier()
        groups = [[2 * i, 2 * i + 1] for i in range(n_cores // 2)]
        for (src, dst, nm, nchunk) in [(kTx[l], kTg[l], 'kTg', 4), (vx[l], vg[l], 'vg', 4), (lfx[l], lfg[l], 'lfg', 1),
                                       (ckx[l], ckg[l], 'ckg', 1), (kpx[l], kpg[l], 'kpg', 1)]:
            rows = src.shape[0] // nchunk
            for c in range(nchunk):
                A('pool', (lambda E, src=src, dst=dst, c=c, rows=rows: E.collective_compute(
                    "AllGather", ALU.bypass, replica_groups=groups, ins=[src[c * rows:(c + 1) * rows, :]],
                    outs=[dst[2 * c * rows:2 * (c + 1) * rows, :]])),
                  [], [(nm, l, c)], kind='cc')
        T_.barrier()

        for cb in range(4):
            wj = load_w(w_in[l * D:(l + 1) * D, :], D, 512, OFF_Q + cb * 512)
            for tb in range(NTB):
                pj, M = proj_tm(tb, hn(tb), hT_lhs, 32, wj, 512)
                j = tb % 2
                src3 = psA[pj][:M, :].rearrange("p (g d) -> p g d", g=4)
                dst3 = wf[j][:M, :512].rearrange("p (g d) -> p g d", g=4)
                rms_groups(src3, M, 4, 128, gqf_t, dst3, [('psA', pj)], [('wf', j)])
                A('act', lambda E, j=j, M=M: E.copy(out=wb[j][:M, :512], in_=wf[j][:M, :512]), [('wf', j)], [('wb', j)])
                def tail_q(tb=tb, j=j, M=M, cb=cb):
                    transpose_to(wb[j], M, 4, 128, (lambda b, j=j, M=M: tsb[j][:, b * 128: b * 128 + M]),
                                 [('wb', j)], (lambda b, j=j: [('tsb', j)]))
                    dma('sp', qfx[cb * 512:(cb + 1) * 512, tb * 128:tb * 128 + M].rearrange("(g d) t -> d g t", g=4),
                        tsb[j][:, :512].rearrange("d (g t) -> d g t", g=4)[:, :, :M], [('tsb', j)], [('qfx', cb, tb)])
                defer(tail_q)
            flush()
        wj0 = load_w(w_in[l * D:(l + 1) * D, :], D, 512, OFF_CQ)
        wj1 = load_w(w_in[l * D:(l + 1) * D, :], D, 512, OFF_CQ + 512)
        for tb in range(NTB):
            p0, M = proj_tm(tb, hn(tb), hT_lhs, 32, wj0, 512)
            p1, M = proj_tm(tb, hn(tb), hT_lhs, 32, wj1, 512)
            pp = [p0, p1]
            for hf in range(2):
                A('act', lambda E, hf=hf, M=M, pp=pp: E.activation(out=sq[:M, hf * 512:(hf + 1) * 512],
                                                                  in_=psA[pp[hf]][:M, :], func=AF.Square),
                  [('psA', pp[hf])], ['sq'])
            A('dve', lambda E, M=M: E.tensor_reduce(out=ss[:M, 0:1], in_=sq[:M, :1024], axis=AX.X, op=ALU.add), ['sq'], ['ss'])
            A('dve', lambda E, M=M: E.tensor_scalar(out=ss[:M, 0:1], in0=ss[:M, 0:1], scalar1=1.0 / 1024, scalar2=EPS,
                                                    op0=ALU.mult, op1=ALU.add), ['ss'], ['ss'])
            A('act', lambda E, M=M: E.activation(out=ss[:M, 0:1], in_=ss[:M, 0:1], func=AF.Sqrt), ['ss'], ['ss'])
            A('dve', lambda E, M=M: E.reciprocal(out=rs[:M, 0:1], in_=ss[:M, 0:1]), ['ss'], ['rs'])
            for hf in range(2):
                A('dve', lambda E, hf=hf, M=M, pp=pp: E.tensor_scalar(out=wf[hf][:M, :512], in0=psA[pp[hf]][:M, :],
                                                                     scalar1=rs[:M, 0:1], scalar2=None, op0=ALU.mult),
                  [('psA', pp[hf]), 'rs'], [('wf', hf)])
                A('dve', lambda E, hf=hf, M=M: E.tensor_tensor(out=wb[hf][:M, :512], in0=wf[hf][:M, :512],
                                                              in1=gcq_t[:M, hf * 512:(hf + 1) * 512], op=ALU.mult),
                  [('wf', hf), 'gains'], [('wb', hf)])
                transpose_to(wb[hf], M, 4, 128, (lambda b, hf=hf, tb=tb, M=M: cqT[:, hf * 4 + b, tb * 128: tb * 128 + M]),
                             [('wb', hf)], (lambda b, tb=tb: [('cqT', tb)]))
        cq_lhs = lambda kc, tb, M: cqT[:, kc, tb * 128: tb * 128 + M]
        for hp in range(8):
            wj = load_w(w_qb[l * 1024:(l + 1) * 1024, :], 1024, 384, hp * 384)
            for tb in range(NTB):
                pj, M = proj_tm(tb, [('cqT', tb)], cq_lhs, 8, wj, 384)
                j = tb % 2
                ps3 = psA[pj][:M, :384].rearrange("p (g d) -> p g d", g=2)
                dn3 = wf[j][:M, 0:256].rearrange("p (g d) -> p g d", g=2)
                dp3 = wf[j][:M, 256:384].rearrange("p (g d) -> p g d", g=2)
                rms_groups(ps3[:, :, 0:128], M, 2, 128, gqn_t, dn3, [('psA', pj)], [('wf', j)])
                rms_groups(ps3[:, :, 128:192], M, 2, 64, gqp_t, dp3, [('psA', pj)], [('wf', j)])
                rope(dp3, M, 2, tb, rt, rt2, [('wf', j)])
                A('act', lambda E, j=j, M=M: E.copy(out=wb[j][:M, :384], in_=wf[j][:M, :384]), [('wf', j)], [('wb', j)])
                transpose_to(wb[j], M, 2, 128, (lambda b, j=j, M=M: tsb[j][:, b * 128: b * 128 + M]),
                             [('wb', j)], (lambda b, j=j: [('tsb', j)]))
                dma('sp', qnx[hp * 256:(hp + 1) * 256, tb * 128:tb * 128 + M].rearrange("(g d) t -> d g t", g=2),
                    tsb[j][:, :256].rearrange("d (g t) -> d g t", g=2)[:, :, :M], [('tsb', j)], [('qnx', hp, tb)])
                transpose_to(wb[j][:, 256:384], M, 2, 64, (lambda b, j=j, M=M: tsb[j][:64, 256 + b * 128: 256 + b * 128 + M]),
                             [('wb', j)], (lambda b, j=j: [('tsb', j)]))
                dma('sp', qpx[hp * 128:(hp + 1) * 128, tb * 128:tb * 128 + M].rearrange("(g d) t -> d g t", g=2),
                    tsb[j][:64, 256:512].rearrange("d (g t) -> d g t", g=2)[:, :, :M], [('tsb', j)], [('qpx', hp, tb)])
        ttiles = [(0, 512), (512, 512), (1024, TS)]
        for (zoff, zdst, znm) in [(OFF_ZA, zax, 'zax'), (OFF_ZB, zbx, 'zbx')]:
            for cb in range(4):
                wj = load_w(w_in[l * D:(l + 1) * D, :], D, 512, zoff + cb * 512)
                for ct in range(4):
                    for ti, (t0, tn) in enumerate(ttiles):
                        pj = next_psA()
                        tbs = [('actT', tb) for tb in range(NTB) if t0 <= tb * 128 < t0 + tn]
                        for kc in range(32):
                            A('pe', (lambda E, kc=kc, pj=pj, wj=wj, ct=ct, t0=t0, tn=tn: E.matmul(
                                psA[pj][:, :tn], lhsT=wt[wj][:, kc, ct * 128:(ct + 1) * 128],
                                rhs=actT[:, kc, t0:t0 + tn], start=(kc == 0), stop=(kc == 31))),
                              tbs + [('wt', wj, (kc // 8) * 8)], [('psA', pj)], signal=(kc == 31))
                        j = (ct * 3 + ti) % 2
                        A('act', lambda E, pj=pj, tn=tn, j=j: E.activation(out=tsb[j][:, :tn], in_=psA[pj][:, :tn], func=AF.Silu),
                          [('psA', pj)], [('tsb', j)])
                        r0 = (cb * 4 + ct) * 128
                        dma('sp', zdst[r0:r0 + 128, t0:t0 + tn], tsb[j][:, :tn], [('tsb', j)], [(znm, cb * 4 + ct, ti)])
        T_.barrier()
        ph[0].close()
        if stop_after <= 2:
            break

        ph[0] = ExitStack()
        npsA[0] = 2
        gidx = lambda g: (0 if gblock(0, g // 2) == g else 1) * 8 + g // 2
        maskf = sbp(f"maskf{l}", [128, 2048], BF16)
        lf_t = sbp(f"lf_t{l}", [128, 16, 32]); win = sbp(f"win{l}", [128, 16, 32]); tots = sbp(f"tots{l}", [128, 16, 32])
        carry = sbp(f"carry{l}", [128, 17, 32]); ck = sbp(f"ck{l}", [128, 16, 32]); bias = sbp(f"bias{l}", [128, 8, 16, 16])
        crefs = sbp(f"crefs{l}", [128, 32]); ckn = sbp(f"ckn{l}", [32, 16]); biasn = sbp(f"biasn{l}", [32, 32])
        kTh = [sbp(f"kTh{l}_{j}", [128, 2, TP], BF16) for j in range(2)]
        qTh = [sbp(f"qTh{l}_{j}", [128, T], BF16) for j in range(2)]
        zTh = [sbp(f"zTh{l}_{j}", [128, T], BF16) for j in range(2)]
        pT = [sbp(f"pT{l}_{j}", [128, 512], BF16) for j in range(2)]
        rden = sbp(f"rden{l}", [128, 512]); osb = sbp(f"osb{l}", [128, 512])
        qTs = sbp(f"qTs{l}", [128, 16, TS], BF16); zTs = sbp(f"zTs{l}", [128, 16, TS], BF16)
        indw = sbp(f"indw{l}", [32, 256])
        vgrp = wt[0][:, 0:16, :]; kc_t = wt[0][:, 16:32, :]; vc_t = wt[1][:, 0:16, :]
        kTc = wt[1][:, 16:32, :].rearrange("p a b -> p (a b)").rearrange("p (h t) -> p h t", h=4)
        dma('sp', maskf[:, :], c_maskf[:, :], [], ['maskf'])
        dma('sp', indw[:, :], c_indw[:, :], [], ['indw'])
        all_q = lambda nm, n0: [(nm, a, tb) for a in range(n0) for tb in range(NTB)]
        all_z = lambda nm: [(nm, a, ti) for a in range(16) for ti in range(3)]

        def cumsum_tables(ncol, carry_order):
            flat = lf_t[:, :, :ncol]
            A('pe', lambda E: E.matmul(psS[0][:, :16 * ncol].rearrange("p (j c) -> p j c", j=16), lhsT=trif[:, :], rhs=flat,
                                       start=True, stop=True), ['lf_t', 'trif'], [('psS', 0)])
            A('pe', lambda E: E.matmul(psS[1][:, :16 * ncol].rearrange("p (j c) -> p j c", j=16), lhsT=onesf[:, :], rhs=flat,
                                       start=True, stop=True), ['lf_t', 'onesf'], [('psS', 1)])
            A('act', lambda E: E.copy(out=win[:, :, :ncol], in_=psS[0][:, :16 * ncol].rearrange("p (j c) -> p j c", j=16)),
              [('psS', 0)], ['win'])
            A('act', lambda E: E.copy(out=tots[:, :, :ncol], in_=psS[1][:, :16 * ncol].rearrange("p (j c) -> p j c", j=16)),
              [('psS', 1)], ['tots'])
            A('dve', lambda E: E.memset(carry[:, 0, :], 0.0), [], ['carry'])
            for g in range(16):
                A('dve', lambda E, g=g: E.tensor_tensor(out=carry[:, g + 1, :ncol], in0=carry[:, g, :ncol],
                                                        in1=tots[:, carry_order(g), :ncol], op=ALU.add),
                  ['carry', 'tots'], ['carry'])

        dma('sp', lf_t[:, :, :16], lfg[l].rearrange("(j p) h -> p j h", p=128), [('lfg', l, 0)], ['lf_t'])
        cumsum_tables(16, gidx)
        for j in range(16):
            g = gblock(j // 8, j % 8)
            A('dve', lambda E, j=j, g=g: E.tensor_tensor(out=ck[:, j, :16], in0=win[:, j, :16], in1=carry[:, g, :16], op=ALU.add),
              ['win', 'carry'], ['ck'])
        for i in range(8):
            A('dve', lambda E, i=i: E.tensor_tensor(out=bias[:, i, :, :],
                                                    in0=carry[:, 2 * i + 2, :16].unsqueeze(1).to_broadcast([128, 16, 16]),
                                                    in1=ck[:, :, :16], op=ALU.subtract), ['carry', 'ck'], ['bias'])
        for hg in range(4):
            for r in range(2):
                for c in range(4):
                    r0 = c * 512 + r * 256
                    dma('sp', vgrp[:, r * 8 + c * 2: r * 8 + c * 2 + 2, :],
                        vg[l][r0:r0 + 256, hg * 512:(hg + 1) * 512].rearrange("(q p) n -> p q n", p=128),
                        [], [('vgrp', r, c)])
            for hh in range(4):
                h = hg * 4 + hh
                hb_ = h % 2
                dma('sp', kTh[hb_][:, :, :],
                    kTg[l][(h // 4) * 1024:(h // 4 + 1) * 1024, :].rearrange("(r f) t -> f r t", r=2)[(h % 4) * 128:(h % 4 + 1) * 128, :, :],
                    [], [('kTh', hb_)])
                dma('sp', qTh[hb_][:, :], qfx[h * 128:(h + 1) * 128, :], all_q('qfx', 4), [('qTh', hb_)])
                dma('sp', zTh[hb_][:, :], zax[h * 128:(h + 1) * 128, :], all_z('zax'), [('zTh', hb_)])
                for t in range(2):
                    blocks = [(ip, r) for ip in range(4 * t + 4) for r in range(2)]
                    for bi, (ip, r) in enumerate(blocks):
                        j = r * 8 + ip
                        c0 = max(0, ip - 4 * t) * 128
                        sj = bi % 2
                        A('pe', (lambda E, sj=sj, c0=c0, r=r, ip=ip, t=t, hb_=hb_: E.matmul(
                            psS[sj][:, c0:512], lhsT=kTh[hb_][:, r, ip * 128:(ip + 1) * 128],
                            rhs=qTh[hb_][:, t * 512 + c0:(t + 1) * 512], start=True, stop=True)),
                          [('kTh', hb_), ('qTh', hb_)], [('psS', sj)])
                        for isub in range(c0 // 128, 4):
                            i = 4 * t + isub
                            A('act', (lambda E, sj=sj, isub=isub, i=i, j=j, h=h: E.activation(
                                out=pT[sj][:, isub * 128:(isub + 1) * 128], in_=psS[sj][:, isub * 128:(isub + 1) * 128],
                                func=AF.Exp, bias=bias[:, i, j, h:h + 1], scale=float(FOX_SCALE))),
                              [('psS', sj), 'bias'], [('pT', sj, isub)])
                            if ip == i:
                                A('dve', (lambda E, sj=sj, isub=isub, i=i, r=r: E.tensor_tensor(
                                    out=pT[sj][:, isub * 128:(isub + 1) * 128], in0=pT[sj][:, isub * 128:(isub + 1) * 128],
                                    in1=maskf[:, (i * 2 + r) * 128:(i * 2 + r + 1) * 128], op=ALU.mult)),
                                  [('pT', sj, isub), 'maskf'], [('pT', sj, isub)])
                        first = (bi == 0)
                        last = (bi == len(blocks) - 1)
                        pnames = [('pT', sj, isub) for isub in range(c0 // 128, 4)]

                        def tail_pv(sj=sj, c0=c0, j=j, hh=hh, first=first, last=last, pnames=pnames, r=r, ip=ip):
                            A('pe', (lambda E, sj=sj, c0=c0, j=j, hh=hh, first=first, last=last: E.matmul(
                                psO[:, c0:512], lhsT=vgrp[:, j, hh * 128:(hh + 1) * 128], rhs=pT[sj][:, c0:512],
                                start=first, stop=last)), pnames + [('vgrp', r, ip // 2)], ['psO'], signal=False)
                            A('pe', (lambda E, sj=sj, c0=c0, first=first, last=last: E.matmul(
                                psD[:, c0:512], lhsT=onesb[:, :], rhs=pT[sj][:, c0:512], start=first, stop=last)),
                              pnames + ['onesb'], ['psD'])
                        defer(tail_pv)
                    flush()
                    A('dve', lambda E: E.reciprocal(out=rden[:, :], in_=psD[:, :]), ['psD'], ['rden'])
                    A('dve', lambda E: E.tensor_tensor(out=osb[:, :], in0=psO[:, :], in1=rden[:, :], op=ALU.mult),
                      ['psO', 'rden'], ['osb'])
                    A('dve', (lambda E, h=h, t=t, hb_=hb_: E.tensor_tensor(
                        out=actT[:, h, t * 512:(t + 1) * 512], in0=osb[:, :], in1=zTh[hb_][:, t * 512:(t + 1) * 512], op=ALU.mult)),
                      ['osb', ('zTh', hb_)], [('actT', tb) for tb in range(4 * t, 4 * t + 4)])
        T_.barrier()
        for b in range(2):
            dma('sp', lf_t[:, :, b * 16:(b + 1) * 16],
                cfl[(l * 2 + b) * PAST:(l * 2 + b + 1) * PAST, :].rearrange("(j p) h -> p j h", p=128), [], [('lf_tb', b)])
        T_.barrier()
        cumsum_tables(32, (lambda g: g))
        A('dve', lambda E: E.tensor_tensor(out=ck[:, :, :], in0=win[:, :, :], in1=carry[:, 0:16, :], op=ALU.add),
          ['win', 'carry'], ['ck'])
        A('pe', lambda E: E.matmul(psS[0][:32, :16], lhsT=tris[:32, :32], rhs=lfn[:32, :16], start=True, stop=True),
          ['tris', 'lfn'], [('psS', 0)])
        for b in range(2):
            A('pe', lambda E, b=b: E.matmul(psS[1][:, b * 16:(b + 1) * 16], lhsT=indw[:32, b * 128:(b + 1) * 128],
                                            rhs=lfn[:32, :16], start=True, stop=True), ['indw', 'lfn'], [('psS', 1)])
        A('act', lambda E: E.copy(out=ckn[:, :], in_=psS[0][:32, :16]), [('psS', 0)], ['ckn'])
        for b in range(2):
            A('dve', lambda E, b=b: E.scalar_tensor_tensor(out=ckn[:, :], in0=carry[:32, 16, b * 16:(b + 1) * 16],
                                                          scalar=inds[:32, b:b + 1], in1=ckn[:, :], op0=ALU.mult, op1=ALU.add),
              ['carry', 'ckn', 'inds'], ['ckn'])
        A('dve', lambda E: E.tensor_tensor(out=crefs[:, :], in0=carry[:, 16, :], in1=psS[1][:, :32], op=ALU.add),
          ['carry', ('psS', 1)], ['crefs'])
        biasc = win
        A('dve', lambda E: E.tensor_tensor(out=biasc[:, :, :], in0=crefs[:, :].unsqueeze(1).to_broadcast([128, 16, 32]),
                                           in1=ck[:, :, :], op=ALU.subtract), ['crefs', 'ck', 'win'], ['biasc'])
        for b in range(2):
            A('dve', lambda E, b=b: E.tensor_tensor(out=biasn[:, b * 16:(b + 1) * 16], in0=crefs[:32, b * 16:(b + 1) * 16],
                                                    in1=ckn[:, :], op=ALU.subtract), ['crefs', 'ckn'], ['biasn'])
        dma('sp', qTs[:, :, :], qfx[:, TP:T].rearrange("(h d) t -> d h t", d=128), all_q('qfx', 4), ['qTs'])
        dma('sp', zTs[:, :, :], zax[:, TP:T].rearrange("(h d) t -> d h t", d=128), all_z('zax'), ['zTs'])

        def sample_attn(b, kT_fn, v_fn, kTnew_list, vnew, bias_fn, biasn_ap, msk, scale, zrow, out_chunk):
            for j in range(16):
                parts = kT_fn(j)
                for pi, (ka, qa, kn) in enumerate(parts):
                    A('pe', (lambda E, j=j, ka=ka, qa=qa, pi=pi, n=len(parts): E.matmul(
                        psS[0][:, j * 16:(j + 1) * 16], lhsT=ka, rhs=qa, start=(pi == 0), stop=(pi == n - 1))),
                      kn, [('psS', 0)], signal=(pi == len(parts) - 1))
            for pi, (ka, qa, kn) in enumerate(kTnew_list):
                A('pe', (lambda E, ka=ka, qa=qa, pi=pi, n=len(kTnew_list): E.matmul(
                    psS[0][:32, 256:272], lhsT=ka, rhs=qa, start=(pi == 0), stop=(pi == n - 1))),
                  kn, [('psS', 0)], signal=(pi == len(kTnew_list) - 1))
            if bias_fn is None:
                A('act', (lambda E: E.activation(out=pT[0][:, 0:256], in_=psS[0][:, 0:256], func=AF.Exp, scale=float(scale))),
                  [('psS', 0)], [('pTs', j) for j in range(16)])
            for j in range(16):
                if bias_fn is None:
                    pass
                else:
                    bj_ = bias_fn(j)
                    A('act', (lambda E, j=j, bj_=bj_: E.activation(out=pT[0][:, j * 16:(j + 1) * 16], in_=psS[0][:, j * 16:(j + 1) * 16],
                                                          func=AF.Exp, bias=bj_, scale=float(scale))),
                      [('psS', 0), 'biasc'], [('pTs', j)])
            if biasn_ap is None:
                A('act', (lambda E: E.activation(out=pT[0][:32, 256:272], in_=psS[0][:32, 256:272], func=AF.Exp,
                                                 scale=float(scale))), [('psS', 0)], [('pTs', 16)])
            else:
                A('act', (lambda E: E.activation(out=pT[0][:32, 256:272], in_=psS[0][:32, 256:272], func=AF.Exp,
                                                 bias=biasn_ap, scale=float(scale))), [('psS', 0), 'biasn'], [('pTs', 16)])
            A('dve', (lambda E: E.tensor_tensor(out=pT[0][:32, 256:272], in0=pT[0][:32, 256:272], in1=msk, op=ALU.mult)),
              [('pTs', 16), 'masks'], [('pTs', 16)])
            for j in range(17):
                if j < 16:
                    va, vnm = v_fn(j)
                    pa = pT[0][:, j * 16:(j + 1) * 16]
                    oa = onesb[:, :]
                else:
                    va, vnm = vnew
                    pa = pT[0][:32, 256:272]
                    oa = onesb[:32, :]
                A('pe', (lambda E, va=va, pa=pa, j=j: E.matmul(psO[:, :16], lhsT=va, rhs=pa, start=(j == 0), stop=(j == 16))),
                  [('pTs', j)] + vnm, ['psO'], signal=False)
                A('pe', (lambda E, oa=oa, pa=pa, j=j: E.matmul(psD[:, :16], lhsT=oa, rhs=pa, start=(j == 0), stop=(j == 16))),
                  [('pTs', j), 'onesb'], ['psD'])
            A('dve', lambda E: E.reciprocal(out=rden[:, :16], in_=psD[:, :16]), ['psD'], ['rden'])
            A('dve', lambda E: E.tensor_tensor(out=osb[:, :16], in0=psO[:, :16], in1=rden[:, :16], op=ALU.mult),
              ['psO', 'rden'], ['osb'])
            A('dve', (lambda E: E.tensor_tensor(out=actT[:, out_chunk, TP + 16 * b:TP + 16 * b + 16], in0=osb[:, :16],
                                                in1=zrow, op=ALU.mult)), ['osb', 'zTs'], [('actT', 8)])

        for b in range(2):
            for hg in range(4):
                rows = slice((l * 2 + b) * PAST, (l * 2 + b + 1) * PAST)
                A('pool', (lambda E, rows=rows, hg=hg: E.dma_start(
                    out=kc_t, in_=cfk[rows, hg * 512:(hg + 1) * 512].rearrange("(j p) c -> p j c", p=128))),
                  [], ['kc_t'], kind='dma')
                A('pool', (lambda E, rows=rows, hg=hg: E.dma_start(
                    out=vc_t, in_=cfv[rows, hg * 512:(hg + 1) * 512].rearrange("(j p) c -> p j c", p=128))),
                  [], ['vc_t'], kind='dma')
                for hh in range(4):
                    for half in range(2):
                        pj = pt_i[0] % 2
                        pt_i[0] += 1
                        for bb in range(8):
                            A('pe', (lambda E, pj=pj, bb=bb, half=half, hh=hh: E.transpose(
                                psT[pj][:, bb * 128:(bb + 1) * 128], kc_t[:, half * 8 + bb, hh * 128:(hh + 1) * 128], identb[:, :])),
                              ['kc_t', 'identb'], [('psT', pj)], signal=(bb == 7))
                        A('act', (lambda E, pj=pj, half=half, hh=hh: E.copy(
                            out=kTc[:, hh, half * 1024:(half + 1) * 1024], in_=psT[pj][:, :])), [('psT', pj)], [('kTc', hh)])
                for hh in range(4):
                    h = hg * 4 + hh
                    qa = qTs[:, h, 16 * b:16 * b + 16]
                    sample_attn(
                        b,
                        (lambda j, hh=hh, qa=qa: [(kTc[:, hh, j * 128:(j + 1) * 128], qa, [('kTc', hh), 'qTs'])]),
                        (lambda j, hh=hh: (vc_t[:, j, hh * 128:(hh + 1) * 128], ['vc_t'])),
                        [(kTn[:, h, :], qa, ['kTn', 'qTs'])],
                        (vn[:32, h * 128:(h + 1) * 128], ['vn']),
                        (lambda j, b=b, h=h: biasc[:, j, b * 16 + h:b * 16 + h + 1]),
                        biasn[:32, b * 16 + h:b * 16 + h + 1], masks[:32, b * 16:(b + 1) * 16], FOX_SCALE,
                        zTs[:, h, 16 * b:16 * b + 16], h)
        T_.barrier()
        ph[0].close()
        if stop_after <= 3:
            break

        ph[0] = ExitStack()
        maskm = sbp(f"maskm{l}", [128, 2048], BF16); masks2 = sbp(f"masks2{l}", [32, 32], BF16)
        kpTg = sbp(f"kpTg{l}", [64, 2, TP], BF16)
        knT = sbp(f"knT{l}", [128, 2, 2048], BF16); vb = sbp(f"vb{l}", [128, 16, 256], BF16)
        vbn = sbp(f"vbn{l}", [32, 256], BF16); knTn = sbp(f"knTn{l}", [128, 2, 32], BF16)
        qnh = sbp(f"qnh{l}", [128, T], BF16); qph = sbp(f"qph{l}", [64, T], BF16); zTh2 = sbp(f"zTh2{l}", [128, T], BF16)
        pT = [sbp(f"pTm{l}_{j}", [128, 512], BF16) for j in range(2)]
        rden = sbp(f"rdenm{l}", [128, 512]); osb = sbp(f"osbm{l}", [128, 512])
        sq = sbp(f"sqm{l}", [128, 256]); wf2 = sbp(f"wf2{l}", [128, 256]); wb2 = sbp(f"wb2{l}", [128, 256], BF16)
        qnTs = sbp(f"qnTs{l}", [128, 16, TS], BF16); qpTs = sbp(f"qpTs{l}", [64, 16, TS], BF16)
        zTs = sbp(f"zTs2{l}", [128, 16, TS], BF16)
        wkv = wt[0][:, :, :].rearrange("p a b -> p (a b)").rearrange("p (kc n) -> p kc n", kc=4)
        ckTg = wt[1][:, 0:16, :].rearrange("p a b -> p (a b)").rearrange("p (kc r t) -> p kc r t", kc=4, r=2)
        ckc = wt[1][:, 16:32, :]
        dma('sp', maskm[:, :], c_maskm[:, :], [], ['maskm'])
        dma('sp', masks2[:, :], c_masks2[:, :], [], ['masks'])
        for kc in range(4):
            A('pool', (lambda E, kc=kc: E.dma_start(out=wkv[:, kc, :], in_=w_kvb[l * 512 + kc * 128: l * 512 + (kc + 1) * 128, :])),
              [], [('wkv', kc)], kind='dma')
            for r in range(2):
                dma('sp', ckTg[:, kc, r, :], ckg[l][r * 512 + kc * 128: r * 512 + (kc + 1) * 128, :], [('ckg', l, 0)], [('ckTg', kc, r)])
        dma('sp', kpTg[:, :, :], kpg[l].rearrange("(r d) t -> d r t", r=2), [('kpg', l, 0)], ['kpTg'])
        T_.barrier()

        def upproj(lhs_fn, lnames, M, hp, knT_dst_fn, v_dst, vnames, knames):
            pj = next_psA()
            for kc in range(4):
                lh = lhs_fn(kc)
                A('pe', (lambda E, kc=kc, pj=pj, lh=lh: E.matmul(psA[pj][:M, :512], lhsT=lh, rhs=wkv[:, kc, hp * 512:(hp + 1) * 512],
                                                         start=(kc == 0), stop=(kc == 3))),
                  list(lnames) + [('wkv', kc)], [('psA', pj)], signal=(kc == 3))
            ps3 = psA[pj][:M, :].rearrange("p (g c) -> p g c", g=2)
            dst3 = wf2[:M, :256].rearrange("p (g d) -> p g d", g=2)
            rms_groups(ps3[:, :, 0:128], M, 2, 128, gkn_t, dst3, [('psA', pj)], ['wf2'])
            A('act', lambda E: E.copy(out=wb2[:M, :256], in_=wf2[:M, :256]), ['wf2'], ['wb2'])
            A('dve', lambda E: E.tensor_copy(out=v_dst, in_=ps3[:, :, 128:256]), [('psA', pj)], vnames)
            transpose_to(wb2, M, 2, 128, knT_dst_fn, ['wb2'], (lambda b: knames))

        def mla_head_prompt(h, hh):
            dma('sp', qnh[:, :], qnx[h * 128:(h + 1) * 128, :], [('qnx', h // 2, tb) for tb in range(NTB)], ['qnh'])
            dma('sp', qph[:, :], qpx[h * 64:(h + 1) * 64, :], [('qpx', h // 2, tb) for tb in range(NTB)], ['qph'])
            dma('sp', zTh2[:, :], zbx[h * 128:(h + 1) * 128, :], [('zbx', h, ti) for ti in range(3)], ['zTh2'])
            for t in range(2):
                blocks = [(ip, r) for ip in range(4 * t + 4) for r in range(2)]
                for bi, (ip, r) in enumerate(blocks):
                    j = r * 8 + ip
                    c0 = max(0, ip - 4 * t) * 128
                    sj = bi % 2
                    A('pe', (lambda E, sj=sj, c0=c0, j=j, t=t: E.matmul(
                        psS[sj][:, c0:512], lhsT=knT[:, hh, j * 128:(j + 1) * 128],
                        rhs=qnh[:, t * 512 + c0:(t + 1) * 512], start=True, stop=False)),
                      [('knT', j), 'qnh'], [('psS', sj)], signal=False)
                    A('pe', (lambda E, sj=sj, c0=c0, r=r, ip=ip, t=t: E.matmul(
                        psS[sj][:, c0:512], lhsT=kpTg[:, r, ip * 128:(ip + 1) * 128],
                        rhs=qph[:, t * 512 + c0:(t + 1) * 512], start=False, stop=True)),
                      ['kpTg', 'qph'], [('psS', sj)])
                    A('act', (lambda E, sj=sj, c0=c0: E.activation(
                        out=pT[sj][:, c0:512], in_=psS[sj][:, c0:512], func=AF.Exp, scale=float(MLA_SCALE))),
                      [('psS', sj)], [('pT', sj, isub) for isub in range(c0 // 128, 4)])
                    for isub in range(c0 // 128, 4):
                        i = 4 * t + isub
                        if ip == i:
                            A('dve', (lambda E, sj=sj, isub=isub, i=i, r=r: E.tensor_tensor(
                                out=pT[sj][:, isub * 128:(isub + 1) * 128], in0=pT[sj][:, isub * 128:(isub + 1) * 128],
                                in1=maskm[:, (i * 2 + r) * 128:(i * 2 + r + 1) * 128], op=ALU.mult)),
                              [('pT', sj, isub), 'maskm'], [('pT', sj, isub)])
                    first = (bi == 0)
                    last = (bi == len(blocks) - 1)
                    pnames = [('pT', sj, isub) for isub in range(c0 // 128, 4)]

                    def tail_pv(sj=sj, c0=c0, j=j, first=first, last=last, pnames=pnames):
                        A('pe', (lambda E, sj=sj, c0=c0, j=j, first=first, last=last: E.matmul(
                            psO[:, c0:512], lhsT=vb[:, j, hh * 128:(hh + 1) * 128], rhs=pT[sj][:, c0:512],
                            start=first, stop=last)), pnames + [('vb', j)], ['psO'], signal=False)
                        A('pe', (lambda E, sj=sj, c0=c0, first=first, last=last: E.matmul(
                            psD[:, c0:512], lhsT=onesb[:, :], rhs=pT[sj][:, c0:512], start=first, stop=last)),
                          pnames + ['onesb'], ['psD'])
                    defer(tail_pv)
                flush()
                A('dve', lambda E: E.reciprocal(out=rden[:, :], in_=psD[:, :]), ['psD'], ['rden'])
                A('dve', lambda E: E.tensor_tensor(out=osb[:, :], in0=psO[:, :], in1=rden[:, :], op=ALU.mult),
                  ['psO', 'rden'], ['osb'])
                A('dve', (lambda E, t=t: E.tensor_tensor(
                    out=actT[:, 16 + h, t * 512:(t + 1) * 512], in0=osb[:, :], in1=zTh2[:, t * 512:(t + 1) * 512], op=ALU.mult)),
                  ['osb', 'zTh2'], [('actT', tb) for tb in range(4 * t, 4 * t + 4)])

        for hp in range(8):
            for j in range(16):
                r, ip = divmod(j, 8)
                upproj((lambda kc, r=r, ip=ip: ckTg[:, kc, r, ip * 128:(ip + 1) * 128]),
                       [('ckTg', kc, r) for kc in range(4)], 128, hp,
                       (lambda b, j=j: knT[:, b, j * 128:(j + 1) * 128]),
                       vb[:, j, :].rearrange("p (g d) -> p g d", g=2), [('vb', j)], [('knT', j)])
            for hh in range(2):
                mla_head_prompt(hp * 2 + hh, hh)
        T_.barrier()
        ckTc = wt[1][:, 0:16, :].rearrange("p a b -> p (a b)").rearrange("p (kc t) -> p kc t", kc=4)
        kpTc = kpTg[:, :, :].rearrange("d r t -> d (r t)")
        kpc = wt[0][:, 0:2, :].rearrange("p a b -> p (a b)").rearrange("p (j c) -> p j c", j=16)
        dma('sp', qnTs[:, :, :], qnx[:, TP:T].rearrange("(h d) t -> d h t", d=128), [('qnx', a, 8) for a in range(8)], ['qTs'])
        dma('sp', qpTs[:, :, :], qpx[:, TP:T].rearrange("(h d) t -> d h t", d=64), [('qpx', a, 8) for a in range(8)], ['qTs2'])
        dma('sp', zTs[:, :, :], zbx[:, TP:T].rearrange("(h d) t -> d h t", d=128), all_z('zbx'), ['zTs'])
        kpc = sq
        kpcb = sbp(f"kpcb{l}", [128, 16, 64], BF16)
        for b in range(2):
            rows = slice((l * 2 + b) * PAST, (l * 2 + b + 1) * PAST)
            A('pool', (lambda E, rows=rows: E.dma_start(out=ckc, in_=cck[rows, :].rearrange("(j p) c -> p j c", p=128))),
              [], ['ckc'], kind='dma')
            A('pool', (lambda E, rows=rows: E.dma_start(out=kpcb[:, :, :], in_=ckp[rows, :].rearrange("(j p) c -> p j c", p=128))),
              [], ['kpcb'], kind='dma')
            for j in range(16):
                pj = pt_i[0] % 2
                pt_i[0] += 1
                for kc in range(4):
                    A('pe', (lambda E, pj=pj, j=j, kc=kc: E.transpose(psT[pj][:, kc * 128:(kc + 1) * 128],
                                                                     ckc[:, j, kc * 128:(kc + 1) * 128], identb[:, :])),
                      ['ckc', 'identb'], [('psT', pj)], signal=(kc == 3))
                A('act', (lambda E, pj=pj, j=j: E.copy(out=ckTc[:, :, j * 128:(j + 1) * 128],
                                                       in_=psT[pj][:, :512].rearrange("p (kc t) -> p kc t", kc=4))),
                  [('psT', pj)], ['ckTc'])
            for half in range(2):
                pj = pt_i[0] % 2
                pt_i[0] += 1
                for bb in range(8):
                    A('pe', (lambda E, pj=pj, bb=bb, half=half: E.transpose(psT[pj][:64, bb * 128:(bb + 1) * 128],
                                                                           kpcb[:, half * 8 + bb, :], identb[:, :])),
                      ['kpcb', 'identb'], [('psT', pj)], signal=(bb == 7))
                A('act', (lambda E, pj=pj, half=half: E.copy(out=kpTc[:, half * 1024:(half + 1) * 1024], in_=psT[pj][:64, :])),
                  [('psT', pj)], ['kpTc'])
            for hp in range(8):
                for j in range(16):
                    upproj((lambda kc, j=j: ckTc[:, kc, j * 128:(j + 1) * 128]), ['ckTc'], 128, hp,
                           (lambda bq, j=j: knT[:, bq, j * 128:(j + 1) * 128]),
                           vb[:, j, :].rearrange("p (g d) -> p g d", g=2), [('vb', j)], [('knT', j)])
                upproj((lambda kc: ckTn[:, kc, :]), ['ckTn'], 32, hp, (lambda bq: knTn[:, bq, :]),
                       vbn[:32, :].rearrange("p (g d) -> p g d", g=2), ['vbn'], ['knTn'])
                for hh in range(2):
                    h = hp * 2 + hh
                    qn_ = qnTs[:, h, 16 * b:16 * b + 16]
                    qp_ = qpTs[:, h, 16 * b:16 * b + 16]
                    sample_attn(
                        b,
                        (lambda j, hh=hh, qn_=qn_, qp_=qp_: [(knT[:, hh, j * 128:(j + 1) * 128], qn_, [('knT', j), 'qTs']),
                                                            (kpTc[:, j * 128:(j + 1) * 128], qp_, ['kpTc', 'qTs2'])]),
                        (lambda j, hh=hh: (vb[:, j, hh * 128:(hh + 1) * 128], [('vb', j)])),
                        [(knTn[:, hh, :], qn_, ['knTn', 'qTs']), (kpTn[:, :], qp_, ['kpTn', 'qTs2'])],
                        (vbn[:32, hh * 128:(hh + 1) * 128], ['vbn']),
                        None, None, masks2[:32, b * 16:(b + 1) * 16], MLA_SCALE,
                        zTs[:, h, 16 * b:16 * b + 16], 16 + h)
        T_.barrier()
        ph[0].close()
        if stop_after <= 4:
            break

        ph[0] = ExitStack()
        npsA[0] = 6
        xr = [sbp(f"xr{l}_{j}", [128, 512]) for j in range(2)]
        yo = [sbp(f"yo{l}_{j}", [128, 512]) for j in range(2)]
        for cb in range(8):
            wj = load_w(w_out[l * D:(l + 1) * D, :], D, 512, cb * 512)
            for tb in range(NTB):
                pj, M = proj_tm(tb, [('actT', tb)], hT_lhs, 32, wj, 512)
                j = tb % 2
                cs = slice(cb * 512, (cb + 1) * 512)
                if l == 0:
                    xsrc = xp[tb * 128: tb * 128 + M, cs] if tb < 8 else xs[:, cs]
                else:
                    xsrc = ymid[tb * 128: tb * 128 + M, cs]
                dma('sp', xr[j][:M, :], xsrc, [], [('xr', j)])
                A('dve', lambda E, j=j, M=M, pj=pj: E.tensor_tensor(out=yo[j][:M, :], in0=psA[pj][:M, :], in1=xr[j][:M, :], op=ALU.add),
                  [('psA', pj), ('xr', j)], [('yo', j)])
                if l == 0:
                    dst = ymid[tb * 128: tb * 128 + M, cs]
                else:
                    dst = yp[tb * 128: tb * 128 + M, cs] if tb < 8 else ys[:, cs]
                dma('sp', dst, yo[j][:M, :], [('yo', j)], [])
        T_.barrier()
        ph[0].close()

    T_.final_wait('sp')
    sem_keys = T_.sem_keys()
    sems = {}
    for k in sem_keys:
        sems[k] = es.enter_context(nc.semaphore("s_" + "_".join(str(x) for x in k)))
    build_program.sem_map = {k: repr(v) for k, v in sems.items()}
    with nc.Block() as block:
        T_.emit(block, sems)
    es.close()
    return nc


def host_constants(r):
    bf = ml_dtypes.bfloat16
    c = {}
    c["c_identb"] = np.eye(128, dtype=np.float32).astype(bf)
    c["c_identf"] = np.eye(128, dtype=np.float32)
    c["c_onesb"] = np.ones((128, 128), np.float32).astype(bf)
    c["c_onesf"] = np.ones((128, 128), np.float32)
    c["c_trif"] = np.triu(np.ones((128, 128), np.float32))
    pos = np.concatenate([np.concatenate([gblock(r, i) * 128 + np.arange(128) for i in range(8)]),
                          PAST + np.arange(16), PAST + np.arange(16)]).astype(np.float32)
    half = 32
    inv_freq = (1.0 / (np.float32(10000.0) ** (np.arange(half, dtype=np.float32) / np.float32(half)))).astype(np.float32)
    ang = (pos[:, None] * inv_freq[None, :]).astype(np.float32)
    c["c_cos"] = np.cos(ang).astype(np.float32)
    c["c_sin"] = np.sin(ang).astype(np.float32)
    mf = np.zeros((128, 16, 128), np.float32)
    mm = np.zeros((128, 16, 128), np.float32)
    for i in range(8):
        qpos = gblock(r, i) * 128 + np.arange(128)
        for rk in range(2):
            kpos = gblock(rk, i) * 128 + np.arange(128)
            mf[:, i * 2 + rk, :] = (kpos[:, None] <= qpos[None, :])
            mm[:, i * 2 + rk, :] = ((kpos[:, None] // 64) <= (qpos[None, :] // 64))
    c["c_maskf"] = mf.reshape(128, 2048).astype(bf)
    c["c_maskm"] = mm.reshape(128, 2048).astype(bf)
    ms = np.zeros((32, 2, 16), np.float32)
    for b in range(2):
        for k in range(16):
            ms[b * 16 + k, b, :] = (k <= np.arange(16))
    c["c_masks"] = ms.reshape(32, 32).astype(bf)
    ts_ = np.zeros((32, 32), np.float32)
    for b in range(2):
        ts_[b * 16:(b + 1) * 16, b * 16:(b + 1) * 16] = np.triu(np.ones((16, 16), np.float32))
    c["c_tris"] = ts_
    ind = np.zeros((32, 2), np.float32)
    ind[:16, 0] = 1
    ind[16:, 1] = 1
    c["c_inds"] = ind
    c["c_indw"] = np.repeat(ind[:, :, None], 128, axis=2).reshape(32, 256).astype(np.float32)
    m2 = np.zeros((32, 2, 16), np.float32)
    m2[:16, 0, :] = 1
    m2[16:, 1, :] = 1
    c["c_masks2"] = m2.reshape(32, 32).astype(bf)
    return c


_NC_CACHE = {}


def make_in_maps(inp):
    f = lambda a: np.ascontiguousarray(np.asarray(a, dtype=np.float32))
    xpr = f(inp["x_prompt"]); xsm = f(inp["x_sample"])
    maps = []
    wshared = {
        "g_norm": f(inp["g_norm"]), "w_in": f(inp["w_in"]).reshape(NL * D, N_IN), "b_f": f(inp["b_f"]),
        "g_q_fox": f(inp["g_q_fox"]), "g_k_fox": f(inp["g_k_fox"]), "g_cq": f(inp["g_cq"]),
        "w_qb": f(inp["w_qb"]).reshape(NL * 1024, 3072), "g_qn": f(inp["g_qn"]), "g_qp": f(inp["g_qp"]),
        "g_ckv": f(inp["g_ckv"]), "g_kp": f(inp["g_kp"]), "w_kvb": f(inp["w_kvb"]).reshape(NL * 512, 4096),
        "g_kn": f(inp["g_kn"]), "w_out": f(inp["w_out"]).reshape(NL * D, D),
    }
    cfk = f(inp["cache_fox_k"]); cfv = f(inp["cache_fox_v"]); cfl = f(inp["cache_fox_logf"])
    cck = f(inp["cache_mla_ckv"]); ckp = f(inp["cache_mla_kpe"])
    for c in range(8):
        p, r = c // 2, c % 2
        m = dict(wshared)
        m["xp"] = np.concatenate([xpr[p, gblock(r, i) * 128:(gblock(r, i) + 1) * 128] for i in range(8)], axis=0)
        m["xs"] = xsm[2 * c:2 * c + 2].reshape(TS, D)
        m["cfk"] = cfk[:, 2 * c:2 * c + 2].reshape(NL * 2 * PAST, 2048)
        m["cfv"] = cfv[:, 2 * c:2 * c + 2].reshape(NL * 2 * PAST, 2048)
        m["cfl"] = cfl[:, 2 * c:2 * c + 2].reshape(NL * 2 * PAST, 16)
        m["cck"] = cck[:, 2 * c:2 * c + 2].reshape(NL * 2 * PAST, 512)
        m["ckp"] = ckp[:, 2 * c:2 * c + 2].reshape(NL * 2 * PAST, 64)
        m.update(host_constants(r))
        maps.append(m)
    return maps


def assemble(results):
    y_p = np.zeros((4, 2048, D), np.float32); y_s = np.zeros((16, 16, D), np.float32)
    fk_p = np.zeros((NL, 4, 2048, H, 128), np.float32); fv_p = np.zeros_like(fk_p)
    fl_p = np.zeros((NL, 4, 2048, H), np.float32); ck_p = np.zeros((NL, 4, 2048, 512), np.float32)
    kp_p = np.zeros((NL, 4, 2048, 64), np.float32)
    fk_s = np.zeros((NL, 16, 16, H, 128), np.float32); fv_s = np.zeros_like(fk_s)
    fl_s = np.zeros((NL, 16, 16, H), np.float32); ck_s = np.zeros((NL, 16, 16, 512), np.float32)
    kp_s = np.zeros((NL, 16, 16, 64), np.float32)
    for c in range(8):
        p, r = c // 2, c % 2
        R = results[c]
        for i in range(8):
            g = gblock(r, i)
            sl = slice(g * 128, (g + 1) * 128)
            y_p[p, sl] = R["yp"][i * 128:(i + 1) * 128]
            for l in range(NL):
                rows = slice(l * TP + i * 128, l * TP + (i + 1) * 128)
                fk_p[l, p, sl] = R["nfk_p"][rows].reshape(128, H, 128)
                fv_p[l, p, sl] = R["nfv_p"][rows].reshape(128, H, 128)
                fl_p[l, p, sl] = R["nfl_p"][rows]
                ck_p[l, p, sl] = R["nck_p"][rows]
                kp_p[l, p, sl] = R["nkp_p"][rows]
        y_s[2 * c:2 * c + 2] = R["ys"].reshape(2, 16, D)
        for l in range(NL):
            rows = slice(l * TS, (l + 1) * TS)
            fk_s[l, 2 * c:2 * c + 2] = R["nfk_s"][rows].reshape(2, 16, H, 128)
            fv_s[l, 2 * c:2 * c + 2] = R["nfv_s"][rows].reshape(2, 16, H, 128)
            fl_s[l, 2 * c:2 * c + 2] = R["nfl_s"][rows].reshape(2, 16, H)
            ck_s[l, 2 * c:2 * c + 2] = R["nck_s"][rows].reshape(2, 16, 512)
            kp_s[l, 2 * c:2 * c + 2] = R["nkp_s"][rows].reshape(2, 16, 64)
    return (y_p, y_s, fk_p, fv_p, fl_p, ck_p, kp_p, fk_s, fv_s, fl_s, ck_s, kp_s)


def kernel(**inputs):
    nc = build_program()
    maps = make_in_maps(inputs)
    res = run_bass_kernel_spmd(nc, maps, core_ids=list(range(8)))
    return assemble(res.results)
```

```python
import numpy as np
import ml_dtypes
from contextlib import ExitStack
import concourse.bass as bass
import concourse.mybir as mybir
from concourse.bass_utils import run_bass_kernel_spmd

F32 = mybir.dt.float32
BF16 = mybir.dt.bfloat16
ALU = mybir.AluOpType
AF = mybir.ActivationFunctionType
AX = mybir.AxisListType

D = 4096
NL = 2
H = 16
TP = 1024
TS = 32
T = TP + TS
NTB = 9
PAST = 2048
EPS = 1e-6
N_IN = 11856
OFF_Q, OFF_K, OFF_V, OFF_F, OFF_ZA, OFF_CQ, OFF_CKV, OFF_KPE, OFF_ZB = (
    0, 2048, 4096, 6144, 6160, 8208, 9232, 9744, 9808)
FOX_SCALE = 1.0 / np.sqrt(128.0)
MLA_SCALE = 1.0 / np.sqrt(192.0)


def gblock(r, i):
    return 2 * i + (i % 2) if r == 0 else 2 * i + 1 - (i % 2)


def tb_rows(tb):
    return 128 if tb < 8 else TS


def _freeze(fn):
    import types
    if fn is None or fn.__closure__ is None:
        return fn
    cells = []
    for c in fn.__closure__:
        try:
            cells.append(types.CellType(c.cell_contents))
        except ValueError:
            cells.append(c)
    return types.FunctionType(fn.__code__, fn.__globals__, fn.__name__, fn.__defaults__, tuple(cells))


class Tr:
    ENG = ('pe', 'act', 'dve', 'pool', 'sp')
    NDS = 20

    def __init__(s):
        s.ops = {e: [] for e in s.ENG}
        s.cnt = {e: 0 for e in s.ENG}
        s.dcnt = {e: 0 for e in s.ENG}
        s.lw = {}
        s.rd = {}
        s.waited = {e: {} for e in s.ENG}
        s.semval = {}
        s.ccn = 0
        s.maxops = None

    def _need(s, eng, deps):
        w = []
        for (k, v) in deps:
            if k == ('c', 'pe') and eng == 'pe':
                continue
            if s.waited[eng].get(k, 0) < v:
                s.waited[eng][k] = v
                w.append((k, v))
        return w

    def add(s, eng, fn, reads=(), writes=(), kind='c', signal=True):
        s.nadd = getattr(s, 'nadd', 0) + 1
        if s.maxops is not None and s.nadd > s.maxops:
            return None
        deps = set()
        for b in reads:
            if b in s.lw:
                deps.add(s.lw[b])
            if (isinstance(b, tuple) and b[0] in ('psA', 'psT', 'psS')) or b in ('psO', 'psD'):
                for t in s.rd.get(b, ()):
                    if t[0] != ('c', eng):
                        deps.add(t)
        for b in writes:
            if b in s.lw and s.lw[b][0] != ('c', eng):
                deps.add(s.lw[b])
            for t in s.rd.get(b, ()):
                if t[0] != ('c', eng):
                    deps.add(t)
        if kind == 'dma':
            k = s.dcnt[eng]
            s.dcnt[eng] += 1
            slot = k % s.NDS
            m = k // s.NDS + 1
            key = ('d', eng, slot)
            if m > 1:
                deps.add((key, 16 * (m - 1)))
            tok = (key, 16 * m)
        elif kind == 'cc':
            key = ('cc',)
            s.ccn += 1
            tok = (key, s.ccn)
        else:
            key = ('c', eng)
            if signal:
                s.cnt[eng] += 1
                tok = (key, s.cnt[eng])
            else:
                tok = (key, s.cnt[eng] + 1)
        s.semval[key] = max(s.semval.get(key, 0), tok[1])
        waits = s._need(eng, sorted(deps, key=str))
        for b in writes:
            s.lw[b] = tok
            s.rd[b] = []
        for b in reads:
            s.rd.setdefault(b, []).append(tok)
        s.ops[eng].append((waits, _freeze(fn), kind, signal, key))
        return tok

    def _allvals(s):
        d = dict(s.semval)
        for e in s.ENG:
            if ('c', e) in d:
                d[('c', e)] = s.cnt[e]
        return sorted([(k, v) for k, v in d.items() if v > 0], key=str)

    def barrier(s):
        if s.maxops is not None and getattr(s, 'nadd', 0) > s.maxops:
            return
        deps = s._allvals()
        for e in s.ENG:
            w = s._need(e, deps)
            if w:
                s.ops[e].append((w, None, 'c', False, None))

    def final_wait(s, eng='sp'):
        deps = s._allvals()
        w = s._need(eng, deps)
        if w:
            s.ops[eng].append((w, None, 'c', False, None))

    def sem_keys(s):
        return list(s.semval.keys())

    def emit(s, block, sems):
        def run(name):
            def f(E):
                for waits, fn, kind, signal, key in s.ops[name]:
                    for (k, v) in waits:
                        E.wait_ge(sems[k], v)
                    if fn is None:
                        continue
                    ins = fn(E)
                    if kind == 'dma':
                        ins.then_inc(sems[key], 16)
                    elif kind == 'cc':
                        ins.then_inc(sems[key])
                    elif signal:
                        ins.then_inc(sems[key], 1)
            return f
        block.tensor(run('pe'))
        block.scalar(run('act'))
        block.vector(run('dve'))
        block.gpsimd(run('pool'))
        block.sync(run('sp'))


def build_program(stop_after=99, n_cores=8, maxops=None, lite=()):
    nc = bass.Bass("TRN2", target_bir_lowering=False)
    T_ = Tr()
    T_.maxops = maxops

    def din(name, shape, dt=F32):
        if name in lite:
            shape = [8, shape[1]]
        return nc.dram_tensor(name, list(shape), dt, kind="ExternalInput").ap()

    def dout(name, shape, dt=F32):
        return nc.dram_tensor(name, list(shape), dt, kind="ExternalOutput").ap()

    def dint(name, shape, dt=F32):
        return nc.dram_tensor(name, list(shape), dt).ap()

    xp = din("xp", [TP, D]); xs = din("xs", [TS, D])
    cfk = din("cfk", [NL * 2 * PAST, 2048]); cfv = din("cfv", [NL * 2 * PAST, 2048])
    cfl = din("cfl", [NL * 2 * PAST, 16]); cck = din("cck", [NL * 2 * PAST, 512])
    ckp = din("ckp", [NL * 2 * PAST, 64])
    g_norm = din("g_norm", [NL, D]); w_in = din("w_in", [NL * D, N_IN])
    b_f = din("b_f", [NL, 16]); g_qf = din("g_q_fox", [NL, 128]); g_kf = din("g_k_fox", [NL, 128])
    g_cq = din("g_cq", [NL, 1024]); w_qb = din("w_qb", [NL * 1024, 3072])
    g_qn = din("g_qn", [NL, 128]); g_qp = din("g_qp", [NL, 64]); g_ckv = din("g_ckv", [NL, 512])
    g_kp = din("g_kp", [NL, 64]); w_kvb = din("w_kvb", [NL * 512, 4096]); g_kn = din("g_kn", [NL, 128])
    w_out = din("w_out", [NL * D, D])
    c_identb = din("c_identb", [128, 128], BF16); c_identf = din("c_identf", [128, 128])
    c_onesb = din("c_onesb", [128, 128], BF16); c_onesf = din("c_onesf", [128, 128])
    c_trif = din("c_trif", [128, 128])
    c_cos = din("c_cos", [T, 32]); c_sin = din("c_sin", [T, 32])
    c_maskf = din("c_maskf", [128, 16 * 128], BF16); c_maskm = din("c_maskm", [128, 16 * 128], BF16)
    c_masks = din("c_masks", [32, 32], BF16)
    c_tris = din("c_tris", [32, 32]); c_inds = din("c_inds", [32, 2]); c_indw = din("c_indw", [32, 256]); c_masks2 = din("c_masks2", [32, 32], BF16)

    yp = dout("yp", [TP, D]); ys = dout("ys", [TS, D])
    nfk_p = dout("nfk_p", [NL * TP, 2048]); nfv_p = dout("nfv_p", [NL * TP, 2048])
    nfl_p = dout("nfl_p", [NL * TP, 16]); nck_p = dout("nck_p", [NL * TP, 512]); nkp_p = dout("nkp_p", [NL * TP, 64])
    nfk_s = dout("nfk_s", [NL * TS, 2048]); nfv_s = dout("nfv_s", [NL * TS, 2048])
    nfl_s = dout("nfl_s", [NL * TS, 16]); nck_s = dout("nck_s", [NL * TS, 512]); nkp_s = dout("nkp_s", [NL * TS, 64])

    ymid = dint("ymid", [T, D])
    kTx = [dint(f"kTx{l}", [2048, TP], BF16) for l in range(NL)]
    vx = [dint(f"vx{l}", [TP, 2048], BF16) for l in range(NL)]
    lfx = [dint(f"lfx{l}", [TP, 16]) for l in range(NL)]
    ckx = [dint(f"ckx{l}", [512, TP], BF16) for l in range(NL)]
    kpx = [dint(f"kpx{l}", [64, TP], BF16) for l in range(NL)]
    kTg = [dint(f"kTg{l}", [2 * 2048, TP], BF16) for l in range(NL)]
    vg = [dint(f"vg{l}", [2 * TP, 2048], BF16) for l in range(NL)]
    lfg = [dint(f"lfg{l}", [2 * TP, 16]) for l in range(NL)]
    ckg = [dint(f"ckg{l}", [2 * 512, TP], BF16) for l in range(NL)]
    kpg = [dint(f"kpg{l}", [2 * 64, TP], BF16) for l in range(NL)]
    qfx = dint("qfx", [2048, T], BF16); qnx = dint("qnx", [2048, T], BF16)
    qpx = dint("qpx", [1024, T], BF16); zax = dint("zax", [2048, T], BF16); zbx = dint("zbx", [2048, T], BF16)

    es = ExitStack()
    ph = [None]

    def sb(name, shape, dt=F32):
        return es.enter_context(nc.sbuf_tensor(name, list(shape), dt))

    def sbp(name, shape, dt=F32):
        return ph[0].enter_context(nc.sbuf_tensor(name, list(shape), dt))

    def pst(name, shape, dt=F32):
        return es.enter_context(nc.psum_tensor(name, list(shape), dt))

    A = T_.add

    identb = sb("identb", [128, 128], BF16); identf = sb("identf", [128, 128])
    onesb = sb("onesb", [128, 128], BF16); onesf = sb("onesf", [128, 128]); trif = sb("trif", [128, 128])
    cosT = sb("cosT", [128, NTB, 32]); sinT = sb("sinT", [128, NTB, 32])
    masks = sb("masks", [32, 32], BF16); tris = sb("tris", [32, 32]); inds = sb("inds", [32, 2])
    actT = sb("actT", [128, 32, T], BF16)
    wt = [sb(f"wt{j}", [128, 32, 512], BF16) for j in range(2)]
    gcol = sb("gcol", [128, 32]); g32 = sb("g32", [32, 128]); gcq_t = sb("gcq_t", [128, 1024]); gckv_t = sb("gckv_t", [128, 512])
    gqf_t = sb("gqf_t", [128, 128]); gkf_t = sb("gkf_t", [128, 128]); gqn_t = sb("gqn_t", [128, 128])
    gkn_t = sb("gkn_t", [128, 128]); gqp_t = sb("gqp_t", [128, 64]); gkp_t = sb("gkp_t", [128, 64])
    bf_t = sb("bf_t", [128, 16])
    kTn = sb("kTn", [128, H, TS], BF16)
    vn = sb("vn", [TS, 2048], BF16)
    lfn = sb("lfn", [TS, 16])
    ckTn = sb("ckTn", [128, 4, TS], BF16)
    kpTn = sb("kpTn", [64, TS], BF16)
    ss = sb("ss", [128, 16]); rs = sb("rs", [128, 16])
    ssr = [sb(f"ssr{k}", [128, 16]) for k in range(4)]; rsr = [sb(f"rsr{k}", [128, 16]) for k in range(4)]
    rms_i = [0]
    rt = sb("rt", [128, 64]); rt2 = sb("rt2", [128, 64])

    psA = [pst(f"psA{j}", [128, 512]) for j in range(2)]
    psT = [pst(f"psT{j}", [128, 1024], BF16) for j in range(2)]
    psS = [pst(f"psS{j}", [128, 512]) for j in range(2)]
    psO = pst("psO", [128, 512]); psD = pst("psD", [128, 512])
    psA = psA + [psS[0], psS[1], psO, psD]
    npsA = [6]

    def dma(eng, out, in_, reads, writes):
        return A(eng, lambda E: E.dma_start(out=out, in_=in_), reads, writes, kind='dma')

    for (dst, src, nm) in [(identb, c_identb, 'identb'), (identf, c_identf, 'identf'), (onesb, c_onesb, 'onesb'),
                           (onesf, c_onesf, 'onesf'), (trif, c_trif, 'trif'), (masks, c_masks, 'masks'), (tris, c_tris, 'tris'),
                           (inds, c_inds, 'inds')]:
        dma('sp', dst[:], src[:, :], [], [nm])
    for tb in range(NTB):
        M = tb_rows(tb)
        dma('sp', cosT[:M, tb, :], c_cos[tb * 128: tb * 128 + M, :], [], [('cos', tb)])
        dma('sp', sinT[:M, tb, :], c_sin[tb * 128: tb * 128 + M, :], [], [('sin', tb)])

    pa_i = [0]

    def next_psA():
        j = pa_i[0] % npsA[0]
        pa_i[0] += 1
        return j

    wt_i = [0]

    def load_w(src2d, K, W, c0):
        j = wt_i[0] % 2
        wt_i[0] += 1
        KC = K // 128
        step = 8
        for k0 in range(0, KC, step):
            k1 = min(KC, k0 + step)
            src = src2d[k0 * 128:k1 * 128, c0:c0 + W].rearrange("(kc p) n -> p kc n", p=128)
            A('pool', (lambda E, o=wt[j][:, k0:k1, :W], i=src: E.dma_start(out=o, in_=i)),
              [], [('wt', j, k0)], kind='dma')
        return j

    def proj_tm(tb, actT_names, lhs_fn, KC, wj, W, wcol0=0):
        M = tb_rows(tb)
        pj = next_psA()
        for kc in range(KC):
            lh = lhs_fn(kc, tb, M)
            A('pe', (lambda E, kc=kc, lh=lh: E.matmul(psA[pj][:M, :W], lhsT=lh,
                                              rhs=wt[wj][:, kc, wcol0:wcol0 + W],
                                              start=(kc == 0), stop=(kc == KC - 1))),
              list(actT_names) + [('wt', wj, (kc // 8) * 8)], [('psA', pj)], signal=(kc == KC - 1))
        return pj, M

    def hT_lhs(kc, tb, M):
        return actT[:, kc, tb * 128: tb * 128 + M]

    def rms_groups(src3, M, G, Dg, gain_t, dst3, src_names, dst_names):
        sq3 = sq[:M, :G * Dg].rearrange("p (g d) -> p g d", g=G)
        k_ = rms_i[0] % 4
        rms_i[0] += 1
        ss_, rs_, sn, rn = ssr[k_], rsr[k_], ('ss', k_), ('rs', k_)
        A('act', lambda E: E.activation(out=sq3, in_=src3, func=AF.Square), src_names, ['sq'])
        A('dve', lambda E: E.tensor_reduce(out=ss_[:M, :G], in_=sq3, axis=AX.X, op=ALU.add), ['sq'], [sn])
        A('dve', lambda E: E.tensor_scalar(out=ss_[:M, :G], in0=ss_[:M, :G], scalar1=1.0 / Dg, scalar2=EPS,
                                           op0=ALU.mult, op1=ALU.add), [sn], [sn])
        A('act', lambda E: E.activation(out=ss_[:M, :G], in_=ss_[:M, :G], func=AF.Sqrt), [sn], [sn])
        A('dve', lambda E: E.reciprocal(out=rs_[:M, :G], in_=ss_[:M, :G]), [sn], [rn])
        A('dve', lambda E: E.tensor_tensor(out=dst3, in0=src3,
                                           in1=rs_[:M, :G].unsqueeze(2).to_broadcast([M, G, Dg]), op=ALU.mult),
          list(src_names) + [rn], dst_names)
        A('dve', lambda E: E.tensor_tensor(out=dst3, in0=dst3,
                                           in1=gain_t[:M, :Dg].unsqueeze(1).to_broadcast([M, G, Dg]), op=ALU.mult),
          list(dst_names) + ['gains'], dst_names)

    pt_i = [0]

    def transpose_to(src_bf, M, nblk, blkw, dst_fn, src_names, dst_names_fn, evac='act', scale_fn=None):
        pj = pt_i[0] % 2
        pt_i[0] += 1
        for b in range(nblk):
            A('pe', (lambda E, b=b: E.transpose(psT[pj][:blkw, b * 128: b * 128 + M],
                                                src_bf[:M, b * blkw:(b + 1) * blkw], identb[:M, :M])),
              list(src_names) + ['identb'], [('psT', pj)], signal=(b == nblk - 1))
        for b in range(nblk):
            d_ = dst_fn(b)
            if scale_fn is not None:
                sc_ = scale_fn(b)
                A('act', (lambda E, b=b, d_=d_, sc_=sc_: E.activation(out=d_, in_=psT[pj][:blkw, b * 128: b * 128 + M],
                                                      func=AF.Copy, scale=sc_)),
                  [('psT', pj), 'gcol'], dst_names_fn(b))
            else:
                A(evac, (lambda E, b=b, d_=d_: (E.copy if evac == 'act' else E.tensor_copy)(
                    out=d_, in_=psT[pj][:blkw, b * 128: b * 128 + M])),
                  [('psT', pj)], dst_names_fn(b))

    def rope(x3, M, G, tb, tmp_a, tmp_b, names):
        c = cosT[:M, tb, :].unsqueeze(1).to_broadcast([M, G, 32])
        s_ = sinT[:M, tb, :].unsqueeze(1).to_broadcast([M, G, 32])
        x1 = x3[:, :, 0:32]
        x2 = x3[:, :, 32:64]
        ta = tmp_a[:M, :G * 32].rearrange("p (g d) -> p g d", g=G)
        tb_ = tmp_b[:M, :G * 32].rearrange("p (g d) -> p g d", g=G)
        rn = list(names) + [('cos', tb), ('sin', tb)]
        A('dve', lambda E: E.tensor_tensor(out=ta, in0=x1, in1=s_, op=ALU.mult), rn, ['rt'])
        A('dve', lambda E: E.tensor_tensor(out=tb_, in0=x2, in1=s_, op=ALU.mult), rn, ['rt2'])
        A('dve', lambda E: E.tensor_tensor(out=x1, in0=x1, in1=c, op=ALU.mult), rn, names)
        A('dve', lambda E: E.tensor_tensor(out=x2, in0=x2, in1=c, op=ALU.mult), rn, names)
        A('dve', lambda E: E.tensor_tensor(out=x1, in0=x1, in1=tb_, op=ALU.subtract), list(names) + ['rt2'], names)
        A('dve', lambda E: E.tensor_tensor(out=x2, in0=x2, in1=ta, op=ALU.add), list(names) + ['rt'], names)

    pend = []

    def defer(f):
        pend.append(f)
        while len(pend) > 1:
            pend.pop(0)()

    def flush():
        while pend:
            pend.pop(0)()

    def out_rows(l, tb, M, pbuf, sbuf_, width, c0=0):
        if tb < 8:
            return pbuf[l * TP + tb * 128: l * TP + tb * 128 + M, c0:c0 + width]
        return sbuf_[l * TS: l * TS + M, c0:c0 + width]

    for l in range(NL):
        T_.barrier()
        ph[0] = ExitStack()
        xt = [sbp(f"xt{l}", [128, D])]
        hb = sbp(f"hb{l}", [128, D], BF16)
        sq = sbp(f"sq{l}", [128, 1024])
        wf = [sbp(f"wf{l}_{j}", [128, 512]) for j in range(2)]
        wb = [sbp(f"wb{l}_{j}", [128, 512], BF16) for j in range(2)]
        tsb = [sbp(f"tsb{l}_{j}", [128, 512], BF16) for j in range(2)]
        cqT = sbp(f"cqT{l}", [128, 8, T], BF16)
        dma('sp', g32[:, :], g_norm[l, :].rearrange("(c p) -> c p", p=128), [], ['g32'])
        A('pe', lambda E: E.transpose(psS[0][:, :32], g32[:32, :], identf[:32, :32]), ['g32', 'identf'], [('psS', 0)])
        A('act', lambda E: E.copy(out=gcol[:, :], in_=psS[0][:, :32]), [('psS', 0)], ['gcol'])
        for (dst, src, w) in [(gcq_t, g_cq, 1024), (gckv_t, g_ckv, 512), (gqf_t, g_qf, 128),
                              (gkf_t, g_kf, 128), (gqn_t, g_qn, 128), (gkn_t, g_kn, 128), (gqp_t, g_qp, 64),
                              (gkp_t, g_kp, 64), (bf_t, b_f, 16)]:
            dma('sp', dst[:, :w], src[l, :].partition_broadcast(128), [], ['gains'])
        T_.barrier()

        for tb in range(NTB):
            M = tb_rows(tb)
            j = 0
            if l == 0:
                src = xp[tb * 128: tb * 128 + M, :] if tb < 8 else xs[:, :]
            else:
                src = ymid[tb * 128: tb * 128 + M, :]
            dma('sp', xt[j][:M, :], src, [('ymid', tb)], [('xt', j)])
            A('dve', lambda E, j=j, M=M: E.scalar_tensor_tensor(out=hb[:M, :], in0=xt[j][:M, :], scalar=1.0,
                                                               in1=xt[j][:M, :], op0=ALU.mult, op1=ALU.mult,
                                                               accum_out=ss[:M, 0:1]),
              [('xt', j)], ['hb', 'ss'])
            A('dve', lambda E, M=M: E.tensor_scalar(out=ss[:M, 0:1], in0=ss[:M, 0:1], scalar1=1.0 / D, scalar2=EPS,
                                                    op0=ALU.mult, op1=ALU.add), ['ss'], ['ss'])
            A('act', lambda E, M=M: E.activation(out=ss[:M, 0:1], in_=ss[:M, 0:1], func=AF.Sqrt), ['ss'], ['ss'])
            A('dve', lambda E, M=M: E.reciprocal(out=rs[:M, 0:1], in_=ss[:M, 0:1]), ['ss'], ['rs'])
            A('dve', lambda E, j=j, M=M: E.tensor_scalar(out=hb[:M, :], in0=xt[j][:M, :], scalar1=rs[:M, 0:1],
                                                        scalar2=None, op0=ALU.mult),
              [('xt', j), 'rs'], ['hb'])
            for g4 in range(4):
                transpose_to(hb[:, g4 * 1024:(g4 + 1) * 1024], M, 8, 128,
                             (lambda b, g4=g4, tb=tb, M=M: actT[:, g4 * 8 + b, tb * 128: tb * 128 + M]),
                             ['hb'], (lambda b, tb=tb: [('actT', tb)]), evac='act',
                             scale_fn=(lambda b, g4=g4: gcol[:, g4 * 8 + b: g4 * 8 + b + 1]))
        if stop_after <= 0:
            T_.barrier()
            ph[0].close()
            break

        hn = lambda tb: [('actT', tb)]
        for cb in range(4):
            wj = load_w(w_in[l * D:(l + 1) * D, :], D, 512, OFF_K + cb * 512)
            for tb in range(NTB):
                pj, M = proj_tm(tb, hn(tb), hT_lhs, 32, wj, 512)
                j = tb % 2
                src3 = psA[pj][:M, :].rearrange("p (g d) -> p g d", g=4)
                dst3 = wf[j][:M, :512].rearrange("p (g d) -> p g d", g=4)
                rms_groups(src3, M, 4, 128, gkf_t, dst3, [('psA', pj)], [('wf', j)])
                dma('sp', out_rows(l, tb, M, nfk_p, nfk_s, 512, cb * 512), wf[j][:M, :512], [('wf', j)], [])
                A('act', lambda E, j=j, M=M: E.copy(out=wb[j][:M, :512], in_=wf[j][:M, :512]), [('wf', j)], [('wb', j)])
                def tail_k(tb=tb, j=j, M=M, cb=cb):
                    if tb < 8:
                        transpose_to(wb[j], M, 4, 128, (lambda b, j=j, M=M: tsb[j][:, b * 128: b * 128 + M]),
                                     [('wb', j)], (lambda b, j=j: [('tsb', j)]))
                        dma('sp', kTx[l][cb * 512:(cb + 1) * 512, tb * 128:(tb + 1) * 128].rearrange("(g d) t -> d g t", g=4),
                            tsb[j][:, :512].rearrange("d (g t) -> d g t", g=4), [('tsb', j)], [('kTx', l)])
                    else:
                        transpose_to(wb[j], M, 4, 128, (lambda b, cb=cb, M=M: kTn[:, cb * 4 + b, :M]),
                                     [('wb', j)], (lambda b: ['kTn']))
                defer(tail_k)
            flush()
        for cb in range(4):
            wj = load_w(w_in[l * D:(l + 1) * D, :], D, 512, OFF_V + cb * 512)
            for tb in range(NTB):
                pj, M = proj_tm(tb, hn(tb), hT_lhs, 32, wj, 512)
                j = tb % 2
                A('act', lambda E, j=j, M=M, pj=pj: E.copy(out=wf[j][:M, :512], in_=psA[pj][:M, :]),
                  [('psA', pj)], [('wf', j)])
                dma('sp', out_rows(l, tb, M, nfv_p, nfv_s, 512, cb * 512), wf[j][:M, :512], [('wf', j)], [])
                if tb < 8:
                    A('dve', lambda E, j=j, M=M, pj=pj: E.tensor_copy(out=wb[j][:M, :512], in_=psA[pj][:M, :]),
                      [('psA', pj)], [('wb', j)])
                    dma('sp', vx[l][tb * 128:(tb + 1) * 128, cb * 512:(cb + 1) * 512], wb[j][:M, :512],
                        [('wb', j)], [('vx', l)])
                else:
                    A('dve', lambda E, M=M, pj=pj, cb=cb: E.tensor_copy(out=vn[:M, cb * 512:(cb + 1) * 512],
                                                                        in_=psA[pj][:M, :]),
                      [('psA', pj)], ['vn'])
        wj = load_w(w_in[l * D:(l + 1) * D, :], D, 16, OFF_F)
        for tb in range(NTB):
            pj, M = proj_tm(tb, hn(tb), hT_lhs, 32, wj, 16)
            j = tb % 2
            A('dve', lambda E, j=j, M=M, pj=pj: E.tensor_tensor(out=wf[j][:M, :16], in0=psA[pj][:M, :16],
                                                               in1=bf_t[:M, :16], op=ALU.add),
              [('psA', pj), 'gains'], [('wf', j)])
            A('act', lambda E, j=j, M=M: E.activation(out=wf[j][:M, :16], in_=wf[j][:M, :16], func=AF.Exp, scale=-1.0),
              [('wf', j)], [('wf', j)])
            A('act', lambda E, j=j, M=M: E.activation(out=wf[j][:M, :16], in_=wf[j][:M, :16], func=AF.Ln, bias=1.0),
              [('wf', j)], [('wf', j)])
            if tb < 8:
                A('dve', lambda E, j=j, M=M: E.tensor_scalar(out=wf[j][:M, :16], in0=wf[j][:M, :16], scalar1=-1.0,
                                                            scalar2=None, op0=ALU.mult), [('wf', j)], [('wf', j)])
                dma('sp', out_rows(l, tb, M, nfl_p, nfl_s, 16), wf[j][:M, :16], [('wf', j)], [])
                dma('sp', lfx[l][tb * 128:(tb + 1) * 128, :], wf[j][:M, :16], [('wf', j)], [('lfx', l)])
            else:
                A('dve', lambda E, j=j, M=M: E.tensor_scalar(out=lfn[:M, :16], in0=wf[j][:M, :16], scalar1=-1.0,
                                                            scalar2=None, op0=ALU.mult), [('wf', j)], ['lfn'])
                dma('sp', out_rows(l, tb, M, nfl_p, nfl_s, 16), lfn[:M, :16], ['lfn'], [])
        wj = load_w(w_in[l * D:(l + 1) * D, :], D, 512, OFF_CKV)
        for tb in range(NTB):
            pj, M = proj_tm(tb, hn(tb), hT_lhs, 32, wj, 512)
            j = tb % 2
            src3 = psA[pj][:M, :].rearrange("p (g d) -> p g d", g=1)
            dst3 = wf[j][:M, :512].rearrange("p (g d) -> p g d", g=1)
            rms_groups(src3, M, 1, 512, gckv_t, dst3, [('psA', pj)], [('wf', j)])
            dma('sp', out_rows(l, tb, M, nck_p, nck_s, 512), wf[j][:M, :512], [('wf', j)], [])
            A('act', lambda E, j=j, M=M: E.copy(out=wb[j][:M, :512], in_=wf[j][:M, :512]), [('wf', j)], [('wb', j)])
            if tb < 8:
                transpose_to(wb[j], M, 4, 128, (lambda b, j=j, M=M: tsb[j][:, b * 128: b * 128 + M]),
                             [('wb', j)], (lambda b, j=j: [('tsb', j)]))
                dma('sp', ckx[l][:, tb * 128:(tb + 1) * 128].rearrange("(g d) t -> d g t", g=4),
                    tsb[j][:, :512].rearrange("d (g t) -> d g t", g=4), [('tsb', j)], [('ckx', l)])
            else:
                transpose_to(wb[j], M, 4, 128, (lambda b, M=M: ckTn[:, b, :M]), [('wb', j)], (lambda b: ['ckTn']))
        wj = load_w(w_in[l * D:(l + 1) * D, :], D, 64, OFF_KPE)
        for tb in range(NTB):
            pj, M = proj_tm(tb, hn(tb), hT_lhs, 32, wj, 64)
            j = tb % 2
            src3 = psA[pj][:M, :64].rearrange("p (g d) -> p g d", g=1)
            dst3 = wf[j][:M, :64].rearrange("p (g d) -> p g d", g=1)
            rms_groups(src3, M, 1, 64, gkp_t, dst3, [('psA', pj)], [('wf', j)])
            rope(dst3, M, 1, tb, rt, rt2, [('wf', j)])
            dma('sp', out_rows(l, tb, M, nkp_p, nkp_s, 64), wf[j][:M, :64], [('wf', j)], [])
            A('act', lambda E, j=j, M=M: E.copy(out=wb[j][:M, :64], in_=wf[j][:M, :64]), [('wf', j)], [('wb', j)])
            if tb < 8:
                transpose_to(wb[j], M, 1, 64, (lambda b, j=j, M=M: tsb[j][:64, :M]),
                             [('wb', j)], (lambda b, j=j: [('tsb', j)]))
                dma('sp', kpx[l][:, tb * 128:(tb + 1) * 128], tsb[j][:64, :128], [('tsb', j)], [('kpx', l)])
            else:
                transpose_to(wb[j], M, 1, 64, (lambda b, M=M: kpTn[:, :M]), [('wb', j)], (lambda b: ['kpTn']))
        if stop_after <= 1:
            T_.barrier()
            ph[0].close()
            break

        T_.barrier()
        groups = [[2 * i, 2 * i + 1] for i in range(n_cores // 2)]
        for (src, dst, nm, nchunk) in [(kTx[l], kTg[l], 'kTg', 4), (vx[l], vg[l], 'vg', 4), (lfx[l], lfg[l], 'lfg', 1),
                                       (ckx[l], ckg[l], 'ckg', 1), (kpx[l], kpg[l], 'kpg', 1)]:
            rows = src.shape[0] // nchunk
            for c in range(nchunk):
                A('pool', (lambda E, src=src, dst=dst, c=c, rows=rows: E.collective_compute(
                    "AllGather", ALU.bypass, replica_groups=groups, ins=[src[c * rows:(c + 1) * rows, :]],
                    outs=[dst[2 * c * rows:2 * (c + 1) * rows, :]])),
                  [], [(nm, l, c)], kind='cc')
        T_.barrier()

        for cb in range(4):
            wj = load_w(w_in[l * D:(l + 1) * D, :], D, 512, OFF_Q + cb * 512)
            for tb in range(NTB):
                pj, M = proj_tm(tb, hn(tb), hT_lhs, 32, wj, 512)
                j = tb % 2
                src3 = psA[pj][:M, :].rearrange("p (g d) -> p g d", g=4)
                dst3 = wf[j][:M, :512].rearrange("p (g d) -> p g d", g=4)
                rms_groups(src3, M, 4, 128, gqf_t, dst3, [('psA', pj)], [('wf', j)])
                A('act', lambda E, j=j, M=M: E.copy(out=wb[j][:M, :512], in_=wf[j][:M, :512]), [('wf', j)], [('wb', j)])
                def tail_q(tb=tb, j=j, M=M, cb=cb):
                    transpose_to(wb[j], M, 4, 128, (lambda b, j=j, M=M: tsb[j][:, b * 128: b * 128 + M]),
                                 [('wb', j)], (lambda b, j=j: [('tsb', j)]))
                    dma('sp', qfx[cb * 512:(cb + 1) * 512, tb * 128:tb * 128 + M].rearrange("(g d) t -> d g t", g=4),
                        tsb[j][:, :512].rearrange("d (g t) -> d g t", g=4)[:, :, :M], [('tsb', j)], [('qfx', cb, tb)])
                defer(tail_q)
            flush()
        wj0 = load_w(w_in[l * D:(l + 1) * D, :], D, 512, OFF_CQ)
        wj1 = load_w(w_in[l * D:(l + 1) * D, :], D, 512, OFF_CQ + 512)
        for tb in range(NTB):
            p0, M = proj_tm(tb, hn(tb), hT_lhs, 32, wj0, 512)
            p1, M = proj_tm(tb, hn(tb), hT_lhs, 32, wj1, 512)
            pp = [p0, p1]
            for hf in range(2):
                A('act', lambda E, hf=hf, M=M, pp=pp: E.activation(out=sq[:M, hf * 512:(hf + 1) * 512],
                                                                  in_=psA[pp[hf]][:M, :], func=AF.Square),
                  [('psA', pp[hf])], ['sq'])
            A('dve', lambda E, M=M: E.tensor_reduce(out=ss[:M, 0:1], in_=sq[:M, :1024], axis=AX.X, op=ALU.add), ['sq'], ['ss'])
            A('dve', lambda E, M=M: E.tensor_scalar(out=ss[:M, 0:1], in0=ss[:M, 0:1], scalar1=1.0 / 1024, scalar2=EPS,
                                                    op0=ALU.mult, op1=ALU.add), ['ss'], ['ss'])
            A('act', lambda E, M=M: E.activation(out=ss[:M, 0:1], in_=ss[:M, 0:1], func=AF.Sqrt), ['ss'], ['ss'])
            A('dve', lambda E, M=M: E.reciprocal(out=rs[:M, 0:1], in_=ss[:M, 0:1]), ['ss'], ['rs'])
            for hf in range(2):
                A('dve', lambda E, hf=hf, M=M, pp=pp: E.tensor_scalar(out=wf[hf][:M, :512], in0=psA[pp[hf]][:M, :],
                                                                     scalar1=rs[:M, 0:1], scalar2=None, op0=ALU.mult),
                  [('psA', pp[hf]), 'rs'], [('wf', hf)])
                A('dve', lambda E, hf=hf, M=M: E.tensor_tensor(out=wb[hf][:M, :512], in0=wf[hf][:M, :512],
                                                              in1=gcq_t[:M, hf * 512:(hf + 1) * 512], op=ALU.mult),
                  [('wf', hf), 'gains'], [('wb', hf)])
                transpose_to(wb[hf], M, 4, 128, (lambda b, hf=hf, tb=tb, M=M: cqT[:, hf * 4 + b, tb * 128: tb * 128 + M]),
                             [('wb', hf)], (lambda b, tb=tb: [('cqT', tb)]))
        cq_lhs = lambda kc, tb, M: cqT[:, kc, tb * 128: tb * 128 + M]
        for hp in range(8):
            wj = load_w(w_qb[l * 1024:(l + 1) * 1024, :], 1024, 384, hp * 384)
            for tb in range(NTB):
                pj, M = proj_tm(tb, [('cqT', tb)], cq_lhs, 8, wj, 384)
                j = tb % 2
                ps3 = psA[pj][:M, :384].rearrange("p (g d) -> p g d", g=2)
                dn3 = wf[j][:M, 0:256].rearrange("p (g d) -> p g d", g=2)
                dp3 = wf[j][:M, 256:384].rearrange("p (g d) -> p g d", g=2)
                rms_groups(ps3[:, :, 0:128], M, 2, 128, gqn_t, dn3, [('psA', pj)], [('wf', j)])
                rms_groups(ps3[:, :, 128:192], M, 2, 64, gqp_t, dp3, [('psA', pj)], [('wf', j)])
                rope(dp3, M, 2, tb, rt, rt2, [('wf', j)])
                A('act', lambda E, j=j, M=M: E.copy(out=wb[j][:M, :384], in_=wf[j][:M, :384]), [('wf', j)], [('wb', j)])
                transpose_to(wb[j], M, 2, 128, (lambda b, j=j, M=M: tsb[j][:, b * 128: b * 128 + M]),
                             [('wb', j)], (lambda b, j=j: [('tsb', j)]))
                dma('sp', qnx[hp * 256:(hp + 1) * 256, tb * 128:tb * 128 + M].rearrange("(g d) t -> d g t", g=2),
                    tsb[j][:, :256].rearrange("d (g t) -> d g t", g=2)[:, :, :M], [('tsb', j)], [('qnx', hp, tb)])
                transpose_to(wb[j][:, 256:384], M, 2, 64, (lambda b, j=j, M=M: tsb[j][:64, 256 + b * 128: 256 + b * 128 + M]),
                             [('wb', j)], (lambda b, j=j: [('tsb', j)]))
                dma('sp', qpx[hp * 128:(hp + 1) * 128, tb * 128:tb * 128 + M].rearrange("(g d) t -> d g t", g=2),
                    tsb[j][:64, 256:512].rearrange("d (g t) -> d g t", g=2)[:, :, :M], [('tsb', j)], [('qpx', hp, tb)])
        ttiles = [(0, 512), (512, 512), (1024, TS)]
        for (zoff, zdst, znm) in [(OFF_ZA, zax, 'zax'), (OFF_ZB, zbx, 'zbx')]:
            for cb in range(4):
                wj = load_w(w_in[l * D:(l + 1) * D, :], D, 512, zoff + cb * 512)
                for ct in range(4):
                    for ti, (t0, tn) in enumerate(ttiles):
                        pj = next_psA()
                        tbs = [('actT', tb) for tb in range(NTB) if t0 <= tb * 128 < t0 + tn]
                        for kc in range(32):
                            A('pe', (lambda E, kc=kc, pj=pj, wj=wj, ct=ct, t0=t0, tn=tn: E.matmul(
                                psA[pj][:, :tn], lhsT=wt[wj][:, kc, ct * 128:(ct + 1) * 128],
                                rhs=actT[:, kc, t0:t0 + tn], start=(kc == 0), stop=(kc == 31))),
                              tbs + [('wt', wj, (kc // 8) * 8)], [('psA', pj)], signal=(kc == 31))
                        j = (ct * 3 + ti) % 2
                        A('act', lambda E, pj=pj, tn=tn, j=j: E.activation(out=tsb[j][:, :tn], in_=psA[pj][:, :tn], func=AF.Silu),
                          [('psA', pj)], [('tsb', j)])
                        r0 = (cb * 4 + ct) * 128
                        dma('sp', zdst[r0:r0 + 128, t0:t0 + tn], tsb[j][:, :tn], [('tsb', j)], [(znm, cb * 4 + ct, ti)])
        T_.barrier()
        ph[0].close()
        if stop_after <= 2:
            break

        ph[0] = ExitStack()
        npsA[0] = 2
        gidx = lambda g: (0 if gblock(0, g // 2) == g else 1) * 8 + g // 2
        maskf = sbp(f"maskf{l}", [128, 2048], BF16)
        lf_t = sbp(f"lf_t{l}", [128, 16, 32]); win = sbp(f"win{l}", [128, 16, 32]); tots = sbp(f"tots{l}", [128, 16, 32])
        carry = sbp(f"carry{l}", [128, 17, 32]); ck = sbp(f"ck{l}", [128, 16, 32]); bias = sbp(f"bias{l}", [128, 8, 16, 16])
        crefs = sbp(f"crefs{l}", [128, 32]); ckn = sbp(f"ckn{l}", [32, 16]); biasn = sbp(f"biasn{l}", [32, 32])
        kTh = [sbp(f"kTh{l}_{j}", [128, 2, TP], BF16) for j in range(2)]
        qTh = [sbp(f"qTh{l}_{j}", [128, T], BF16) for j in range(2)]
        zTh = [sbp(f"zTh{l}_{j}", [128, T], BF16) for j in range(2)]
        pT = [sbp(f"pT{l}_{j}", [128, 512], BF16) for j in range(2)]
        rden = sbp(f"rden{l}", [128, 512]); osb = sbp(f"osb{l}", [128, 512])
        qTs = sbp(f"qTs{l}", [128, 16, TS], BF16); zTs = sbp(f"zTs{l}", [128, 16, TS], BF16)
        indw = sbp(f"indw{l}", [32, 256])
        vgrp = wt[0][:, 0:16, :]; kc_t = wt[0][:, 16:32, :]; vc_t = wt[1][:, 0:16, :]
        kTc = wt[1][:, 16:32, :].rearrange("p a b -> p (a b)").rearrange("p (h t) -> p h t", h=4)
        dma('sp', maskf[:, :], c_maskf[:, :], [], ['maskf'])
        dma('sp', indw[:, :], c_indw[:, :], [], ['indw'])
        all_q = lambda nm, n0: [(nm, a, tb) for a in range(n0) for tb in range(NTB)]
        all_z = lambda nm: [(nm, a, ti) for a in range(16) for ti in range(3)]

        def cumsum_tables(ncol, carry_order):
            flat = lf_t[:, :, :ncol]
            A('pe', lambda E: E.matmul(psS[0][:, :16 * ncol].rearrange("p (j c) -> p j c", j=16), lhsT=trif[:, :], rhs=flat,
                                       start=True, stop=True), ['lf_t', 'trif'], [('psS', 0)])
            A('pe', lambda E: E.matmul(psS[1][:, :16 * ncol].rearrange("p (j c) -> p j c", j=16), lhsT=onesf[:, :], rhs=flat,
                                       start=True, stop=True), ['lf_t', 'onesf'], [('psS', 1)])
            A('act', lambda E: E.copy(out=win[:, :, :ncol], in_=psS[0][:, :16 * ncol].rearrange("p (j c) -> p j c", j=16)),
              [('psS', 0)], ['win'])
            A('act', lambda E: E.copy(out=tots[:, :, :ncol], in_=psS[1][:, :16 * ncol].rearrange("p (j c) -> p j c", j=16)),
              [('psS', 1)], ['tots'])
            A('dve', lambda E: E.memset(carry[:, 0, :], 0.0), [], ['carry'])
            for g in range(16):
                A('dve', lambda E, g=g: E.tensor_tensor(out=carry[:, g + 1, :ncol], in0=carry[:, g, :ncol],
                                                        in1=tots[:, carry_order(g), :ncol], op=ALU.add),
                  ['carry', 'tots'], ['carry'])

        dma('sp', lf_t[:, :, :16], lfg[l].rearrange("(j p) h -> p j h", p=128), [('lfg', l, 0)], ['lf_t'])
        cumsum_tables(16, gidx)
        for j in range(16):
            g = gblock(j // 8, j % 8)
            A('dve', lambda E, j=j, g=g: E.tensor_tensor(out=ck[:, j, :16], in0=win[:, j, :16], in1=carry[:, g, :16], op=ALU.add),
              ['win', 'carry'], ['ck'])
        for i in range(8):
            A('dve', lambda E, i=i: E.tensor_tensor(out=bias[:, i, :, :],
                                                    in0=carry[:, 2 * i + 2, :16].unsqueeze(1).to_broadcast([128, 16, 16]),
                                                    in1=ck[:, :, :16], op=ALU.subtract), ['carry', 'ck'], ['bias'])
        for hg in range(4):
            for r in range(2):
                for c in range(4):
                    r0 = c * 512 + r * 256
                    dma('sp', vgrp[:, r * 8 + c * 2: r * 8 + c * 2 + 2, :],
                        vg[l][r0:r0 + 256, hg * 512:(hg + 1) * 512].rearrange("(q p) n -> p q n", p=128),
                        [], [('vgrp', r, c)])
            for hh in range(4):
                h = hg * 4 + hh
                hb_ = h % 2
                dma('sp', kTh[hb_][:, :, :],
                    kTg[l][(h // 4) * 1024:(h // 4 + 1) * 1024, :].rearrange("(r f) t -> f r t", r=2)[(h % 4) * 128:(h % 4 + 1) * 128, :, :],
                    [], [('kTh', hb_)])
                dma('sp', qTh[hb_][:, :], qfx[h * 128:(h + 1) * 128, :], all_q('qfx', 4), [('qTh', hb_)])
                dma('sp', zTh[hb_][:, :], zax[h * 128:(h + 1) * 128, :], all_z('zax'), [('zTh', hb_)])
                for t in range(2):
                    blocks = [(ip, r) for ip in range(4 * t + 4) for r in range(2)]
                    for bi, (ip, r) in enumerate(blocks):
                        j = r * 8 + ip
                        c0 = max(0, ip - 4 * t) * 128
                        sj = bi % 2
                        A('pe', (lambda E, sj=sj, c0=c0, r=r, ip=ip, t=t, hb_=hb_: E.matmul(
                            psS[sj][:, c0:512], lhsT=kTh[hb_][:, r, ip * 128:(ip + 1) * 128],
                            rhs=qTh[hb_][:, t * 512 + c0:(t + 1) * 512], start=True, stop=True)),
                          [('kTh', hb_), ('qTh', hb_)], [('psS', sj)])
                        for isub in range(c0 // 128, 4):
                            i = 4 * t + isub
                            A('act', (lambda E, sj=sj, isub=isub, i=i, j=j, h=h: E.activation(
                                out=pT[sj][:, isub * 128:(isub + 1) * 128], in_=psS[sj][:, isub * 128:(isub + 1) * 128],
                                func=AF.Exp, bias=bias[:, i, j, h:h + 1], scale=float(FOX_SCALE))),
                              [('psS', sj), 'bias'], [('pT', sj, isub)])
                            if ip == i:
                                A('dve', (lambda E, sj=sj, isub=isub, i=i, r=r: E.tensor_tensor(
                                    out=pT[sj][:, isub * 128:(isub + 1) * 128], in0=pT[sj][:, isub * 128:(isub + 1) * 128],
                                    in1=maskf[:, (i * 2 + r) * 128:(i * 2 + r + 1) * 128], op=ALU.mult)),
                                  [('pT', sj, isub), 'maskf'], [('pT', sj, isub)])
                        first = (bi == 0)
                        last = (bi == len(blocks) - 1)
                        pnames = [('pT', sj, isub) for isub in range(c0 // 128, 4)]

                        def tail_pv(sj=sj, c0=c0, j=j, hh=hh, first=first, last=last, pnames=pnames, r=r, ip=ip):
                            A('pe', (lambda E, sj=sj, c0=c0, j=j, hh=hh, first=first, last=last: E.matmul(
                                psO[:, c0:512], lhsT=vgrp[:, j, hh * 128:(hh + 1) * 128], rhs=pT[sj][:, c0:512],
                                start=first, stop=last)), pnames + [('vgrp', r, ip // 2)], ['psO'], signal=False)
                            A('pe', (lambda E, sj=sj, c0=c0, first=first, last=last: E.matmul(
                                psD[:, c0:512], lhsT=onesb[:, :], rhs=pT[sj][:, c0:512], start=first, stop=last)),
                              pnames + ['onesb'], ['psD'])
                        defer(tail_pv)
                    flush()
                    A('dve', lambda E: E.reciprocal(out=rden[:, :], in_=psD[:, :]), ['psD'], ['rden'])
                    A('dve', lambda E: E.tensor_tensor(out=osb[:, :], in0=psO[:, :], in1=rden[:, :], op=ALU.mult),
                      ['psO', 'rden'], ['osb'])
                    A('dve', (lambda E, h=h, t=t, hb_=hb_: E.tensor_tensor(
                        out=actT[:, h, t * 512:(t + 1) * 512], in0=osb[:, :], in1=zTh[hb_][:, t * 512:(t + 1) * 512], op=ALU.mult)),
                      ['osb', ('zTh', hb_)], [('actT', tb) for tb in range(4 * t, 4 * t + 4)])
        T_.barrier()
        for b in range(2):
            dma('sp', lf_t[:, :, b * 16:(b + 1) * 16],
                cfl[(l * 2 + b) * PAST:(l * 2 + b + 1) * PAST, :].rearrange("(j p) h -> p j h", p=128), [], [('lf_tb', b)])
        T_.barrier()
        cumsum_tables(32, (lambda g: g))
        A('dve', lambda E: E.tensor_tensor(out=ck[:, :, :], in0=win[:, :, :], in1=carry[:, 0:16, :], op=ALU.add),
          ['win', 'carry'], ['ck'])
        A('pe', lambda E: E.matmul(psS[0][:32, :16], lhsT=tris[:32, :32], rhs=lfn[:32, :16], start=True, stop=True),
          ['tris', 'lfn'], [('psS', 0)])
        for b in range(2):
            A('pe', lambda E, b=b: E.matmul(psS[1][:, b * 16:(b + 1) * 16], lhsT=indw[:32, b * 128:(b + 1) * 128],
                                            rhs=lfn[:32, :16], start=True, stop=True), ['indw', 'lfn'], [('psS', 1)])
        A('act', lambda E: E.copy(out=ckn[:, :], in_=psS[0][:32, :16]), [('psS', 0)], ['ckn'])
        for b in range(2):
            A('dve', lambda E, b=b: E.scalar_tensor_tensor(out=ckn[:, :], in0=carry[:32, 16, b * 16:(b + 1) * 16],
                                                          scalar=inds[:32, b:b + 1], in1=ckn[:, :], op0=ALU.mult, op1=ALU.add),
              ['carry', 'ckn', 'inds'], ['ckn'])
        A('dve', lambda E: E.tensor_tensor(out=crefs[:, :], in0=carry[:, 16, :], in1=psS[1][:, :32], op=ALU.add),
          ['carry', ('psS', 1)], ['crefs'])
        biasc = win
        A('dve', lambda E: E.tensor_tensor(out=biasc[:, :, :], in0=crefs[:, :].unsqueeze(1).to_broadcast([128, 16, 32]),
                                           in1=ck[:, :, :], op=ALU.subtract), ['crefs', 'ck', 'win'], ['biasc'])
        for b in range(2):
            A('dve', lambda E, b=b: E.tensor_tensor(out=biasn[:, b * 16:(b + 1) * 16], in0=crefs[:32, b * 16:(b + 1) * 16],
                                                    in1=ckn[:, :], op=ALU.subtract), ['crefs', 'ckn'], ['biasn'])
        dma('sp', qTs[:, :, :], qfx[:, TP:T].rearrange("(h d) t -> d h t", d=128), all_q('qfx', 4), ['qTs'])
        dma('sp', zTs[:, :, :], zax[:, TP:T].rearrange("(h d) t -> d h t", d=128), all_z('zax'), ['zTs'])

        sa_i = [0]

        def sample_attn(b, kT_fn, v_fn, kTnew_list, vnew, bias_fn, biasn_ap, msk, scale, zrow, out_chunk):
            k_ = sa_i[0] % 2
            sa_i[0] += 1
            for j in range(16):
                parts = kT_fn(j)
                for pi, (ka, qa, kn) in enumerate(parts):
                    A('pe', (lambda E, j=j, ka=ka, qa=qa, pi=pi, n=len(parts): E.matmul(
                        psS[k_][:, j * 16:(j + 1) * 16], lhsT=ka, rhs=qa, start=(pi == 0), stop=(pi == n - 1))),
                      kn, [('psS', k_)], signal=(pi == len(parts) - 1))
            for pi, (ka, qa, kn) in enumerate(kTnew_list):
                A('pe', (lambda E, ka=ka, qa=qa, pi=pi, n=len(kTnew_list): E.matmul(
                    psS[k_][:32, 256:272], lhsT=ka, rhs=qa, start=(pi == 0), stop=(pi == n - 1))),
                  kn, [('psS', k_)], signal=(pi == len(kTnew_list) - 1))
            if bias_fn is None:
                A('act', (lambda E: E.activation(out=pT[k_][:, 0:256], in_=psS[k_][:, 0:256], func=AF.Exp, scale=float(scale))),
                  [('psS', k_)], [('pTs', k_, j) for j in range(16)])
            for j in range(16):
                if bias_fn is None:
                    pass
                else:
                    bj_ = bias_fn(j)
                    A('act', (lambda E, j=j, bj_=bj_: E.activation(out=pT[k_][:, j * 16:(j + 1) * 16], in_=psS[k_][:, j * 16:(j + 1) * 16],
                                                          func=AF.Exp, bias=bj_, scale=float(scale))),
                      [('psS', k_), 'biasc'], [('pTs', k_, j)])
            if biasn_ap is None:
                A('act', (lambda E: E.activation(out=pT[k_][:32, 256:272], in_=psS[k_][:32, 256:272], func=AF.Exp,
                                                 scale=float(scale))), [('psS', k_)], [('pTs', k_, 16)])
            else:
                A('act', (lambda E: E.activation(out=pT[k_][:32, 256:272], in_=psS[k_][:32, 256:272], func=AF.Exp,
                                                 bias=biasn_ap, scale=float(scale))), [('psS', k_), 'biasn'], [('pTs', k_, 16)])
            A('dve', (lambda E: E.tensor_tensor(out=pT[k_][:32, 256:272], in0=pT[k_][:32, 256:272], in1=msk, op=ALU.mult)),
              [('pTs', k_, 16), 'masks'], [('pTs', k_, 16)])
            def tail_sa():
                for j in range(17):
                    if j < 16:
                        va, vnm = v_fn(j)
                        pa = pT[k_][:, j * 16:(j + 1) * 16]
                        oa = onesb[:, :]
                    else:
                        va, vnm = vnew
                        pa = pT[k_][:32, 256:272]
                        oa = onesb[:32, :]
                    A('pe', (lambda E, va=va, pa=pa, j=j: E.matmul(psO[:, :16], lhsT=va, rhs=pa, start=(j == 0), stop=(j == 16))),
                      [('pTs', k_, j)] + vnm, ['psO'], signal=False)
                    A('pe', (lambda E, oa=oa, pa=pa, j=j: E.matmul(psD[:, :16], lhsT=oa, rhs=pa, start=(j == 0), stop=(j == 16))),
                      [('pTs', k_, j), 'onesb'], ['psD'])
                A('dve', lambda E: E.reciprocal(out=rden[:, :16], in_=psD[:, :16]), ['psD'], ['rden'])
                A('dve', lambda E: E.tensor_tensor(out=osb[:, :16], in0=psO[:, :16], in1=rden[:, :16], op=ALU.mult),
                  ['psO', 'rden'], ['osb'])
                A('dve', (lambda E: E.tensor_tensor(out=actT[:, out_chunk, TP + 16 * b:TP + 16 * b + 16], in0=osb[:, :16],
                                                    in1=zrow, op=ALU.mult)), ['osb', 'zTs'], [('actT', 8)])
            defer(tail_sa)

        for b in range(2):
            for hg in range(4):
                rows = slice((l * 2 + b) * PAST, (l * 2 + b + 1) * PAST)
                A('pool', (lambda E, rows=rows, hg=hg: E.dma_start(
                    out=kc_t, in_=cfk[rows, hg * 512:(hg + 1) * 512].rearrange("(j p) c -> p j c", p=128))),
                  [], ['kc_t'], kind='dma')
                A('pool', (lambda E, rows=rows, hg=hg: E.dma_start(
                    out=vc_t, in_=cfv[rows, hg * 512:(hg + 1) * 512].rearrange("(j p) c -> p j c", p=128))),
                  [], ['vc_t'], kind='dma')
                for hh in range(4):
                    for half in range(2):
                        pj = pt_i[0] % 2
                        pt_i[0] += 1
                        for bb in range(8):
                            A('pe', (lambda E, pj=pj, bb=bb, half=half, hh=hh: E.transpose(
                                psT[pj][:, bb * 128:(bb + 1) * 128], kc_t[:, half * 8 + bb, hh * 128:(hh + 1) * 128], identb[:, :])),
                              ['kc_t', 'identb'], [('psT', pj)], signal=(bb == 7))
                        A('act', (lambda E, pj=pj, half=half, hh=hh: E.copy(
                            out=kTc[:, hh, half * 1024:(half + 1) * 1024], in_=psT[pj][:, :])), [('psT', pj)], [('kTc', hh)])
                for hh in range(4):
                    h = hg * 4 + hh
                    qa = qTs[:, h, 16 * b:16 * b + 16]
                    sample_attn(
                        b,
                        (lambda j, hh=hh, qa=qa: [(kTc[:, hh, j * 128:(j + 1) * 128], qa, [('kTc', hh), 'qTs'])]),
                        (lambda j, hh=hh: (vc_t[:, j, hh * 128:(hh + 1) * 128], ['vc_t'])),
                        [(kTn[:, h, :], qa, ['kTn', 'qTs'])],
                        (vn[:32, h * 128:(h + 1) * 128], ['vn']),
                        (lambda j, b=b, h=h: biasc[:, j, b * 16 + h:b * 16 + h + 1]),
                        biasn[:32, b * 16 + h:b * 16 + h + 1], masks[:32, b * 16:(b + 1) * 16], FOX_SCALE,
                        zTs[:, h, 16 * b:16 * b + 16], h)
                flush()
        T_.barrier()
        ph[0].close()
        if stop_after <= 3:
            break

        ph[0] = ExitStack()
        maskm = sbp(f"maskm{l}", [128, 2048], BF16); masks2 = sbp(f"masks2{l}", [32, 32], BF16)
        kpTg = sbp(f"kpTg{l}", [64, 2, TP], BF16)
        knT = sbp(f"knT{l}", [128, 2, 2048], BF16); vb = sbp(f"vb{l}", [128, 16, 256], BF16)
        vbn = sbp(f"vbn{l}", [32, 256], BF16); knTn = sbp(f"knTn{l}", [128, 2, 32], BF16)
        qnh = sbp(f"qnh{l}", [128, T], BF16); qph = sbp(f"qph{l}", [64, T], BF16); zTh2 = sbp(f"zTh2{l}", [128, T], BF16)
        pT = [sbp(f"pTm{l}_{j}", [128, 512], BF16) for j in range(2)]
        rden = sbp(f"rdenm{l}", [128, 512]); osb = sbp(f"osbm{l}", [128, 512])
        sq = sbp(f"sqm{l}", [128, 256]); wf2 = sbp(f"wf2{l}", [128, 256]); wb2 = sbp(f"wb2{l}", [128, 256], BF16)
        qnTs = sbp(f"qnTs{l}", [128, 16, TS], BF16); qpTs = sbp(f"qpTs{l}", [64, 16, TS], BF16)
        zTs = sbp(f"zTs2{l}", [128, 16, TS], BF16)
        wkv = wt[0][:, :, :].rearrange("p a b -> p (a b)").rearrange("p (kc n) -> p kc n", kc=4)
        ckTg = wt[1][:, 0:16, :].rearrange("p a b -> p (a b)").rearrange("p (kc r t) -> p kc r t", kc=4, r=2)
        ckc = wt[1][:, 16:32, :]
        dma('sp', maskm[:, :], c_maskm[:, :], [], ['maskm'])
        dma('sp', masks2[:, :], c_masks2[:, :], [], ['masks'])
        for kc in range(4):
            A('pool', (lambda E, kc=kc: E.dma_start(out=wkv[:, kc, :], in_=w_kvb[l * 512 + kc * 128: l * 512 + (kc + 1) * 128, :])),
              [], [('wkv', kc)], kind='dma')
            for r in range(2):
                dma('sp', ckTg[:, kc, r, :], ckg[l][r * 512 + kc * 128: r * 512 + (kc + 1) * 128, :], [('ckg', l, 0)], [('ckTg', kc, r)])
        dma('sp', kpTg[:, :, :], kpg[l].rearrange("(r d) t -> d r t", r=2), [('kpg', l, 0)], ['kpTg'])
        T_.barrier()

        def upproj(lhs_fn, lnames, M, hp, knT_dst_fn, v_dst, vnames, knames):
            pj = next_psA()
            for kc in range(4):
                lh = lhs_fn(kc)
                A('pe', (lambda E, kc=kc, pj=pj, lh=lh: E.matmul(psA[pj][:M, :512], lhsT=lh, rhs=wkv[:, kc, hp * 512:(hp + 1) * 512],
                                                         start=(kc == 0), stop=(kc == 3))),
                  list(lnames) + [('wkv', kc)], [('psA', pj)], signal=(kc == 3))
            ps3 = psA[pj][:M, :].rearrange("p (g c) -> p g c", g=2)
            dst3 = wf2[:M, :256].rearrange("p (g d) -> p g d", g=2)
            rms_groups(ps3[:, :, 0:128], M, 2, 128, gkn_t, dst3, [('psA', pj)], ['wf2'])
            A('act', lambda E: E.copy(out=wb2[:M, :256], in_=wf2[:M, :256]), ['wf2'], ['wb2'])
            A('dve', lambda E: E.tensor_copy(out=v_dst, in_=ps3[:, :, 128:256]), [('psA', pj)], vnames)
            transpose_to(wb2, M, 2, 128, knT_dst_fn, ['wb2'], (lambda b: knames))

        def mla_head_prompt(h, hh):
            dma('sp', qnh[:, :], qnx[h * 128:(h + 1) * 128, :], [('qnx', h // 2, tb) for tb in range(NTB)], ['qnh'])
            dma('sp', qph[:, :], qpx[h * 64:(h + 1) * 64, :], [('qpx', h // 2, tb) for tb in range(NTB)], ['qph'])
            dma('sp', zTh2[:, :], zbx[h * 128:(h + 1) * 128, :], [('zbx', h, ti) for ti in range(3)], ['zTh2'])
            for t in range(2):
                blocks = [(ip, r) for ip in range(4 * t + 4) for r in range(2)]
                for bi, (ip, r) in enumerate(blocks):
                    j = r * 8 + ip
                    c0 = max(0, ip - 4 * t) * 128
                    sj = bi % 2
                    A('pe', (lambda E, sj=sj, c0=c0, j=j, t=t: E.matmul(
                        psS[sj][:, c0:512], lhsT=knT[:, hh, j * 128:(j + 1) * 128],
                        rhs=qnh[:, t * 512 + c0:(t + 1) * 512], start=True, stop=False)),
                      [('knT', j), 'qnh'], [('psS', sj)], signal=False)
                    A('pe', (lambda E, sj=sj, c0=c0, r=r, ip=ip, t=t: E.matmul(
                        psS[sj][:, c0:512], lhsT=kpTg[:, r, ip * 128:(ip + 1) * 128],
                        rhs=qph[:, t * 512 + c0:(t + 1) * 512], start=False, stop=True)),
                      ['kpTg', 'qph'], [('psS', sj)])
                    A('act', (lambda E, sj=sj, c0=c0: E.activation(
                        out=pT[sj][:, c0:512], in_=psS[sj][:, c0:512], func=AF.Exp, scale=float(MLA_SCALE))),
                      [('psS', sj)], [('pT', sj, isub) for isub in range(c0 // 128, 4)])
                    for isub in range(c0 // 128, 4):
                        i = 4 * t + isub
                        if ip == i:
                            A('dve', (lambda E, sj=sj, isub=isub, i=i, r=r: E.tensor_tensor(
                                out=pT[sj][:, isub * 128:(isub + 1) * 128], in0=pT[sj][:, isub * 128:(isub + 1) * 128],
                                in1=maskm[:, (i * 2 + r) * 128:(i * 2 + r + 1) * 128], op=ALU.mult)),
                              [('pT', sj, isub), 'maskm'], [('pT', sj, isub)])
                    first = (bi == 0)
                    last = (bi == len(blocks) - 1)
                    pnames = [('pT', sj, isub) for isub in range(c0 // 128, 4)]

                    def tail_pv(sj=sj, c0=c0, j=j, first=first, last=last, pnames=pnames):
                        A('pe', (lambda E, sj=sj, c0=c0, j=j, first=first, last=last: E.matmul(
                            psO[:, c0:512], lhsT=vb[:, j, hh * 128:(hh + 1) * 128], rhs=pT[sj][:, c0:512],
                            start=first, stop=last)), pnames + [('vb', j)], ['psO'], signal=False)
                        A('pe', (lambda E, sj=sj, c0=c0, first=first, last=last: E.matmul(
                            psD[:, c0:512], lhsT=onesb[:, :], rhs=pT[sj][:, c0:512], start=first, stop=last)),
                          pnames + ['onesb'], ['psD'])
                    defer(tail_pv)
                flush()
                A('dve', lambda E: E.reciprocal(out=rden[:, :], in_=psD[:, :]), ['psD'], ['rden'])
                A('dve', lambda E: E.tensor_tensor(out=osb[:, :], in0=psO[:, :], in1=rden[:, :], op=ALU.mult),
                  ['psO', 'rden'], ['osb'])
                A('dve', (lambda E, t=t: E.tensor_tensor(
                    out=actT[:, 16 + h, t * 512:(t + 1) * 512], in0=osb[:, :], in1=zTh2[:, t * 512:(t + 1) * 512], op=ALU.mult)),
                  ['osb', 'zTh2'], [('actT', tb) for tb in range(4 * t, 4 * t + 4)])

        for hp in range(8):
            for j in range(16):
                r, ip = divmod(j, 8)
                upproj((lambda kc, r=r, ip=ip: ckTg[:, kc, r, ip * 128:(ip + 1) * 128]),
                       [('ckTg', kc, r) for kc in range(4)], 128, hp,
                       (lambda b, j=j: knT[:, b, j * 128:(j + 1) * 128]),
                       vb[:, j, :].rearrange("p (g d) -> p g d", g=2), [('vb', j)], [('knT', j)])
            for hh in range(2):
                mla_head_prompt(hp * 2 + hh, hh)
        T_.barrier()
        ckTc = wt[1][:, 0:16, :].rearrange("p a b -> p (a b)").rearrange("p (kc t) -> p kc t", kc=4)
        kpTc = kpTg[:, :, :].rearrange("d r t -> d (r t)")
        kpc = wt[0][:, 0:2, :].rearrange("p a b -> p (a b)").rearrange("p (j c) -> p j c", j=16)
        dma('sp', qnTs[:, :, :], qnx[:, TP:T].rearrange("(h d) t -> d h t", d=128), [('qnx', a, 8) for a in range(8)], ['qTs'])
        dma('sp', qpTs[:, :, :], qpx[:, TP:T].rearrange("(h d) t -> d h t", d=64), [('qpx', a, 8) for a in range(8)], ['qTs2'])
        dma('sp', zTs[:, :, :], zbx[:, TP:T].rearrange("(h d) t -> d h t", d=128), all_z('zbx'), ['zTs'])
        kpc = sq
        kpcb = sbp(f"kpcb{l}", [128, 16, 64], BF16)
        for b in range(2):
            rows = slice((l * 2 + b) * PAST, (l * 2 + b + 1) * PAST)
            A('pool', (lambda E, rows=rows: E.dma_start(out=ckc, in_=cck[rows, :].rearrange("(j p) c -> p j c", p=128))),
              [], ['ckc'], kind='dma')
            A('pool', (lambda E, rows=rows: E.dma_start(out=kpcb[:, :, :], in_=ckp[rows, :].rearrange("(j p) c -> p j c", p=128))),
              [], ['kpcb'], kind='dma')
            for j in range(16):
                pj = pt_i[0] % 2
                pt_i[0] += 1
                for kc in range(4):
                    A('pe', (lambda E, pj=pj, j=j, kc=kc: E.transpose(psT[pj][:, kc * 128:(kc + 1) * 128],
                                                                     ckc[:, j, kc * 128:(kc + 1) * 128], identb[:, :])),
                      ['ckc', 'identb'], [('psT', pj)], signal=(kc == 3))
                A('act', (lambda E, pj=pj, j=j: E.copy(out=ckTc[:, :, j * 128:(j + 1) * 128],
                                                       in_=psT[pj][:, :512].rearrange("p (kc t) -> p kc t", kc=4))),
                  [('psT', pj)], ['ckTc'])
            for half in range(2):
                pj = pt_i[0] % 2
                pt_i[0] += 1
                for bb in range(8):
                    A('pe', (lambda E, pj=pj, bb=bb, half=half: E.transpose(psT[pj][:64, bb * 128:(bb + 1) * 128],
                                                                           kpcb[:, half * 8 + bb, :], identb[:, :])),
                      ['kpcb', 'identb'], [('psT', pj)], signal=(bb == 7))
                A('act', (lambda E, pj=pj, half=half: E.copy(out=kpTc[:, half * 1024:(half + 1) * 1024], in_=psT[pj][:64, :])),
                  [('psT', pj)], ['kpTc'])
            for hp in range(8):
                for j in range(16):
                    upproj((lambda kc, j=j: ckTc[:, kc, j * 128:(j + 1) * 128]), ['ckTc'], 128, hp,
                           (lambda bq, j=j: knT[:, bq, j * 128:(j + 1) * 128]),
                           vb[:, j, :].rearrange("p (g d) -> p g d", g=2), [('vb', j)], [('knT', j)])
                upproj((lambda kc: ckTn[:, kc, :]), ['ckTn'], 32, hp, (lambda bq: knTn[:, bq, :]),
                       vbn[:32, :].rearrange("p (g d) -> p g d", g=2), ['vbn'], ['knTn'])
                for hh in range(2):
                    h = hp * 2 + hh
                    qn_ = qnTs[:, h, 16 * b:16 * b + 16]
                    qp_ = qpTs[:, h, 16 * b:16 * b + 16]
                    sample_attn(
                        b,
                        (lambda j, hh=hh, qn_=qn_, qp_=qp_: [(knT[:, hh, j * 128:(j + 1) * 128], qn_, [('knT', j), 'qTs']),
                                                            (kpTc[:, j * 128:(j + 1) * 128], qp_, ['kpTc', 'qTs2'])]),
                        (lambda j, hh=hh: (vb[:, j, hh * 128:(hh + 1) * 128], [('vb', j)])),
                        [(knTn[:, hh, :], qn_, ['knTn', 'qTs']), (kpTn[:, :], qp_, ['kpTn', 'qTs2'])],
                        (vbn[:32, hh * 128:(hh + 1) * 128], ['vbn']),
                        None, None, masks2[:32, b * 16:(b + 1) * 16], MLA_SCALE,
                        zTs[:, h, 16 * b:16 * b + 16], 16 + h)
                flush()
        T_.barrier()
        ph[0].close()
        if stop_after <= 4:
            break

        ph[0] = ExitStack()
        npsA[0] = 6
        xr = [sbp(f"xr{l}_{j}", [128, 512]) for j in range(2)]
        yo = [sbp(f"yo{l}_{j}", [128, 512]) for j in range(2)]
        for cb in range(8):
            wj = load_w(w_out[l * D:(l + 1) * D, :], D, 512, cb * 512)
            for tb in range(NTB):
                pj, M = proj_tm(tb, [('actT', tb)], hT_lhs, 32, wj, 512)
                j = tb % 2
                cs = slice(cb * 512, (cb + 1) * 512)
                if l == 0:
                    xsrc = xp[tb * 128: tb * 128 + M, cs] if tb < 8 else xs[:, cs]
                else:
                    xsrc = ymid[tb * 128: tb * 128 + M, cs]
                dma('sp', xr[j][:M, :], xsrc, [], [('xr', j)])
                A('dve', lambda E, j=j, M=M, pj=pj: E.tensor_tensor(out=yo[j][:M, :], in0=psA[pj][:M, :], in1=xr[j][:M, :], op=ALU.add),
                  [('psA', pj), ('xr', j)], [('yo', j)])
                if l == 0:
                    dst = ymid[tb * 128: tb * 128 + M, cs]
                else:
                    dst = yp[tb * 128: tb * 128 + M, cs] if tb < 8 else ys[:, cs]
                dma('sp', dst, yo[j][:M, :], [('yo', j)], [])
        T_.barrier()
        ph[0].close()

    T_.final_wait('sp')
    sem_keys = T_.sem_keys()
    sems = {}
    for k in sem_keys:
        sems[k] = es.enter_context(nc.semaphore("s_" + "_".join(str(x) for x in k)))
    build_program.sem_map = {k: repr(v) for k, v in sems.items()}
    with nc.Block() as block:
        T_.emit(block, sems)
    es.close()
    return nc


def host_constants(r):
    bf = ml_dtypes.bfloat16
    c = {}
    c["c_identb"] = np.eye(128, dtype=np.float32).astype(bf)
    c["c_identf"] = np.eye(128, dtype=np.float32)
    c["c_onesb"] = np.ones((128, 128), np.float32).astype(bf)
    c["c_onesf"] = np.ones((128, 128), np.float32)
    c["c_trif"] = np.triu(np.ones((128, 128), np.float32))
    pos = np.concatenate([np.concatenate([gblock(r, i) * 128 + np.arange(128) for i in range(8)]),
                          PAST + np.arange(16), PAST + np.arange(16)]).astype(np.float32)
    half = 32
    inv_freq = (1.0 / (np.float32(10000.0) ** (np.arange(half, dtype=np.float32) / np.float32(half)))).astype(np.float32)
    ang = (pos[:, None] * inv_freq[None, :]).astype(np.float32)
    c["c_cos"] = np.cos(ang).astype(np.float32)
    c["c_sin"] = np.sin(ang).astype(np.float32)
    mf = np.zeros((128, 16, 128), np.float32)
    mm = np.zeros((128, 16, 128), np.float32)
    for i in range(8):
        qpos = gblock(r, i) * 128 + np.arange(128)
        for rk in range(2):
            kpos = gblock(rk, i) * 128 + np.arange(128)
            mf[:, i * 2 + rk, :] = (kpos[:, None] <= qpos[None, :])
            mm[:, i * 2 + rk, :] = ((kpos[:, None] // 64) <= (qpos[None, :] // 64))
    c["c_maskf"] = mf.reshape(128, 2048).astype(bf)
    c["c_maskm"] = mm.reshape(128, 2048).astype(bf)
    ms = np.zeros((32, 2, 16), np.float32)
    for b in range(2):
        for k in range(16):
            ms[b * 16 + k, b, :] = (k <= np.arange(16))
    c["c_masks"] = ms.reshape(32, 32).astype(bf)
    ts_ = np.zeros((32, 32), np.float32)
    for b in range(2):
        ts_[b * 16:(b + 1) * 16, b * 16:(b + 1) * 16] = np.triu(np.ones((16, 16), np.float32))
    c["c_tris"] = ts_
    ind = np.zeros((32, 2), np.float32)
    ind[:16, 0] = 1
    ind[16:, 1] = 1
    c["c_inds"] = ind
    c["c_indw"] = np.repeat(ind[:, :, None], 128, axis=2).reshape(32, 256).astype(np.float32)
    m2 = np.zeros((32, 2, 16), np.float32)
    m2[:16, 0, :] = 1
    m2[16:, 1, :] = 1
    c["c_masks2"] = m2.reshape(32, 32).astype(bf)
    return c


_NC_CACHE = {}


def make_in_maps(inp):
    f = lambda a: np.ascontiguousarray(np.asarray(a, dtype=np.float32))
    xpr = f(inp["x_prompt"]); xsm = f(inp["x_sample"])
    maps = []
    wshared = {
        "g_norm": f(inp["g_norm"]), "w_in": f(inp["w_in"]).reshape(NL * D, N_IN), "b_f": f(inp["b_f"]),
        "g_q_fox": f(inp["g_q_fox"]), "g_k_fox": f(inp["g_k_fox"]), "g_cq": f(inp["g_cq"]),
        "w_qb": f(inp["w_qb"]).reshape(NL * 1024, 3072), "g_qn": f(inp["g_qn"]), "g_qp": f(inp["g_qp"]),
        "g_ckv": f(inp["g_ckv"]), "g_kp": f(inp["g_kp"]), "w_kvb": f(inp["w_kvb"]).reshape(NL * 512, 4096),
        "g_kn": f(inp["g_kn"]), "w_out": f(inp["w_out"]).reshape(NL * D, D),
    }
    cfk = f(inp["cache_fox_k"]); cfv = f(inp["cache_fox_v"]); cfl = f(inp["cache_fox_logf"])
    cck = f(inp["cache_mla_ckv"]); ckp = f(inp["cache_mla_kpe"])
    for c in range(8):
        p, r = c // 2, c % 2
        m = dict(wshared)
        m["xp"] = np.concatenate([xpr[p, gblock(r, i) * 128:(gblock(r, i) + 1) * 128] for i in range(8)], axis=0)
        m["xs"] = xsm[2 * c:2 * c + 2].reshape(TS, D)
        m["cfk"] = cfk[:, 2 * c:2 * c + 2].reshape(NL * 2 * PAST, 2048)
        m["cfv"] = cfv[:, 2 * c:2 * c + 2].reshape(NL * 2 * PAST, 2048)
        m["cfl"] = cfl[:, 2 * c:2 * c + 2].reshape(NL * 2 * PAST, 16)
        m["cck"] = cck[:, 2 * c:2 * c + 2].reshape(NL * 2 * PAST, 512)
        m["ckp"] = ckp[:, 2 * c:2 * c + 2].reshape(NL * 2 * PAST, 64)
        m.update(host_constants(r))
        maps.append(m)
    return maps


def assemble(results):
    y_p = np.zeros((4, 2048, D), np.float32); y_s = np.zeros((16, 16, D), np.float32)
    fk_p = np.zeros((NL, 4, 2048, H, 128), np.float32); fv_p = np.zeros_like(fk_p)
    fl_p = np.zeros((NL, 4, 2048, H), np.float32); ck_p = np.zeros((NL, 4, 2048, 512), np.float32)
    kp_p = np.zeros((NL, 4, 2048, 64), np.float32)
    fk_s = np.zeros((NL, 16, 16, H, 128), np.float32); fv_s = np.zeros_like(fk_s)
    fl_s = np.zeros((NL, 16, 16, H), np.float32); ck_s = np.zeros((NL, 16, 16, 512), np.float32)
    kp_s = np.zeros((NL, 16, 16, 64), np.float32)
    for c in range(8):
        p, r = c // 2, c % 2
        R = results[c]
        for i in range(8):
            g = gblock(r, i)
            sl = slice(g * 128, (g + 1) * 128)
            y_p[p, sl] = R["yp"][i * 128:(i + 1) * 128]
            for l in range(NL):
                rows = slice(l * TP + i * 128, l * TP + (i + 1) * 128)
                fk_p[l, p, sl] = R["nfk_p"][rows].reshape(128, H, 128)
                fv_p[l, p, sl] = R["nfv_p"][rows].reshape(128, H, 128)
                fl_p[l, p, sl] = R["nfl_p"][rows]
                ck_p[l, p, sl] = R["nck_p"][rows]
                kp_p[l, p, sl] = R["nkp_p"][rows]
        y_s[2 * c:2 * c + 2] = R["ys"].reshape(2, 16, D)
        for l in range(NL):
            rows = slice(l * TS, (l + 1) * TS)
            fk_s[l, 2 * c:2 * c + 2] = R["nfk_s"][rows].reshape(2, 16, H, 128)
            fv_s[l, 2 * c:2 * c + 2] = R["nfv_s"][rows].reshape(2, 16, H, 128)
            fl_s[l, 2 * c:2 * c + 2] = R["nfl_s"][rows].reshape(2, 16, H)
            ck_s[l, 2 * c:2 * c + 2] = R["nck_s"][rows].reshape(2, 16, 512)
            kp_s[l, 2 * c:2 * c + 2] = R["nkp_s"][rows].reshape(2, 16, 64)
    return (y_p, y_s, fk_p, fv_p, fl_p, ck_p, kp_p, fk_s, fv_s, fl_s, ck_s, kp_s)


def kernel(**inputs):
    nc = build_program()
    maps = make_in_maps(inputs)
    res = run_bass_kernel_spmd(nc, maps, core_ids=list(range(8)))
    return assemble(res.results)
```
